# Optimizing a Trainium2 kernel written in Bass

```python
import math
import jax
import jax.numpy as jnp
from jax import lax
import numpy as np

D_MODEL = 1024
BATCH = 8
SEQ = 4096
DEPTH = 4

GRID_W = 64
CTX_LEN = 256
EPS = 1e-6

BRANCH_W = 512
N_BRANCH = 3

ATT_HEADS = 4
ATT_QK_DIM = 64
ATT_V_DIM = 2 * ATT_QK_DIM
ATT_WIDTH = ATT_HEADS * ATT_V_DIM
ROPE_BASE = 10000.0
Q_BLOCK = 128

LRU_WIDTH = BRANCH_W
LRU_BLOCKS = 8
LRU_BLOCK = LRU_WIDTH // LRU_BLOCKS
LRU_C = 8.0
CONV_W = 4
CONV_PAD = (2, 1)

S5_WIDTH = BRANCH_W
S5_GROUP = 16
S5_GROUPS = S5_WIDTH // S5_GROUP
S5_STATE = 64

SPLITS = (ATT_HEADS * 2 * ATT_QK_DIM, ATT_HEADS * 2 * ATT_QK_DIM, ATT_WIDTH, ATT_WIDTH,
          LRU_WIDTH, LRU_WIDTH, S5_WIDTH, S5_WIDTH, N_BRANCH * D_MODEL)
IN_DIM = sum(SPLITS)

kernel_name = 'hybrid_diffattn_rglru_s5_prefix_block'


def _split_points(sizes):
    pts, acc = [], 0
    for s in sizes[:-1]:
        acc += s
        pts.append(acc)
    return pts


def rmsnorm(x, g):
    xf = x.astype(jnp.float32)
    y = xf * lax.rsqrt(jnp.mean(jnp.square(xf), axis=-1, keepdims=True) + EPS)
    return (y * g.astype(jnp.float32)).astype(x.dtype)


def ada_mod(cond, w_mod, b_mod):
    m = jax.nn.silu(cond) @ w_mod + b_mod
    return jnp.split(m, 3, axis=-1)


def axial_rope(n_tokens):
    rows = n_tokens // GRID_W
    r = jnp.repeat(jnp.arange(rows, dtype=jnp.float32), GRID_W)
    col = jnp.tile(jnp.arange(GRID_W, dtype=jnp.float32), rows)
    n_freq = ATT_QK_DIM // 4
    inv = ROPE_BASE ** (-jnp.arange(n_freq, dtype=jnp.float32) / n_freq)
    ang = jnp.concatenate([r[:, None] * inv, col[:, None] * inv], axis=-1)
    return jnp.cos(ang), jnp.sin(ang)


def apply_rope(x, cos, sin):
    cos = cos[:, None, None, :]
    sin = sin[:, None, None, :]
    x1 = x[..., 0::2].astype(jnp.float32)
    x2 = x[..., 1::2].astype(jnp.float32)
    out = jnp.stack([x1 * cos - x2 * sin, x1 * sin + x2 * cos], axis=-1).reshape(x.shape)
    return out.astype(x.dtype)


def diff_attend(q, k, v, lam):
    s = jnp.einsum('bqhcd,bkhcd->bhcqk', q, k, preferred_element_type=jnp.float32) * (ATT_QK_DIM ** -0.5)
    p = jax.nn.softmax(s, axis=-1)
    w = p[:, :, 0] - lam * p[:, :, 1]
    return jnp.einsum('bhqk,bkhd->bqhd', w.astype(v.dtype), v)


def diff_attention_branch(q_l, k_l, v_l, q_c, k_c, v_c, lam_qk, subln_g, lam_init, need_ctx):
    b, t = q_l.shape[:2]
    lq = lam_qk.astype(jnp.float32)
    lam = jnp.exp(jnp.sum(lq[0] * lq[1])) - jnp.exp(jnp.sum(lq[2] * lq[3])) + lam_init
    cos, sin = axial_rope(t)
    q_l = apply_rope(q_l, cos, sin)
    k_l = apply_rope(k_l, cos, sin)
    k_all = jnp.concatenate([k_l, k_c], axis=1)
    v_all = jnp.concatenate([v_l, v_c], axis=1)
    n_blk = t // Q_BLOCK
    q_blocks = jnp.moveaxis(q_l.reshape(b, n_blk, Q_BLOCK, *q_l.shape[2:]), 1, 0)
    o = lax.map(lambda qb: diff_attend(qb, k_all, v_all, lam), q_blocks)
    o_l = jnp.moveaxis(o, 0, 1).reshape(b, t, ATT_HEADS, ATT_V_DIM)

    def post(o_):
        return (rmsnorm(o_, subln_g) * (1.0 - lam_init)).reshape(*o_.shape[:2], ATT_WIDTH)

    y_c = post(diff_attend(q_c, k_c, v_c, lam)) if need_ctx else None
    return post(o_l), y_c


def depthwise_conv(x, w, b):
    y = lax.conv_general_dilated(x, w[:, None, :].astype(x.dtype), (1,), [CONV_PAD],
                                 dimension_numbers=('NWC', 'WIO', 'NWC'),
                                 feature_group_count=x.shape[-1])
    return y + b


def block_diag(x, w, b):
    xb = x.reshape(*x.shape[:-1], LRU_BLOCKS, LRU_BLOCK)
    return jnp.einsum('btnc,ncd->btnd', xb, w).reshape(x.shape) + b


def rglru_coeffs(x, w_a, b_a, w_x, b_x, lam):
    r = jax.nn.sigmoid(block_diag(x, w_a, b_a).astype(jnp.float32))
    i = jax.nn.sigmoid(block_diag(x, w_x, b_x).astype(jnp.float32))
    log_a = -LRU_C * r * jax.nn.softplus(-lam.astype(jnp.float32))
    a = jnp.exp(log_a)
    bterm = jnp.sqrt(-jnp.expm1(2.0 * log_a)) * (i * x.astype(jnp.float32))
    return a, bterm


def _lin_combine(e1, e2):
    a1, b1 = e1
    a2, b2 = e2
    return a1 * a2, a2 * b1 + b2


def linear_scan(a, b, h0, reverse):
    if h0 is not None:
        first = -1 if reverse else 0
        b = b.at[:, first].add(a[:, first] * h0)
    return lax.associative_scan(_lin_combine, (a, b), axis=1, reverse=reverse)[1]


def rglru_branch(x_l, x_c, conv_w, conv_b, w_a, b_a, w_x, b_x, lam):
    u_l = depthwise_conv(x_l, conv_w, conv_b)
    u_c = depthwise_conv(x_c, conv_w, conv_b)
    hs_l, hs_c = [], []
    for d, rev in enumerate((False, True)):
        a_c, b_c = rglru_coeffs(u_c, w_a[d], b_a[d], w_x[d], b_x[d], lam[d])
        h_c = linear_scan(a_c, b_c, None, rev)
        h0 = h_c[:, 0] if rev else h_c[:, -1]
        a_l, b_l = rglru_coeffs(u_l, w_a[d], b_a[d], w_x[d], b_x[d], lam[d])
        hs_l.append(linear_scan(a_l, b_l, h0, rev))
        hs_c.append(h_c)
    return (hs_l[0] + hs_l[1]).astype(x_l.dtype), (hs_c[0] + hs_c[1]).astype(x_c.dtype)


def s5_discretise(lam_re, lam_im, log_dt, b_re, b_im):
    dt = jnp.exp(log_dt.astype(jnp.float32))[:, None]
    lr = lam_re.astype(jnp.float32)
    li = lam_im.astype(jnp.float32)
    mag = jnp.exp(lr * dt)
    ar = mag * jnp.cos(li * dt)
    ai = mag * jnp.sin(li * dt)
    den = lr * lr + li * li
    nr = ar - 1.0
    cr = (nr * lr + ai * li) / den
    ci = (ai * lr - nr * li) / den
    br = cr[..., None] * b_re - ci[..., None] * b_im
    bi = cr[..., None] * b_im + ci[..., None] * b_re
    return ar, ai, br, bi


def _cplx_combine(e1, e2):
    ar1, ai1, br1, bi1 = e1
    ar2, ai2, br2, bi2 = e2
    return (ar1 * ar2 - ai1 * ai2, ar1 * ai2 + ai1 * ar2,
            ar2 * br1 - ai2 * bi1 + br2, ar2 * bi1 + ai2 * br1 + bi2)


def complex_scan(ar, ai, br, bi, h0, reverse):
    if h0 is not None:
        h0r, h0i = h0
        first = -1 if reverse else 0
        br = br.at[:, first].add(ar * h0r - ai * h0i)
        bi = bi.at[:, first].add(ar * h0i + ai * h0r)
    a_r = jnp.broadcast_to(ar, br.shape)
    a_i = jnp.broadcast_to(ai, br.shape)
    _, _, hr, hi = lax.associative_scan(_cplx_combine, (a_r, a_i, br, bi), axis=1, reverse=reverse)
    return hr, hi


def s5_branch(u_l, u_c, lam_re, lam_im, log_dt, b_re, b_im, c_re, c_im, d_skip, w_glu, b_glu, need_ctx):
    def drive(u, bbr, bbi):
        ug = u.reshape(*u.shape[:2], S5_GROUPS, S5_GROUP).astype(jnp.float32)
        return jnp.einsum('btgh,gph->btgp', ug, bbr), jnp.einsum('btgh,gph->btgp', ug, bbi)

    def readout(u, hr, hi):
        y = jnp.einsum('btgp,ghp->btgh', hr, c_re) - jnp.einsum('btgp,ghp->btgh', hi, c_im)
        y = y.reshape(u.shape).astype(u.dtype) + d_skip * u
        g = jax.nn.gelu(y)
        return g * jax.nn.sigmoid(g @ w_glu + b_glu)

    hl_r, hl_i, hc_r, hc_i = [], [], [], []
    for d, rev in enumerate((False, True)):
        ar, ai, bbr, bbi = s5_discretise(lam_re[d], lam_im[d], log_dt[d], b_re[d], b_im[d])
        cr_, ci_ = complex_scan(ar, ai, *drive(u_c, bbr, bbi), None, rev)
        idx = 0 if rev else -1
        lr_, li_ = complex_scan(ar, ai, *drive(u_l, bbr, bbi), (cr_[:, idx], ci_[:, idx]), rev)
        hl_r.append(lr_)
        hl_i.append(li_)
        hc_r.append(cr_)
        hc_i.append(ci_)
    y_l = readout(u_l, hl_r[0] + hl_r[1], hl_i[0] + hl_i[1])
    y_c = readout(u_c, hc_r[0] + hc_r[1], hc_i[0] + hc_i[1]) if need_ctx else None
    return y_l, y_c


def gated_merge(ya, yr, ys, za, zr, zs, gl, w_branch, w_out):
    y = jnp.stack([ya * jax.nn.silu(za), yr * jax.nn.silu(zr), ys * jax.nn.silu(zs)], axis=2)
    yb = jnp.einsum('btnw,nwd->btnd', y, w_branch)
    g = jax.nn.sigmoid(gl.reshape(*gl.shape[:2], N_BRANCH, D_MODEL))
    return jnp.sum(g * yb, axis=2) @ w_out


def _qk_heads(t):
    return t.reshape(*t.shape[:2], ATT_HEADS, 2, ATT_QK_DIM)


def _v_heads(t):
    return t.reshape(*t.shape[:2], ATT_HEADS, ATT_V_DIM)


def hybrid_layer(x, ctx, c, c_ctx, p, layer_idx, need_ctx):
    lam_init = 0.8 - 0.6 * math.exp(-0.3 * layer_idx)
    sh_x, sc_x, gt_x = ada_mod(c, p['w_mod'], p['b_mod'])
    sh_c, sc_c, gt_c = ada_mod(c_ctx, p['w_mod'], p['b_mod'])
    h_l = rmsnorm(x, p['norm_g']) * (1.0 + sc_x[:, None]) + sh_x[:, None]
    h_c = rmsnorm(ctx, p['norm_g']) * (1.0 + sc_c) + sh_c
    pts = _split_points(SPLITS)
    q_l, k_l, v_l, za_l, xr_l, zr_l, us_l, zs_l, gl_l = jnp.split(h_l @ p['w_in'], pts, axis=-1)
    q_c, k_c, v_c, za_c, xr_c, zr_c, us_c, zs_c, gl_c = jnp.split(h_c @ p['w_in'], pts, axis=-1)

    ya_l, ya_c = diff_attention_branch(_qk_heads(q_l), _qk_heads(k_l), _v_heads(v_l),
                                       _qk_heads(q_c), _qk_heads(k_c), _v_heads(v_c),
                                       p['lam_qk'], p['subln_g'], lam_init, need_ctx)
    yr_l, yr_c = rglru_branch(xr_l, xr_c, p['conv_w'], p['conv_b'], p['lru_wa'], p['lru_ba'],
                              p['lru_wx'], p['lru_bx'], p['lru_lam'])
    ys_l, ys_c = s5_branch(us_l, us_c, p['s5_lam_re'], p['s5_lam_im'], p['s5_log_dt'], p['s5_b_re'],
                           p['s5_b_im'], p['s5_c_re'], p['s5_c_im'], p['s5_d'], p['s5_w_glu'],
                           p['s5_b_glu'], need_ctx)

    x = x + gt_x[:, None] * gated_merge(ya_l, yr_l, ys_l, za_l, zr_l, zs_l, gl_l, p['w_branch'], p['w_out'])
    if need_ctx:
        ctx = ctx + gt_c * gated_merge(ya_c, yr_c, ys_c, za_c, zr_c, zs_c, gl_c, p['w_branch'], p['w_out'])
    return x, ctx


def setup_inputs(seed: int = 0) -> dict:
    key = jax.random.key(seed)
    ks = jax.random.split(key, 32)
    f32 = jnp.float32
    L = DEPTH

    def nrm(k, shape, s):
        return jax.random.normal(k, shape, f32) * s

    a_c = jax.random.uniform(ks[16], (L, 2, LRU_WIDTH), f32, minval=0.9, maxval=0.999)
    a_base = a_c ** (1.0 / LRU_C)
    return {
        'x': nrm(ks[0], (BATCH, SEQ, D_MODEL), 1.0),
        'c': nrm(ks[1], (BATCH, D_MODEL), 1.0),
        'ctx': nrm(ks[2], (BATCH, CTX_LEN, D_MODEL), 1.0),
        'c_ctx': nrm(ks[3], (D_MODEL,), 1.0),
        'w_mod': nrm(ks[4], (L, D_MODEL, 3 * D_MODEL), 0.5 * D_MODEL ** -0.5),
        'b_mod': nrm(ks[5], (L, 3 * D_MODEL), 0.01),
        'norm_g': 1.0 + nrm(ks[6], (L, D_MODEL), 0.01),
        'w_in': nrm(ks[7], (L, D_MODEL, IN_DIM), D_MODEL ** -0.5),
        'lam_qk': nrm(ks[8], (L, 4, ATT_QK_DIM), 0.1),
        'subln_g': 1.0 + nrm(ks[9], (L, ATT_V_DIM), 0.01),
        'conv_w': nrm(ks[10], (L, CONV_W, LRU_WIDTH), CONV_W ** -0.5),
        'conv_b': nrm(ks[11], (L, LRU_WIDTH), 0.01),
        'lru_wa': nrm(ks[12], (L, 2, LRU_BLOCKS, LRU_BLOCK, LRU_BLOCK), LRU_BLOCK ** -0.5),
        'lru_ba': nrm(ks[13], (L, 2, LRU_WIDTH), 0.01),
        'lru_wx': nrm(ks[14], (L, 2, LRU_BLOCKS, LRU_BLOCK, LRU_BLOCK), LRU_BLOCK ** -0.5),
        'lru_bx': nrm(ks[15], (L, 2, LRU_WIDTH), 0.01),
        'lru_lam': jnp.log(a_base) - jnp.log1p(-a_base),
        's5_lam_re': -0.5 + nrm(ks[17], (L, 2, S5_GROUPS, S5_STATE), 0.01),
        's5_lam_im': jnp.pi * jnp.arange(S5_STATE, dtype=f32) + nrm(ks[18], (L, 2, S5_GROUPS, S5_STATE), 0.01),
        's5_log_dt': jax.random.uniform(ks[19], (L, 2, S5_GROUPS), f32,
                                        minval=math.log(1e-3), maxval=math.log(1e-1)),
        's5_b_re': nrm(ks[20], (L, 2, S5_GROUPS, S5_STATE, S5_GROUP), (2 * S5_GROUP) ** -0.5),
        's5_b_im': nrm(ks[21], (L, 2, S5_GROUPS, S5_STATE, S5_GROUP), (2 * S5_GROUP) ** -0.5),
        's5_c_re': nrm(ks[22], (L, S5_GROUPS, S5_GROUP, S5_STATE), (2 * S5_STATE) ** -0.5),
        's5_c_im': nrm(ks[23], (L, S5_GROUPS, S5_GROUP, S5_STATE), (2 * S5_STATE) ** -0.5),
        's5_d': nrm(ks[24], (L, S5_WIDTH), 1.0),
        's5_w_glu': nrm(ks[25], (L, S5_WIDTH, S5_WIDTH), S5_WIDTH ** -0.5),
        's5_b_glu': nrm(ks[26], (L, S5_WIDTH), 0.01),
        'w_branch': nrm(ks[27], (L, N_BRANCH, BRANCH_W, D_MODEL), BRANCH_W ** -0.5),
        'w_out': nrm(ks[28], (L, D_MODEL, D_MODEL), D_MODEL ** -0.5),
        'final_g': 1.0 + nrm(ks[29], (D_MODEL,), 0.01),
    }


def reference(x, c, ctx, c_ctx, w_mod, b_mod, norm_g, w_in, lam_qk, subln_g, conv_w, conv_b,
              lru_wa, lru_ba, lru_wx, lru_bx, lru_lam, s5_lam_re, s5_lam_im, s5_log_dt,
              s5_b_re, s5_b_im, s5_c_re, s5_c_im, s5_d, s5_w_glu, s5_b_glu, w_branch, w_out, final_g):
    for l in range(DEPTH):
        p = dict(w_mod=w_mod[l], b_mod=b_mod[l], norm_g=norm_g[l], w_in=w_in[l], lam_qk=lam_qk[l],
                 subln_g=subln_g[l], conv_w=conv_w[l], conv_b=conv_b[l], lru_wa=lru_wa[l],
                 lru_ba=lru_ba[l], lru_wx=lru_wx[l], lru_bx=lru_bx[l], lru_lam=lru_lam[l],
                 s5_lam_re=s5_lam_re[l], s5_lam_im=s5_lam_im[l], s5_log_dt=s5_log_dt[l],
                 s5_b_re=s5_b_re[l], s5_b_im=s5_b_im[l], s5_c_re=s5_c_re[l], s5_c_im=s5_c_im[l],
                 s5_d=s5_d[l], s5_w_glu=s5_w_glu[l], s5_b_glu=s5_b_glu[l], w_branch=w_branch[l],
                 w_out=w_out[l])
        x, ctx = hybrid_layer(x, ctx, c, c_ctx, p, l, l < DEPTH - 1)
    return rmsnorm(x, final_g)
```

```python
import math
from contextlib import ExitStack

import numpy as np
import ml_dtypes

import concourse.bass as bass
import concourse.mybir as mybir
from concourse.bass_utils import run_bass_kernel_spmd

F32 = mybir.dt.float32
BF16 = mybir.dt.bfloat16
ALU = mybir.AluOpType
AF = mybir.ActivationFunctionType
AX = mybir.AxisListType

D = 1024
SEQ = 4096
CTX = 256
T = SEQ + CTX
NT = T // 128
DEPTH = 4
EPS = 1e-6
NCOL = 8192
TG = [(i * 512, 512) for i in range(8)] + [(4096, 256)]


class Buf:
    __slots__ = ("name", "lw", "rd")

    def __init__(self, name=""):
        self.name = name
        self.lw = None
        self.rd = {}


class Sched:
    ENG = ("pe", "act", "dve", "pool", "sp")

    def __init__(self, nc, stack):
        self.nc = nc
        self.stack = stack
        self.streams = {e: [] for e in self.ENG}
        self.sem = {}
        self.cnt = {}
        self.seen = {e: {} for e in self.ENG}
        for e in self.ENG:
            self.sem[e] = stack.enter_context(nc.semaphore("s_" + e))
            self.cnt[e] = 0
        self.ndma = 0
        self.free = []
        self.live = []

    def new_dma_sem(self, name, fresh=False):
        if fresh:
            key = U("f%d" % self.ndma)
            self.ndma += 1
            self.sem[key] = self.stack.enter_context(self.nc.semaphore(key))
            self.cnt[key] = 0
            return key
        if self.free:
            key = self.free.pop()
        else:
            key = U("d%d" % self.ndma)
            self.ndma += 1
            self.sem[key] = self.stack.enter_context(self.nc.semaphore(key))
            self.cnt[key] = 0
        self.live.append(key)
        return key

    def mark(self):
        return len(self.live)

    def release_to(self, mark):
        while len(self.live) > mark:
            self.free.append(self.live.pop())

    def _waits(self, eng, reads, writes):
        w = {}

        def add(tok):
            if tok is None:
                return
            k, v = tok
            if w.get(k, 0) < v:
                w[k] = v

        for b in reads:
            add(b.lw)
        for b in writes:
            add(b.lw)
            for k, v in b.rd.items():
                add((k, v))
        need = []
        seen = self.seen[eng]
        for k, v in w.items():
            if k == "pe" and eng == "pe":
                continue
            if seen.get(k, 0) < v:
                seen[k] = v
                need.append((k, v))
        return need

    def _commit(self, tok, reads, writes):
        for b in writes:
            b.lw = tok
            b.rd = {}
        k, v = tok
        for b in reads:
            if b.rd.get(k, 0) < v:
                b.rd[k] = v

    def op(self, eng, fn, reads=(), writes=()):
        if isinstance(fn, tuple):
            fn = [fn]
        need = self._waits(eng, reads, writes)
        self.cnt[eng] += 1
        tok = (eng, self.cnt[eng])
        self.streams[eng].append((need, fn, eng, 1))
        self._commit(tok, reads, writes)
        return tok

    def dma(self, eng, semkey, fns, reads=(), writes=()):
        need = self._waits(eng, reads, writes)
        for i, fn in enumerate(fns):
            self.cnt[semkey] += 16
            self.streams[eng].append((need if i == 0 else [], fn, semkey, 16))
        tok = (semkey, self.cnt[semkey])
        self._commit(tok, reads, writes)
        return tok

    def barrier(self):
        for e in self.ENG:
            need = []
            seen = self.seen[e]
            for k, v in self.cnt.items():
                if v > 0 and seen.get(k, 0) < v:
                    seen[k] = v
                    need.append((k, v))
            if need:
                self.streams[e].append((need, None, None, 0))

    def emit(self, block):
        nc = self.nc
        sems = self.sem

        def run(e, stream):
            for need, fn, semkey, inc in stream:
                for k, v in need:
                    e.wait_ge(sems[k], v)
                if fn is not None:
                    if isinstance(fn, tuple):
                        fn = [fn]
                    for m, kw in fn:
                        ins = getattr(e, m)(**kw)
                    ins.then_inc(sems[semkey], inc)

        @block.sync
        def _(e):
            run(e, self.streams["sp"])

        @block.tensor
        def _(e):
            run(e, self.streams["pe"])

        @block.scalar
        def _(e):
            run(e, self.streams["act"])

        @block.vector
        def _(e):
            run(e, self.streams["dve"])

        @block.gpsimd
        def _(e):
            run(e, self.streams["pool"])


class PsRing:
    def __init__(self, G):
        self.G = G
        self.i = 0

    def next(self):
        i = self.i
        self.i = (i + 1) % 8
        return self.G.psall[:, i * 512:(i + 1) * 512], self.G.psb[i], None


class Ring:
    def __init__(self, S, stack, name, n, shape, dtype, psum=False, dma=False, fresh=False):
        self.tiles = []
        self.bufs = []
        self.sems = []
        for i in range(n):
            nm = U("%s%d" % (name, i))
            if psum:
                t = stack.enter_context(S.nc.psum_tensor(nm, shape, dtype))
            else:
                t = stack.enter_context(S.nc.sbuf_tensor(nm, shape, dtype))
            self.tiles.append(t)
            self.bufs.append(Buf(nm))
            self.sems.append(S.new_dma_sem(nm, fresh=fresh) if dma else None)
        self.i = 0
        self.n = n

    def next(self):
        i = self.i
        self.i = (i + 1) % self.n
        return self.tiles[i], self.bufs[i], self.sems[i]


def I(m, **kw):
    return (m, kw)


_UID = [0]


def U(name):
    _UID[0] += 1
    return "%s_%d" % (name, _UID[0])


class Ctx:
    pass


def build_program(n_layers=DEPTH, debug=None, stop_after=None):
    nc = bass.Bass("TRN2", target_bir_lowering=False)
    G = Ctx()
    G.nc = nc

    def din(name, shape, dt=F32):
        return nc.dram_tensor(name, list(shape), dt, kind="ExternalInput").ap()

    def dscr(name, shape, dt):
        return nc.dram_tensor(name, list(shape), dt, kind="Internal").ap()

    G.x_in = din("x", [SEQ, D])
    G.ctx_in = din("ctx", [CTX, D])
    G.ccol = din("ccol", [128, 16])
    G.w_mod = din("w_mod", [DEPTH, D, 3 * D])
    G.b_mod = din("b_mod", [DEPTH, 1, 3 * D])
    G.gcol = din("gcol", [DEPTH, 128, 8])
    G.w_in = din("w_in", [DEPTH, D, NCOL])
    G.ropeC = din("ropeC", [128, T])
    G.ropeS = din("ropeS", [128, T])
    G.ident_in = din("ident", [128, 128], BF16)
    G.final_g = din("final_g", [1, D])
    G.lam_qk = din("lam_qk", [DEPTH, 1, 256])
    G.subln = din("subln", [DEPTH, 128, 1])
    G.lrup = din("lrup", [DEPTH, 128, 44])
    G.lruw = din("lruw", [DEPTH, 2, 2, 4, 128, 128])
    G.s5H = din("s5H", [DEPTH, 2, 4, 128, 4, 64])
    G.s5N = din("s5N", [DEPTH, 2, 4, 128, 8])
    G.s5BN = din("s5BN", [DEPTH, 2, 4, 128, 2, 4, 16])
    G.s5CN = din("s5CN", [DEPTH, 4, 128, 2, 4, 16])
    G.s5dt = din("s5dt", [DEPTH, 128, 8])
    G.s5misc = din("s5misc", [DEPTH, 128, 8])
    G.s5tab = din("s5tab", [128, 36 + 512 + 544 + 128])
    G.w_glu = din("w_glu", [DEPTH, 512, 512])
    G.w_branch = din("w_branch", [DEPTH, 3, 512, D])
    G.w_out = din("w_out", [DEPTH, D, D])
    G.out = nc.dram_tensor("out", [SEQ, D], F32, kind="ExternalOutput").ap()

    G.xres = dscr("xres", [T, D], F32)
    G.QT = dscr("QT", [512, T], BF16)
    G.KT = dscr("KT", [512, T], BF16)
    G.Vtm = dscr("Vtm", [T, 512], BF16)
    G.ZA = dscr("ZA", [512, T], BF16)
    G.ZR = dscr("ZR", [512, T], BF16)
    G.ZS = dscr("ZS", [512, T], BF16)
    G.XR = dscr("XR", [512, T], F32)
    G.US = dscr("US", [512, T], F32)
    G.SG = dscr("SG", [3072, T], BF16)
    G.GS5 = dscr("GS5", [512, T], BF16)
    G.S5W1 = dscr("S5W1", [8, 128, 8192], BF16)
    G.S5W3 = dscr("S5W3", [8, 128, 8192], BF16)
    G.S5XB = dscr("S5XB", [8, 128, 8192], BF16)
    G.S5TB = dscr("S5TB", [8, 128, 2, 4, T // 8], F32)
    G.S5RH = dscr("S5RH", [8, 128, 4], F32)
    G.s5_items = None
    G.YG = dscr("YG", [3, 512, T], BF16)

    G.dbg = {}
    if debug:
        for name, shape, dt in debug:
            G.dbg[name] = nc.dram_tensor("dbg_" + name, list(shape), dt, kind="ExternalOutput").ap()

    with ExitStack() as top:
        S = Sched(nc, top)
        G.S = S
        sb = lambda name, shape, dt: top.enter_context(nc.sbuf_tensor(U(name), list(shape), dt))

        G.ident = sb("ident", [128, 128], BF16)
        G.ones_f = sb("ones_f", [128, 128], F32)
        G.ones_b = sb("ones_b", [128, 128], BF16)
        G.ccol_sb = sb("ccol_sb", [128, 16], F32)
        G.silu_c = sb("silu_c", [128, 16], F32)
        G.eps_col = sb("eps_col", [128, 1], F32)
        G.b_const = Buf("const")
        csem = S.new_dma_sem("const")
        S.dma("sp", csem, [I("dma_start", out=G.ident[:], in_=G.ident_in[:, :]),
                           I("dma_start", out=G.ccol_sb[:], in_=G.ccol[:, :])], writes=[G.b_const])
        S.op("pool", I("memset", ap=G.ones_f[:], constant=1.0), writes=[G.b_const])
        S.op("pool", I("memset", ap=G.ones_b[:], constant=1.0), writes=[G.b_const])
        S.op("pool", I("memset", ap=G.eps_col[:], constant=EPS), writes=[G.b_const])
        S.op("act", I("activation", out=G.silu_c[:], in_=G.ccol_sb[:], func=AF.Silu),
             reads=[G.b_const], writes=[G.b_const])

        G.psall = top.enter_context(nc.psum_tensor(U("psall"), [128, 4096], F32))
        G.psb = [Buf("psb%d" % i) for i in range(8)]
        G.PS = PsRing(G)

        G.xres_b = [Buf("xres%d" % i) for i in range(NT)]
        G.scr_b = {}
        G.gtbc = sb("gtbc", [128, 2, D], F32)
        G.gtbc_b = Buf("gtbc")

        for l in range(n_layers):
            phase1_and_2(G, l, stop_after)
            if stop_after in ("p1", "p2"):
                break
            if stop_after not in ("p4only", "p5only"):
                phase3_attn(G, l)
            if stop_after == "p3":
                break
            if stop_after != "p5only":
                phase4_lru(G, l)
            if stop_after in ("p4", "p4only"):
                break
            phase5_s5(G, l)
            if stop_after in ("p5", "p5only"):
                break
            phase6_merge(G, l)
            if stop_after is not None:
                break

        S.barrier()
        with nc.Block() as block:
            S.emit(block)
    return nc


def phase3_attn(G, l):
    nc, S = G.nc, G.S
    lam_init = 0.8 - 0.6 * math.exp(-0.3 * l)
    need_ctx = l < DEPTH - 1
    ps = G.psall
    psb = G.psb
    mk = S.mark()
    with ExitStack() as p3:
        sbt = lambda name, shape, dt: p3.enter_context(nc.sbuf_tensor(U(name), list(shape), dt))
        KTs = sbt("KTs", [128, 4, T], BF16)
        KT_b = [Buf("KTs%d" % h) for h in range(4)]
        Vs = sbt("Vs", [128, NT, 512], BF16)
        V_b = Buf("Vs")
        lq = sbt("lq", [1, 4, 64], F32)
        lw = sbt("lw", [1, 8], F32)
        prm = sbt("prm", [128, 4], F32)
        prm_b = Buf("prm")
        for h in range(4):
            ksem = S.new_dma_sem("kt%d" % h)
            S.dma("sp", ksem, [I("dma_start", out=KTs[:, h, :], in_=G.KT[h * 128:(h + 1) * 128, :])],
                  reads=[scr_buf(G, "KT", h)], writes=[KT_b[h]])
        vsem = S.new_dma_sem("v")
        S.dma("sp", vsem, [I("dma_start", out=Vs[:, i * 17:(i + 1) * 17, :],
                             in_=G.Vtm[i * 17 * 128:(i + 1) * 17 * 128, :].rearrange("(t p) n -> p t n", p=128))
                           for i in range(2)],
              reads=[scr_buf(G, "V", ti) for ti in range(NT)], writes=[V_b])
        psem = S.new_dma_sem("prm")
        S.dma("sp", psem, [I("dma_start", out=lq[:].rearrange("a b c -> a (b c)"), in_=G.lam_qk[l, :, :]),
                           I("dma_start", out=prm[:, 1:2], in_=G.subln[l, :, :])], writes=[prm_b])
        S.op("dve", I("tensor_tensor", out=lq[0:1, 0::2, :], in0=lq[0:1, 0::2, :], in1=lq[0:1, 1::2, :], op=ALU.mult),
             reads=[prm_b], writes=[prm_b])
        S.op("dve", I("reduce_sum", out=lw[0:1, 0:2], in_=lq[0:1, 0::2, :], axis=AX.X), reads=[prm_b], writes=[prm_b])
        S.op("act", I("activation", out=lw[0:1, 2:4], in_=lw[0:1, 0:2], func=AF.Exp), reads=[prm_b], writes=[prm_b])
        S.op("dve", I("tensor_tensor", out=lw[0:1, 4:5], in0=lw[0:1, 3:4], in1=lw[0:1, 2:3], op=ALU.subtract),
             reads=[prm_b], writes=[prm_b])
        S.op("dve", I("tensor_scalar", out=lw[0:1, 5:6], in0=lw[0:1, 4:5], scalar1=-lam_init, scalar2=None, op0=ALU.add),
             reads=[prm_b], writes=[prm_b])
        S.op("pe", I("matmul", out=ps[:, 0:1], lhsT=G.ones_f[0:1, :], rhs=lw[0:1, 5:6], start=True, stop=True),
             reads=[prm_b, G.b_const], writes=[psb[0]])
        S.op("dve", I("tensor_copy", out=prm[:, 0:1], in_=ps[:, 0:1]), reads=[psb[0]], writes=[prm_b])
        S.op("dve", I("tensor_scalar", out=prm[:, 1:2], in0=prm[:, 1:2], scalar1=1.0 - lam_init, scalar2=None, op0=ALU.mult),
             reads=[prm_b], writes=[prm_b])

        QR = Ring(S, p3, "qr", 3, [128, 512], BF16, dma=True)
        ZR_ = Ring(S, p3, "za", 3, [128, 512], BF16, dma=True)
        PT = Ring(S, p3, "pT", 3, [128, 1024], BF16)
        WK = Ring(S, p3, "wk", 2, [128, 4, 512], F32)
        SQ = Ring(S, p3, "sq", 2, [128, 512], BF16)
        YO = Ring(S, p3, "yo", 2, [128, 512], BF16, dma=True, fresh=True)
        ACC = Ring(S, p3, "acc", 2, [128, 512], F32)
        ACC1 = Ring(S, p3, "acc1", 2, [128, 512], F32)
        sc_i = [0]

        groups = [(t0, tn, list(range(NT))) for (t0, tn) in TG[:8]]
        if need_ctx:
            groups.append((4096, 256, [32, 33]))
        heads = []
        for (t0, tn, ktiles) in groups:
            for h in range(4):
                heads.append(dict(t0=t0, tn=tn, kts=ktiles, h=h))
        steps = []
        for hi, hd in enumerate(heads):
            for ki, kt in enumerate(hd["kts"]):
                steps.append((hi, ki, kt))

        def emit_loads(hd):
            t0, tn, h = hd["t0"], hd["tn"], hd["h"]
            qt, qtb, qts = QR.next()
            S.dma("sp", qts, [I("dma_start", out=qt[:, 0:tn], in_=G.QT[h * 128:(h + 1) * 128, t0:t0 + tn])],
                  reads=[scr_buf(G, "QT", h)], writes=[qtb])
            za, zab, zas = ZR_.next()
            S.dma("sp", zas, [I("dma_start", out=za[:, 0:tn], in_=G.ZA[h * 128:(h + 1) * 128, t0:t0 + tn])],
                  reads=[scr_buf(G, "ZA", h)], writes=[zab])
            hd.update(qt=qt, qtb=qtb, za=za, zab=zab)

        def emit_S(step):
            hi, ki, kt = step
            hd = heads[hi]
            tn, h = hd["tn"], hd["h"]
            sb0 = (sc_i[0] % 2) * 2
            sc_i[0] += 1
            sc = ps[:, sb0 * 512:(sb0 + 2) * 512]
            scb = [psb[sb0], psb[sb0 + 1]]
            S.op("pe", [I("matmul", out=sc[:, c * 512:c * 512 + tn], lhsT=KTs[c * 64:(c + 1) * 64, h, kt * 128:(kt + 1) * 128],
                          rhs=hd["qt"][c * 64:(c + 1) * 64, 0:tn], start=True, stop=True) for c in range(2)],
                 reads=[KT_b[h], hd["qtb"]], writes=scb)
            return sc, scb

        def emit_exp_pv(step, sc, scb):
            hi, ki, kt = step
            hd = heads[hi]
            tn, h = hd["tn"], hd["h"]
            nk = len(hd["kts"])
            pT, pTb, _ = PT.next()
            if tn == 512:
                S.op("act", I("activation", out=pT[:, :], in_=sc[:, :], func=AF.Exp, scale=0.125), reads=scb, writes=[pTb])
            else:
                S.op("act", I("activation", out=pT[:].rearrange("p (c n) -> p c n", c=2)[:, :, 0:tn],
                              in_=sc.rearrange("p (c n) -> p c n", c=2)[:, :, 0:tn], func=AF.Exp, scale=0.125),
                     reads=scb, writes=[pTb])
            mm = []
            for c in range(2):
                mm.append(I("matmul", out=ps[:, (4 + 2 * c) * 512:(4 + 2 * c) * 512 + tn],
                            lhsT=Vs[:, kt, h * 128:(h + 1) * 128], rhs=pT[:, c * 512:c * 512 + tn],
                            start=(ki == 0), stop=(ki == nk - 1)))
            mm.append(I("matmul", out=ps[:, 7 * 512:7 * 512 + tn], lhsT=G.ones_b[:, :], rhs=pT[:, 512:512 + tn],
                        start=(ki == 0), stop=(ki == nk - 1)))
            mm.append(I("matmul", out=ps[:, 5 * 512:5 * 512 + tn], lhsT=G.ones_b[:, :], rhs=pT[:, 0:tn],
                        start=(ki == 0), stop=(ki == nk - 1)))
            S.op("pe", mm, reads=[V_b, pTb, G.b_const], writes=[psb[4], psb[5], psb[6], psb[7]])

        def emit_combine(hd):
            t0, tn, h = hd["t0"], hd["tn"], hd["h"]
            wk, wkb, _ = WK.next()
            S.op("dve", I("tensor_copy", out=wk[:, 0, 0:tn], in_=ps[:, 4 * 512:4 * 512 + tn]), reads=[psb[4]], writes=[wkb])
            S.op("dve", I("tensor_copy", out=wk[:, 1, 0:tn], in_=ps[:, 6 * 512:6 * 512 + tn]), reads=[psb[6]], writes=[wkb])
            S.op("dve", I("tensor_copy", out=wk[:, 3, 0:tn], in_=ps[:, 7 * 512:7 * 512 + tn]), reads=[psb[7]], writes=[wkb])
            S.op("dve", I("tensor_copy", out=wk[:, 2, 0:tn], in_=ps[:, 5 * 512:5 * 512 + tn]), reads=[psb[5]], writes=[wkb])
            S.op("dve", I("reciprocal", out=wk[:, 2, 0:tn], in_=wk[:, 2, 0:tn]), reads=[wkb], writes=[wkb])
            S.op("dve", I("tensor_tensor", out=wk[:, 0, 0:tn], in0=wk[:, 0, 0:tn], in1=wk[:, 2, 0:tn], op=ALU.mult),
                 reads=[wkb], writes=[wkb])
            S.op("dve", I("reciprocal", out=wk[:, 3, 0:tn], in_=wk[:, 3, 0:tn]), reads=[wkb], writes=[wkb])
            S.op("dve", I("tensor_tensor", out=wk[:, 1, 0:tn], in0=wk[:, 1, 0:tn], in1=wk[:, 3, 0:tn], op=ALU.mult),
                 reads=[wkb], writes=[wkb])
            S.op("dve", I("scalar_tensor_tensor", out=wk[:, 2, 0:tn], in0=wk[:, 1, 0:tn], scalar=prm[:, 0:1],
                          in1=wk[:, 0, 0:tn], op0=ALU.mult, op1=ALU.add), reads=[wkb, prm_b], writes=[wkb])
            sq, sqb, _ = SQ.next()
            S.op("pool", I("tensor_tensor", out=sq[:, 0:tn], in0=wk[:, 2, 0:tn], in1=wk[:, 2, 0:tn], op=ALU.mult),
                 reads=[wkb], writes=[sqb])
            hd.update(wk=wk, wkb=wkb, sq=sq, sqb=sqb)

        def emit_finish(hd, slot):
            t0, tn, h = hd["t0"], hd["tn"], hd["h"]
            za, zab, wk, wkb, sq, sqb = hd["za"], hd["zab"], hd["wk"], hd["wkb"], hd["sq"], hd["sqb"]
            sc_, scb_ = slot
            S.op("pe", I("matmul", out=sc_[:, 0:tn], lhsT=G.ones_b[:, :], rhs=sq[:, 0:tn],
                         start=True, stop=True), reads=[sqb, G.b_const], writes=[scb_[0]])
            S.op("act", I("activation", out=wk[:, 3, 0:tn], in_=sc_[:, 0:tn], func=AF.Ln,
                          scale=1.0 / 128, bias=G.eps_col[:, 0:1]), reads=[scb_[0], G.b_const], writes=[wkb])
            S.op("act", I("activation", out=wk[:, 3, 0:tn], in_=wk[:, 3, 0:tn], func=AF.Exp, scale=-0.5),
                 reads=[wkb], writes=[wkb])
            S.op("dve", I("tensor_tensor", out=wk[:, 2, 0:tn], in0=wk[:, 2, 0:tn], in1=wk[:, 3, 0:tn], op=ALU.mult),
                 reads=[wkb], writes=[wkb])
            yo, yob, yos = YO.next()
            S.op("dve", I("scalar_tensor_tensor", out=yo[:, 0:tn], in0=wk[:, 2, 0:tn], scalar=prm[:, 1:2],
                          in1=za[:, 0:tn], op0=ALU.mult, op1=ALU.mult), reads=[wkb, prm_b, zab], writes=[yob])
            S.dma("pool", yos, [I("dma_start", out=G.YG[0, h * 128:(h + 1) * 128, t0:t0 + tn], in_=yo[:, 0:tn])],
                  reads=[yob], writes=[scr_buf(G, "YG0", h)])

        items = s5_prep(G, l, p3)
        G.s5_items = items
        per_step = -(-len(items) // max(1, len(steps) - 8))
        emit_loads(heads[0])
        if len(heads) > 1:
            emit_loads(heads[1])
        cur = emit_S(steps[0])
        pending_finish = None
        for i, step in enumerate(steps):
            hi, ki, kt = step
            nxt = None
            if i + 1 < len(steps):
                nhi = steps[i + 1][0]
                if "qt" not in heads[nhi]:
                    emit_loads(heads[nhi])
                nxt = emit_S(steps[i + 1])
            emit_exp_pv(step, cur[0], cur[1])
            replay(S, items, per_step)
            last_k = (ki == len(heads[hi]["kts"]) - 1)
            if pending_finish is not None and (ki == 5 or last_k):
                emit_finish(pending_finish, cur)
                nl = pending_finish["idx"] + 2
                if nl < len(heads) and "qt" not in heads[nl]:
                    emit_loads(heads[nl])
                pending_finish = None
            if last_k:
                emit_combine(heads[hi])
                heads[hi]["idx"] = hi
                pending_finish = heads[hi]
            cur = nxt
        if pending_finish is not None:
            emit_finish(pending_finish, (ps[:, 0:1024], [psb[0], psb[1]]))
            pending_finish = None
        replay(S, items, len(items))
        if "YG" in G.dbg:
            S.barrier()
            dsem = S.new_dma_sem("dbg")
            S.dma("sp", dsem, [I("dma_start", out=G.dbg["YG"][:, :, :], in_=G.YG[:, :, :])])
        S.barrier()
        S.release_to(mk)


def phase4_lru(G, l):
    nc, S = G.nc, G.S
    ps, psb = G.psall, G.psb
    mk = S.mark()
    SEGS = [(0, SEQ), (SEQ, CTX)]
    CH = [(i * 1024, 1024) for i in range(4)] + [(4096, 256)]
    with ExitStack() as p4:
        sbt = lambda name, shape, dt: p4.enter_context(nc.sbuf_tensor(U(name), list(shape), dt))
        prm = sbt("lprm", [128, 44], F32)
        sp8 = sbt("sp8", [128, 8], F32)
        one_col = sbt("one_col", [128, 1], F32)
        prm_b = Buf("lprm")
        wf = sbt("lwf", [128, 16, 128], F32)
        wb = sbt("lwb", [128, 16, 128], BF16)
        w_b = Buf("lw")
        psem = S.new_dma_sem("lprm")
        S.dma("sp", psem, [I("dma_start", out=prm[:], in_=G.lrup[l, :, :]),
                           I("dma_start", out=wf[:], in_=G.lruw[l].rearrange("a d c p n -> p (a d c) n"))],
              writes=[prm_b, w_b])
        S.op("pool", I("tensor_copy", out=wb[:], in_=wf[:]), reads=[w_b], writes=[w_b])
        S.op("pool", I("memset", ap=one_col[:], constant=1.0), writes=[prm_b])
        S.op("act", I("activation", out=sp8[:], in_=prm[:, 36:44], func=AF.Exp, scale=-1.0), reads=[prm_b], writes=[prm_b])
        S.op("act", I("activation", out=sp8[:], in_=sp8[:], func=AF.Ln, bias=one_col[:, 0:1]), reads=[prm_b], writes=[prm_b])
        S.op("dve", I("tensor_scalar", out=sp8[:], in0=sp8[:], scalar1=-8.0, scalar2=None, op0=ALU.mult),
             reads=[prm_b], writes=[prm_b])

        xr = sbt("xr", [128, T], F32); xr_b = Buf("xr")
        u = sbt("u", [128, T], F32); u_b = Buf("u")
        ub = sbt("ub", [128, T], BF16); ub_b = Buf("ub")
        ra = sbt("ra", [128, T], F32); ra_b = Buf("ra")
        ib = sbt("ib", [128, T], F32); ib_b = Buf("ib")
        tmp = sbt("ltmp", [128, T], F32); tmp_b = Buf("ltmp")
        hf = sbt("hf", [128, T], F32); hf_b = Buf("hf")
        hbr = sbt("hbr", [128, T], F32); hbr_b = Buf("hbr")
        zr = sbt("zr", [128, T], BF16); zr_b = Buf("zr")
        yo = sbt("lyo", [128, T], BF16); yo_b = Buf("lyo")
        xsem = S.new_dma_sem("xr"); zsem = S.new_dma_sem("zr"); ysem = S.new_dma_sem("lyo", fresh=True)
        for ct in range(4):
            S.dma("sp", xsem, [I("dma_start", out=xr[:, :], in_=G.XR[ct * 128:(ct + 1) * 128, :])],
                  reads=[scr_buf(G, "XR", ct)], writes=[xr_b])
            S.dma("sp", zsem, [I("dma_start", out=zr[:, :], in_=G.ZR[ct * 128:(ct + 1) * 128, :])],
                  reads=[scr_buf(G, "ZR", ct)], writes=[zr_b])
            for (s0, n) in SEGS:
                S.op("pool", I("tensor_scalar", out=u[:, s0:s0 + n], in0=xr[:, s0:s0 + n],
                               scalar1=prm[:, ct * 4 + 2:ct * 4 + 3], scalar2=prm[:, 16 + ct:17 + ct],
                               op0=ALU.mult, op1=ALU.add), reads=[xr_b, prm_b], writes=[u_b])
                for (k, oa, ob_, ia, ib_) in ((0, 2, n, 0, n - 2), (1, 1, n, 0, n - 1), (3, 0, n - 1, 1, n)):
                    S.op("dve", I("scalar_tensor_tensor", out=u[:, s0 + oa:s0 + ob_], in0=xr[:, s0 + ia:s0 + ib_],
                                   scalar=prm[:, ct * 4 + k:ct * 4 + k + 1], in1=u[:, s0 + oa:s0 + ob_],
                                   op0=ALU.mult, op1=ALU.add), reads=[xr_b, prm_b, u_b], writes=[u_b])
            S.op("act", I("activation", out=ub[:, :], in_=u[:, :], func=AF.Copy), reads=[u_b], writes=[ub_b])
            for d in range(2):
                for ci, (c0, cn) in enumerate(CH):
                    bk = (ci % 2) * 4
                    for gi in range(2):
                        widx = gi * 8 + d * 4 + ct
                        mm = []
                        for s in range(0, cn, 512):
                            sn = min(512, cn - s)
                            mm.append(I("matmul", out=ps[:, (bk + gi * 2) * 512 + s:(bk + gi * 2) * 512 + s + sn],
                                        lhsT=wb[:, widx, :], rhs=ub[:, c0 + s:c0 + s + sn], start=True, stop=True))
                        S.op("pe", mm, reads=[w_b, ub_b], writes=[psb[bk + gi * 2], psb[bk + gi * 2 + 1]])
                        dst, dst_b = (ra, ra_b) if gi == 0 else (ib, ib_b)
                        bcol = (20 if gi == 0 else 28) + d * 4 + ct
                        S.op("act", I("activation", out=dst[:, c0:c0 + cn], in_=ps[:, (bk + gi * 2) * 512:(bk + gi * 2) * 512 + cn],
                                      func=AF.Sigmoid, bias=prm[:, bcol:bcol + 1]),
                             reads=[psb[bk + gi * 2], psb[bk + gi * 2 + 1], prm_b], writes=[dst_b])
                S.op("act", I("activation", out=ra[:, :], in_=ra[:, :], func=AF.Exp, scale=sp8[:, d * 4 + ct:d * 4 + ct + 1]),
                     reads=[ra_b, prm_b], writes=[ra_b])
                S.op("dve", I("tensor_tensor", out=tmp[:, :], in0=ra[:, :], in1=ra[:, :], op=ALU.mult), reads=[ra_b], writes=[tmp_b])
                S.op("act", I("activation", out=tmp[:, :], in_=tmp[:, :], func=AF.Sqrt, scale=-1.0, bias=one_col[:, 0:1]),
                     reads=[tmp_b, prm_b], writes=[tmp_b])
                S.op("pool", I("tensor_tensor", out=ib[:, :], in0=ib[:, :], in1=u[:, :], op=ALU.mult), reads=[ib_b, u_b], writes=[ib_b])
                S.op("dve", I("tensor_tensor", out=ib[:, :], in0=ib[:, :], in1=tmp[:, :], op=ALU.mult), reads=[ib_b, tmp_b], writes=[ib_b])
                if d == 0:
                    S.op("dve", I("tensor_tensor_scan", out=hf[:, SEQ:T], data0=ra[:, SEQ:T], data1=ib[:, SEQ:T], initial=0.0,
                                  op0=ALU.mult, op1=ALU.add), reads=[ra_b, ib_b], writes=[hf_b])
                    S.op("dve", I("tensor_tensor_scan", out=hf[:, 0:SEQ], data0=ra[:, 0:SEQ], data1=ib[:, 0:SEQ],
                                  initial=hf[:, T - 1:T], op0=ALU.mult, op1=ALU.add), reads=[ra_b, ib_b, hf_b], writes=[hf_b])
                else:
                    S.op("dve", I("tensor_tensor_scan", out=hbr[:, 0:CTX], data0=ra[:, SEQ:T][:, ::-1], data1=ib[:, SEQ:T][:, ::-1],
                                  initial=0.0, op0=ALU.mult, op1=ALU.add), reads=[ra_b, ib_b], writes=[hbr_b])
                    S.op("dve", I("tensor_tensor_scan", out=hbr[:, CTX:T], data0=ra[:, 0:SEQ][:, ::-1], data1=ib[:, 0:SEQ][:, ::-1],
                                  initial=hbr[:, CTX - 1:CTX], op0=ALU.mult, op1=ALU.add), reads=[ra_b, ib_b, hbr_b], writes=[hbr_b])
            S.op("dve", I("tensor_tensor", out=hf[:, 0:SEQ], in0=hf[:, 0:SEQ], in1=hbr[:, CTX:T][:, ::-1], op=ALU.add),
                 reads=[hf_b, hbr_b], writes=[hf_b])
            S.op("dve", I("tensor_tensor", out=hf[:, SEQ:T], in0=hf[:, SEQ:T], in1=hbr[:, 0:CTX][:, ::-1], op=ALU.add),
                 reads=[hf_b, hbr_b], writes=[hf_b])
            S.op("pool", I("tensor_tensor", out=yo[:, :], in0=hf[:, :], in1=zr[:, :], op=ALU.mult),
                 reads=[hf_b, zr_b], writes=[yo_b])
            S.dma("pool", ysem, [I("dma_start", out=G.YG[1, ct * 128:(ct + 1) * 128, :], in_=yo[:, :])],
                  reads=[yo_b], writes=[scr_buf(G, "YG1", ct)])
        if "YG" in G.dbg:
            S.barrier()
            dsem = S.new_dma_sem("dbg")
            S.dma("sp", dsem, [I("dma_start", out=G.dbg["YG"][:, :, :], in_=G.YG[:, :, :])])
        S.barrier()
        S.release_to(mk)


TWO_PI = 2.0 * math.pi
I32 = mybir.dt.int32


class Rec:
    def __init__(self, S):
        self.S = S
        self.items = []

    def new_dma_sem(self, name):
        return self.S.new_dma_sem(name)

    def op(self, eng, fn, reads=(), writes=()):
        self.items.append(("op", eng, fn, list(reads), list(writes)))

    def dma(self, eng, semkey, fns, reads=(), writes=()):
        self.items.append(("dma", eng, semkey, fns, list(reads), list(writes)))


def replay(S, items, n):
    for _ in range(n):
        if not items:
            return
        it = items.pop(0)
        if it[0] == "op":
            S.op(it[1], it[2], it[3], it[4])
        else:
            S.dma(it[1], it[2], it[3], it[4], it[5])


NCH = T // 8


def s5_prep(G, l, stack):
    nc = G.nc
    R = Rec(G.S)
    S = R
    sbt = lambda name, shape, dt: stack.enter_context(nc.sbuf_tensor(U(name), list(shape), dt))
    tab = sbt("s5tab", [128, 36 + 512 + 544 + 128], F32)
    ptab = tab[:, 0:36].rearrange("p (j q) -> p j q", q=9)
    etab = tab[:, 36:548].rearrange("p (e n) -> p e n", n=64)
    ctab = tab[:, 548:1092]
    mask = tab[:, 1092:1220].rearrange("p (g n) -> p g n", n=16)
    dtall = sbt("dtall", [128, 8], F32)
    cst_b = Buf("s5cst")
    csem = S.new_dma_sem("s5c")
    S.dma("sp", csem, [I("dma_start", out=tab[:], in_=G.s5tab[:, :]),
                       I("dma_start", out=dtall[:], in_=G.s5dt[l, :, :])], writes=[cst_b])
    S.op("act", I("activation", out=dtall[:], in_=dtall[:], func=AF.Exp), reads=[cst_b], writes=[cst_b])
    lamN = sbt("lamN", [128, 8], F32)
    BN = sbt("BN", [128, 2, 4, 16], F32)
    CN = sbt("CN", [128, 2, 4, 16], F32)
    nsem = S.new_dma_sem("s5n")
    np_b = Buf("nprm")
    n4 = sbt("n4", [128, 12, 4], F32)
    nP = sbt("nP", [128, 8, 4, 9], F32)
    nPi = sbt("nPi", [128, 4, 9], I32)
    bbN = sbt("bbN", [128, 2, 4, 16], F32)
    tN = sbt("tN", [128, 4, 4, 8, 16], F32)
    Hp = sbt("Hp", [128, 4, 64], F32)
    hsem = S.new_dma_sem("s5h")
    hp_b = Buf("hprm")
    h64 = sbt("h64", [128, 10, 64], F32)
    hE = sbt("hE", [128, 4, 8, 64], F32)
    W1d = sbt("W1d", [128, 8, 128], F32)
    tNf = tN[:].rearrange("p a b c d -> p a (b c d)")
    hEi = tNf[:, 2, :].bitcast(I32).rearrange("p (e n) -> p e n", n=64)
    OUT = [sbt("s5out%d" % i, [128, 8192], BF16) for i in range(2)]
    out_b = [Buf("s5out%d" % i) for i in range(2)]
    out_s = [S.new_dma_sem("s5o%d" % i) for i in range(2)]
    oi = [0]
    tc = sbt("tc", [128, 6, NCH], F32)
    tc_b = Buf("tc")
    tcs = S.new_dma_sem("tc")
    rsem = S.new_dma_sem("rh")

    def trig(turns, ti, tf, r, out_sin, out_cos, bufs, eng="pool"):
        for which, outp in ((0, out_sin), (1, out_cos)):
            S.op(eng, I("tensor_scalar", out=tf, in0=turns, scalar1=16.0 + 0.25 * which, scalar2=None, op0=ALU.add),
                 reads=bufs, writes=bufs)
            S.op(eng, I("tensor_copy", out=ti, in_=tf), reads=bufs, writes=bufs)
            S.op(eng, I("tensor_copy", out=r, in_=ti), reads=bufs, writes=bufs)
            S.op(eng, I("tensor_tensor", out=tf, in0=tf, in1=r, op=ALU.subtract), reads=bufs, writes=bufs)
            S.op(eng, I("tensor_single_scalar", out=r, in_=tf, scalar=0.5, op=ALU.is_ge), reads=bufs, writes=bufs)
            S.op(eng, I("tensor_tensor", out=tf, in0=tf, in1=r, op=ALU.subtract), reads=bufs, writes=bufs)
            S.op("act", I("activation", out=outp, in_=tf, func=AF.Sin, scale=TWO_PI), reads=bufs, writes=bufs)

    def cplx_coef(pr1, pi1, lr, li, t, bre, bim, obr, obi, shp_b, bufs, eng="pool", tfull=None):
        nr, den, cr, ci, t4, t5 = t
        S.op(eng, I("tensor_scalar", out=nr, in0=pr1, scalar1=-1.0, scalar2=None, op0=ALU.add), reads=bufs, writes=bufs)
        S.op(eng, I("tensor_tensor", out=den, in0=lr, in1=lr, op=ALU.mult), reads=bufs, writes=bufs)
        S.op(eng, I("tensor_tensor", out=t4, in0=li, in1=li, op=ALU.mult), reads=bufs, writes=bufs)
        S.op(eng, I("tensor_tensor", out=den, in0=den, in1=t4, op=ALU.add), reads=bufs, writes=bufs)
        S.op("dve", I("reciprocal", out=den, in_=den), reads=bufs, writes=bufs)
        S.op(eng, I("tensor_tensor", out=cr, in0=nr, in1=lr, op=ALU.mult), reads=bufs, writes=bufs)
        S.op(eng, I("tensor_tensor", out=t4, in0=pi1, in1=li, op=ALU.mult), reads=bufs, writes=bufs)
        S.op(eng, I("tensor_tensor", out=cr, in0=cr, in1=t4, op=ALU.add), reads=bufs, writes=bufs)
        S.op(eng, I("tensor_tensor", out=cr, in0=cr, in1=den, op=ALU.mult), reads=bufs, writes=bufs)
        S.op(eng, I("tensor_tensor", out=ci, in0=pi1, in1=lr, op=ALU.mult), reads=bufs, writes=bufs)
        S.op(eng, I("tensor_tensor", out=t4, in0=nr, in1=li, op=ALU.mult), reads=bufs, writes=bufs)
        S.op(eng, I("tensor_tensor", out=ci, in0=ci, in1=t4, op=ALU.subtract), reads=bufs, writes=bufs)
        S.op(eng, I("tensor_tensor", out=ci, in0=ci, in1=den, op=ALU.mult), reads=bufs, writes=bufs)
        crb = cr if shp_b is None else cr.unsqueeze(2).broadcast_to(shp_b)
        cib = ci if shp_b is None else ci.unsqueeze(2).broadcast_to(shp_b)
        S.op(eng, I("tensor_tensor", out=obr, in0=bre, in1=crb, op=ALU.mult), reads=bufs, writes=bufs)
        S.op(eng, I("tensor_tensor", out=obi, in0=bim, in1=cib, op=ALU.mult), reads=bufs, writes=bufs)
        S.op(eng, I("tensor_tensor", out=obr, in0=obr, in1=obi, op=ALU.subtract), reads=bufs, writes=bufs)
        S.op(eng, I("tensor_tensor", out=obi, in0=bim, in1=crb, op=ALU.mult), reads=bufs, writes=bufs)
        t6 = t5 if tfull is None else tfull
        S.op(eng, I("tensor_tensor", out=t6, in0=bre, in1=cib, op=ALU.mult), reads=bufs, writes=bufs)
        S.op(eng, I("tensor_tensor", out=obi, in0=obi, in1=t6, op=ALU.add), reads=bufs, writes=bufs)

    def out_slot():
        i = oi[0] % 2
        oi[0] += 1
        return OUT[i], out_b[i], out_s[i]

    cnsem = S.new_dma_sem("s5cn")
    for gt in range(4):
        S.dma("sp", cnsem, [I("dma_start", out=CN[:], in_=G.s5CN[l, gt])], writes=[np_b])
        for d in range(2):
            idx = gt * 2 + d
            dcol = dtall[:, d * 4 + gt:d * 4 + gt + 1]
            nb = [np_b, cst_b]
            S.dma("sp", nsem, [I("dma_start", out=lamN[:], in_=G.s5N[l, d, gt]),
                               I("dma_start", out=BN[:], in_=G.s5BN[l, d, gt])], writes=[np_b])
            ld4, tu4 = n4[:, 0, :], n4[:, 1, :]
            S.op("pool", I("tensor_scalar", out=ld4, in0=lamN[:, 0:4], scalar1=dcol, scalar2=None, op0=ALU.mult), reads=nb, writes=nb)
            S.op("pool", I("tensor_scalar", out=tu4, in0=lamN[:, 4:8], scalar1=dcol, scalar2=1.0 / TWO_PI,
                           op0=ALU.mult, op1=ALU.mult), reads=nb, writes=nb)
            angP, mgP, cosP, sinP, prP, piP, tfP, rP = [nP[:, i] for i in range(8)]
            S.op("pool", I("tensor_tensor", out=angP, in0=ptab, in1=tu4.unsqueeze(2).broadcast_to([128, 4, 9]), op=ALU.mult), reads=nb, writes=nb)
            S.op("pool", I("tensor_tensor", out=mgP, in0=ptab, in1=ld4.unsqueeze(2).broadcast_to([128, 4, 9]), op=ALU.mult), reads=nb, writes=nb)
            S.op("act", I("activation", out=mgP, in_=mgP, func=AF.Exp), reads=nb, writes=nb)
            trig(angP, nPi[:], tfP, rP, sinP, cosP, nb)
            S.op("pool", I("tensor_tensor", out=prP, in0=mgP, in1=cosP, op=ALU.mult), reads=nb, writes=nb)
            S.op("pool", I("tensor_tensor", out=piP, in0=mgP, in1=sinP, op=ALU.mult), reads=nb, writes=nb)
            cplx_coef(prP[:, :, 1], piP[:, :, 1], lamN[:, 0:4], lamN[:, 4:8], [n4[:, i, :] for i in range(2, 8)],
                      BN[:, 0], BN[:, 1], bbN[:, 0], bbN[:, 1], [128, 4, 16], nb, tfull=tN[:, 3, :, 0, :])
            S.op("pool", I("tensor_copy", out=n4[:, 10, :], in_=mgP[:, :, 8]), reads=nb, writes=nb)
            S.dma("sp", rsem, [I("dma_start", out=G.S5RH[idx], in_=n4[:, 10, :])], reads=nb)
            f8 = n4[:, 8, :]
            S.op("pool", I("tensor_copy", out=nPi[:, :, 0], in_=angP[:, :, 8]), reads=nb, writes=nb)
            S.op("pool", I("tensor_copy", out=n4[:, 9, :], in_=nPi[:, :, 0]), reads=nb, writes=nb)
            S.op("pool", I("tensor_tensor", out=f8, in0=angP[:, :, 8], in1=n4[:, 9, :], op=ALU.subtract), reads=nb, writes=nb)
            for jc in range(4):
                tb = [tc_b, np_b, cst_b]
                S.op("dve", I("tensor_scalar", out=tc[:, 0, :], in0=ctab, scalar1=f8[:, jc:jc + 1], scalar2=None, op0=ALU.mult),
                     reads=tb, writes=tb)
                trig(tc[:, 0, :], tc[:, 5, :].bitcast(I32), tc[:, 1, :], tc[:, 2, :], tc[:, 3, :], tc[:, 4, :], tb, eng="dve")
                S.dma("sp", tcs, [I("dma_start", out=G.S5TB[idx, :, 0, jc, :], in_=tc[:, 4, :]),
                                  I("dma_start", out=G.S5TB[idx, :, 1, jc, :], in_=tc[:, 3, :])], reads=tb)
            hb = [hp_b, cst_b]
            S.dma("sp", hsem, [I("dma_start", out=Hp[:], in_=G.s5H[l, d, gt])], writes=[hp_b])
            ldH, tuH = h64[:, 0, :], h64[:, 1, :]
            S.op("pool", I("tensor_scalar", out=ldH, in0=Hp[:, 0, :], scalar1=dcol, scalar2=None, op0=ALU.mult), reads=hb, writes=hb)
            S.op("pool", I("tensor_scalar", out=tuH, in0=Hp[:, 1, :], scalar1=dcol, scalar2=1.0 / TWO_PI,
                           op0=ALU.mult, op1=ALU.mult), reads=hb, writes=hb)
            angE, mgE, cosE, sinE = [hE[:, i] for i in range(4)]
            tfE = tNf[:, 0, :].rearrange("p (e n) -> p e n", n=64)
            rE = tNf[:, 1, :].rearrange("p (e n) -> p e n", n=64)
            S.op("pool", I("tensor_tensor", out=angE, in0=etab, in1=tuH.unsqueeze(1).broadcast_to([128, 8, 64]), op=ALU.mult), reads=hb, writes=hb)
            S.op("pool", I("tensor_tensor", out=mgE, in0=etab, in1=ldH.unsqueeze(1).broadcast_to([128, 8, 64]), op=ALU.mult), reads=hb, writes=hb)
            S.op("act", I("activation", out=mgE, in_=mgE, func=AF.Exp), reads=hb, writes=hb)
            trig(angE, hEi, tfE, rE, sinE, cosE, hb + [np_b], eng="dve")
            S.op("pool", I("tensor_tensor", out=cosE, in0=mgE, in1=cosE, op=ALU.mult), reads=hb, writes=hb)
            S.op("pool", I("tensor_tensor", out=sinE, in0=mgE, in1=sinE, op=ALU.mult), reads=hb, writes=hb)
            bbrH, bbiH = h64[:, 8, :], h64[:, 9, :]
            cplx_coef(cosE[:, 1, :], sinE[:, 1, :], Hp[:, 0, :], Hp[:, 1, :], [h64[:, i, :] for i in range(2, 8)],
                      Hp[:, 2, :], Hp[:, 3, :], bbrH, bbiH, None, hb, eng="pool")
            bbrB = bbrH.unsqueeze(1).broadcast_to([128, 8, 64]); bbiB = bbiH.unsqueeze(1).broadcast_to([128, 8, 64])
            hb2 = hb + [np_b]
            S.op("pool", I("tensor_tensor", out=W1d[:, :, 0:64], in0=cosE, in1=bbrB, op=ALU.mult), reads=hb, writes=hb)
            S.op("pool", I("tensor_tensor", out=tfE, in0=sinE, in1=bbiB, op=ALU.mult), reads=hb2, writes=hb2)
            S.op("pool", I("tensor_tensor", out=W1d[:, :, 0:64], in0=W1d[:, :, 0:64], in1=tfE, op=ALU.subtract), reads=hb2, writes=hb)
            S.op("pool", I("tensor_tensor", out=W1d[:, :, 64:128], in0=cosE, in1=bbiB, op=ALU.mult), reads=hb, writes=hb)
            S.op("pool", I("tensor_tensor", out=tfE, in0=sinE, in1=bbrB, op=ALU.mult), reads=hb2, writes=hb2)
            S.op("pool", I("tensor_tensor", out=W1d[:, :, 64:128], in0=W1d[:, :, 64:128], in1=tfE, op=ALU.add), reads=hb2, writes=hb)
            ot, otb, ots = out_slot()
            W1bd = ot[:].rearrange("p (e j n) -> p e j n", e=8, j=8)
            for e in range(8):
                S.op("pool" if e % 4 else "dve",
                     I("tensor_tensor", out=W1bd[:, e].rearrange("p j (g n) -> p j g n", n=16),
                       in0=W1d[:, e, :].rearrange("p (j n) -> p j n", n=16).unsqueeze(2).broadcast_to([128, 8, 8, 16]),
                       in1=mask.unsqueeze(1).broadcast_to([128, 8, 8, 16]), op=ALU.mult),
                     reads=hb, writes=[otb])
            S.dma("sp", ots, [I("dma_start", out=G.S5W1[idx], in_=ot[:])], reads=[otb])
            if d == 0:
                prS, piS = prP[:, :, 1:9], piP[:, :, 1:9]
            else:
                prS, piS = prP[:, :, 1:9][:, :, ::-1], piP[:, :, 1:9][:, :, ::-1]
            prB = prS.unsqueeze(3).broadcast_to([128, 4, 8, 16]); piB = piS.unsqueeze(3).broadcast_to([128, 4, 8, 16])
            creB = CN[:, 0].unsqueeze(2).broadcast_to([128, 4, 8, 16]); cimB = CN[:, 1].unsqueeze(2).broadcast_to([128, 4, 8, 16])
            wR, wI, t2, t3 = [tN[:, i] for i in range(4)]
            S.op("pool", I("tensor_tensor", out=wR, in0=creB, in1=prB, op=ALU.mult), reads=nb, writes=nb)
            S.op("pool", I("tensor_tensor", out=t2, in0=cimB, in1=piB, op=ALU.mult), reads=nb, writes=nb)
            S.op("pool", I("tensor_tensor", out=wR, in0=wR, in1=t2, op=ALU.subtract), reads=nb, writes=nb)
            S.op("pool", I("tensor_tensor", out=wI, in0=creB, in1=piB, op=ALU.mult), reads=nb, writes=nb)
            S.op("pool", I("tensor_tensor", out=t2, in0=cimB, in1=prB, op=ALU.mult), reads=nb, writes=nb)
            S.op("pool", I("tensor_tensor", out=wI, in0=wI, in1=t2, op=ALU.add), reads=nb, writes=nb)
            S.op("pool", I("tensor_scalar", out=wI, in0=wI, scalar1=-1.0, scalar2=None, op0=ALU.mult), reads=nb, writes=nb)
            ot, otb, ots = out_slot()
            W3bd = ot[:].rearrange("p (j s n) -> p j s n", j=8, s=8)
            for j in range(8):
                srcw = (wR if j < 4 else wI)[:, j % 4]
                S.op("pool" if j % 4 else "dve",
                     I("tensor_tensor", out=W3bd[:, j].rearrange("p s (g n) -> p s g n", n=16),
                       in0=srcw.unsqueeze(2).broadcast_to([128, 8, 8, 16]),
                       in1=mask.unsqueeze(1).broadcast_to([128, 8, 8, 16]), op=ALU.mult),
                     reads=nb, writes=[otb])
            S.dma("sp", ots, [I("dma_start", out=G.S5W3[idx], in_=ot[:])], reads=[otb])
            prK = prP[:, :, 0:8].unsqueeze(3).broadcast_to([128, 4, 8, 16]); piK = piP[:, :, 0:8].unsqueeze(3).broadcast_to([128, 4, 8, 16])
            bbrB = bbN[:, 0].unsqueeze(2).broadcast_to([128, 4, 8, 16]); bbiB = bbN[:, 1].unsqueeze(2).broadcast_to([128, 4, 8, 16])
            S.op("pool", I("tensor_tensor", out=wR, in0=bbrB, in1=prK, op=ALU.mult), reads=nb, writes=nb)
            S.op("pool", I("tensor_tensor", out=t2, in0=bbiB, in1=piK, op=ALU.mult), reads=nb, writes=nb)
            S.op("pool", I("tensor_tensor", out=wR, in0=wR, in1=t2, op=ALU.subtract), reads=nb, writes=nb)
            S.op("pool", I("tensor_tensor", out=wI, in0=bbiB, in1=prK, op=ALU.mult), reads=nb, writes=nb)
            S.op("pool", I("tensor_tensor", out=t2, in0=bbrB, in1=piK, op=ALU.mult), reads=nb, writes=nb)
            S.op("pool", I("tensor_tensor", out=wI, in0=wI, in1=t2, op=ALU.add), reads=nb, writes=nb)
            ot, otb, ots = out_slot()
            XBD = ot[:].rearrange("p (r q n) -> p r q n", r=2, q=32)
            for ri in range(2):
                srcx = (wR if ri == 0 else wI).rearrange("p j k h -> p (j k) h")
                for half in range(2):
                    S.op("pool",
                         I("tensor_tensor", out=XBD[:, ri, half * 16:(half + 1) * 16].rearrange("p q (g n) -> p q g n", n=16),
                           in0=srcx[:, half * 16:(half + 1) * 16].unsqueeze(2).broadcast_to([128, 16, 8, 16]),
                           in1=mask.unsqueeze(1).broadcast_to([128, 16, 8, 16]), op=ALU.mult),
                         reads=nb, writes=[otb])
            S.dma("sp", ots, [I("dma_start", out=G.S5XB[idx], in_=ot[:])], reads=[otb])
    return R.items


def phase5_s5(G, l):
    nc, S = G.nc, G.S
    ps, psb = G.psall, G.psb
    if G.s5_items is None:
        mk0 = S.mark()
        with ExitStack() as pp:
            items = s5_prep(G, l, pp)
            replay(S, items, len(items))
            S.barrier()
            S.release_to(mk0)
    else:
        assert not G.s5_items
    G.s5_items = None
    mk = S.mark()
    RANGES = [(0, 256), (256, 256), (512, 32)]
    with ExitStack() as p5:
        sbt = lambda name, shape, dt: p5.enter_context(nc.sbuf_tensor(U(name), list(shape), dt))
        tab = sbt("s5tabm", [128, 128], F32)
        mask = tab[:, 0:128].rearrange("p (g n) -> p g n", n=16)
        misc = sbt("s5misc", [128, 8], F32)
        cst_b = Buf("s5cst")
        csem = S.new_dma_sem("s5c")
        S.dma("sp", csem, [I("dma_start", out=tab[:], in_=G.s5tab[:, 1092:1220]),
                           I("dma_start", out=misc[:], in_=G.s5misc[l, :, :])], writes=[cst_b])
        Ut = sbt("Ut", [128, T], F32); U_b = Buf("Ut")
        Ub = sbt("Ub", [128, T], BF16); Ub_b = Buf("Ub")
        gst, gst_b = Ub, Ub_b
        gsem = S.new_dma_sem("gst", fresh=True)
        yt, y_b = Ut, U_b
        usem = S.new_dma_sem("s5u")
        CN = sbt("CN", [128, 2, 4, 16], F32)
        nsem = S.new_dma_sem("s5n")
        CBD = sbt("CBD", [128, 2, 4, 128], BF16); cbd_b = Buf("CBD")
        Kf = sbt("Kf", [128, 16, 128], BF16); k_b = Buf("Kf")
        K0 = sbt("K0", [128, 128], F32)
        Srot = sbt("Srot", [128, 8, NCH], F32); sr_b = Buf("Srot")
        Gs = sbt("Gs", [128, 8, NCH], F32); gs_b = Buf("Gs")
        Ebf = [sbt("Ebf%d" % d, [128, 8, NCH + 1], BF16) for d in range(2)]
        e_b = [Buf("Ebf%d" % d) for d in range(2)]
        rt = Gs[:, 0:4, :].rearrange("p a c -> p (a c)")[:, 0:2048].rearrange("p (r j c) -> p r j c", r=2, j=4); rt_b = gs_b
        WA = Ring(S, p5, "wa", 2, [128, 8192], BF16, dma=True)
        TBr = Ring(S, p5, "tbr", 2, [128, 2, 4, NCH], F32, dma=True)
        RHr = Ring(S, p5, "rhr", 2, [128, 4], F32, dma=True)
        XBr = Ring(S, p5, "xbr", 1, [128, 8192], BF16, dma=True)
        W3t = [sbt("W3bd%d" % d, [128, 8192], BF16) for d in range(2)]
        w3_b = [Buf("W3bd%d" % d) for d in range(2)]
        w3s = [S.new_dma_sem("w3%d" % d) for d in range(2)]
        loaded = {}

        def load_w(idx):
            if idx in loaded or idx >= 8:
                return
            wa, wab, was = WA.next()
            S.dma("sp", was, [I("dma_start", out=wa[:], in_=G.S5W1[idx])], writes=[wab])
            tb, tbb, tbs = TBr.next()
            S.dma("sp", tbs, [I("dma_start", out=tb[:], in_=G.S5TB[idx])], writes=[tbb])
            rh, rhb, rhs = RHr.next()
            S.dma("sp", rhs, [I("dma_start", out=rh[:], in_=G.S5RH[idx])], writes=[rhb])
            loaded[idx] = (wa, wab, tb, tbb, rh, rhb)

        load_w(0)
        for gt in range(4):
            S.dma("sp", usem, [I("dma_start", out=Ut[:, :], in_=G.US[gt * 128:(gt + 1) * 128, :])], writes=[U_b])
            S.op("act", I("activation", out=Ub[:, :], in_=Ut[:, :], func=AF.Copy), reads=[U_b], writes=[Ub_b])
            S.dma("sp", nsem, [I("dma_start", out=CN[:], in_=G.s5CN[l, gt])], writes=[cbd_b])
            for ri in range(2):
                S.op("pool", I("tensor_tensor", out=CBD[:, ri].rearrange("p j (g n) -> p j g n", n=16),
                               in0=CN[:, ri].unsqueeze(2).broadcast_to([128, 4, 8, 16]),
                               in1=mask.unsqueeze(1).broadcast_to([128, 4, 8, 16]), op=ALU.mult),
                     reads=[cst_b, cbd_b], writes=[cbd_b])
            S.op("pool", I("tensor_scalar", out=CBD[:, 1], in0=CBD[:, 1], scalar1=-1.0, scalar2=None, op0=ALU.mult),
                 reads=[cbd_b], writes=[cbd_b])
            for d in range(2):
                S.dma("sp", w3s[d], [I("dma_start", out=W3t[d][:], in_=G.S5W3[gt * 2 + d])], writes=[w3_b[d]])
            for d in range(2):
                idx = gt * 2 + d
                load_w(idx)
                wa, w1_b, tbl, tbl_b, rho, rho_b = loaded[idx]
                W1bd = wa[:].rearrange("p (e j n) -> p e j n", e=8, j=8)
                cosT, sinT = tbl[:, 0], tbl[:, 1]
                for ri_, (c0, n) in enumerate(RANGES):
                    b0 = (ri_ % 2) * 4
                    for j in range(8):
                        off = b0 * 512 + j * 256
                        S.op("pe", [I("matmul", out=ps[:, off:off + n], lhsT=W1bd[:, (7 - s) if d == 0 else s, j, :],
                                      rhs=Ub[:, c0 * 8 + s:(c0 + n) * 8:8], start=(s == 0), stop=(s == 7)) for s in range(8)],
                             reads=[w1_b, Ub_b], writes=psb[b0:b0 + 4])
                    pv = ps[:, b0 * 512:(b0 + 4) * 512].rearrange("p (j c) -> p j c", c=256)
                    Sre, Sim = pv[:, 0:4, 0:n], pv[:, 4:8, 0:n]
                    if d == 0:
                        cp = c0 + 32 if c0 < 512 else 0
                        cT, sT = cosT[:, :, cp:cp + n], sinT[:, :, cp:cp + n]
                        ore, oim = Srot[:, 0:4, cp:cp + n], Srot[:, 4:8, cp:cp + n]
                    else:
                        lo = NCH - 1 - (c0 + n - 1)
                        cT, sT = cosT[:, :, lo:lo + n][:, :, ::-1], sinT[:, :, lo:lo + n][:, :, ::-1]
                        ore, oim = Srot[:, 0:4, c0:c0 + n], Srot[:, 4:8, c0:c0 + n]
                    rb = psb[b0:b0 + 4] + [tbl_b]
                    S.op("dve", I("tensor_tensor", out=rt[:, 0, :, 0:n], in0=Sre, in1=cT, op=ALU.mult), reads=rb, writes=[rt_b])
                    S.op("dve", I("tensor_tensor", out=rt[:, 1, :, 0:n], in0=Sim, in1=sT, op=ALU.mult), reads=rb, writes=[rt_b])
                    S.op("pool", I("tensor_tensor", out=ore, in0=rt[:, 0, :, 0:n], in1=rt[:, 1, :, 0:n], op=ALU.add), reads=[rt_b], writes=[sr_b])
                    S.op("dve", I("tensor_tensor", out=rt[:, 0, :, 0:n], in0=Sim, in1=cT, op=ALU.mult), reads=rb, writes=[rt_b])
                    S.op("dve", I("tensor_tensor", out=rt[:, 1, :, 0:n], in0=Sre, in1=sT, op=ALU.mult), reads=rb, writes=[rt_b])
                    S.op("pool", I("tensor_tensor", out=oim, in0=rt[:, 0, :, 0:n], in1=rt[:, 1, :, 0:n], op=ALU.subtract), reads=[rt_b], writes=[sr_b])
                xb, xbb, xbs = XBr.next()
                S.dma("sp", xbs, [I("dma_start", out=xb[:], in_=G.S5XB[idx])], writes=[xbb])
                XBD = xb[:].rearrange("p (r q n) -> p r q n", r=2, q=32)
                load_w(idx + 1)
                for j in range(8):
                    src_ = Srot[:, j, :] if d == 0 else Srot[:, j, ::-1]
                    S.op("dve", I("tensor_tensor_scan", out=Gs[:, j, :], data0=rho[:, j % 4:j % 4 + 1].broadcast_to([128, NCH]),
                                  data1=src_, initial=0.0, op0=ALU.mult, op1=ALU.add), reads=[sr_b, rho_b], writes=[gs_b])
                S.op("pool", I("memset", ap=Ebf[d][:, :, 0:1], constant=0.0), writes=[e_b[d]])
                Gre, Gim = Gs[:, 0:4, :], Gs[:, 4:8, :]
                p0, p1 = Srot[:, 0:4, :], Srot[:, 4:8, :]
                S.op("dve", I("tensor_tensor", out=p0, in0=Gre, in1=cosT, op=ALU.mult), reads=[gs_b, tbl_b, sr_b], writes=[sr_b])
                S.op("pool", I("tensor_tensor", out=p1, in0=Gim, in1=sinT, op=ALU.mult), reads=[gs_b, tbl_b, sr_b], writes=[sr_b])
                S.op("dve", I("tensor_tensor", out=Ebf[d][:, 0:4, 1:NCH + 1], in0=p0, in1=p1, op=ALU.subtract),
                     reads=[sr_b], writes=[e_b[d]])
                S.op("dve", I("tensor_tensor", out=p0, in0=Gim, in1=cosT, op=ALU.mult), reads=[gs_b, tbl_b, sr_b], writes=[sr_b])
                S.op("pool", I("tensor_tensor", out=p1, in0=Gre, in1=sinT, op=ALU.mult), reads=[gs_b, tbl_b, sr_b], writes=[sr_b])
                S.op("dve", I("tensor_tensor", out=Ebf[d][:, 4:8, 1:NCH + 1], in0=p0, in1=p1, op=ALU.add),
                     reads=[sr_b], writes=[e_b[d]])
                for k in range(8):
                    pt, pb, _ = G.PS.next()
                    mm = []
                    for ri in range(2):
                        for jc in range(4):
                            mm.append(I("matmul", out=pt[:, 0:128], lhsT=XBD[:, ri, jc * 8 + k, :], rhs=CBD[:, ri, jc, :],
                                        start=(ri == 0 and jc == 0), stop=(ri == 1 and jc == 3)))
                    S.op("pe", mm, reads=[xbb, cbd_b], writes=[pb])
                    if k == 0 and d == 0:
                        S.op("dve", I("tensor_copy", out=K0[:], in_=pt[:, 0:128]), reads=[pb], writes=[k_b])
                    elif k == 0:
                        S.op("dve", I("tensor_tensor", out=Kf[:, 0, :], in0=pt[:, 0:128], in1=K0[:], op=ALU.add), reads=[pb, k_b], writes=[k_b])
                    else:
                        S.op("act", I("activation", out=Kf[:, d * 8 + k, :], in_=pt[:, 0:128], func=AF.Copy), reads=[pb], writes=[k_b])

            W3bd = [W3t[d][:].rearrange("p (j s n) -> p j s n", j=8, s=8) for d in range(2)]
            for ri_, (c0, n) in enumerate(RANGES):
                b0 = (ri_ % 2) * 4
                for s in range(8):
                    off = b0 * 512 + s * 256
                    mm = []
                    for s2 in range(8):
                        kk = (s - s2) if s2 <= s else 8 + (s2 - s)
                        mm.append(I("matmul", out=ps[:, off:off + n], lhsT=Kf[:, kk, :], rhs=Ub[:, c0 * 8 + s2:(c0 + n) * 8:8],
                                    start=(s2 == 0), stop=False))
                    cp = c0 + 32 if c0 < 512 else 0
                    for j in range(8):
                        mm.append(I("matmul", out=ps[:, off:off + n], lhsT=W3bd[0][:, j, s, :], rhs=Ebf[0][:, j, cp:cp + n],
                                    start=False, stop=False))
                    lo = NCH - 1 - (c0 + n - 1)
                    for j in range(8):
                        mm.append(I("matmul", out=ps[:, off:off + n], lhsT=W3bd[1][:, j, s, :], rhs=Ebf[1][:, j, lo:lo + n][:, ::-1],
                                    start=False, stop=(j == 7)))
                    S.op("pe", mm, reads=[k_b, Ub_b, w3_b[0], w3_b[1], e_b[0], e_b[1]], writes=psb[b0:b0 + 4])
                pv = ps[:, b0 * 512:(b0 + 4) * 512].rearrange("p (s c) -> p s c", c=256)[:, :, 0:n]
                S.op("dve", I("scalar_tensor_tensor", out=yt[:, c0 * 8:(c0 + n) * 8].rearrange("p (c s) -> p s c", s=8),
                              in0=Ut[:, c0 * 8:(c0 + n) * 8].rearrange("p (c s) -> p s c", s=8), scalar=misc[:, gt:gt + 1],
                              in1=pv, op0=ALU.mult, op1=ALU.add), reads=psb[b0:b0 + 4] + [U_b, cst_b], writes=[y_b])
            S.op("act", I("activation", out=gst[:, :], in_=yt[:, :], func=AF.Gelu), reads=[y_b], writes=[gst_b])
            S.dma("pool", gsem, [I("dma_start", out=G.GS5[gt * 128:(gt + 1) * 128, :], in_=gst[:, :])],
                  reads=[gst_b], writes=[scr_buf(G, "GS5", gt)])
            if "S5Y" in G.dbg:
                dsem = S.new_dma_sem("dbg")
                S.dma("sp", dsem, [I("dma_start", out=G.dbg["S5Y"][gt * 128:(gt + 1) * 128, :], in_=yt[:, :])], reads=[y_b])

        S.barrier()
        S.release_to(mk)
    mk = S.mark()
    with ExitStack() as p5:
        sbt = lambda name, shape, dt: p5.enter_context(nc.sbuf_tensor(U(name), list(shape), dt))
        misc = sbt("s5misc2", [128, 8], F32)
        cst_b = Buf("s5cst2")
        csem = S.new_dma_sem("s5c2")
        S.dma("sp", csem, [I("dma_start", out=misc[:], in_=G.s5misc[l, :, :])], writes=[cst_b])
        wgf = sbt("wgf", [128, 4, 512], F32)
        wgb = sbt("wgb", [128, 4, 512], BF16); wg_b = Buf("wg")
        wsem = S.new_dma_sem("wg")
        S.dma("sp", wsem, [I("dma_start", out=wgf[:], in_=G.w_glu[l].rearrange("(c p) n -> p c n", p=128))], writes=[wg_b])
        S.op("pool", I("tensor_copy", out=wgb[:], in_=wgf[:]), reads=[wg_b], writes=[wg_b])
        ZSr = Ring(S, p5, "zs", 2, [128, 512], BF16, dma=True)
        SGr = Ring(S, p5, "sg5", 2, [128, 512], BF16)
        YOr = Ring(S, p5, "yo5", 2, [128, 512], BF16, dma=True, fresh=True)
        GTr = Ring(S, p5, "gtr", 2, [128, 4, 512], BF16, dma=True)
        for (t0, tn) in TG:
            gT, gTb, gTs = GTr.next()
            S.dma("sp", gTs, [I("dma_start", out=gT[:, :, 0:tn], in_=G.GS5[:, t0:t0 + tn].rearrange("(c p) n -> p c n", p=128))],
                  reads=[scr_buf(G, "GS5", i) for i in range(4)], writes=[gTb])
            for mo in range(4):
                zs, zsb, zss = ZSr.next()
                S.dma("sp", zss, [I("dma_start", out=zs[:, 0:tn], in_=G.ZS[mo * 128:(mo + 1) * 128, t0:t0 + tn])],
                      reads=[scr_buf(G, "ZS", mo)], writes=[zsb])
                pt, pb, _ = G.PS.next()
                S.op("pe", [I("matmul", out=pt[:, 0:tn], lhsT=wgb[:, kc, mo * 128:(mo + 1) * 128], rhs=gT[:, kc, 0:tn],
                              start=(kc == 0), stop=(kc == 3)) for kc in range(4)], reads=[wg_b, gTb], writes=[pb])
                sg, sgb, _ = SGr.next()
                S.op("act", I("activation", out=sg[:, 0:tn], in_=pt[:, 0:tn], func=AF.Sigmoid, bias=misc[:, 4 + mo:5 + mo]),
                     reads=[pb, cst_b], writes=[sgb])
                S.op("pool", I("tensor_tensor", out=sg[:, 0:tn], in0=sg[:, 0:tn], in1=zs[:, 0:tn], op=ALU.mult),
                     reads=[sgb, zsb], writes=[sgb])
                yo, yob, yos = YOr.next()
                S.op("dve", I("tensor_tensor", out=yo[:, 0:tn], in0=sg[:, 0:tn], in1=gT[:, mo, 0:tn], op=ALU.mult),
                     reads=[sgb, gTb], writes=[yob])
                S.dma("pool", yos, [I("dma_start", out=G.YG[2, mo * 128:(mo + 1) * 128, t0:t0 + tn], in_=yo[:, 0:tn])],
                      reads=[yob], writes=[scr_buf(G, "YG2", mo)])
        if "YG" in G.dbg:
            S.barrier()
            dsem = S.new_dma_sem("dbg")
            S.dma("sp", dsem, [I("dma_start", out=G.dbg["YG"][:, :, :], in_=G.YG[:, :, :])])
        S.barrier()
        S.release_to(mk)


def phase6_merge(G, l):
    nc, S = G.nc, G.S
    ps, psb = G.psall, G.psb
    last = (l == DEPTH - 1)
    mk = S.mark()
    with ExitStack() as p6:
        sbt = lambda name, shape, dt: p6.enter_context(nc.sbuf_tensor(U(name), list(shape), dt))
        wbr = sbt("wbr", [128, 3, 4, D], BF16)
        wou = sbt("wou", [128, 8, D], BF16)
        w_b = Buf("w6")
        STG = Ring(S, p6, "stg6", 2, [128, 4, D], F32, dma=True)
        for n in range(3):
            st, stb, sts = STG.next()
            S.dma("sp", sts, [I("dma_start", out=st[:], in_=G.w_branch[l, n].rearrange("(c p) n -> p c n", p=128))], writes=[stb])
            S.op("pool", I("tensor_copy", out=wbr[:, n], in_=st[:]), reads=[stb], writes=[w_b])
        for hf in range(2):
            st, stb, sts = STG.next()
            S.dma("sp", sts, [I("dma_start", out=st[:], in_=G.w_out[l, hf * 512:(hf + 1) * 512, :].rearrange("(c p) n -> p c n", p=128))],
                  writes=[stb])
            S.op("pool", I("tensor_copy", out=wou[:, hf * 4:(hf + 1) * 4], in_=st[:]), reads=[stb], writes=[w_b])
        if last:
            fgr = sbt("fgr", [1, D], F32)
            fgbc = sbt("fgbc", [128, D], F32)
            fg_b = Buf("fg")
            fsem = S.new_dma_sem("fg")
            S.dma("sp", fsem, [I("dma_start", out=fgr[:], in_=G.final_g[:, :])], writes=[fg_b])
            for n in range(2):
                pt, pb, _ = G.PS.next()
                S.op("pe", I("matmul", out=pt[:, :], lhsT=G.ones_f[0:1, :], rhs=fgr[0:1, n * 512:(n + 1) * 512], start=True, stop=True),
                     reads=[fg_b, G.b_const], writes=[pb])
                S.op("dve", I("tensor_copy", out=fgbc[:, n * 512:(n + 1) * 512], in_=pt[:, :]), reads=[pb], writes=[fg_b])
            junk = sbt("junk6", [128, D], BF16); junk_b = Buf("junk6")
            st4 = Ring(S, p6, "st6", 4, [128, 4], F32)
        YGr = Ring(S, p6, "yg6", 2, [128, 3, 4, 512], BF16, dma=True)
        SGr = Ring(S, p6, "sg6", 3, [128, 3, 512], BF16, dma=True)
        MG = Ring(S, p6, "mg6", 2, [128, 8, 512], BF16)
        TM = Ring(S, p6, "tm6", 2, [128, 3, 512], F32)
        XR_ = Ring(S, p6, "x6", 3, [128, D], F32, dma=True)
        XO = Ring(S, p6, "xo6", 2, [128, D], F32, dma=True, fresh=True)
        groups = TG[:8] if last else TG
        for (t0, tn) in groups:
            v = 0 if t0 < SEQ else 1
            yg, ygb, ygs = YGr.next()
            S.dma("sp", ygs, [I("dma_start", out=yg[:, n, :, 0:tn], in_=G.YG[n, :, t0:t0 + tn].rearrange("(c p) t -> p c t", p=128))
                              for n in range(3)],
                  reads=[scr_buf(G, "YG%d" % n, i) for n in range(3) for i in range(4)], writes=[ygb])
            mg, mgb, _ = MG.next()
            for dc in range(8):
                sg, sgb, sgs = SGr.next()
                S.dma("sp", sgs, [I("dma_start", out=sg[:, :, 0:tn],
                                    in_=G.SG[:, t0:t0 + tn].rearrange("(n c p) t -> c p n t", n=3, c=8)[dc])],
                      reads=[scr_buf(G, "SG", n * 8 + dc) for n in range(3)], writes=[sgb])
                pts = []
                for n in range(3):
                    pt, pb, _ = G.PS.next()
                    S.op("pe", [I("matmul", out=pt[:, 0:tn], lhsT=wbr[:, n, kc, dc * 128:(dc + 1) * 128], rhs=yg[:, n, kc, 0:tn],
                                  start=(kc == 0), stop=(kc == 3)) for kc in range(4)], reads=[w_b, ygb], writes=[pb])
                    pts.append((pt, pb))
                tm, tmb, _ = TM.next()
                for n in range(3):
                    S.op("dve", I("tensor_tensor", out=tm[:, n, 0:tn], in0=pts[n][0][:, 0:tn], in1=sg[:, n, 0:tn], op=ALU.mult),
                         reads=[pts[n][1], sgb], writes=[tmb])
                S.op("pool", I("tensor_tensor", out=tm[:, 0, 0:tn], in0=tm[:, 0, 0:tn], in1=tm[:, 1, 0:tn], op=ALU.add),
                     reads=[tmb], writes=[tmb])
                S.op("pool", I("tensor_tensor", out=mg[:, dc, 0:tn], in0=tm[:, 0, 0:tn], in1=tm[:, 2, 0:tn], op=ALU.add),
                     reads=[tmb], writes=[mgb])
            for tt in range(tn // 128):
                ti = t0 // 128 + tt
                xt, xb, xs = XR_.next()
                S.dma("sp", xs, [I("dma_start", out=xt[:], in_=x_src(G, l, ti))], reads=[G.xres_b[ti]], writes=[xb])
                xo, xob, xos = XO.next()
                for n in range(2):
                    pt, pb, _ = G.PS.next()
                    S.op("pe", [I("matmul", out=pt[:, :], lhsT=mg[:, dc, tt * 128:(tt + 1) * 128], rhs=wou[:, dc, n * 512:(n + 1) * 512],
                                  start=(dc == 0), stop=(dc == 7)) for dc in range(8)], reads=[w_b, mgb], writes=[pb])
                    S.op("dve", I("tensor_tensor", out=xo[:, n * 512:(n + 1) * 512], in0=pt[:, :],
                                  in1=G.gtbc[:, v, n * 512:(n + 1) * 512], op=ALU.mult), reads=[pb, G.gtbc_b], writes=[xob])
                S.op("pool", I("tensor_tensor", out=xo[:, :], in0=xo[:, :], in1=xt[:, :], op=ALU.add), reads=[xob, xb], writes=[xob])
                if not last:
                    S.dma("pool", xos, [I("dma_start", out=G.xres[ti * 128:(ti + 1) * 128, :], in_=xo[:, :])],
                          reads=[xob], writes=[G.xres_b[ti]])
                else:
                    st, stb, _ = st4.next()
                    S.op("act", I("activation", out=junk[:], in_=xo[:], func=AF.Square, accum_out=st[:, 0:1]),
                         reads=[xob], writes=[junk_b, stb])
                    S.op("act", I("activation", out=st[:, 1:2], in_=st[:, 0:1], func=AF.Sqrt, scale=1.0 / D, bias=G.eps_col[:, 0:1]),
                         reads=[stb, G.b_const], writes=[stb])
                    S.op("dve", I("reciprocal", out=st[:, 2:3], in_=st[:, 1:2]), reads=[stb], writes=[stb])
                    S.op("dve", I("scalar_tensor_tensor", out=xo[:, :], in0=xo[:, :], scalar=st[:, 2:3], in1=fgbc[:, :],
                                  op0=ALU.mult, op1=ALU.mult), reads=[xob, stb, fg_b], writes=[xob])
                    S.dma("pool", xos, [I("dma_start", out=G.out[ti * 128:(ti + 1) * 128, :], in_=xo[:, :])],
                          reads=[xob], writes=[G.xres_b[ti]])
        if "XRES" in G.dbg:
            S.barrier()
            dsem = S.new_dma_sem("dbg")
            S.dma("sp", dsem, [I("dma_start", out=G.dbg["XRES"][:, :], in_=G.xres[:, :])])
        S.barrier()
        S.release_to(mk)


def scr_buf(G, name, i):
    return Buf("scr")


def x_src(G, l, ti):
    if l == 0:
        if ti < 32:
            return G.x_in[ti * 128:(ti + 1) * 128, :]
        return G.ctx_in[(ti - 32) * 128:(ti - 31) * 128, :]
    return G.xres[ti * 128:(ti + 1) * 128, :]


def phase1_and_2(G, l, stop_after=None):
    nc, S, PS = G.nc, G.S, G.PS
    with ExitStack() as ph:
        psb = lambda name, shape, dt: ph.enter_context(nc.sbuf_tensor(U(name), list(shape), dt))
        hT = psb("hT", [128, 8, T], BF16)
        hT_b = [Buf("hT%d" % i) for i in range(NT)]
        mcols = psb("mcols", [128, 32], F32)
        gs = psb("gs", [128, 16], F32)
        mc_b = Buf("mcols")
        mk = S.mark()
        with ExitStack() as p1:
            p1sb = lambda name, shape, dt: p1.enter_context(nc.sbuf_tensor(U(name), list(shape), dt))
            mrow = [p1sb("mrow%d" % v, [1, 3 * D], F32) for v in range(2)]
            mrow_b = [Buf("mrow%d" % v) for v in range(2)]
            brow = p1sb("brow", [1, 3 * D], F32)
            brow_b = Buf("brow")
            gcol_sb = p1sb("gcol_sb", [128, 8], F32)
            WM = Ring(S, p1, "wm", 2, [128, 8, 512], F32, dma=True)
            bsem = S.new_dma_sem("brow")
            S.dma("sp", bsem, [I("dma_start", out=brow[:], in_=G.b_mod[l, :, :]),
                               I("dma_start", out=gcol_sb[:], in_=G.gcol[l, :, :])], writes=[brow_b])
            for n in range(6):
                wt, wb, ws = WM.next()
                S.dma("sp", ws, [I("dma_start", out=wt[:],
                                   in_=G.w_mod[l, :, n * 512:(n + 1) * 512].rearrange("(c p) n -> p c n", p=128))],
                      writes=[wb])
                for v in range(2):
                    pt, pb, _ = PS.next()
                    S.op("pe", [I("matmul", out=pt[0:1, :], lhsT=G.silu_c[:, v * 8 + c:v * 8 + c + 1], rhs=wt[:, c, :],
                                  start=(c == 0), stop=(c == 7)) for c in range(8)],
                         reads=[wb, G.b_const], writes=[pb])
                    S.op("dve", I("tensor_tensor", out=mrow[v][0:1, n * 512:(n + 1) * 512], in0=pt[0:1, :],
                                  in1=brow[0:1, n * 512:(n + 1) * 512], op=ALU.add),
                         reads=[pb, brow_b], writes=[mrow_b[v]])
            pt, pb, _ = PS.next()
            mm = []
            for v in range(2):
                for w in range(2):
                    for c in range(8):
                        j = v * 16 + w * 8 + c
                        mm.append(I("matmul", out=pt[:, j:j + 1],
                                    lhsT=mrow[v][0:1, w * 1024 + c * 128: w * 1024 + (c + 1) * 128],
                                    rhs=G.ones_f[0:1, 0:1], start=True, stop=True))
            S.op("pe", mm, reads=[mrow_b[0], mrow_b[1], G.b_const], writes=[pb])
            S.op("dve", I("tensor_copy", out=mcols[:], in_=pt[:, 0:32]), reads=[pb], writes=[mc_b])
            for v in range(2):
                S.op("dve", I("scalar_tensor_tensor", out=gs[:, v * 8:(v + 1) * 8], in0=mcols[:, v * 16 + 8:v * 16 + 16],
                              scalar=1.0, in1=gcol_sb[:], op0=ALU.add, op1=ALU.mult),
                     reads=[mc_b, brow_b], writes=[mc_b])
            for v in range(2):
                for n in range(2):
                    pt, pb, _ = PS.next()
                    S.op("pe", I("matmul", out=pt[:, :], lhsT=G.ones_f[0:1, :],
                                 rhs=mrow[v][0:1, 2048 + n * 512:2048 + (n + 1) * 512], start=True, stop=True),
                         reads=[mrow_b[v], G.b_const], writes=[pb])
                    S.op("dve", I("tensor_copy", out=G.gtbc[:, v, n * 512:(n + 1) * 512], in_=pt[:, :]),
                         reads=[pb], writes=[G.gtbc_b])
            if "mrow" in G.dbg:
                dsem = S.new_dma_sem("dbg")
                S.dma("sp", dsem, [I("dma_start", out=G.dbg["mrow"][0:1, :], in_=mrow[0][:]),
                                   I("dma_start", out=G.dbg["mrow"][1:2, :], in_=mrow[1][:])], reads=mrow_b)
            S.barrier()
            S.release_to(mk)

        with ExitStack() as p1:
            XT = Ring(S, p1, "xt", 3, [128, D], F32, dma=True)
            XN = Ring(S, p1, "xn", 2, [128, D], BF16)
            junk = p1.enter_context(nc.sbuf_tensor(U("junk"), [128, D], BF16))
            junk_b = Buf("junk")
            st4 = Ring(S, p1, "st", 4, [128, 4], F32)
            for ti in range(NT):
                v = 0 if ti < 32 else 1
                xt, xb, xs = XT.next()
                S.dma("sp", xs, [I("dma_start", out=xt[:], in_=x_src(G, l, ti))], reads=[G.xres_b[ti]], writes=[xb])
                st, stb, _ = st4.next()
                S.op("act", I("activation", out=junk[:], in_=xt[:], func=AF.Square, accum_out=st[:, 0:1]),
                     reads=[xb], writes=[junk_b, stb])
                S.op("act", I("activation", out=st[:, 1:2], in_=st[:, 0:1], func=AF.Sqrt, scale=1.0 / D, bias=G.eps_col[:, 0:1]),
                     reads=[stb, G.b_const], writes=[stb])
                S.op("dve", I("reciprocal", out=st[:, 2:3], in_=st[:, 1:2]), reads=[stb], writes=[stb])
                xn, xnb, _ = XN.next()
                S.op("dve", I("tensor_scalar", out=xn[:], in0=xt[:], scalar1=st[:, 2:3], scalar2=None, op0=ALU.mult),
                     reads=[xb, stb], writes=[xnb])
                for half in range(2):
                    pt, pb, _ = PS.next()
                    ptb = pt.bitcast(BF16)
                    S.op("pe", [I("transpose", out=ptb[:, cc * 128:(cc + 1) * 128],
                                  in_=xn[:, (half * 4 + cc) * 128:(half * 4 + cc + 1) * 128], identity=G.ident[:])
                                for cc in range(4)], reads=[xnb, G.b_const], writes=[pb])
                    for cc in range(4):
                        c = half * 4 + cc
                        if cc % 2 == 1:
                            S.op("dve", I("tensor_scalar", out=hT[:, c, ti * 128:(ti + 1) * 128],
                                          in0=ptb[:, cc * 128:(cc + 1) * 128],
                                          scalar1=gs[:, v * 8 + c:v * 8 + c + 1],
                                          scalar2=mcols[:, v * 16 + c:v * 16 + c + 1], op0=ALU.mult, op1=ALU.add),
                                 reads=[pb, mc_b], writes=[hT_b[ti]])
                        else:
                            S.op("act", I("activation", out=hT[:, c, ti * 128:(ti + 1) * 128],
                                          in_=ptb[:, cc * 128:(cc + 1) * 128], func=AF.Identity,
                                          scale=gs[:, v * 8 + c:v * 8 + c + 1],
                                          bias=mcols[:, v * 16 + c:v * 16 + c + 1]),
                                 reads=[pb, mc_b], writes=[hT_b[ti]])
            if "hT" in G.dbg:
                dsem = S.new_dma_sem("dbg")
                S.dma("sp", dsem, [I("dma_start", out=G.dbg["hT"][:, :, :], in_=hT[:])], reads=hT_b)
            S.barrier()
            S.release_to(mk)
        if stop_after == "p1":
            return

        mk = S.mark()
        with ExitStack() as p2:
            WF = Ring(S, p2, "wf", 2, [128, 8, 512], F32, dma=True)
            WB = Ring(S, p2, "wb", 4, [128, 8, 512], BF16)
            RC = Ring(S, p2, "rc", 2, [128, 2, 512], F32, dma=True)
            OB = Ring(S, p2, "ob", 8, [128, 512], BF16, dma=True)
            OF = Ring(S, p2, "of", 6, [128, 512], F32, dma=True)
            TMP = Ring(S, p2, "tmp", 2, [128, 2, 512], F32)

            def load_group(g):
                wf, wfb, wfs = WF.next()
                S.dma("sp", wfs, [I("dma_start", out=wf[:], in_=G.w_in[l, :, g * 512:(g + 1) * 512]
                                    .rearrange("(c p) n -> p c n", p=128))], writes=[wfb])
                wb, wbb, _ = WB.next()
                S.op("pool", I("tensor_copy", out=wb[:, 0:4, :], in_=wf[:, 0:4, :]), reads=[wfb], writes=[wbb])
                S.op("pool", I("tensor_copy", out=wb[:, 4:8, :], in_=wf[:, 4:8, :]), reads=[wfb], writes=[wbb])
                return wb, wbb

            def proj(pt, pb, wb, wbb, j, t0, tn):
                tis = list(range(t0 // 128, (t0 + tn) // 128))
                S.op("pe", [I("matmul", out=pt[:, 0:tn], lhsT=wb[:, c, j * 128:(j + 1) * 128], rhs=hT[:, c, t0:t0 + tn],
                              start=(c == 0), stop=(c == 7)) for c in range(8)],
                     reads=[wbb] + [hT_b[i] for i in tis], writes=[pb])

            wq = [load_group(g) for g in range(4)]
            for (t0, tn) in TG:
                rc, rcb, rcs = RC.next()
                S.dma("sp", rcs, [I("dma_start", out=rc[:, 0, 0:tn], in_=G.ropeC[:, t0:t0 + tn]),
                                  I("dma_start", out=rc[:, 1, 0:tn], in_=G.ropeS[:, t0:t0 + tn])], writes=[rcb])
                for qk in range(2):
                    dst = G.QT if qk == 0 else G.KT
                    for j in range(4):
                        pa, pab, _ = PS.next()
                        proj(pa, pab, wq[2 * qk][0], wq[2 * qk][1], j, t0, tn)
                        pbt, pbb, _ = PS.next()
                        proj(pbt, pbb, wq[2 * qk + 1][0], wq[2 * qk + 1][1], j, t0, tn)
                        tmp, tmpb, _ = TMP.next()
                        S.op("dve", I("tensor_tensor", out=tmp[:, 0, 0:tn], in0=pa[:, 0:tn], in1=rc[:, 0, 0:tn], op=ALU.mult),
                             reads=[pab, rcb], writes=[tmpb])
                        S.op("dve", I("tensor_tensor", out=tmp[:, 1, 0:tn], in0=pbt[:, 0:tn], in1=rc[:, 1, 0:tn], op=ALU.mult),
                             reads=[pbb, rcb], writes=[tmpb])
                        ob, obb, obs = OB.next()
                        S.op("pool", I("tensor_tensor", out=ob[:, 0:tn], in0=tmp[:, 0, 0:tn], in1=tmp[:, 1, 0:tn], op=ALU.add),
                             reads=[tmpb], writes=[obb])
                        S.dma("sp", obs, [I("dma_start", out=dst[j * 128:(j + 1) * 128, t0:t0 + tn], in_=ob[:, 0:tn])],
                              reads=[obb], writes=[scr_buf(G, "QT" if qk == 0 else "KT", j)])
            wv, wvb = load_group(4)
            for ti in range(NT):
                pt, pb, _ = PS.next()
                S.op("pe", [I("matmul", out=pt[:, :], lhsT=hT[:, c, ti * 128:(ti + 1) * 128], rhs=wv[:, c, :],
                              start=(c == 0), stop=(c == 7)) for c in range(8)],
                     reads=[wvb, hT_b[ti]], writes=[pb])
                ob, obb, obs = OB.next()
                S.op("act", I("activation", out=ob[:, :], in_=pt[:, :], func=AF.Copy), reads=[pb], writes=[obb])
                S.dma("sp", obs, [I("dma_start", out=G.Vtm[ti * 128:(ti + 1) * 128, :], in_=ob[:, :])],
                      reads=[obb], writes=[scr_buf(G, "V", ti)])
            plan = [(5, G.ZA, 0, "ZA", 0), (7, G.ZR, 0, "ZR", 0), (9, G.ZS, 0, "ZS", 0),
                    (6, G.XR, 1, "XR", 0), (8, G.US, 1, "US", 0)]
            plan += [(10 + i, G.SG, 2, "SG", i * 4) for i in range(6)]
            for (g, dst, kind, nm, boff) in plan:
                wg, wgb = load_group(g)
                for j in range(4):
                    for (t0, tn) in TG:
                        pt, pb, _ = PS.next()
                        proj(pt, pb, wg, wgb, j, t0, tn)
                        if kind == 1:
                            ob, obb, obs = OF.next()
                            S.op("dve", I("tensor_copy", out=ob[:, 0:tn], in_=pt[:, 0:tn]), reads=[pb], writes=[obb])
                        else:
                            ob, obb, obs = OB.next()
                            S.op("act", I("activation", out=ob[:, 0:tn], in_=pt[:, 0:tn],
                                          func=(AF.Silu if kind == 0 else AF.Sigmoid)), reads=[pb], writes=[obb])
                        r0 = (boff + j) * 128
                        S.dma("sp", obs, [I("dma_start", out=dst[r0:r0 + 128, t0:t0 + tn], in_=ob[:, 0:tn])],
                              reads=[obb], writes=[scr_buf(G, nm, boff + j)])
            if "QT" in G.dbg:
                S.barrier()
                dsem = S.new_dma_sem("dbg")
                for nm in ("QT", "KT", "Vtm", "ZA", "XR", "SG"):
                    if nm in G.dbg:
                        S.dma("sp", dsem, [I("dma_start", out=G.dbg[nm][:, :], in_=getattr(G, nm)[:, :])])
            S.barrier()
            S.release_to(mk)
        if stop_after == "p2":
            return


def _rope_tables():
    rows = SEQ // 64
    r = np.repeat(np.arange(rows, dtype=np.float32), 64)
    col = np.tile(np.arange(64, dtype=np.float32), rows)
    inv = (10000.0 ** (-np.arange(16, dtype=np.float32) / 16)).astype(np.float32)
    ang = np.concatenate([r[:, None] * inv, col[:, None] * inv], axis=-1).astype(np.float32)
    cos = np.cos(ang).T.astype(np.float32)
    sin = np.sin(ang).T.astype(np.float32)
    C = np.ones((128, T), np.float32)
    Sg = np.zeros((128, T), np.float32)
    for p in range(128):
        j = p % 64
        C[p, :SEQ] = cos[j % 32]
        Sg[p, :SEQ] = -sin[j % 32] if j < 32 else sin[j % 32]
    return C, Sg


def _w_in_ext(w_in):
    L = w_in.shape[0]
    perm = np.concatenate([np.arange(0, 64, 2), np.arange(1, 64, 2)])
    swp = np.concatenate([np.arange(1, 64, 2), np.arange(0, 64, 2)])
    idx = []
    for base in (0, 512):
        p_cols = np.concatenate([base + b * 64 + perm for b in range(8)])
        s_cols = np.concatenate([base + b * 64 + swp for b in range(8)])
        idx += [p_cols, s_cols]
    idx.append(np.arange(1024, 7168))
    idx = np.concatenate(idx)
    return np.ascontiguousarray(w_in[:, :, idx])


def make_in_maps(inputs):
    f = lambda a: np.ascontiguousarray(np.asarray(a, dtype=np.float32))
    x = f(inputs["x"]); ctx = f(inputs["ctx"]); c = f(inputs["c"]); c_ctx = f(inputs["c_ctx"])
    w_in_e = _w_in_ext(f(inputs["w_in"]))
    C, Sg = _rope_tables()
    ident = np.eye(128, dtype=np.float32).astype(ml_dtypes.bfloat16)
    gcol = np.ascontiguousarray(f(inputs["norm_g"]).reshape(DEPTH, 8, 128).transpose(0, 2, 1))
    shared = dict(
        w_mod=f(inputs["w_mod"]), b_mod=f(inputs["b_mod"]).reshape(DEPTH, 1, 3 * D), gcol=gcol, w_in=w_in_e,
        ropeC=C, ropeS=Sg, ident=ident, final_g=f(inputs["final_g"]).reshape(1, D),
        lam_qk=f(inputs["lam_qk"]).reshape(DEPTH, 1, 256), subln=f(inputs["subln_g"]).reshape(DEPTH, 128, 1),
    )
    lrup = np.zeros((DEPTH, 128, 44), np.float32)
    cw = f(inputs["conv_w"]); cb = f(inputs["conv_b"])
    for ct in range(4):
        for k in range(4):
            lrup[:, :, ct * 4 + k] = cw[:, k, ct * 128:(ct + 1) * 128]
        lrup[:, :, 16 + ct] = cb[:, ct * 128:(ct + 1) * 128]
        for d in range(2):
            lrup[:, :, 20 + d * 4 + ct] = f(inputs["lru_ba"])[:, d, ct * 128:(ct + 1) * 128]
            lrup[:, :, 28 + d * 4 + ct] = f(inputs["lru_bx"])[:, d, ct * 128:(ct + 1) * 128]
            lrup[:, :, 36 + d * 4 + ct] = f(inputs["lru_lam"])[:, d, ct * 128:(ct + 1) * 128]
    lruw = np.zeros((DEPTH, 2, 2, 4, 128, 128), np.float32)
    for gi, nm in enumerate(("lru_wa", "lru_wx")):
        w = f(inputs[nm])
        for ct in range(4):
            for j in range(2):
                lruw[:, gi, :, ct, j * 64:(j + 1) * 64, j * 64:(j + 1) * 64] = w[:, :, 2 * ct + j]
    shared.update(lrup=lrup, lruw=lruw)
    lre = f(inputs["s5_lam_re"]); lim = f(inputs["s5_lam_im"]); ldt = f(inputs["s5_log_dt"])
    bre = f(inputs["s5_b_re"]); bim = f(inputs["s5_b_im"]); cre = f(inputs["s5_c_re"]); cim = f(inputs["s5_c_im"])
    L = DEPTH
    s5H = np.zeros((L, 2, 4, 128, 4, 64), np.float32)
    lre_g = lre.reshape(L, 2, 4, 8, 64); lim_g = lim.reshape(L, 2, 4, 8, 64)
    s5H[:, :, :, :, 0, :] = np.repeat(lre_g, 16, axis=3)
    s5H[:, :, :, :, 1, :] = np.repeat(lim_g, 16, axis=3)
    s5H[:, :, :, :, 2, :] = bre.reshape(L, 2, 4, 8, 64, 16).transpose(0, 1, 2, 3, 5, 4).reshape(L, 2, 4, 128, 64)
    s5H[:, :, :, :, 3, :] = bim.reshape(L, 2, 4, 8, 64, 16).transpose(0, 1, 2, 3, 5, 4).reshape(L, 2, 4, 128, 64)
    def nlay(a):
        return a.reshape(L, 2, 4, 8, 4, 16).transpose(0, 1, 2, 3, 5, 4).reshape(L, 2, 4, 128, 4)
    s5N = np.concatenate([nlay(lre_g), nlay(lim_g)], axis=-1)
    def bnlay(a):
        return a.reshape(L, 2, 4, 8, 4, 16, 16).transpose(0, 1, 2, 3, 5, 4, 6).reshape(L, 2, 4, 128, 4, 16)
    s5BN = np.stack([bnlay(bre), bnlay(bim)], axis=4)
    def cnlay(a):
        return a.reshape(L, 4, 8, 16, 4, 16).transpose(0, 1, 2, 5, 4, 3).reshape(L, 4, 128, 4, 16)
    s5CN = np.stack([cnlay(cre), cnlay(cim)], axis=3)
    s5dt = np.zeros((L, 128, 8), np.float32)
    for d in range(2):
        for gt in range(4):
            s5dt[:, :, d * 4 + gt] = np.repeat(ldt[:, d, gt * 8:(gt + 1) * 8], 16, axis=1)
    s5misc = np.zeros((L, 128, 8), np.float32)
    s5misc[:, :, 0:4] = f(inputs["s5_d"]).reshape(L, 4, 128).transpose(0, 2, 1)
    s5misc[:, :, 4:8] = f(inputs["s5_b_glu"]).reshape(L, 4, 128).transpose(0, 2, 1)
    tabs = np.zeros((128, 36 + 512 + 544 + 128), np.float32)
    tabs[:, 0:36] = np.tile(np.arange(9, dtype=np.float32), 4)[None, :]
    tabs[:, 36:548] = np.repeat(np.arange(8, dtype=np.float32), 64)[None, :]
    tabs[:, 548:1092] = np.arange(544, dtype=np.float32)[None, :]
    mk = np.zeros((128, 8, 16), np.float32)
    for p in range(128):
        mk[p, p // 16, :] = 1.0
    tabs[:, 1092:1220] = mk.reshape(128, 128)
    shared.update(s5H=s5H, s5N=np.ascontiguousarray(s5N), s5BN=np.ascontiguousarray(s5BN), s5CN=np.ascontiguousarray(s5CN),
                  s5dt=s5dt, s5misc=s5misc, s5tab=tabs, w_glu=f(inputs["s5_w_glu"]),
                  w_branch=f(inputs["w_branch"]), w_out=f(inputs["w_out"]))
    maps = []
    for b in range(8):
        cc = np.concatenate([c[b].reshape(8, 128).T, c_ctx.reshape(8, 128).T], axis=1)
        m = dict(shared)
        m.update(x=x[b], ctx=ctx[b], ccol=np.ascontiguousarray(cc))
        maps.append(m)
    return maps


def kernel(**inputs):
    nc = build_program()
    maps = make_in_maps(inputs)
    res = run_bass_kernel_spmd(nc, maps, core_ids=list(range(8)))
    return np.stack([np.asarray(r["out"], dtype=np.float32) for r in res.results], axis=0)
```

```python
import math
from contextlib import ExitStack

import numpy as np
import ml_dtypes

import concourse.bass as bass
import concourse.mybir as mybir
from concourse.bass_utils import run_bass_kernel_spmd

F32 = mybir.dt.float32
BF16 = mybir.dt.bfloat16
ALU = mybir.AluOpType
AF = mybir.ActivationFunctionType
AX = mybir.AxisListType

D = 1024
SEQ = 4096
CTX = 256
T = SEQ + CTX
NT = T // 128
DEPTH = 4
EPS = 1e-6
NCOL = 8192
TG = [(i * 512, 512) for i in range(8)] + [(4096, 256)]


class Buf:
    __slots__ = ("name", "lw", "rd")

    def __init__(self, name=""):
        self.name = name
        self.lw = None
        self.rd = {}


class Sched:
    ENG = ("pe", "act", "dve", "pool", "sp")

    def __init__(self, nc, stack):
        self.nc = nc
        self.stack = stack
        self.streams = {e: [] for e in self.ENG}
        self.sem = {}
        self.cnt = {}
        self.seen = {e: {} for e in self.ENG}
        for e in self.ENG:
            self.sem[e] = stack.enter_context(nc.semaphore("s_" + e))
            self.cnt[e] = 0
        self.ndma = 0
        self.free = []
        self.live = []

    def new_dma_sem(self, name, fresh=False):
        if fresh:
            key = U("f%d" % self.ndma)
            self.ndma += 1
            self.sem[key] = self.stack.enter_context(self.nc.semaphore(key))
            self.cnt[key] = 0
            return key
        if self.free:
            key = self.free.pop()
        else:
            key = U("d%d" % self.ndma)
            self.ndma += 1
            self.sem[key] = self.stack.enter_context(self.nc.semaphore(key))
            self.cnt[key] = 0
        self.live.append(key)
        return key

    def mark(self):
        return len(self.live)

    def release_to(self, mark):
        while len(self.live) > mark:
            self.free.append(self.live.pop())

    def _waits(self, eng, reads, writes):
        w = {}

        def add(tok):
            if tok is None:
                return
            k, v = tok
            if w.get(k, 0) < v:
                w[k] = v

        for b in reads:
            add(b.lw)
        for b in writes:
            add(b.lw)
            for k, v in b.rd.items():
                add((k, v))
        need = []
        seen = self.seen[eng]
        for k, v in w.items():
            if k == "pe" and eng == "pe":
                continue
            if seen.get(k, 0) < v:
                seen[k] = v
                need.append((k, v))
        return need

    def _commit(self, tok, reads, writes):
        for b in writes:
            b.lw = tok
            b.rd = {}
        k, v = tok
        for b in reads:
            if b.rd.get(k, 0) < v:
                b.rd[k] = v

    def op(self, eng, fn, reads=(), writes=()):
        if isinstance(fn, tuple):
            fn = [fn]
        need = self._waits(eng, reads, writes)
        self.cnt[eng] += 1
        tok = (eng, self.cnt[eng])
        self.streams[eng].append((need, fn, eng, 1))
        self._commit(tok, reads, writes)
        return tok

    def dma(self, eng, semkey, fns, reads=(), writes=()):
        need = self._waits(eng, reads, writes)
        for i, fn in enumerate(fns):
            self.cnt[semkey] += 16
            self.streams[eng].append((need if i == 0 else [], fn, semkey, 16))
        tok = (semkey, self.cnt[semkey])
        self._commit(tok, reads, writes)
        return tok

    def barrier(self):
        for e in self.ENG:
            need = []
            seen = self.seen[e]
            for k, v in self.cnt.items():
                if v > 0 and seen.get(k, 0) < v:
                    seen[k] = v
                    need.append((k, v))
            if need:
                self.streams[e].append((need, None, None, 0))

    def emit(self, block):
        nc = self.nc
        sems = self.sem

        def run(e, stream):
            for need, fn, semkey, inc in stream:
                for k, v in need:
                    e.wait_ge(sems[k], v)
                if fn is not None:
                    if isinstance(fn, tuple):
                        fn = [fn]
                    for m, kw in fn:
                        ins = getattr(e, m)(**kw)
                    ins.then_inc(sems[semkey], inc)

        @block.sync
        def _(e):
            run(e, self.streams["sp"])

        @block.tensor
        def _(e):
            run(e, self.streams["pe"])

        @block.scalar
        def _(e):
            run(e, self.streams["act"])

        @block.vector
        def _(e):
            run(e, self.streams["dve"])

        @block.gpsimd
        def _(e):
            run(e, self.streams["pool"])


class PsRing:
    def __init__(self, G):
        self.G = G
        self.i = 0

    def next(self):
        i = self.i
        self.i = (i + 1) % 8
        return self.G.psall[:, i * 512:(i + 1) * 512], self.G.psb[i], None


class Ring:
    def __init__(self, S, stack, name, n, shape, dtype, psum=False, dma=False, fresh=False):
        self.tiles = []
        self.bufs = []
        self.sems = []
        for i in range(n):
            nm = U("%s%d" % (name, i))
            if psum:
                t = stack.enter_context(S.nc.psum_tensor(nm, shape, dtype))
            else:
                t = stack.enter_context(S.nc.sbuf_tensor(nm, shape, dtype))
            self.tiles.append(t)
            self.bufs.append(Buf(nm))
            self.sems.append(S.new_dma_sem(nm, fresh=fresh) if dma else None)
        self.i = 0
        self.n = n

    def next(self):
        i = self.i
        self.i = (i + 1) % self.n
        return self.tiles[i], self.bufs[i], self.sems[i]


def I(m, **kw):
    return (m, kw)


_UID = [0]


def U(name):
    _UID[0] += 1
    return "%s_%d" % (name, _UID[0])


class Ctx:
    pass


def build_program(n_layers=DEPTH, debug=None, stop_after=None):
    nc = bass.Bass("TRN2", target_bir_lowering=False)
    G = Ctx()
    G.nc = nc

    def din(name, shape, dt=F32):
        return nc.dram_tensor(name, list(shape), dt, kind="ExternalInput").ap()

    def dscr(name, shape, dt):
        return nc.dram_tensor(name, list(shape), dt, kind="Internal").ap()

    G.x_in = din("x", [SEQ, D])
    G.ctx_in = din("ctx", [CTX, D])
    G.ccol = din("ccol", [128, 16])
    G.w_mod = din("w_mod", [DEPTH, D, 3 * D])
    G.b_mod = din("b_mod", [DEPTH, 1, 3 * D])
    G.gcol = din("gcol", [DEPTH, 128, 8])
    G.w_in = din("w_in", [DEPTH, D, NCOL])
    G.ropeC = din("ropeC", [128, T])
    G.ropeS = din("ropeS", [128, T])
    G.ident_in = din("ident", [128, 128], BF16)
    G.final_g = din("final_g", [1, D])
    G.lam_qk = din("lam_qk", [DEPTH, 1, 256])
    G.subln = din("subln", [DEPTH, 128, 1])
    G.lrup = din("lrup", [DEPTH, 128, 44])
    G.lruw = din("lruw", [DEPTH, 2, 2, 4, 128, 128])
    G.s5H = din("s5H", [DEPTH, 2, 4, 128, 4, 64])
    G.s5N = din("s5N", [DEPTH, 2, 4, 128, 8])
    G.s5BN = din("s5BN", [DEPTH, 2, 4, 128, 2, 4, 16])
    G.s5CN = din("s5CN", [DEPTH, 4, 128, 2, 4, 16])
    G.s5dt = din("s5dt", [DEPTH, 128, 8])
    G.s5misc = din("s5misc", [DEPTH, 128, 8])
    G.s5tab = din("s5tab", [128, 36 + 512 + 544 + 128])
    G.w_glu = din("w_glu", [DEPTH, 512, 512])
    G.w_branch = din("w_branch", [DEPTH, 3, 512, D])
    G.w_out = din("w_out", [DEPTH, D, D])
    G.out = nc.dram_tensor("out", [SEQ, D], F32, kind="ExternalOutput").ap()

    G.xres = dscr("xres", [T, D], F32)
    G.QT = dscr("QT", [512, T], BF16)
    G.KT = dscr("KT", [512, T], BF16)
    G.Vtm = dscr("Vtm", [T, 512], BF16)
    G.ZA = dscr("ZA", [512, T], BF16)
    G.ZR = dscr("ZR", [512, T], BF16)
    G.ZS = dscr("ZS", [512, T], BF16)
    G.XR = dscr("XR", [512, T], F32)
    G.US = dscr("US", [512, T], F32)
    G.SG = dscr("SG", [3072, T], BF16)
    G.GS5 = dscr("GS5", [512, T], BF16)
    G.S5W1 = dscr("S5W1", [8, 128, 8192], BF16)
    G.S5W3 = dscr("S5W3", [8, 128, 8192], BF16)
    G.S5XB = dscr("S5XB", [8, 128, 8192], BF16)
    G.S5TB = dscr("S5TB", [8, 128, 2, 4, T // 8], F32)
    G.S5RH = dscr("S5RH", [8, 128, 4], F32)
    G.s5_items = None
    G.YG = dscr("YG", [3, 512, T], BF16)

    G.dbg = {}
    if debug:
        for name, shape, dt in debug:
            G.dbg[name] = nc.dram_tensor("dbg_" + name, list(shape), dt, kind="ExternalOutput").ap()

    with ExitStack() as top:
        S = Sched(nc, top)
        G.S = S
        sb = lambda name, shape, dt: top.enter_context(nc.sbuf_tensor(U(name), list(shape), dt))

        G.ident = sb("ident", [128, 128], BF16)
        G.ones_f = sb("ones_f", [128, 128], F32)
        G.ones_b = sb("ones_b", [128, 128], BF16)
        G.ccol_sb = sb("ccol_sb", [128, 16], F32)
        G.silu_c = sb("silu_c", [128, 16], F32)
        G.eps_col = sb("eps_col", [128, 1], F32)
        G.b_const = Buf("const")
        csem = S.new_dma_sem("const")
        S.dma("sp", csem, [I("dma_start", out=G.ident[:], in_=G.ident_in[:, :]),
                           I("dma_start", out=G.ccol_sb[:], in_=G.ccol[:, :])], writes=[G.b_const])
        S.op("pool", I("memset", ap=G.ones_f[:], constant=1.0), writes=[G.b_const])
        S.op("pool", I("memset", ap=G.ones_b[:], constant=1.0), writes=[G.b_const])
        S.op("pool", I("memset", ap=G.eps_col[:], constant=EPS), writes=[G.b_const])
        S.op("act", I("activation", out=G.silu_c[:], in_=G.ccol_sb[:], func=AF.Silu),
             reads=[G.b_const], writes=[G.b_const])

        G.psall = top.enter_context(nc.psum_tensor(U("psall"), [128, 4096], F32))
        G.psb = [Buf("psb%d" % i) for i in range(8)]
        G.PS = PsRing(G)

        G.xres_b = [Buf("xres%d" % i) for i in range(NT)]
        G.scr_b = {}
        G.gtbc = sb("gtbc", [128, 2, D], F32)
        G.gtbc_b = Buf("gtbc")

        for l in range(n_layers):
            phase1_and_2(G, l, stop_after)
            if stop_after in ("p1", "p2"):
                break
            if stop_after not in ("p4only", "p5only"):
                phase3_attn(G, l)
            if stop_after == "p3":
                break
            if stop_after != "p5only":
                phase4_lru(G, l)
            if stop_after in ("p4", "p4only"):
                break
            phase5_s5(G, l)
            if stop_after in ("p5", "p5only"):
                break
            phase6_merge(G, l)
            if stop_after is not None:
                break

        S.barrier()
        with nc.Block() as block:
            S.emit(block)
    return nc


def phase3_attn(G, l):
    nc, S = G.nc, G.S
    lam_init = 0.8 - 0.6 * math.exp(-0.3 * l)
    need_ctx = l < DEPTH - 1
    ps = G.psall
    psb = G.psb
    mk = S.mark()
    with ExitStack() as p3:
        sbt = lambda name, shape, dt: p3.enter_context(nc.sbuf_tensor(U(name), list(shape), dt))
        KTs = sbt("KTs", [128, 4, T], BF16)
        KT_b = [Buf("KTs%d" % h) for h in range(4)]
        Vs = sbt("Vs", [128, NT, 512], BF16)
        V_b = Buf("Vs")
        lq = sbt("lq", [1, 4, 64], F32)
        lw = sbt("lw", [1, 8], F32)
        prm = sbt("prm", [128, 4], F32)
        prm_b = Buf("prm")
        for h in range(4):
            ksem = S.new_dma_sem("kt%d" % h)
            S.dma("sp", ksem, [I("dma_start", out=KTs[:, h, :], in_=G.KT[h * 128:(h + 1) * 128, :])],
                  reads=[scr_buf(G, "KT", h)], writes=[KT_b[h]])
        vsem = S.new_dma_sem("v")
        S.dma("sp", vsem, [I("dma_start", out=Vs[:, i * 17:(i + 1) * 17, :],
                             in_=G.Vtm[i * 17 * 128:(i + 1) * 17 * 128, :].rearrange("(t p) n -> p t n", p=128))
                           for i in range(2)],
              reads=[scr_buf(G, "V", ti) for ti in range(NT)], writes=[V_b])
        psem = S.new_dma_sem("prm")
        S.dma("sp", psem, [I("dma_start", out=lq[:].rearrange("a b c -> a (b c)"), in_=G.lam_qk[l, :, :]),
                           I("dma_start", out=prm[:, 1:2], in_=G.subln[l, :, :])], writes=[prm_b])
        S.op("dve", I("tensor_tensor", out=lq[0:1, 0::2, :], in0=lq[0:1, 0::2, :], in1=lq[0:1, 1::2, :], op=ALU.mult),
             reads=[prm_b], writes=[prm_b])
        S.op("dve", I("reduce_sum", out=lw[0:1, 0:2], in_=lq[0:1, 0::2, :], axis=AX.X), reads=[prm_b], writes=[prm_b])
        S.op("act", I("activation", out=lw[0:1, 2:4], in_=lw[0:1, 0:2], func=AF.Exp), reads=[prm_b], writes=[prm_b])
        S.op("dve", I("tensor_tensor", out=lw[0:1, 4:5], in0=lw[0:1, 3:4], in1=lw[0:1, 2:3], op=ALU.subtract),
             reads=[prm_b], writes=[prm_b])
        S.op("dve", I("tensor_scalar", out=lw[0:1, 5:6], in0=lw[0:1, 4:5], scalar1=-lam_init, scalar2=None, op0=ALU.add),
             reads=[prm_b], writes=[prm_b])
        S.op("pe", I("matmul", out=ps[:, 0:1], lhsT=G.ones_f[0:1, :], rhs=lw[0:1, 5:6], start=True, stop=True),
             reads=[prm_b, G.b_const], writes=[psb[0]])
        S.op("dve", I("tensor_copy", out=prm[:, 0:1], in_=ps[:, 0:1]), reads=[psb[0]], writes=[prm_b])
        S.op("dve", I("tensor_scalar", out=prm[:, 1:2], in0=prm[:, 1:2], scalar1=1.0 - lam_init, scalar2=None, op0=ALU.mult),
             reads=[prm_b], writes=[prm_b])

        QR = Ring(S, p3, "qr", 3, [128, 512], BF16, dma=True)
        ZR_ = Ring(S, p3, "za", 3, [128, 512], BF16, dma=True)
        PT = Ring(S, p3, "pT", 3, [128, 1024], BF16)
        WK = Ring(S, p3, "wk", 2, [128, 4, 512], F32)
        SQ = Ring(S, p3, "sq", 2, [128, 512], BF16)
        YO = Ring(S, p3, "yo", 2, [128, 512], BF16, dma=True, fresh=True)
        ACC = Ring(S, p3, "acc", 2, [128, 512], F32)
        ACC1 = Ring(S, p3, "acc1", 2, [128, 512], F32)
        sc_i = [0]

        groups = [(t0, tn, list(range(NT))) for (t0, tn) in TG[:8]]
        if need_ctx:
            groups.append((4096, 256, [32, 33]))
        heads = []
        for (t0, tn, ktiles) in groups:
            for h in range(4):
                heads.append(dict(t0=t0, tn=tn, kts=ktiles, h=h))
        steps = []
        for hi, hd in enumerate(heads):
            for ki, kt in enumerate(hd["kts"]):
                steps.append((hi, ki, kt))

        def emit_loads(hd):
            t0, tn, h = hd["t0"], hd["tn"], hd["h"]
            qt, qtb, qts = QR.next()
            S.dma("sp", qts, [I("dma_start", out=qt[:, 0:tn], in_=G.QT[h * 128:(h + 1) * 128, t0:t0 + tn])],
                  reads=[scr_buf(G, "QT", h)], writes=[qtb])
            za, zab, zas = ZR_.next()
            S.dma("sp", zas, [I("dma_start", out=za[:, 0:tn], in_=G.ZA[h * 128:(h + 1) * 128, t0:t0 + tn])],
                  reads=[scr_buf(G, "ZA", h)], writes=[zab])
            hd.update(qt=qt, qtb=qtb, za=za, zab=zab)

        def emit_S(step):
            hi, ki, kt = step
            hd = heads[hi]
            tn, h = hd["tn"], hd["h"]
            sb0 = (sc_i[0] % 2) * 2
            sc_i[0] += 1
            sc = ps[:, sb0 * 512:(sb0 + 2) * 512]
            scb = [psb[sb0], psb[sb0 + 1]]
            S.op("pe", [I("matmul", out=sc[:, c * 512:c * 512 + tn], lhsT=KTs[c * 64:(c + 1) * 64, h, kt * 128:(kt + 1) * 128],
                          rhs=hd["qt"][c * 64:(c + 1) * 64, 0:tn], start=True, stop=True) for c in range(2)],
                 reads=[KT_b[h], hd["qtb"]], writes=scb)
            return sc, scb

        def emit_exp_pv(step, sc, scb):
            hi, ki, kt = step
            hd = heads[hi]
            tn, h = hd["tn"], hd["h"]
            nk = len(hd["kts"])
            pT, pTb, _ = PT.next()
            if tn == 512:
                S.op("act", I("activation", out=pT[:, :], in_=sc[:, :], func=AF.Exp, scale=0.125), reads=scb, writes=[pTb])
            else:
                S.op("act", I("activation", out=pT[:].rearrange("p (c n) -> p c n", c=2)[:, :, 0:tn],
                              in_=sc.rearrange("p (c n) -> p c n", c=2)[:, :, 0:tn], func=AF.Exp, scale=0.125),
                     reads=scb, writes=[pTb])
            mm = []
            for c in range(2):
                mm.append(I("matmul", out=ps[:, (4 + 2 * c) * 512:(4 + 2 * c) * 512 + tn],
                            lhsT=Vs[:, kt, h * 128:(h + 1) * 128], rhs=pT[:, c * 512:c * 512 + tn],
                            start=(ki == 0), stop=(ki == nk - 1)))
            mm.append(I("matmul", out=ps[:, 7 * 512:7 * 512 + tn], lhsT=G.ones_b[:, :], rhs=pT[:, 512:512 + tn],
                        start=(ki == 0), stop=(ki == nk - 1)))
            mm.append(I("matmul", out=ps[:, 5 * 512:5 * 512 + tn], lhsT=G.ones_b[:, :], rhs=pT[:, 0:tn],
                        start=(ki == 0), stop=(ki == nk - 1)))
            S.op("pe", mm, reads=[V_b, pTb, G.b_const], writes=[psb[4], psb[5], psb[6], psb[7]])

        def emit_combine(hd):
            t0, tn, h = hd["t0"], hd["tn"], hd["h"]
            wk, wkb, _ = WK.next()
            S.op("dve", I("tensor_copy", out=wk[:, 0, 0:tn], in_=ps[:, 4 * 512:4 * 512 + tn]), reads=[psb[4]], writes=[wkb])
            S.op("dve", I("tensor_copy", out=wk[:, 1, 0:tn], in_=ps[:, 6 * 512:6 * 512 + tn]), reads=[psb[6]], writes=[wkb])
            S.op("dve", I("tensor_copy", out=wk[:, 3, 0:tn], in_=ps[:, 7 * 512:7 * 512 + tn]), reads=[psb[7]], writes=[wkb])
            S.op("dve", I("tensor_copy", out=wk[:, 2, 0:tn], in_=ps[:, 5 * 512:5 * 512 + tn]), reads=[psb[5]], writes=[wkb])
            S.op("dve", I("reciprocal", out=wk[:, 2, 0:tn], in_=wk[:, 2, 0:tn]), reads=[wkb], writes=[wkb])
            S.op("dve", I("tensor_tensor", out=wk[:, 0, 0:tn], in0=wk[:, 0, 0:tn], in1=wk[:, 2, 0:tn], op=ALU.mult),
                 reads=[wkb], writes=[wkb])
            S.op("dve", I("reciprocal", out=wk[:, 3, 0:tn], in_=wk[:, 3, 0:tn]), reads=[wkb], writes=[wkb])
            S.op("dve", I("tensor_tensor", out=wk[:, 1, 0:tn], in0=wk[:, 1, 0:tn], in1=wk[:, 3, 0:tn], op=ALU.mult),
                 reads=[wkb], writes=[wkb])
            S.op("dve", I("scalar_tensor_tensor", out=wk[:, 2, 0:tn], in0=wk[:, 1, 0:tn], scalar=prm[:, 0:1],
                          in1=wk[:, 0, 0:tn], op0=ALU.mult, op1=ALU.add), reads=[wkb, prm_b], writes=[wkb])
            sq, sqb, _ = SQ.next()
            S.op("pool", I("tensor_tensor", out=sq[:, 0:tn], in0=wk[:, 2, 0:tn], in1=wk[:, 2, 0:tn], op=ALU.mult),
                 reads=[wkb], writes=[sqb])
            hd.update(wk=wk, wkb=wkb, sq=sq, sqb=sqb)

        def emit_finish(hd, slot):
            t0, tn, h = hd["t0"], hd["tn"], hd["h"]
            za, zab, wk, wkb, sq, sqb = hd["za"], hd["zab"], hd["wk"], hd["wkb"], hd["sq"], hd["sqb"]
            sc_, scb_ = slot
            S.op("pe", I("matmul", out=sc_[:, 0:tn], lhsT=G.ones_b[:, :], rhs=sq[:, 0:tn],
                         start=True, stop=True), reads=[sqb, G.b_const], writes=[scb_[0]])
            S.op("act", I("activation", out=wk[:, 3, 0:tn], in_=sc_[:, 0:tn], func=AF.Ln,
                          scale=1.0 / 128, bias=G.eps_col[:, 0:1]), reads=[scb_[0], G.b_const], writes=[wkb])
            S.op("act", I("activation", out=wk[:, 3, 0:tn], in_=wk[:, 3, 0:tn], func=AF.Exp, scale=-0.5),
                 reads=[wkb], writes=[wkb])
            S.op("dve", I("tensor_tensor", out=wk[:, 2, 0:tn], in0=wk[:, 2, 0:tn], in1=wk[:, 3, 0:tn], op=ALU.mult),
                 reads=[wkb], writes=[wkb])
            yo, yob, yos = YO.next()
            S.op("dve", I("scalar_tensor_tensor", out=yo[:, 0:tn], in0=wk[:, 2, 0:tn], scalar=prm[:, 1:2],
                          in1=za[:, 0:tn], op0=ALU.mult, op1=ALU.mult), reads=[wkb, prm_b, zab], writes=[yob])
            S.dma("pool", yos, [I("dma_start", out=G.YG[0, h * 128:(h + 1) * 128, t0:t0 + tn], in_=yo[:, 0:tn])],
                  reads=[yob], writes=[scr_buf(G, "YG0", h)])

        items = s5_prep(G, l, p3)
        G.s5_items = items
        per_step = -(-len(items) // max(1, len(steps) - 8))
        emit_loads(heads[0])
        if len(heads) > 1:
            emit_loads(heads[1])
        cur = emit_S(steps[0])
        pending_finish = None
        for i, step in enumerate(steps):
            hi, ki, kt = step
            nxt = None
            if i + 1 < len(steps):
                nhi = steps[i + 1][0]
                if "qt" not in heads[nhi]:
                    emit_loads(heads[nhi])
                nxt = emit_S(steps[i + 1])
            emit_exp_pv(step, cur[0], cur[1])
            replay(S, items, per_step)
            last_k = (ki == len(heads[hi]["kts"]) - 1)
            if pending_finish is not None and (ki == 5 or last_k):
                emit_finish(pending_finish, cur)
                nl = pending_finish["idx"] + 2
                if nl < len(heads) and "qt" not in heads[nl]:
                    emit_loads(heads[nl])
                pending_finish = None
            if last_k:
                emit_combine(heads[hi])
                heads[hi]["idx"] = hi
                pending_finish = heads[hi]
            cur = nxt
        if pending_finish is not None:
            emit_finish(pending_finish, (ps[:, 0:1024], [psb[0], psb[1]]))
            pending_finish = None
        replay(S, items, len(items))
        if "YG" in G.dbg:
            S.barrier()
            dsem = S.new_dma_sem("dbg")
            S.dma("sp", dsem, [I("dma_start", out=G.dbg["YG"][:, :, :], in_=G.YG[:, :, :])])
        S.barrier()
        S.release_to(mk)


def phase4_lru(G, l):
    nc, S = G.nc, G.S
    ps, psb = G.psall, G.psb
    mk = S.mark()
    SEGS = [(0, SEQ), (SEQ, CTX)]
    CH = [(i * 1024, 1024) for i in range(4)] + [(4096, 256)]
    with ExitStack() as p4:
        sbt = lambda name, shape, dt: p4.enter_context(nc.sbuf_tensor(U(name), list(shape), dt))
        prm = sbt("lprm", [128, 44], F32)
        sp8 = sbt("sp8", [128, 8], F32)
        one_col = sbt("one_col", [128, 1], F32)
        prm_b = Buf("lprm")
        wf = sbt("lwf", [128, 16, 128], F32)
        wb = sbt("lwb", [128, 16, 128], BF16)
        w_b = Buf("lw")
        psem = S.new_dma_sem("lprm")
        S.dma("sp", psem, [I("dma_start", out=prm[:], in_=G.lrup[l, :, :]),
                           I("dma_start", out=wf[:], in_=G.lruw[l].rearrange("a d c p n -> p (a d c) n"))],
              writes=[prm_b, w_b])
        S.op("pool", I("tensor_copy", out=wb[:], in_=wf[:]), reads=[w_b], writes=[w_b])
        S.op("pool", I("memset", ap=one_col[:], constant=1.0), writes=[prm_b])
        S.op("act", I("activation", out=sp8[:], in_=prm[:, 36:44], func=AF.Exp, scale=-1.0), reads=[prm_b], writes=[prm_b])
        S.op("act", I("activation", out=sp8[:], in_=sp8[:], func=AF.Ln, bias=one_col[:, 0:1]), reads=[prm_b], writes=[prm_b])
        S.op("dve", I("tensor_scalar", out=sp8[:], in0=sp8[:], scalar1=-8.0, scalar2=None, op0=ALU.mult),
             reads=[prm_b], writes=[prm_b])

        xr = sbt("xr", [128, T], F32); xr_b = Buf("xr")
        u = sbt("u", [128, T], F32); u_b = Buf("u")
        ub = sbt("ub", [128, T], BF16); ub_b = Buf("ub")
        ra = sbt("ra", [128, T], F32); ra_b = Buf("ra")
        ib = sbt("ib", [128, T], F32); ib_b = Buf("ib")
        tmp = sbt("ltmp", [128, T], F32); tmp_b = Buf("ltmp")
        hf = sbt("hf", [128, T], F32); hf_b = Buf("hf")
        hbr = sbt("hbr", [128, T], F32); hbr_b = Buf("hbr")
        zr = sbt("zr", [128, T], BF16); zr_b = Buf("zr")
        yo = sbt("lyo", [128, T], BF16); yo_b = Buf("lyo")
        xsem = S.new_dma_sem("xr"); zsem = S.new_dma_sem("zr"); ysem = S.new_dma_sem("lyo", fresh=True)
        for ct in range(4):
            S.dma("sp", xsem, [I("dma_start", out=xr[:, :], in_=G.XR[ct * 128:(ct + 1) * 128, :])],
                  reads=[scr_buf(G, "XR", ct)], writes=[xr_b])
            S.dma("sp", zsem, [I("dma_start", out=zr[:, :], in_=G.ZR[ct * 128:(ct + 1) * 128, :])],
                  reads=[scr_buf(G, "ZR", ct)], writes=[zr_b])
            for (s0, n) in SEGS:
                S.op("pool", I("tensor_scalar", out=u[:, s0:s0 + n], in0=xr[:, s0:s0 + n],
                               scalar1=prm[:, ct * 4 + 2:ct * 4 + 3], scalar2=prm[:, 16 + ct:17 + ct],
                               op0=ALU.mult, op1=ALU.add), reads=[xr_b, prm_b], writes=[u_b])
                for (k, oa, ob_, ia, ib_) in ((0, 2, n, 0, n - 2), (1, 1, n, 0, n - 1), (3, 0, n - 1, 1, n)):
                    S.op("dve", I("scalar_tensor_tensor", out=u[:, s0 + oa:s0 + ob_], in0=xr[:, s0 + ia:s0 + ib_],
                                   scalar=prm[:, ct * 4 + k:ct * 4 + k + 1], in1=u[:, s0 + oa:s0 + ob_],
                                   op0=ALU.mult, op1=ALU.add), reads=[xr_b, prm_b, u_b], writes=[u_b])
            S.op("act", I("activation", out=ub[:, :], in_=u[:, :], func=AF.Copy), reads=[u_b], writes=[ub_b])
            for d in range(2):
                for ci, (c0, cn) in enumerate(CH):
                    bk = (ci % 2) * 4
                    for gi in range(2):
                        widx = gi * 8 + d * 4 + ct
                        mm = []
                        for s in range(0, cn, 512):
                            sn = min(512, cn - s)
                            mm.append(I("matmul", out=ps[:, (bk + gi * 2) * 512 + s:(bk + gi * 2) * 512 + s + sn],
                                        lhsT=wb[:, widx, :], rhs=ub[:, c0 + s:c0 + s + sn], start=True, stop=True))
                        S.op("pe", mm, reads=[w_b, ub_b], writes=[psb[bk + gi * 2], psb[bk + gi * 2 + 1]])
                        dst, dst_b = (ra, ra_b) if gi == 0 else (ib, ib_b)
                        bcol = (20 if gi == 0 else 28) + d * 4 + ct
                        S.op("act", I("activation", out=dst[:, c0:c0 + cn], in_=ps[:, (bk + gi * 2) * 512:(bk + gi * 2) * 512 + cn],
                                      func=AF.Sigmoid, bias=prm[:, bcol:bcol + 1]),
                             reads=[psb[bk + gi * 2], psb[bk + gi * 2 + 1], prm_b], writes=[dst_b])
                S.op("act", I("activation", out=ra[:, :], in_=ra[:, :], func=AF.Exp, scale=sp8[:, d * 4 + ct:d * 4 + ct + 1]),
                     reads=[ra_b, prm_b], writes=[ra_b])
                S.op("dve", I("tensor_tensor", out=tmp[:, :], in0=ra[:, :], in1=ra[:, :], op=ALU.mult), reads=[ra_b], writes=[tmp_b])
                S.op("act", I("activation", out=tmp[:, :], in_=tmp[:, :], func=AF.Sqrt, scale=-1.0, bias=one_col[:, 0:1]),
                     reads=[tmp_b, prm_b], writes=[tmp_b])
                S.op("pool", I("tensor_tensor", out=ib[:, :], in0=ib[:, :], in1=u[:, :], op=ALU.mult), reads=[ib_b, u_b], writes=[ib_b])
                S.op("dve", I("tensor_tensor", out=ib[:, :], in0=ib[:, :], in1=tmp[:, :], op=ALU.mult), reads=[ib_b, tmp_b], writes=[ib_b])
                if d == 0:
                    S.op("dve", I("tensor_tensor_scan", out=hf[:, SEQ:T], data0=ra[:, SEQ:T], data1=ib[:, SEQ:T], initial=0.0,
                                  op0=ALU.mult, op1=ALU.add), reads=[ra_b, ib_b], writes=[hf_b])
                    S.op("dve", I("tensor_tensor_scan", out=hf[:, 0:SEQ], data0=ra[:, 0:SEQ], data1=ib[:, 0:SEQ],
                                  initial=hf[:, T - 1:T], op0=ALU.mult, op1=ALU.add), reads=[ra_b, ib_b, hf_b], writes=[hf_b])
                else:
                    S.op("dve", I("tensor_tensor_scan", out=hbr[:, 0:CTX], data0=ra[:, SEQ:T][:, ::-1], data1=ib[:, SEQ:T][:, ::-1],
                                  initial=0.0, op0=ALU.mult, op1=ALU.add), reads=[ra_b, ib_b], writes=[hbr_b])
                    S.op("dve", I("tensor_tensor_scan", out=hbr[:, CTX:T], data0=ra[:, 0:SEQ][:, ::-1], data1=ib[:, 0:SEQ][:, ::-1],
                                  initial=hbr[:, CTX - 1:CTX], op0=ALU.mult, op1=ALU.add), reads=[ra_b, ib_b, hbr_b], writes=[hbr_b])
            S.op("dve", I("tensor_tensor", out=hf[:, 0:SEQ], in0=hf[:, 0:SEQ], in1=hbr[:, CTX:T][:, ::-1], op=ALU.add),
                 reads=[hf_b, hbr_b], writes=[hf_b])
            S.op("dve", I("tensor_tensor", out=hf[:, SEQ:T], in0=hf[:, SEQ:T], in1=hbr[:, 0:CTX][:, ::-1], op=ALU.add),
                 reads=[hf_b, hbr_b], writes=[hf_b])
            S.op("pool", I("tensor_tensor", out=yo[:, :], in0=hf[:, :], in1=zr[:, :], op=ALU.mult),
                 reads=[hf_b, zr_b], writes=[yo_b])
            S.dma("pool", ysem, [I("dma_start", out=G.YG[1, ct * 128:(ct + 1) * 128, :], in_=yo[:, :])],
                  reads=[yo_b], writes=[scr_buf(G, "YG1", ct)])
        if "YG" in G.dbg:
            S.barrier()
            dsem = S.new_dma_sem("dbg")
            S.dma("sp", dsem, [I("dma_start", out=G.dbg["YG"][:, :, :], in_=G.YG[:, :, :])])
        S.barrier()
        S.release_to(mk)


TWO_PI = 2.0 * math.pi
I32 = mybir.dt.int32


class Rec:
    def __init__(self, S):
        self.S = S
        self.items = []

    def new_dma_sem(self, name):
        return self.S.new_dma_sem(name)

    def op(self, eng, fn, reads=(), writes=()):
        self.items.append(("op", eng, fn, list(reads), list(writes)))

    def dma(self, eng, semkey, fns, reads=(), writes=()):
        self.items.append(("dma", eng, semkey, fns, list(reads), list(writes)))


def replay(S, items, n):
    for _ in range(n):
        if not items:
            return
        it = items.pop(0)
        if it[0] == "op":
            S.op(it[1], it[2], it[3], it[4])
        else:
            S.dma(it[1], it[2], it[3], it[4], it[5])


NCH = T // 8


def s5_prep(G, l, stack):
    nc = G.nc
    R = Rec(G.S)
    S = R
    sbt = lambda name, shape, dt: stack.enter_context(nc.sbuf_tensor(U(name), list(shape), dt))
    tab = sbt("s5tab", [128, 36 + 512 + 544 + 128], F32)
    ptab = tab[:, 0:36].rearrange("p (j q) -> p j q", q=9)
    etab = tab[:, 36:548].rearrange("p (e n) -> p e n", n=64)
    ctab = tab[:, 548:1092]
    mask = tab[:, 1092:1220].rearrange("p (g n) -> p g n", n=16)
    dtall = sbt("dtall", [128, 8], F32)
    cst_b = Buf("s5cst")
    csem = S.new_dma_sem("s5c")
    S.dma("sp", csem, [I("dma_start", out=tab[:], in_=G.s5tab[:, :]),
                       I("dma_start", out=dtall[:], in_=G.s5dt[l, :, :])], writes=[cst_b])
    S.op("act", I("activation", out=dtall[:], in_=dtall[:], func=AF.Exp), reads=[cst_b], writes=[cst_b])
    lamN = sbt("lamN", [128, 8], F32)
    BN = sbt("BN", [128, 2, 4, 16], F32)
    CN = sbt("CN", [128, 2, 4, 16], F32)
    nsem = S.new_dma_sem("s5n")
    np_b = Buf("nprm")
    n4 = sbt("n4", [128, 12, 4], F32)
    nP = sbt("nP", [128, 8, 4, 9], F32)
    nPi = sbt("nPi", [128, 4, 9], I32)
    bbN = sbt("bbN", [128, 2, 4, 16], F32)
    tN = sbt("tN", [128, 4, 4, 8, 16], F32)
    Hp = sbt("Hp", [128, 4, 64], F32)
    hsem = S.new_dma_sem("s5h")
    hp_b = Buf("hprm")
    h64 = sbt("h64", [128, 10, 64], F32)
    hE = sbt("hE", [128, 4, 8, 64], F32)
    W1d = sbt("W1d", [128, 8, 128], F32)
    tNf = tN[:].rearrange("p a b c d -> p a (b c d)")
    hEi = tNf[:, 2, :].bitcast(I32).rearrange("p (e n) -> p e n", n=64)
    OUT = [sbt("s5out%d" % i, [128, 8192], BF16) for i in range(2)]
    out_b = [Buf("s5out%d" % i) for i in range(2)]
    out_s = [S.new_dma_sem("s5o%d" % i) for i in range(2)]
    oi = [0]
    tc = sbt("tc", [128, 7, NCH], F32)
    tc_b = Buf("tc")
    tcs = S.new_dma_sem("tc")
    rsem = S.new_dma_sem("rh")

    def trig(turns, ti, tf2, r, out2, bufs, eng="pool"):
        for which in (0, 1):
            tf = tf2[which]
            S.op(eng, I("tensor_scalar", out=tf, in0=turns, scalar1=16.25 - 0.25 * which, scalar2=None, op0=ALU.add),
                 reads=bufs, writes=bufs)
            S.op(eng, I("tensor_copy", out=ti, in_=tf), reads=bufs, writes=bufs)
            S.op(eng, I("tensor_copy", out=r, in_=ti), reads=bufs, writes=bufs)
            S.op(eng, I("tensor_tensor", out=tf, in0=tf, in1=r, op=ALU.subtract), reads=bufs, writes=bufs)
            S.op(eng, I("tensor_single_scalar", out=r, in_=tf, scalar=0.5, op=ALU.is_ge), reads=bufs, writes=bufs)
            S.op(eng, I("tensor_tensor", out=tf, in0=tf, in1=r, op=ALU.subtract), reads=bufs, writes=bufs)
        S.op("act", I("activation", out=out2, in_=tf2[2], func=AF.Sin, scale=TWO_PI), reads=bufs, writes=bufs)

    def cplx_coef(pr1, pi1, lr, li, t, bre, bim, obr, obi, shp_b, bufs, eng="pool", tfull=None):
        nr, den, cr, ci, t4, t5 = t
        S.op(eng, I("tensor_scalar", out=nr, in0=pr1, scalar1=-1.0, scalar2=None, op0=ALU.add), reads=bufs, writes=bufs)
        S.op(eng, I("tensor_tensor", out=den, in0=lr, in1=lr, op=ALU.mult), reads=bufs, writes=bufs)
        S.op(eng, I("tensor_tensor", out=t4, in0=li, in1=li, op=ALU.mult), reads=bufs, writes=bufs)
        S.op(eng, I("tensor_tensor", out=den, in0=den, in1=t4, op=ALU.add), reads=bufs, writes=bufs)
        S.op("dve", I("reciprocal", out=den, in_=den), reads=bufs, writes=bufs)
        S.op(eng, I("tensor_tensor", out=cr, in0=nr, in1=lr, op=ALU.mult), reads=bufs, writes=bufs)
        S.op(eng, I("tensor_tensor", out=t4, in0=pi1, in1=li, op=ALU.mult), reads=bufs, writes=bufs)
        S.op(eng, I("tensor_tensor", out=cr, in0=cr, in1=t4, op=ALU.add), reads=bufs, writes=bufs)
        S.op(eng, I("tensor_tensor", out=cr, in0=cr, in1=den, op=ALU.mult), reads=bufs, writes=bufs)
        S.op(eng, I("tensor_tensor", out=ci, in0=pi1, in1=lr, op=ALU.mult), reads=bufs, writes=bufs)
        S.op(eng, I("tensor_tensor", out=t4, in0=nr, in1=li, op=ALU.mult), reads=bufs, writes=bufs)
        S.op(eng, I("tensor_tensor", out=ci, in0=ci, in1=t4, op=ALU.subtract), reads=bufs, writes=bufs)
        S.op(eng, I("tensor_tensor", out=ci, in0=ci, in1=den, op=ALU.mult), reads=bufs, writes=bufs)
        crb = cr if shp_b is None else cr.unsqueeze(2).broadcast_to(shp_b)
        cib = ci if shp_b is None else ci.unsqueeze(2).broadcast_to(shp_b)
        S.op(eng, I("tensor_tensor", out=obr, in0=bre, in1=crb, op=ALU.mult), reads=bufs, writes=bufs)
        S.op(eng, I("tensor_tensor", out=obi, in0=bim, in1=cib, op=ALU.mult), reads=bufs, writes=bufs)
        S.op(eng, I("tensor_tensor", out=obr, in0=obr, in1=obi, op=ALU.subtract), reads=bufs, writes=bufs)
        S.op(eng, I("tensor_tensor", out=obi, in0=bim, in1=crb, op=ALU.mult), reads=bufs, writes=bufs)
        t6 = t5 if tfull is None else tfull
        S.op(eng, I("tensor_tensor", out=t6, in0=bre, in1=cib, op=ALU.mult), reads=bufs, writes=bufs)
        S.op(eng, I("tensor_tensor", out=obi, in0=obi, in1=t6, op=ALU.add), reads=bufs, writes=bufs)

    def out_slot():
        i = oi[0] % 2
        oi[0] += 1
        return OUT[i], out_b[i], out_s[i]

    cnsem = S.new_dma_sem("s5cn")
    for gt in range(4):
        S.dma("sp", cnsem, [I("dma_start", out=CN[:], in_=G.s5CN[l, gt])], writes=[np_b])
        for d in range(2):
            idx = gt * 2 + d
            dcol = dtall[:, d * 4 + gt:d * 4 + gt + 1]
            nb = [np_b, cst_b]
            S.dma("sp", nsem, [I("dma_start", out=lamN[:], in_=G.s5N[l, d, gt]),
                               I("dma_start", out=BN[:], in_=G.s5BN[l, d, gt])], writes=[np_b])
            ld4, tu4 = n4[:, 0, :], n4[:, 1, :]
            S.op("pool", I("tensor_scalar", out=ld4, in0=lamN[:, 0:4], scalar1=dcol, scalar2=None, op0=ALU.mult), reads=nb, writes=nb)
            S.op("pool", I("tensor_scalar", out=tu4, in0=lamN[:, 4:8], scalar1=dcol, scalar2=1.0 / TWO_PI,
                           op0=ALU.mult, op1=ALU.mult), reads=nb, writes=nb)
            angP, mgP, cosP, sinP, prP, piP, tfP, rP = [nP[:, i] for i in range(8)]
            S.op("pool", I("tensor_tensor", out=angP, in0=ptab, in1=tu4.unsqueeze(2).broadcast_to([128, 4, 9]), op=ALU.mult), reads=nb, writes=nb)
            S.op("pool", I("tensor_tensor", out=mgP, in0=ptab, in1=ld4.unsqueeze(2).broadcast_to([128, 4, 9]), op=ALU.mult), reads=nb, writes=nb)
            S.op("act", I("activation", out=mgP, in_=mgP, func=AF.Exp), reads=nb, writes=nb)
            trig(angP, nPi[:], (nP[:, 4], nP[:, 5], nP[:, 4:6]), rP, nP[:, 2:4], nb)
            S.op("pool", I("tensor_tensor", out=prP, in0=mgP, in1=cosP, op=ALU.mult), reads=nb, writes=nb)
            S.op("pool", I("tensor_tensor", out=piP, in0=mgP, in1=sinP, op=ALU.mult), reads=nb, writes=nb)
            cplx_coef(prP[:, :, 1], piP[:, :, 1], lamN[:, 0:4], lamN[:, 4:8], [n4[:, i, :] for i in range(2, 8)],
                      BN[:, 0], BN[:, 1], bbN[:, 0], bbN[:, 1], [128, 4, 16], nb, tfull=tN[:, 3, :, 0, :])
            S.op("pool", I("tensor_copy", out=n4[:, 10, :], in_=mgP[:, :, 8]), reads=nb, writes=nb)
            S.dma("sp", rsem, [I("dma_start", out=G.S5RH[idx], in_=n4[:, 10, :])], reads=nb)
            f8 = n4[:, 8, :]
            S.op("pool", I("tensor_copy", out=nPi[:, :, 0], in_=angP[:, :, 8]), reads=nb, writes=nb)
            S.op("pool", I("tensor_copy", out=n4[:, 9, :], in_=nPi[:, :, 0]), reads=nb, writes=nb)
            S.op("pool", I("tensor_tensor", out=f8, in0=angP[:, :, 8], in1=n4[:, 9, :], op=ALU.subtract), reads=nb, writes=nb)
            for jc in range(4):
                tb = [tc_b, np_b, cst_b]
                S.op("dve", I("tensor_scalar", out=tc[:, 0, :], in0=ctab, scalar1=f8[:, jc:jc + 1], scalar2=None, op0=ALU.mult),
                     reads=tb, writes=tb)
                trig(tc[:, 0, :], tc[:, 5, :].bitcast(I32), (tc[:, 1, :], tc[:, 2, :], tc[:, 1:3, :]), tc[:, 6, :], tc[:, 3:5, :], tb, eng="dve")
                S.dma("sp", tcs, [I("dma_start", out=G.S5TB[idx, :, 0, jc, :], in_=tc[:, 3, :]),
                                  I("dma_start", out=G.S5TB[idx, :, 1, jc, :], in_=tc[:, 4, :])], reads=tb)
            hb = [hp_b, cst_b]
            S.dma("sp", hsem, [I("dma_start", out=Hp[:], in_=G.s5H[l, d, gt])], writes=[hp_b])
            ldH, tuH = h64[:, 0, :], h64[:, 1, :]
            S.op("pool", I("tensor_scalar", out=ldH, in0=Hp[:, 0, :], scalar1=dcol, scalar2=None, op0=ALU.mult), reads=hb, writes=hb)
            S.op("pool", I("tensor_scalar", out=tuH, in0=Hp[:, 1, :], scalar1=dcol, scalar2=1.0 / TWO_PI,
                           op0=ALU.mult, op1=ALU.mult), reads=hb, writes=hb)
            angE, mgE, cosE, sinE = [hE[:, i] for i in range(4)]
            tfE = tNf[:, 0, :].rearrange("p (e n) -> p e n", n=64)
            rE = tNf[:, 1, :].rearrange("p (e n) -> p e n", n=64)
            S.op("pool", I("tensor_tensor", out=angE, in0=etab, in1=tuH.unsqueeze(1).broadcast_to([128, 8, 64]), op=ALU.mult), reads=hb, writes=hb)
            S.op("pool", I("tensor_tensor", out=mgE, in0=etab, in1=ldH.unsqueeze(1).broadcast_to([128, 8, 64]), op=ALU.mult), reads=hb, writes=hb)
            S.op("act", I("activation", out=mgE, in_=mgE, func=AF.Exp), reads=hb, writes=hb)
            trig(angE, hEi, (tfE, rE, tNf[:, 0:2, :].rearrange("p a (e n) -> p a e n", n=64)),
                 tNf[:, 3, :].rearrange("p (e n) -> p e n", n=64), hE[:, 2:4], hb + [np_b], eng="dve")
            S.op("pool", I("tensor_tensor", out=cosE, in0=mgE, in1=cosE, op=ALU.mult), reads=hb, writes=hb)
            S.op("pool", I("tensor_tensor", out=sinE, in0=mgE, in1=sinE, op=ALU.mult), reads=hb, writes=hb)
            bbrH, bbiH = h64[:, 8, :], h64[:, 9, :]
            cplx_coef(cosE[:, 1, :], sinE[:, 1, :], Hp[:, 0, :], Hp[:, 1, :], [h64[:, i, :] for i in range(2, 8)],
                      Hp[:, 2, :], Hp[:, 3, :], bbrH, bbiH, None, hb, eng="pool")
            bbrB = bbrH.unsqueeze(1).broadcast_to([128, 8, 64]); bbiB = bbiH.unsqueeze(1).broadcast_to([128, 8, 64])
            hb2 = hb + [np_b]
            S.op("pool", I("tensor_tensor", out=W1d[:, :, 0:64], in0=cosE, in1=bbrB, op=ALU.mult), reads=hb, writes=hb)
            S.op("pool", I("tensor_tensor", out=tfE, in0=sinE, in1=bbiB, op=ALU.mult), reads=hb2, writes=hb2)
            S.op("pool", I("tensor_tensor", out=W1d[:, :, 0:64], in0=W1d[:, :, 0:64], in1=tfE, op=ALU.subtract), reads=hb2, writes=hb)
            S.op("pool", I("tensor_tensor", out=W1d[:, :, 64:128], in0=cosE, in1=bbiB, op=ALU.mult), reads=hb, writes=hb)
            S.op("pool", I("tensor_tensor", out=tfE, in0=sinE, in1=bbrB, op=ALU.mult), reads=hb2, writes=hb2)
            S.op("pool", I("tensor_tensor", out=W1d[:, :, 64:128], in0=W1d[:, :, 64:128], in1=tfE, op=ALU.add), reads=hb2, writes=hb)
            ot, otb, ots = out_slot()
            W1bd = ot[:].rearrange("p (e j n) -> p e j n", e=8, j=8)
            for e in range(8):
                S.op("pool" if e % 4 else "dve",
                     I("tensor_tensor", out=W1bd[:, e].rearrange("p j (g n) -> p j g n", n=16),
                       in0=W1d[:, e, :].rearrange("p (j n) -> p j n", n=16).unsqueeze(2).broadcast_to([128, 8, 8, 16]),
                       in1=mask.unsqueeze(1).broadcast_to([128, 8, 8, 16]), op=ALU.mult),
                     reads=hb, writes=[otb])
            S.dma("sp", ots, [I("dma_start", out=G.S5W1[idx], in_=ot[:])], reads=[otb])
            if d == 0:
                prS, piS = prP[:, :, 1:9], piP[:, :, 1:9]
            else:
                prS, piS = prP[:, :, 1:9][:, :, ::-1], piP[:, :, 1:9][:, :, ::-1]
            prB = prS.unsqueeze(3).broadcast_to([128, 4, 8, 16]); piB = piS.unsqueeze(3).broadcast_to([128, 4, 8, 16])
            creB = CN[:, 0].unsqueeze(2).broadcast_to([128, 4, 8, 16]); cimB = CN[:, 1].unsqueeze(2).broadcast_to([128, 4, 8, 16])
            wR, wI, t2, t3 = [tN[:, i] for i in range(4)]
            S.op("pool", I("tensor_tensor", out=wR, in0=creB, in1=prB, op=ALU.mult), reads=nb, writes=nb)
            S.op("pool", I("tensor_tensor", out=t2, in0=cimB, in1=piB, op=ALU.mult), reads=nb, writes=nb)
            S.op("pool", I("tensor_tensor", out=wR, in0=wR, in1=t2, op=ALU.subtract), reads=nb, writes=nb)
            S.op("pool", I("tensor_tensor", out=wI, in0=creB, in1=piB, op=ALU.mult), reads=nb, writes=nb)
            S.op("pool", I("tensor_tensor", out=t2, in0=cimB, in1=prB, op=ALU.mult), reads=nb, writes=nb)
            S.op("pool", I("tensor_tensor", out=wI, in0=wI, in1=t2, op=ALU.add), reads=nb, writes=nb)
            S.op("pool", I("tensor_scalar", out=wI, in0=wI, scalar1=-1.0, scalar2=None, op0=ALU.mult), reads=nb, writes=nb)
            ot, otb, ots = out_slot()
            W3bd = ot[:].rearrange("p (j s n) -> p j s n", j=8, s=8)
            for j in range(8):
                srcw = (wR if j < 4 else wI)[:, j % 4]
                S.op("pool" if j % 4 else "dve",
                     I("tensor_tensor", out=W3bd[:, j].rearrange("p s (g n) -> p s g n", n=16),
                       in0=srcw.unsqueeze(2).broadcast_to([128, 8, 8, 16]),
                       in1=mask.unsqueeze(1).broadcast_to([128, 8, 8, 16]), op=ALU.mult),
                     reads=nb, writes=[otb])
            S.dma("sp", ots, [I("dma_start", out=G.S5W3[idx], in_=ot[:])], reads=[otb])
            prK = prP[:, :, 0:8].unsqueeze(3).broadcast_to([128, 4, 8, 16]); piK = piP[:, :, 0:8].unsqueeze(3).broadcast_to([128, 4, 8, 16])
            bbrB = bbN[:, 0].unsqueeze(2).broadcast_to([128, 4, 8, 16]); bbiB = bbN[:, 1].unsqueeze(2).broadcast_to([128, 4, 8, 16])
            S.op("pool", I("tensor_tensor", out=wR, in0=bbrB, in1=prK, op=ALU.mult), reads=nb, writes=nb)
            S.op("pool", I("tensor_tensor", out=t2, in0=bbiB, in1=piK, op=ALU.mult), reads=nb, writes=nb)
            S.op("pool", I("tensor_tensor", out=wR, in0=wR, in1=t2, op=ALU.subtract), reads=nb, writes=nb)
            S.op("pool", I("tensor_tensor", out=wI, in0=bbiB, in1=prK, op=ALU.mult), reads=nb, writes=nb)
            S.op("pool", I("tensor_tensor", out=t2, in0=bbrB, in1=piK, op=ALU.mult), reads=nb, writes=nb)
            S.op("pool", I("tensor_tensor", out=wI, in0=wI, in1=t2, op=ALU.add), reads=nb, writes=nb)
            ot, otb, ots = out_slot()
            XBD = ot[:].rearrange("p (r q n) -> p r q n", r=2, q=32)
            for ri in range(2):
                srcx = (wR if ri == 0 else wI).rearrange("p j k h -> p (j k) h")
                for half in range(2):
                    S.op("pool",
                         I("tensor_tensor", out=XBD[:, ri, half * 16:(half + 1) * 16].rearrange("p q (g n) -> p q g n", n=16),
                           in0=srcx[:, half * 16:(half + 1) * 16].unsqueeze(2).broadcast_to([128, 16, 8, 16]),
                           in1=mask.unsqueeze(1).broadcast_to([128, 16, 8, 16]), op=ALU.mult),
                         reads=nb, writes=[otb])
            S.dma("sp", ots, [I("dma_start", out=G.S5XB[idx], in_=ot[:])], reads=[otb])
    return R.items


def phase5_s5(G, l):
    nc, S = G.nc, G.S
    ps, psb = G.psall, G.psb
    if G.s5_items is None:
        mk0 = S.mark()
        with ExitStack() as pp:
            items = s5_prep(G, l, pp)
            replay(S, items, len(items))
            S.barrier()
            S.release_to(mk0)
    else:
        assert not G.s5_items
    G.s5_items = None
    mk = S.mark()
    RANGES = [(0, 256), (256, 256), (512, 32)]
    with ExitStack() as p5:
        sbt = lambda name, shape, dt: p5.enter_context(nc.sbuf_tensor(U(name), list(shape), dt))
        tab = sbt("s5tabm", [128, 128], F32)
        mask = tab[:, 0:128].rearrange("p (g n) -> p g n", n=16)
        misc = sbt("s5misc", [128, 8], F32)
        cst_b = Buf("s5cst")
        csem = S.new_dma_sem("s5c")
        S.dma("sp", csem, [I("dma_start", out=tab[:], in_=G.s5tab[:, 1092:1220]),
                           I("dma_start", out=misc[:], in_=G.s5misc[l, :, :])], writes=[cst_b])
        Ut = sbt("Ut", [128, T], F32); U_b = Buf("Ut")
        Ub = sbt("Ub", [128, T], BF16); Ub_b = Buf("Ub")
        gst, gst_b = Ub, Ub_b
        gsem = S.new_dma_sem("gst", fresh=True)
        yt, y_b = Ut, U_b
        usem = S.new_dma_sem("s5u")
        CN = sbt("CN", [128, 2, 4, 16], F32)
        nsem = S.new_dma_sem("s5n")
        CBD = sbt("CBD", [128, 2, 4, 128], BF16); cbd_b = Buf("CBD")
        Kf = sbt("Kf", [128, 16, 128], BF16); k_b = Buf("Kf")
        K0 = sbt("K0", [128, 128], F32)
        Srot = sbt("Srot", [128, 8, NCH], F32); sr_b = Buf("Srot")
        Gs = sbt("Gs", [128, 8, NCH], F32); gs_b = Buf("Gs")
        Ebf = [sbt("Ebf%d" % d, [128, 8, NCH + 1], BF16) for d in range(2)]
        e_b = [Buf("Ebf%d" % d) for d in range(2)]
        rt = Gs[:, 0:4, :].rearrange("p a c -> p (a c)")[:, 0:2048].rearrange("p (r j c) -> p r j c", r=2, j=4); rt_b = gs_b
        WA = Ring(S, p5, "wa", 2, [128, 8192], BF16, dma=True)
        TBr = Ring(S, p5, "tbr", 2, [128, 2, 4, NCH], F32, dma=True)
        RHr = Ring(S, p5, "rhr", 2, [128, 4], F32, dma=True)
        XBr = Ring(S, p5, "xbr", 1, [128, 8192], BF16, dma=True)
        W3t = [sbt("W3bd%d" % d, [128, 8192], BF16) for d in range(2)]
        w3_b = [Buf("W3bd%d" % d) for d in range(2)]
        w3s = [S.new_dma_sem("w3%d" % d) for d in range(2)]
        loaded = {}

        def load_w(idx):
            if idx in loaded or idx >= 8:
                return
            wa, wab, was = WA.next()
            S.dma("sp", was, [I("dma_start", out=wa[:], in_=G.S5W1[idx])], writes=[wab])
            tb, tbb, tbs = TBr.next()
            S.dma("sp", tbs, [I("dma_start", out=tb[:], in_=G.S5TB[idx])], writes=[tbb])
            rh, rhb, rhs = RHr.next()
            S.dma("sp", rhs, [I("dma_start", out=rh[:], in_=G.S5RH[idx])], writes=[rhb])
            loaded[idx] = (wa, wab, tb, tbb, rh, rhb)

        load_w(0)
        for gt in range(4):
            S.dma("sp", usem, [I("dma_start", out=Ut[:, :], in_=G.US[gt * 128:(gt + 1) * 128, :])], writes=[U_b])
            S.op("act", I("activation", out=Ub[:, :], in_=Ut[:, :], func=AF.Copy), reads=[U_b], writes=[Ub_b])
            S.dma("sp", nsem, [I("dma_start", out=CN[:], in_=G.s5CN[l, gt])], writes=[cbd_b])
            for ri in range(2):
                S.op("pool", I("tensor_tensor", out=CBD[:, ri].rearrange("p j (g n) -> p j g n", n=16),
                               in0=CN[:, ri].unsqueeze(2).broadcast_to([128, 4, 8, 16]),
                               in1=mask.unsqueeze(1).broadcast_to([128, 4, 8, 16]), op=ALU.mult),
                     reads=[cst_b, cbd_b], writes=[cbd_b])
            S.op("pool", I("tensor_scalar", out=CBD[:, 1], in0=CBD[:, 1], scalar1=-1.0, scalar2=None, op0=ALU.mult),
                 reads=[cbd_b], writes=[cbd_b])
            for d in range(2):
                S.dma("sp", w3s[d], [I("dma_start", out=W3t[d][:], in_=G.S5W3[gt * 2 + d])], writes=[w3_b[d]])
            for d in range(2):
                idx = gt * 2 + d
                load_w(idx)
                wa, w1_b, tbl, tbl_b, rho, rho_b = loaded[idx]
                W1bd = wa[:].rearrange("p (e j n) -> p e j n", e=8, j=8)
                cosT, sinT = tbl[:, 0], tbl[:, 1]
                for ri_, (c0, n) in enumerate(RANGES):
                    b0 = (ri_ % 2) * 4
                    for j in range(8):
                        off = b0 * 512 + j * 256
                        S.op("pe", [I("matmul", out=ps[:, off:off + n], lhsT=W1bd[:, (7 - s) if d == 0 else s, j, :],
                                      rhs=Ub[:, c0 * 8 + s:(c0 + n) * 8:8], start=(s == 0), stop=(s == 7)) for s in range(8)],
                             reads=[w1_b, Ub_b], writes=psb[b0:b0 + 4])
                    pv = ps[:, b0 * 512:(b0 + 4) * 512].rearrange("p (j c) -> p j c", c=256)
                    Sre, Sim = pv[:, 0:4, 0:n], pv[:, 4:8, 0:n]
                    if d == 0:
                        cp = c0 + 32 if c0 < 512 else 0
                        cT, sT = cosT[:, :, cp:cp + n], sinT[:, :, cp:cp + n]
                        ore, oim = Srot[:, 0:4, cp:cp + n], Srot[:, 4:8, cp:cp + n]
                    else:
                        lo = NCH - 1 - (c0 + n - 1)
                        cT, sT = cosT[:, :, lo:lo + n][:, :, ::-1], sinT[:, :, lo:lo + n][:, :, ::-1]
                        ore, oim = Srot[:, 0:4, c0:c0 + n], Srot[:, 4:8, c0:c0 + n]
                    rb = psb[b0:b0 + 4] + [tbl_b]
                    S.op("dve", I("tensor_tensor", out=rt[:, 0, :, 0:n], in0=Sre, in1=cT, op=ALU.mult), reads=rb, writes=[rt_b])
                    S.op("dve", I("tensor_tensor", out=rt[:, 1, :, 0:n], in0=Sim, in1=sT, op=ALU.mult), reads=rb, writes=[rt_b])
                    S.op("pool", I("tensor_tensor", out=ore, in0=rt[:, 0, :, 0:n], in1=rt[:, 1, :, 0:n], op=ALU.add), reads=[rt_b], writes=[sr_b])
                    S.op("dve", I("tensor_tensor", out=rt[:, 0, :, 0:n], in0=Sim, in1=cT, op=ALU.mult), reads=rb, writes=[rt_b])
                    S.op("dve", I("tensor_tensor", out=rt[:, 1, :, 0:n], in0=Sre, in1=sT, op=ALU.mult), reads=rb, writes=[rt_b])
                    S.op("pool", I("tensor_tensor", out=oim, in0=rt[:, 0, :, 0:n], in1=rt[:, 1, :, 0:n], op=ALU.subtract), reads=[rt_b], writes=[sr_b])
                xb, xbb, xbs = XBr.next()
                S.dma("sp", xbs, [I("dma_start", out=xb[:], in_=G.S5XB[idx])], writes=[xbb])
                XBD = xb[:].rearrange("p (r q n) -> p r q n", r=2, q=32)
                load_w(idx + 1)
                for j in range(8):
                    src_ = Srot[:, j, :] if d == 0 else Srot[:, j, ::-1]
                    S.op("dve", I("tensor_tensor_scan", out=Gs[:, j, :], data0=rho[:, j % 4:j % 4 + 1].broadcast_to([128, NCH]),
                                  data1=src_, initial=0.0, op0=ALU.mult, op1=ALU.add), reads=[sr_b, rho_b], writes=[gs_b])
                S.op("pool", I("memset", ap=Ebf[d][:, :, 0:1], constant=0.0), writes=[e_b[d]])
                Gre, Gim = Gs[:, 0:4, :], Gs[:, 4:8, :]
                p0, p1 = Srot[:, 0:4, :], Srot[:, 4:8, :]
                S.op("dve", I("tensor_tensor", out=p0, in0=Gre, in1=cosT, op=ALU.mult), reads=[gs_b, tbl_b, sr_b], writes=[sr_b])
                S.op("pool", I("tensor_tensor", out=p1, in0=Gim, in1=sinT, op=ALU.mult), reads=[gs_b, tbl_b, sr_b], writes=[sr_b])
                S.op("dve", I("tensor_tensor", out=Ebf[d][:, 0:4, 1:NCH + 1], in0=p0, in1=p1, op=ALU.subtract),
                     reads=[sr_b], writes=[e_b[d]])
                S.op("dve", I("tensor_tensor", out=p0, in0=Gim, in1=cosT, op=ALU.mult), reads=[gs_b, tbl_b, sr_b], writes=[sr_b])
                S.op("pool", I("tensor_tensor", out=p1, in0=Gre, in1=sinT, op=ALU.mult), reads=[gs_b, tbl_b, sr_b], writes=[sr_b])
                S.op("dve", I("tensor_tensor", out=Ebf[d][:, 4:8, 1:NCH + 1], in0=p0, in1=p1, op=ALU.add),
                     reads=[sr_b], writes=[e_b[d]])
                for k in range(8):
                    pt, pb, _ = G.PS.next()
                    mm = []
                    for ri in range(2):
                        for jc in range(4):
                            mm.append(I("matmul", out=pt[:, 0:128], lhsT=XBD[:, ri, jc * 8 + k, :], rhs=CBD[:, ri, jc, :],
                                        start=(ri == 0 and jc == 0), stop=(ri == 1 and jc == 3)))
                    S.op("pe", mm, reads=[xbb, cbd_b], writes=[pb])
                    if k == 0 and d == 0:
                        S.op("dve", I("tensor_copy", out=K0[:], in_=pt[:, 0:128]), reads=[pb], writes=[k_b])
                    elif k == 0:
                        S.op("dve", I("tensor_tensor", out=Kf[:, 0, :], in0=pt[:, 0:128], in1=K0[:], op=ALU.add), reads=[pb, k_b], writes=[k_b])
                    else:
                        S.op("act", I("activation", out=Kf[:, d * 8 + k, :], in_=pt[:, 0:128], func=AF.Copy), reads=[pb], writes=[k_b])

            W3bd = [W3t[d][:].rearrange("p (j s n) -> p j s n", j=8, s=8) for d in range(2)]
            for ri_, (c0, n) in enumerate(RANGES):
                b0 = (ri_ % 2) * 4
                for s in range(8):
                    off = b0 * 512 + s * 256
                    mm = []
                    for s2 in range(8):
                        kk = (s - s2) if s2 <= s else 8 + (s2 - s)
                        mm.append(I("matmul", out=ps[:, off:off + n], lhsT=Kf[:, kk, :], rhs=Ub[:, c0 * 8 + s2:(c0 + n) * 8:8],
                                    start=(s2 == 0), stop=False))
                    cp = c0 + 32 if c0 < 512 else 0
                    for j in range(8):
                        mm.append(I("matmul", out=ps[:, off:off + n], lhsT=W3bd[0][:, j, s, :], rhs=Ebf[0][:, j, cp:cp + n],
                                    start=False, stop=False))
                    lo = NCH - 1 - (c0 + n - 1)
                    for j in range(8):
                        mm.append(I("matmul", out=ps[:, off:off + n], lhsT=W3bd[1][:, j, s, :], rhs=Ebf[1][:, j, lo:lo + n][:, ::-1],
                                    start=False, stop=(j == 7)))
                    S.op("pe", mm, reads=[k_b, Ub_b, w3_b[0], w3_b[1], e_b[0], e_b[1]], writes=psb[b0:b0 + 4])
                pv = ps[:, b0 * 512:(b0 + 4) * 512].rearrange("p (s c) -> p s c", c=256)[:, :, 0:n]
                S.op("dve", I("scalar_tensor_tensor", out=yt[:, c0 * 8:(c0 + n) * 8].rearrange("p (c s) -> p s c", s=8),
                              in0=Ut[:, c0 * 8:(c0 + n) * 8].rearrange("p (c s) -> p s c", s=8), scalar=misc[:, gt:gt + 1],
                              in1=pv, op0=ALU.mult, op1=ALU.add), reads=psb[b0:b0 + 4] + [U_b, cst_b], writes=[y_b])
            S.op("act", I("activation", out=gst[:, :], in_=yt[:, :], func=AF.Gelu), reads=[y_b], writes=[gst_b])
            S.dma("pool", gsem, [I("dma_start", out=G.GS5[gt * 128:(gt + 1) * 128, :], in_=gst[:, :])],
                  reads=[gst_b], writes=[scr_buf(G, "GS5", gt)])
            if "S5Y" in G.dbg:
                dsem = S.new_dma_sem("dbg")
                S.dma("sp", dsem, [I("dma_start", out=G.dbg["S5Y"][gt * 128:(gt + 1) * 128, :], in_=yt[:, :])], reads=[y_b])

        S.barrier()
        S.release_to(mk)
    mk = S.mark()
    with ExitStack() as p5:
        sbt = lambda name, shape, dt: p5.enter_context(nc.sbuf_tensor(U(name), list(shape), dt))
        misc = sbt("s5misc2", [128, 8], F32)
        cst_b = Buf("s5cst2")
        csem = S.new_dma_sem("s5c2")
        S.dma("sp", csem, [I("dma_start", out=misc[:], in_=G.s5misc[l, :, :])], writes=[cst_b])
        wgf = sbt("wgf", [128, 4, 512], F32)
        wgb = sbt("wgb", [128, 4, 512], BF16); wg_b = Buf("wg")
        wsem = S.new_dma_sem("wg")
        S.dma("sp", wsem, [I("dma_start", out=wgf[:], in_=G.w_glu[l].rearrange("(c p) n -> p c n", p=128))], writes=[wg_b])
        S.op("pool", I("tensor_copy", out=wgb[:], in_=wgf[:]), reads=[wg_b], writes=[wg_b])
        ZSr = Ring(S, p5, "zs", 2, [128, 512], BF16, dma=True)
        SGr = Ring(S, p5, "sg5", 2, [128, 512], BF16)
        YOr = Ring(S, p5, "yo5", 2, [128, 512], BF16, dma=True, fresh=True)
        GTr = Ring(S, p5, "gtr", 2, [128, 4, 512], BF16, dma=True)
        for (t0, tn) in TG:
            gT, gTb, gTs = GTr.next()
            S.dma("sp", gTs, [I("dma_start", out=gT[:, :, 0:tn], in_=G.GS5[:, t0:t0 + tn].rearrange("(c p) n -> p c n", p=128))],
                  reads=[scr_buf(G, "GS5", i) for i in range(4)], writes=[gTb])
            for mo in range(4):
                zs, zsb, zss = ZSr.next()
                S.dma("sp", zss, [I("dma_start", out=zs[:, 0:tn], in_=G.ZS[mo * 128:(mo + 1) * 128, t0:t0 + tn])],
                      reads=[scr_buf(G, "ZS", mo)], writes=[zsb])
                pt, pb, _ = G.PS.next()
                S.op("pe", [I("matmul", out=pt[:, 0:tn], lhsT=wgb[:, kc, mo * 128:(mo + 1) * 128], rhs=gT[:, kc, 0:tn],
                              start=(kc == 0), stop=(kc == 3)) for kc in range(4)], reads=[wg_b, gTb], writes=[pb])
                sg, sgb, _ = SGr.next()
                S.op("act", I("activation", out=sg[:, 0:tn], in_=pt[:, 0:tn], func=AF.Sigmoid, bias=misc[:, 4 + mo:5 + mo]),
                     reads=[pb, cst_b], writes=[sgb])
                S.op("pool", I("tensor_tensor", out=sg[:, 0:tn], in0=sg[:, 0:tn], in1=zs[:, 0:tn], op=ALU.mult),
                     reads=[sgb, zsb], writes=[sgb])
                yo, yob, yos = YOr.next()
                S.op("dve", I("tensor_tensor", out=yo[:, 0:tn], in0=sg[:, 0:tn], in1=gT[:, mo, 0:tn], op=ALU.mult),
                     reads=[sgb, gTb], writes=[yob])
                S.dma("pool", yos, [I("dma_start", out=G.YG[2, mo * 128:(mo + 1) * 128, t0:t0 + tn], in_=yo[:, 0:tn])],
                      reads=[yob], writes=[scr_buf(G, "YG2", mo)])
        if "YG" in G.dbg:
            S.barrier()
            dsem = S.new_dma_sem("dbg")
            S.dma("sp", dsem, [I("dma_start", out=G.dbg["YG"][:, :, :], in_=G.YG[:, :, :])])
        S.barrier()
        S.release_to(mk)


def phase6_merge(G, l):
    nc, S = G.nc, G.S
    ps, psb = G.psall, G.psb
    last = (l == DEPTH - 1)
    mk = S.mark()
    with ExitStack() as p6:
        sbt = lambda name, shape, dt: p6.enter_context(nc.sbuf_tensor(U(name), list(shape), dt))
        wbr = sbt("wbr", [128, 3, 4, D], BF16)
        wou = sbt("wou", [128, 8, D], BF16)
        w_b = Buf("w6")
        STG = Ring(S, p6, "stg6", 2, [128, 4, D], F32, dma=True)
        for n in range(3):
            st, stb, sts = STG.next()
            S.dma("sp", sts, [I("dma_start", out=st[:], in_=G.w_branch[l, n].rearrange("(c p) n -> p c n", p=128))], writes=[stb])
            S.op("pool", I("tensor_copy", out=wbr[:, n], in_=st[:]), reads=[stb], writes=[w_b])
        for hf in range(2):
            st, stb, sts = STG.next()
            S.dma("sp", sts, [I("dma_start", out=st[:], in_=G.w_out[l, hf * 512:(hf + 1) * 512, :].rearrange("(c p) n -> p c n", p=128))],
                  writes=[stb])
            S.op("pool", I("tensor_copy", out=wou[:, hf * 4:(hf + 1) * 4], in_=st[:]), reads=[stb], writes=[w_b])
        if last:
            fgr = sbt("fgr", [1, D], F32)
            fgbc = sbt("fgbc", [128, D], F32)
            fg_b = Buf("fg")
            fsem = S.new_dma_sem("fg")
            S.dma("sp", fsem, [I("dma_start", out=fgr[:], in_=G.final_g[:, :])], writes=[fg_b])
            for n in range(2):
                pt, pb, _ = G.PS.next()
                S.op("pe", I("matmul", out=pt[:, :], lhsT=G.ones_f[0:1, :], rhs=fgr[0:1, n * 512:(n + 1) * 512], start=True, stop=True),
                     reads=[fg_b, G.b_const], writes=[pb])
                S.op("dve", I("tensor_copy", out=fgbc[:, n * 512:(n + 1) * 512], in_=pt[:, :]), reads=[pb], writes=[fg_b])
            junk = sbt("junk6", [128, D], BF16); junk_b = Buf("junk6")
            st4 = Ring(S, p6, "st6", 4, [128, 4], F32)
        YGr = Ring(S, p6, "yg6", 2, [128, 3, 4, 512], BF16, dma=True)
        SGr = Ring(S, p6, "sg6", 3, [128, 3, 512], BF16, dma=True)
        MG = Ring(S, p6, "mg6", 2, [128, 8, 512], BF16)
        TM = Ring(S, p6, "tm6", 2, [128, 3, 512], F32)
        XR_ = Ring(S, p6, "x6", 3, [128, D], F32, dma=True)
        XO = Ring(S, p6, "xo6", 2, [128, D], F32, dma=True, fresh=True)
        groups = TG[:8] if last else TG
        for (t0, tn) in groups:
            v = 0 if t0 < SEQ else 1
            yg, ygb, ygs = YGr.next()
            S.dma("sp", ygs, [I("dma_start", out=yg[:, n, :, 0:tn], in_=G.YG[n, :, t0:t0 + tn].rearrange("(c p) t -> p c t", p=128))
                              for n in range(3)],
                  reads=[scr_buf(G, "YG%d" % n, i) for n in range(3) for i in range(4)], writes=[ygb])
            mg, mgb, _ = MG.next()
            for dc in range(8):
                sg, sgb, sgs = SGr.next()
                S.dma("sp", sgs, [I("dma_start", out=sg[:, :, 0:tn],
                                    in_=G.SG[:, t0:t0 + tn].rearrange("(n c p) t -> c p n t", n=3, c=8)[dc])],
                      reads=[scr_buf(G, "SG", n * 8 + dc) for n in range(3)], writes=[sgb])
                pts = []
                for n in range(3):
                    pt, pb, _ = G.PS.next()
                    S.op("pe", [I("matmul", out=pt[:, 0:tn], lhsT=wbr[:, n, kc, dc * 128:(dc + 1) * 128], rhs=yg[:, n, kc, 0:tn],
                                  start=(kc == 0), stop=(kc == 3)) for kc in range(4)], reads=[w_b, ygb], writes=[pb])
                    pts.append((pt, pb))
                tm, tmb, _ = TM.next()
                for n in range(3):
                    S.op("dve", I("tensor_tensor", out=tm[:, n, 0:tn], in0=pts[n][0][:, 0:tn], in1=sg[:, n, 0:tn], op=ALU.mult),
                         reads=[pts[n][1], sgb], writes=[tmb])
                S.op("pool", I("tensor_tensor", out=tm[:, 0, 0:tn], in0=tm[:, 0, 0:tn], in1=tm[:, 1, 0:tn], op=ALU.add),
                     reads=[tmb], writes=[tmb])
                S.op("pool", I("tensor_tensor", out=mg[:, dc, 0:tn], in0=tm[:, 0, 0:tn], in1=tm[:, 2, 0:tn], op=ALU.add),
                     reads=[tmb], writes=[mgb])
            for tt in range(tn // 128):
                ti = t0 // 128 + tt
                xt, xb, xs = XR_.next()
                S.dma("sp", xs, [I("dma_start", out=xt[:], in_=x_src(G, l, ti))], reads=[G.xres_b[ti]], writes=[xb])
                xo, xob, xos = XO.next()
                for n in range(2):
                    pt, pb, _ = G.PS.next()
                    S.op("pe", [I("matmul", out=pt[:, :], lhsT=mg[:, dc, tt * 128:(tt + 1) * 128], rhs=wou[:, dc, n * 512:(n + 1) * 512],
                                  start=(dc == 0), stop=(dc == 7)) for dc in range(8)], reads=[w_b, mgb], writes=[pb])
                    S.op("dve", I("tensor_tensor", out=xo[:, n * 512:(n + 1) * 512], in0=pt[:, :],
                                  in1=G.gtbc[:, v, n * 512:(n + 1) * 512], op=ALU.mult), reads=[pb, G.gtbc_b], writes=[xob])
                S.op("pool", I("tensor_tensor", out=xo[:, :], in0=xo[:, :], in1=xt[:, :], op=ALU.add), reads=[xob, xb], writes=[xob])
                if not last:
                    S.dma("pool", xos, [I("dma_start", out=G.xres[ti * 128:(ti + 1) * 128, :], in_=xo[:, :])],
                          reads=[xob], writes=[G.xres_b[ti]])
                else:
                    st, stb, _ = st4.next()
                    S.op("act", I("activation", out=junk[:], in_=xo[:], func=AF.Square, accum_out=st[:, 0:1]),
                         reads=[xob], writes=[junk_b, stb])
                    S.op("act", I("activation", out=st[:, 1:2], in_=st[:, 0:1], func=AF.Sqrt, scale=1.0 / D, bias=G.eps_col[:, 0:1]),
                         reads=[stb, G.b_const], writes=[stb])
                    S.op("dve", I("reciprocal", out=st[:, 2:3], in_=st[:, 1:2]), reads=[stb], writes=[stb])
                    S.op("dve", I("scalar_tensor_tensor", out=xo[:, :], in0=xo[:, :], scalar=st[:, 2:3], in1=fgbc[:, :],
                                  op0=ALU.mult, op1=ALU.mult), reads=[xob, stb, fg_b], writes=[xob])
                    S.dma("pool", xos, [I("dma_start", out=G.out[ti * 128:(ti + 1) * 128, :], in_=xo[:, :])],
                          reads=[xob], writes=[G.xres_b[ti]])
        if "XRES" in G.dbg:
            S.barrier()
            dsem = S.new_dma_sem("dbg")
            S.dma("sp", dsem, [I("dma_start", out=G.dbg["XRES"][:, :], in_=G.xres[:, :])])
        S.barrier()
        S.release_to(mk)


def scr_buf(G, name, i):
    return Buf("scr")


def x_src(G, l, ti):
    if l == 0:
        if ti < 32:
            return G.x_in[ti * 128:(ti + 1) * 128, :]
        return G.ctx_in[(ti - 32) * 128:(ti - 31) * 128, :]
    return G.xres[ti * 128:(ti + 1) * 128, :]


def phase1_and_2(G, l, stop_after=None):
    nc, S, PS = G.nc, G.S, G.PS
    with ExitStack() as ph:
        psb = lambda name, shape, dt: ph.enter_context(nc.sbuf_tensor(U(name), list(shape), dt))
        hT = psb("hT", [128, 8, T], BF16)
        hT_b = [Buf("hT%d" % i) for i in range(NT)]
        mcols = psb("mcols", [128, 32], F32)
        gs = psb("gs", [128, 16], F32)
        mc_b = Buf("mcols")
        mk = S.mark()
        with ExitStack() as p1:
            p1sb = lambda name, shape, dt: p1.enter_context(nc.sbuf_tensor(U(name), list(shape), dt))
            mrow = [p1sb("mrow%d" % v, [1, 3 * D], F32) for v in range(2)]
            mrow_b = [Buf("mrow%d" % v) for v in range(2)]
            brow = p1sb("brow", [1, 3 * D], F32)
            brow_b = Buf("brow")
            gcol_sb = p1sb("gcol_sb", [128, 8], F32)
            WM = Ring(S, p1, "wm", 2, [128, 8, 512], F32, dma=True)
            bsem = S.new_dma_sem("brow")
            S.dma("sp", bsem, [I("dma_start", out=brow[:], in_=G.b_mod[l, :, :]),
                               I("dma_start", out=gcol_sb[:], in_=G.gcol[l, :, :])], writes=[brow_b])
            for n in range(6):
                wt, wb, ws = WM.next()
                S.dma("sp", ws, [I("dma_start", out=wt[:],
                                   in_=G.w_mod[l, :, n * 512:(n + 1) * 512].rearrange("(c p) n -> p c n", p=128))],
                      writes=[wb])
                for v in range(2):
                    pt, pb, _ = PS.next()
                    S.op("pe", [I("matmul", out=pt[0:1, :], lhsT=G.silu_c[:, v * 8 + c:v * 8 + c + 1], rhs=wt[:, c, :],
                                  start=(c == 0), stop=(c == 7)) for c in range(8)],
                         reads=[wb, G.b_const], writes=[pb])
                    S.op("dve", I("tensor_tensor", out=mrow[v][0:1, n * 512:(n + 1) * 512], in0=pt[0:1, :],
                                  in1=brow[0:1, n * 512:(n + 1) * 512], op=ALU.add),
                         reads=[pb, brow_b], writes=[mrow_b[v]])
            pt, pb, _ = PS.next()
            mm = []
            for v in range(2):
                for w in range(2):
                    for c in range(8):
                        j = v * 16 + w * 8 + c
                        mm.append(I("matmul", out=pt[:, j:j + 1],
                                    lhsT=mrow[v][0:1, w * 1024 + c * 128: w * 1024 + (c + 1) * 128],
                                    rhs=G.ones_f[0:1, 0:1], start=True, stop=True))
            S.op("pe", mm, reads=[mrow_b[0], mrow_b[1], G.b_const], writes=[pb])
            S.op("dve", I("tensor_copy", out=mcols[:], in_=pt[:, 0:32]), reads=[pb], writes=[mc_b])
            for v in range(2):
                S.op("dve", I("scalar_tensor_tensor", out=gs[:, v * 8:(v + 1) * 8], in0=mcols[:, v * 16 + 8:v * 16 + 16],
                              scalar=1.0, in1=gcol_sb[:], op0=ALU.add, op1=ALU.mult),
                     reads=[mc_b, brow_b], writes=[mc_b])
            for v in range(2):
                for n in range(2):
                    pt, pb, _ = PS.next()
                    S.op("pe", I("matmul", out=pt[:, :], lhsT=G.ones_f[0:1, :],
                                 rhs=mrow[v][0:1, 2048 + n * 512:2048 + (n + 1) * 512], start=True, stop=True),
                         reads=[mrow_b[v], G.b_const], writes=[pb])
                    S.op("dve", I("tensor_copy", out=G.gtbc[:, v, n * 512:(n + 1) * 512], in_=pt[:, :]),
                         reads=[pb], writes=[G.gtbc_b])
            if "mrow" in G.dbg:
                dsem = S.new_dma_sem("dbg")
                S.dma("sp", dsem, [I("dma_start", out=G.dbg["mrow"][0:1, :], in_=mrow[0][:]),
                                   I("dma_start", out=G.dbg["mrow"][1:2, :], in_=mrow[1][:])], reads=mrow_b)
            S.barrier()
            S.release_to(mk)

        with ExitStack() as p1:
            XT = Ring(S, p1, "xt", 3, [128, D], F32, dma=True)
            XN = Ring(S, p1, "xn", 2, [128, D], BF16)
            junk = p1.enter_context(nc.sbuf_tensor(U("junk"), [128, D], BF16))
            junk_b = Buf("junk")
            st4 = Ring(S, p1, "st", 4, [128, 4], F32)
            for ti in range(NT):
                v = 0 if ti < 32 else 1
                xt, xb, xs = XT.next()
                S.dma("sp", xs, [I("dma_start", out=xt[:], in_=x_src(G, l, ti))], reads=[G.xres_b[ti]], writes=[xb])
                st, stb, _ = st4.next()
                S.op("act", I("activation", out=junk[:], in_=xt[:], func=AF.Square, accum_out=st[:, 0:1]),
                     reads=[xb], writes=[junk_b, stb])
                S.op("act", I("activation", out=st[:, 1:2], in_=st[:, 0:1], func=AF.Sqrt, scale=1.0 / D, bias=G.eps_col[:, 0:1]),
                     reads=[stb, G.b_const], writes=[stb])
                S.op("dve", I("reciprocal", out=st[:, 2:3], in_=st[:, 1:2]), reads=[stb], writes=[stb])
                xn, xnb, _ = XN.next()
                S.op("dve", I("tensor_scalar", out=xn[:], in0=xt[:], scalar1=st[:, 2:3], scalar2=None, op0=ALU.mult),
                     reads=[xb, stb], writes=[xnb])
                for half in range(2):
                    pt, pb, _ = PS.next()
                    ptb = pt.bitcast(BF16)
                    S.op("pe", [I("transpose", out=ptb[:, cc * 128:(cc + 1) * 128],
                                  in_=xn[:, (half * 4 + cc) * 128:(half * 4 + cc + 1) * 128], identity=G.ident[:])
                                for cc in range(4)], reads=[xnb, G.b_const], writes=[pb])
                    for cc in range(4):
                        c = half * 4 + cc
                        if cc % 2 == 1:
                            S.op("dve", I("tensor_scalar", out=hT[:, c, ti * 128:(ti + 1) * 128],
                                          in0=ptb[:, cc * 128:(cc + 1) * 128],
                                          scalar1=gs[:, v * 8 + c:v * 8 + c + 1],
                                          scalar2=mcols[:, v * 16 + c:v * 16 + c + 1], op0=ALU.mult, op1=ALU.add),
                                 reads=[pb, mc_b], writes=[hT_b[ti]])
                        else:
                            S.op("act", I("activation", out=hT[:, c, ti * 128:(ti + 1) * 128],
                                          in_=ptb[:, cc * 128:(cc + 1) * 128], func=AF.Identity,
                                          scale=gs[:, v * 8 + c:v * 8 + c + 1],
                                          bias=mcols[:, v * 16 + c:v * 16 + c + 1]),
                                 reads=[pb, mc_b], writes=[hT_b[ti]])
            if "hT" in G.dbg:
                dsem = S.new_dma_sem("dbg")
                S.dma("sp", dsem, [I("dma_start", out=G.dbg["hT"][:, :, :], in_=hT[:])], reads=hT_b)
            S.barrier()
            S.release_to(mk)
        if stop_after == "p1":
            return

        mk = S.mark()
        with ExitStack() as p2:
            WF = Ring(S, p2, "wf", 2, [128, 8, 512], F32, dma=True)
            WB = Ring(S, p2, "wb", 4, [128, 8, 512], BF16)
            RC = Ring(S, p2, "rc", 2, [128, 2, 512], F32, dma=True)
            OB = Ring(S, p2, "ob", 8, [128, 512], BF16, dma=True)
            OF = Ring(S, p2, "of", 6, [128, 512], F32, dma=True)
            TMP = Ring(S, p2, "tmp", 2, [128, 2, 512], F32)

            def load_group(g):
                wf, wfb, wfs = WF.next()
                S.dma("sp", wfs, [I("dma_start", out=wf[:], in_=G.w_in[l, :, g * 512:(g + 1) * 512]
                                    .rearrange("(c p) n -> p c n", p=128))], writes=[wfb])
                wb, wbb, _ = WB.next()
                S.op("pool", I("tensor_copy", out=wb[:, 0:4, :], in_=wf[:, 0:4, :]), reads=[wfb], writes=[wbb])
                S.op("pool", I("tensor_copy", out=wb[:, 4:8, :], in_=wf[:, 4:8, :]), reads=[wfb], writes=[wbb])
                return wb, wbb

            def proj(pt, pb, wb, wbb, j, t0, tn):
                tis = list(range(t0 // 128, (t0 + tn) // 128))
                S.op("pe", [I("matmul", out=pt[:, 0:tn], lhsT=wb[:, c, j * 128:(j + 1) * 128], rhs=hT[:, c, t0:t0 + tn],
                              start=(c == 0), stop=(c == 7)) for c in range(8)],
                     reads=[wbb] + [hT_b[i] for i in tis], writes=[pb])

            wq = [load_group(g) for g in range(4)]
            for (t0, tn) in TG:
                rc, rcb, rcs = RC.next()
                S.dma("sp", rcs, [I("dma_start", out=rc[:, 0, 0:tn], in_=G.ropeC[:, t0:t0 + tn]),
                                  I("dma_start", out=rc[:, 1, 0:tn], in_=G.ropeS[:, t0:t0 + tn])], writes=[rcb])
                for qk in range(2):
                    dst = G.QT if qk == 0 else G.KT
                    for j in range(4):
                        pa, pab, _ = PS.next()
                        proj(pa, pab, wq[2 * qk][0], wq[2 * qk][1], j, t0, tn)
                        pbt, pbb, _ = PS.next()
                        proj(pbt, pbb, wq[2 * qk + 1][0], wq[2 * qk + 1][1], j, t0, tn)
                        tmp, tmpb, _ = TMP.next()
                        S.op("dve", I("tensor_tensor", out=tmp[:, 0, 0:tn], in0=pa[:, 0:tn], in1=rc[:, 0, 0:tn], op=ALU.mult),
                             reads=[pab, rcb], writes=[tmpb])
                        S.op("dve", I("tensor_tensor", out=tmp[:, 1, 0:tn], in0=pbt[:, 0:tn], in1=rc[:, 1, 0:tn], op=ALU.mult),
                             reads=[pbb, rcb], writes=[tmpb])
                        ob, obb, obs = OB.next()
                        S.op("pool", I("tensor_tensor", out=ob[:, 0:tn], in0=tmp[:, 0, 0:tn], in1=tmp[:, 1, 0:tn], op=ALU.add),
                             reads=[tmpb], writes=[obb])
                        S.dma("sp", obs, [I("dma_start", out=dst[j * 128:(j + 1) * 128, t0:t0 + tn], in_=ob[:, 0:tn])],
                              reads=[obb], writes=[scr_buf(G, "QT" if qk == 0 else "KT", j)])
            wv, wvb = load_group(4)
            for ti in range(NT):
                pt, pb, _ = PS.next()
                S.op("pe", [I("matmul", out=pt[:, :], lhsT=hT[:, c, ti * 128:(ti + 1) * 128], rhs=wv[:, c, :],
                              start=(c == 0), stop=(c == 7)) for c in range(8)],
                     reads=[wvb, hT_b[ti]], writes=[pb])
                ob, obb, obs = OB.next()
                S.op("act", I("activation", out=ob[:, :], in_=pt[:, :], func=AF.Copy), reads=[pb], writes=[obb])
                S.dma("sp", obs, [I("dma_start", out=G.Vtm[ti * 128:(ti + 1) * 128, :], in_=ob[:, :])],
                      reads=[obb], writes=[scr_buf(G, "V", ti)])
            plan = [(5, G.ZA, 0, "ZA", 0), (7, G.ZR, 0, "ZR", 0), (9, G.ZS, 0, "ZS", 0),
                    (6, G.XR, 1, "XR", 0), (8, G.US, 1, "US", 0)]
            plan += [(10 + i, G.SG, 2, "SG", i * 4) for i in range(6)]
            for (g, dst, kind, nm, boff) in plan:
                wg, wgb = load_group(g)
                for j in range(4):
                    for (t0, tn) in TG:
                        pt, pb, _ = PS.next()
                        proj(pt, pb, wg, wgb, j, t0, tn)
                        if kind == 1:
                            ob, obb, obs = OF.next()
                            S.op("dve", I("tensor_copy", out=ob[:, 0:tn], in_=pt[:, 0:tn]), reads=[pb], writes=[obb])
                        else:
                            ob, obb, obs = OB.next()
                            S.op("act", I("activation", out=ob[:, 0:tn], in_=pt[:, 0:tn],
                                          func=(AF.Silu if kind == 0 else AF.Sigmoid)), reads=[pb], writes=[obb])
                        r0 = (boff + j) * 128
                        S.dma("sp", obs, [I("dma_start", out=dst[r0:r0 + 128, t0:t0 + tn], in_=ob[:, 0:tn])],
                              reads=[obb], writes=[scr_buf(G, nm, boff + j)])
            if "QT" in G.dbg:
                S.barrier()
                dsem = S.new_dma_sem("dbg")
                for nm in ("QT", "KT", "Vtm", "ZA", "XR", "SG"):
                    if nm in G.dbg:
                        S.dma("sp", dsem, [I("dma_start", out=G.dbg[nm][:, :], in_=getattr(G, nm)[:, :])])
            S.barrier()
            S.release_to(mk)
        if stop_after == "p2":
            return


def _rope_tables():
    rows = SEQ // 64
    r = np.repeat(np.arange(rows, dtype=np.float32), 64)
    col = np.tile(np.arange(64, dtype=np.float32), rows)
    inv = (10000.0 ** (-np.arange(16, dtype=np.float32) / 16)).astype(np.float32)
    ang = np.concatenate([r[:, None] * inv, col[:, None] * inv], axis=-1).astype(np.float32)
    cos = np.cos(ang).T.astype(np.float32)
    sin = np.sin(ang).T.astype(np.float32)
    C = np.ones((128, T), np.float32)
    Sg = np.zeros((128, T), np.float32)
    for p in range(128):
        j = p % 64
        C[p, :SEQ] = cos[j % 32]
        Sg[p, :SEQ] = -sin[j % 32] if j < 32 else sin[j % 32]
    return C, Sg


def _w_in_ext(w_in):
    L = w_in.shape[0]
    perm = np.concatenate([np.arange(0, 64, 2), np.arange(1, 64, 2)])
    swp = np.concatenate([np.arange(1, 64, 2), np.arange(0, 64, 2)])
    idx = []
    for base in (0, 512):
        p_cols = np.concatenate([base + b * 64 + perm for b in range(8)])
        s_cols = np.concatenate([base + b * 64 + swp for b in range(8)])
        idx += [p_cols, s_cols]
    idx.append(np.arange(1024, 7168))
    idx = np.concatenate(idx)
    return np.ascontiguousarray(w_in[:, :, idx])


def make_in_maps(inputs):
    f = lambda a: np.ascontiguousarray(np.asarray(a, dtype=np.float32))
    x = f(inputs["x"]); ctx = f(inputs["ctx"]); c = f(inputs["c"]); c_ctx = f(inputs["c_ctx"])
    w_in_e = _w_in_ext(f(inputs["w_in"]))
    C, Sg = _rope_tables()
    ident = np.eye(128, dtype=np.float32).astype(ml_dtypes.bfloat16)
    gcol = np.ascontiguousarray(f(inputs["norm_g"]).reshape(DEPTH, 8, 128).transpose(0, 2, 1))
    shared = dict(
        w_mod=f(inputs["w_mod"]), b_mod=f(inputs["b_mod"]).reshape(DEPTH, 1, 3 * D), gcol=gcol, w_in=w_in_e,
        ropeC=C, ropeS=Sg, ident=ident, final_g=f(inputs["final_g"]).reshape(1, D),
        lam_qk=f(inputs["lam_qk"]).reshape(DEPTH, 1, 256), subln=f(inputs["subln_g"]).reshape(DEPTH, 128, 1),
    )
    lrup = np.zeros((DEPTH, 128, 44), np.float32)
    cw = f(inputs["conv_w"]); cb = f(inputs["conv_b"])
    for ct in range(4):
        for k in range(4):
            lrup[:, :, ct * 4 + k] = cw[:, k, ct * 128:(ct + 1) * 128]
        lrup[:, :, 16 + ct] = cb[:, ct * 128:(ct + 1) * 128]
        for d in range(2):
            lrup[:, :, 20 + d * 4 + ct] = f(inputs["lru_ba"])[:, d, ct * 128:(ct + 1) * 128]
            lrup[:, :, 28 + d * 4 + ct] = f(inputs["lru_bx"])[:, d, ct * 128:(ct + 1) * 128]
            lrup[:, :, 36 + d * 4 + ct] = f(inputs["lru_lam"])[:, d, ct * 128:(ct + 1) * 128]
    lruw = np.zeros((DEPTH, 2, 2, 4, 128, 128), np.float32)
    for gi, nm in enumerate(("lru_wa", "lru_wx")):
        w = f(inputs[nm])
        for ct in range(4):
            for j in range(2):
                lruw[:, gi, :, ct, j * 64:(j + 1) * 64, j * 64:(j + 1) * 64] = w[:, :, 2 * ct + j]
    shared.update(lrup=lrup, lruw=lruw)
    lre = f(inputs["s5_lam_re"]); lim = f(inputs["s5_lam_im"]); ldt = f(inputs["s5_log_dt"])
    bre = f(inputs["s5_b_re"]); bim = f(inputs["s5_b_im"]); cre = f(inputs["s5_c_re"]); cim = f(inputs["s5_c_im"])
    L = DEPTH
    s5H = np.zeros((L, 2, 4, 128, 4, 64), np.float32)
    lre_g = lre.reshape(L, 2, 4, 8, 64); lim_g = lim.reshape(L, 2, 4, 8, 64)
    s5H[:, :, :, :, 0, :] = np.repeat(lre_g, 16, axis=3)
    s5H[:, :, :, :, 1, :] = np.repeat(lim_g, 16, axis=3)
    s5H[:, :, :, :, 2, :] = bre.reshape(L, 2, 4, 8, 64, 16).transpose(0, 1, 2, 3, 5, 4).reshape(L, 2, 4, 128, 64)
    s5H[:, :, :, :, 3, :] = bim.reshape(L, 2, 4, 8, 64, 16).transpose(0, 1, 2, 3, 5, 4).reshape(L, 2, 4, 128, 64)
    def nlay(a):
        return a.reshape(L, 2, 4, 8, 4, 16).transpose(0, 1, 2, 3, 5, 4).reshape(L, 2, 4, 128, 4)
    s5N = np.concatenate([nlay(lre_g), nlay(lim_g)], axis=-1)
    def bnlay(a):
        return a.reshape(L, 2, 4, 8, 4, 16, 16).transpose(0, 1, 2, 3, 5, 4, 6).reshape(L, 2, 4, 128, 4, 16)
    s5BN = np.stack([bnlay(bre), bnlay(bim)], axis=4)
    def cnlay(a):
        return a.reshape(L, 4, 8, 16, 4, 16).transpose(0, 1, 2, 5, 4, 3).reshape(L, 4, 128, 4, 16)
    s5CN = np.stack([cnlay(cre), cnlay(cim)], axis=3)
    s5dt = np.zeros((L, 128, 8), np.float32)
    for d in range(2):
        for gt in range(4):
            s5dt[:, :, d * 4 + gt] = np.repeat(ldt[:, d, gt * 8:(gt + 1) * 8], 16, axis=1)
    s5misc = np.zeros((L, 128, 8), np.float32)
    s5misc[:, :, 0:4] = f(inputs["s5_d"]).reshape(L, 4, 128).transpose(0, 2, 1)
    s5misc[:, :, 4:8] = f(inputs["s5_b_glu"]).reshape(L, 4, 128).transpose(0, 2, 1)
    tabs = np.zeros((128, 36 + 512 + 544 + 128), np.float32)
    tabs[:, 0:36] = np.tile(np.arange(9, dtype=np.float32), 4)[None, :]
    tabs[:, 36:548] = np.repeat(np.arange(8, dtype=np.float32), 64)[None, :]
    tabs[:, 548:1092] = np.arange(544, dtype=np.float32)[None, :]
    mk = np.zeros((128, 8, 16), np.float32)
    for p in range(128):
        mk[p, p // 16, :] = 1.0
    tabs[:, 1092:1220] = mk.reshape(128, 128)
    shared.update(s5H=s5H, s5N=np.ascontiguousarray(s5N), s5BN=np.ascontiguousarray(s5BN), s5CN=np.ascontiguousarray(s5CN),
                  s5dt=s5dt, s5misc=s5misc, s5tab=tabs, w_glu=f(inputs["s5_w_glu"]),
                  w_branch=f(inputs["w_branch"]), w_out=f(inputs["w_out"]))
    maps = []
    for b in range(8):
        cc = np.concatenate([c[b].reshape(8, 128).T, c_ctx.reshape(8, 128).T], axis=1)
        m = dict(shared)
        m.update(x=x[b], ctx=ctx[b], ccol=np.ascontiguousarray(cc))
        maps.append(m)
    return maps


def kernel(**inputs):
    nc = build_program()
    maps = make_in_maps(inputs)
    res = run_bass_kernel_spmd(nc, maps, core_ids=list(range(8)))
    return np.stack([np.asarray(r["out"], dtype=np.float32) for r in res.results], axis=0)
```

```python
import math
from contextlib import ExitStack

import numpy as np
import ml_dtypes

import concourse.bass as bass
import concourse.mybir as mybir
from concourse.bass_utils import run_bass_kernel_spmd

F32 = mybir.dt.float32
BF16 = mybir.dt.bfloat16
ALU = mybir.AluOpType
AF = mybir.ActivationFunctionType
AX = mybir.AxisListType

D = 1024
SEQ = 4096
CTX = 256
T = SEQ + CTX
NT = T // 128
DEPTH = 4
EPS = 1e-6
NCOL = 8192
TG = [(i * 512, 512) for i in range(8)] + [(4096, 256)]


class Buf:
    __slots__ = ("name", "lw", "rd")

    def __init__(self, name=""):
        self.name = name
        self.lw = None
        self.rd = {}


class Sched:
    ENG = ("pe", "act", "dve", "pool", "sp")

    def __init__(self, nc, stack):
        self.nc = nc
        self.stack = stack
        self.streams = {e: [] for e in self.ENG}
        self.sem = {}
        self.cnt = {}
        self.seen = {e: {} for e in self.ENG}
        for e in self.ENG:
            self.sem[e] = stack.enter_context(nc.semaphore("s_" + e))
            self.cnt[e] = 0
        self.ndma = 0
        self.free = []
        self.live = []

    def new_dma_sem(self, name, fresh=False):
        if fresh:
            key = U("f%d" % self.ndma)
            self.ndma += 1
            self.sem[key] = self.stack.enter_context(self.nc.semaphore(key))
            self.cnt[key] = 0
            return key
        if self.free:
            key = self.free.pop()
        else:
            key = U("d%d" % self.ndma)
            self.ndma += 1
            self.sem[key] = self.stack.enter_context(self.nc.semaphore(key))
            self.cnt[key] = 0
        self.live.append(key)
        return key

    def mark(self):
        return len(self.live)

    def release_to(self, mark):
        while len(self.live) > mark:
            self.free.append(self.live.pop())

    def _waits(self, eng, reads, writes):
        w = {}

        def add(tok):
            if tok is None:
                return
            k, v = tok
            if w.get(k, 0) < v:
                w[k] = v

        for b in reads:
            add(b.lw)
        for b in writes:
            add(b.lw)
            for k, v in b.rd.items():
                add((k, v))
        need = []
        seen = self.seen[eng]
        for k, v in w.items():
            if k == "pe" and eng == "pe":
                continue
            if seen.get(k, 0) < v:
                seen[k] = v
                need.append((k, v))
        return need

    def _commit(self, tok, reads, writes):
        for b in writes:
            b.lw = tok
            b.rd = {}
        k, v = tok
        for b in reads:
            if b.rd.get(k, 0) < v:
                b.rd[k] = v

    def op(self, eng, fn, reads=(), writes=()):
        if isinstance(fn, tuple):
            fn = [fn]
        need = self._waits(eng, reads, writes)
        self.cnt[eng] += 1
        tok = (eng, self.cnt[eng])
        self.streams[eng].append((need, fn, eng, 1))
        self._commit(tok, reads, writes)
        return tok

    def dma(self, eng, semkey, fns, reads=(), writes=()):
        need = self._waits(eng, reads, writes)
        for i, fn in enumerate(fns):
            self.cnt[semkey] += 16
            self.streams[eng].append((need if i == 0 else [], fn, semkey, 16))
        tok = (semkey, self.cnt[semkey])
        self._commit(tok, reads, writes)
        return tok

    def barrier(self):
        for e in self.ENG:
            need = []
            seen = self.seen[e]
            for k, v in self.cnt.items():
                if v > 0 and seen.get(k, 0) < v:
                    seen[k] = v
                    need.append((k, v))
            if need:
                self.streams[e].append((need, None, None, 0))

    def emit(self, block):
        nc = self.nc
        sems = self.sem

        def run(e, stream):
            for need, fn, semkey, inc in stream:
                for k, v in need:
                    e.wait_ge(sems[k], v)
                if fn is not None:
                    if isinstance(fn, tuple):
                        fn = [fn]
                    for m, kw in fn:
                        ins = getattr(e, m)(**kw)
                    ins.then_inc(sems[semkey], inc)

        @block.sync
        def _(e):
            run(e, self.streams["sp"])

        @block.tensor
        def _(e):
            run(e, self.streams["pe"])

        @block.scalar
        def _(e):
            run(e, self.streams["act"])

        @block.vector
        def _(e):
            run(e, self.streams["dve"])

        @block.gpsimd
        def _(e):
            run(e, self.streams["pool"])


class PsRing:
    def __init__(self, G):
        self.G = G
        self.i = 0

    def next(self):
        i = self.i
        self.i = (i + 1) % 8
        return self.G.psall[:, i * 512:(i + 1) * 512], self.G.psb[i], None


class Ring:
    def __init__(self, S, stack, name, n, shape, dtype, psum=False, dma=False, fresh=False):
        self.tiles = []
        self.bufs = []
        self.sems = []
        for i in range(n):
            nm = U("%s%d" % (name, i))
            if psum:
                t = stack.enter_context(S.nc.psum_tensor(nm, shape, dtype))
            else:
                t = stack.enter_context(S.nc.sbuf_tensor(nm, shape, dtype))
            self.tiles.append(t)
            self.bufs.append(Buf(nm))
            self.sems.append(S.new_dma_sem(nm, fresh=fresh) if dma else None)
        self.i = 0
        self.n = n

    def next(self):
        i = self.i
        self.i = (i + 1) % self.n
        return self.tiles[i], self.bufs[i], self.sems[i]


def I(m, **kw):
    return (m, kw)


_UID = [0]


def U(name):
    _UID[0] += 1
    return "%s_%d" % (name, _UID[0])


class Ctx:
    pass


def build_program(n_layers=DEPTH, debug=None, stop_after=None):
    nc = bass.Bass("TRN2", target_bir_lowering=False)
    G = Ctx()
    G.nc = nc

    def din(name, shape, dt=F32):
        return nc.dram_tensor(name, list(shape), dt, kind="ExternalInput").ap()

    def dscr(name, shape, dt):
        return nc.dram_tensor(name, list(shape), dt, kind="Internal").ap()

    G.x_in = din("x", [SEQ, D])
    G.ctx_in = din("ctx", [CTX, D])
    G.ccol = din("ccol", [128, 16])
    G.w_mod = din("w_mod", [DEPTH, D, 3 * D])
    G.b_mod = din("b_mod", [DEPTH, 1, 3 * D])
    G.gcol = din("gcol", [DEPTH, 128, 8])
    G.w_in = din("w_in", [DEPTH, D, NCOL])
    G.ropeC = din("ropeC", [128, T])
    G.ropeS = din("ropeS", [128, T])
    G.ident_in = din("ident", [128, 128], BF16)
    G.final_g = din("final_g", [1, D])
    G.lam_qk = din("lam_qk", [DEPTH, 1, 256])
    G.subln = din("subln", [DEPTH, 128, 1])
    G.lrup = din("lrup", [DEPTH, 128, 44])
    G.lruw = din("lruw", [DEPTH, 2, 2, 4, 128, 128])
    G.s5H = din("s5H", [DEPTH, 2, 4, 128, 4, 64])
    G.s5N = din("s5N", [DEPTH, 2, 4, 128, 8])
    G.s5BN = din("s5BN", [DEPTH, 2, 4, 128, 2, 4, 16])
    G.s5CN = din("s5CN", [DEPTH, 4, 128, 2, 4, 16])
    G.s5dt = din("s5dt", [DEPTH, 128, 8])
    G.s5misc = din("s5misc", [DEPTH, 128, 8])
    G.s5tab = din("s5tab", [128, 36 + 512 + 544 + 128])
    G.w_glu = din("w_glu", [DEPTH, 512, 512])
    G.w_branch = din("w_branch", [DEPTH, 3, 512, D])
    G.w_out = din("w_out", [DEPTH, D, D])
    G.out = nc.dram_tensor("out", [SEQ, D], F32, kind="ExternalOutput").ap()

    G.xres = dscr("xres", [T, D], F32)
    G.QT = dscr("QT", [512, T], BF16)
    G.KT = dscr("KT", [512, T], BF16)
    G.Vtm = dscr("Vtm", [T, 512], BF16)
    G.ZA = dscr("ZA", [512, T], BF16)
    G.ZR = dscr("ZR", [512, T], BF16)
    G.ZS = dscr("ZS", [512, T], BF16)
    G.XR = dscr("XR", [512, T], F32)
    G.US = dscr("US", [512, T], F32)
    G.SG = dscr("SG", [3072, T], BF16)
    G.GS5 = dscr("GS5", [512, T], BF16)
    G.S5W1 = dscr("S5W1", [8, 128, 8192], BF16)
    G.S5W3 = dscr("S5W3", [8, 128, 8192], BF16)
    G.S5XB = dscr("S5XB", [8, 128, 8192], BF16)
    G.S5TB = dscr("S5TB", [8, 128, 2, 4, T // 8], F32)
    G.S5RH = dscr("S5RH", [8, 128, 4], F32)
    G.s5_items = None
    G.YG = dscr("YG", [3, 512, T], BF16)

    G.dbg = {}
    if debug:
        for name, shape, dt in debug:
            G.dbg[name] = nc.dram_tensor("dbg_" + name, list(shape), dt, kind="ExternalOutput").ap()

    with ExitStack() as top:
        S = Sched(nc, top)
        G.S = S
        sb = lambda name, shape, dt: top.enter_context(nc.sbuf_tensor(U(name), list(shape), dt))

        G.ident = sb("ident", [128, 128], BF16)
        G.ones_f = sb("ones_f", [128, 128], F32)
        G.ones_b = sb("ones_b", [128, 128], BF16)
        G.ccol_sb = sb("ccol_sb", [128, 16], F32)
        G.silu_c = sb("silu_c", [128, 16], F32)
        G.eps_col = sb("eps_col", [128, 1], F32)
        G.b_const = Buf("const")
        csem = S.new_dma_sem("const")
        S.dma("sp", csem, [I("dma_start", out=G.ident[:], in_=G.ident_in[:, :]),
                           I("dma_start", out=G.ccol_sb[:], in_=G.ccol[:, :])], writes=[G.b_const])
        S.op("pool", I("memset", ap=G.ones_f[:], constant=1.0), writes=[G.b_const])
        S.op("pool", I("memset", ap=G.ones_b[:], constant=1.0), writes=[G.b_const])
        S.op("pool", I("memset", ap=G.eps_col[:], constant=EPS), writes=[G.b_const])
        S.op("act", I("activation", out=G.silu_c[:], in_=G.ccol_sb[:], func=AF.Silu),
             reads=[G.b_const], writes=[G.b_const])

        G.psall = top.enter_context(nc.psum_tensor(U("psall"), [128, 4096], F32))
        G.psb = [Buf("psb%d" % i) for i in range(8)]
        G.PS = PsRing(G)

        G.xres_b = [Buf("xres%d" % i) for i in range(NT)]
        G.scr_b = {}
        G.gtbc = sb("gtbc", [128, 2, D], F32)
        G.gtbc_b = Buf("gtbc")

        for l in range(n_layers):
            phase1_and_2(G, l, stop_after)
            if stop_after in ("p1", "p2"):
                break
            if stop_after not in ("p4only", "p5only"):
                phase3_attn(G, l)
            if stop_after == "p3":
                break
            if stop_after != "p5only":
                phase4_lru(G, l)
            if stop_after in ("p4", "p4only"):
                break
            phase5_s5(G, l)
            if stop_after in ("p5", "p5only"):
                break
            phase6_merge(G, l)
            if stop_after is not None:
                break

        S.barrier()
        with nc.Block() as block:
            S.emit(block)
    return nc


def phase3_attn(G, l):
    nc, S = G.nc, G.S
    lam_init = 0.8 - 0.6 * math.exp(-0.3 * l)
    need_ctx = l < DEPTH - 1
    ps = G.psall
    psb = G.psb
    mk = S.mark()
    with ExitStack() as p3:
        sbt = lambda name, shape, dt: p3.enter_context(nc.sbuf_tensor(U(name), list(shape), dt))
        KTs = sbt("KTs", [128, 4, T], BF16)
        KT_b = [Buf("KTs%d" % h) for h in range(4)]
        Vs = sbt("Vs", [128, NT, 512], BF16)
        V_b = Buf("Vs")
        lq = sbt("lq", [1, 4, 64], F32)
        lw = sbt("lw", [1, 8], F32)
        prm = sbt("prm", [128, 4], F32)
        prm_b = Buf("prm")
        for h in range(4):
            ksem = S.new_dma_sem("kt%d" % h)
            S.dma("sp", ksem, [I("dma_start", out=KTs[:, h, :], in_=G.KT[h * 128:(h + 1) * 128, :])],
                  reads=[scr_buf(G, "KT", h)], writes=[KT_b[h]])
        vsem = S.new_dma_sem("v")
        S.dma("sp", vsem, [I("dma_start", out=Vs[:, i * 17:(i + 1) * 17, :],
                             in_=G.Vtm[i * 17 * 128:(i + 1) * 17 * 128, :].rearrange("(t p) n -> p t n", p=128))
                           for i in range(2)],
              reads=[scr_buf(G, "V", ti) for ti in range(NT)], writes=[V_b])
        psem = S.new_dma_sem("prm")
        S.dma("sp", psem, [I("dma_start", out=lq[:].rearrange("a b c -> a (b c)"), in_=G.lam_qk[l, :, :]),
                           I("dma_start", out=prm[:, 1:2], in_=G.subln[l, :, :])], writes=[prm_b])
        S.op("dve", I("tensor_tensor", out=lq[0:1, 0::2, :], in0=lq[0:1, 0::2, :], in1=lq[0:1, 1::2, :], op=ALU.mult),
             reads=[prm_b], writes=[prm_b])
        S.op("dve", I("reduce_sum", out=lw[0:1, 0:2], in_=lq[0:1, 0::2, :], axis=AX.X), reads=[prm_b], writes=[prm_b])
        S.op("act", I("activation", out=lw[0:1, 2:4], in_=lw[0:1, 0:2], func=AF.Exp), reads=[prm_b], writes=[prm_b])
        S.op("dve", I("tensor_tensor", out=lw[0:1, 4:5], in0=lw[0:1, 3:4], in1=lw[0:1, 2:3], op=ALU.subtract),
             reads=[prm_b], writes=[prm_b])
        S.op("dve", I("tensor_scalar", out=lw[0:1, 5:6], in0=lw[0:1, 4:5], scalar1=-lam_init, scalar2=None, op0=ALU.add),
             reads=[prm_b], writes=[prm_b])
        S.op("pe", I("matmul", out=ps[:, 0:1], lhsT=G.ones_f[0:1, :], rhs=lw[0:1, 5:6], start=True, stop=True),
             reads=[prm_b, G.b_const], writes=[psb[0]])
        S.op("dve", I("tensor_copy", out=prm[:, 0:1], in_=ps[:, 0:1]), reads=[psb[0]], writes=[prm_b])
        S.op("dve", I("tensor_scalar", out=prm[:, 1:2], in0=prm[:, 1:2], scalar1=1.0 - lam_init, scalar2=None, op0=ALU.mult),
             reads=[prm_b], writes=[prm_b])

        QR = Ring(S, p3, "qr", 3, [128, 512], BF16, dma=True)
        ZR_ = Ring(S, p3, "za", 3, [128, 512], BF16, dma=True)
        PT = Ring(S, p3, "pT", 3, [128, 1024], BF16)
        WK = Ring(S, p3, "wk", 2, [128, 4, 512], F32)
        SQ = Ring(S, p3, "sq", 2, [128, 512], BF16)
        YO = Ring(S, p3, "yo", 2, [128, 512], BF16, dma=True, fresh=True)
        ACC = Ring(S, p3, "acc", 2, [128, 512], F32)
        ACC1 = Ring(S, p3, "acc1", 2, [128, 512], F32)
        sc_i = [0]

        groups = [(t0, tn, list(range(NT))) for (t0, tn) in TG[:8]]
        if need_ctx:
            groups.append((4096, 256, [32, 33]))
        heads = []
        for (t0, tn, ktiles) in groups:
            for h in range(4):
                heads.append(dict(t0=t0, tn=tn, kts=ktiles, h=h))
        steps = []
        for hi, hd in enumerate(heads):
            for ki, kt in enumerate(hd["kts"]):
                steps.append((hi, ki, kt))

        def emit_loads(hd):
            t0, tn, h = hd["t0"], hd["tn"], hd["h"]
            qt, qtb, qts = QR.next()
            S.dma("sp", qts, [I("dma_start", out=qt[:, 0:tn], in_=G.QT[h * 128:(h + 1) * 128, t0:t0 + tn])],
                  reads=[scr_buf(G, "QT", h)], writes=[qtb])
            za, zab, zas = ZR_.next()
            S.dma("sp", zas, [I("dma_start", out=za[:, 0:tn], in_=G.ZA[h * 128:(h + 1) * 128, t0:t0 + tn])],
                  reads=[scr_buf(G, "ZA", h)], writes=[zab])
            hd.update(qt=qt, qtb=qtb, za=za, zab=zab)

        def emit_S(step):
            hi, ki, kt = step
            hd = heads[hi]
            tn, h = hd["tn"], hd["h"]
            sb0 = (sc_i[0] % 2) * 2
            sc_i[0] += 1
            sc = ps[:, sb0 * 512:(sb0 + 2) * 512]
            scb = [psb[sb0], psb[sb0 + 1]]
            S.op("pe", [I("matmul", out=sc[:, c * 512:c * 512 + tn], lhsT=KTs[c * 64:(c + 1) * 64, h, kt * 128:(kt + 1) * 128],
                          rhs=hd["qt"][c * 64:(c + 1) * 64, 0:tn], start=True, stop=True) for c in range(2)],
                 reads=[KT_b[h], hd["qtb"]], writes=scb)
            return sc, scb

        def emit_exp_pv(step, sc, scb):
            hi, ki, kt = step
            hd = heads[hi]
            tn, h = hd["tn"], hd["h"]
            nk = len(hd["kts"])
            pT, pTb, _ = PT.next()
            if tn == 512:
                S.op("act", I("activation", out=pT[:, :], in_=sc[:, :], func=AF.Exp, scale=0.125), reads=scb, writes=[pTb])
            else:
                S.op("act", I("activation", out=pT[:].rearrange("p (c n) -> p c n", c=2)[:, :, 0:tn],
                              in_=sc.rearrange("p (c n) -> p c n", c=2)[:, :, 0:tn], func=AF.Exp, scale=0.125),
                     reads=scb, writes=[pTb])
            mm = []
            for c in range(2):
                mm.append(I("matmul", out=ps[:, (4 + 2 * c) * 512:(4 + 2 * c) * 512 + tn],
                            lhsT=Vs[:, kt, h * 128:(h + 1) * 128], rhs=pT[:, c * 512:c * 512 + tn],
                            start=(ki == 0), stop=(ki == nk - 1)))
            mm.append(I("matmul", out=ps[:, 7 * 512:7 * 512 + tn], lhsT=G.ones_b[:, :], rhs=pT[:, 512:512 + tn],
                        start=(ki == 0), stop=(ki == nk - 1)))
            mm.append(I("matmul", out=ps[:, 5 * 512:5 * 512 + tn], lhsT=G.ones_b[:, :], rhs=pT[:, 0:tn],
                        start=(ki == 0), stop=(ki == nk - 1)))
            S.op("pe", mm, reads=[V_b, pTb, G.b_const], writes=[psb[4], psb[5], psb[6], psb[7]])

        def emit_combine(hd):
            t0, tn, h = hd["t0"], hd["tn"], hd["h"]
            wk, wkb, _ = WK.next()
            S.op("dve", I("tensor_copy", out=wk[:, 0, 0:tn], in_=ps[:, 4 * 512:4 * 512 + tn]), reads=[psb[4]], writes=[wkb])
            S.op("dve", I("tensor_copy", out=wk[:, 1, 0:tn], in_=ps[:, 6 * 512:6 * 512 + tn]), reads=[psb[6]], writes=[wkb])
            S.op("dve", I("tensor_copy", out=wk[:, 3, 0:tn], in_=ps[:, 7 * 512:7 * 512 + tn]), reads=[psb[7]], writes=[wkb])
            S.op("dve", I("tensor_copy", out=wk[:, 2, 0:tn], in_=ps[:, 5 * 512:5 * 512 + tn]), reads=[psb[5]], writes=[wkb])
            S.op("dve", I("reciprocal", out=wk[:, 2, 0:tn], in_=wk[:, 2, 0:tn]), reads=[wkb], writes=[wkb])
            S.op("dve", I("tensor_tensor", out=wk[:, 0, 0:tn], in0=wk[:, 0, 0:tn], in1=wk[:, 2, 0:tn], op=ALU.mult),
                 reads=[wkb], writes=[wkb])
            S.op("dve", I("reciprocal", out=wk[:, 3, 0:tn], in_=wk[:, 3, 0:tn]), reads=[wkb], writes=[wkb])
            S.op("dve", I("tensor_tensor", out=wk[:, 1, 0:tn], in0=wk[:, 1, 0:tn], in1=wk[:, 3, 0:tn], op=ALU.mult),
                 reads=[wkb], writes=[wkb])
            S.op("dve", I("scalar_tensor_tensor", out=wk[:, 2, 0:tn], in0=wk[:, 1, 0:tn], scalar=prm[:, 0:1],
                          in1=wk[:, 0, 0:tn], op0=ALU.mult, op1=ALU.add), reads=[wkb, prm_b], writes=[wkb])
            sq, sqb, _ = SQ.next()
            S.op("pool", I("tensor_tensor", out=sq[:, 0:tn], in0=wk[:, 2, 0:tn], in1=wk[:, 2, 0:tn], op=ALU.mult),
                 reads=[wkb], writes=[sqb])
            hd.update(wk=wk, wkb=wkb, sq=sq, sqb=sqb)

        def emit_finish(hd, slot):
            t0, tn, h = hd["t0"], hd["tn"], hd["h"]
            za, zab, wk, wkb, sq, sqb = hd["za"], hd["zab"], hd["wk"], hd["wkb"], hd["sq"], hd["sqb"]
            sc_, scb_ = slot
            S.op("pe", I("matmul", out=sc_[:, 0:tn], lhsT=G.ones_b[:, :], rhs=sq[:, 0:tn],
                         start=True, stop=True), reads=[sqb, G.b_const], writes=[scb_[0]])
            S.op("act", I("activation", out=wk[:, 3, 0:tn], in_=sc_[:, 0:tn], func=AF.Ln,
                          scale=1.0 / 128, bias=G.eps_col[:, 0:1]), reads=[scb_[0], G.b_const], writes=[wkb])
            S.op("act", I("activation", out=wk[:, 3, 0:tn], in_=wk[:, 3, 0:tn], func=AF.Exp, scale=-0.5),
                 reads=[wkb], writes=[wkb])
            S.op("dve", I("tensor_tensor", out=wk[:, 2, 0:tn], in0=wk[:, 2, 0:tn], in1=wk[:, 3, 0:tn], op=ALU.mult),
                 reads=[wkb], writes=[wkb])
            yo, yob, yos = YO.next()
            S.op("dve", I("scalar_tensor_tensor", out=yo[:, 0:tn], in0=wk[:, 2, 0:tn], scalar=prm[:, 1:2],
                          in1=za[:, 0:tn], op0=ALU.mult, op1=ALU.mult), reads=[wkb, prm_b, zab], writes=[yob])
            S.dma("pool", yos, [I("dma_start", out=G.YG[0, h * 128:(h + 1) * 128, t0:t0 + tn], in_=yo[:, 0:tn])],
                  reads=[yob], writes=[scr_buf(G, "YG0", h)])

        items = s5_prep(G, l, p3)
        G.s5_items = items
        per_step = -(-len(items) // max(1, len(steps) - 8))
        emit_loads(heads[0])
        if len(heads) > 1:
            emit_loads(heads[1])
        cur = emit_S(steps[0])
        pending_finish = None
        for i, step in enumerate(steps):
            hi, ki, kt = step
            nxt = None
            if i + 1 < len(steps):
                nhi = steps[i + 1][0]
                if "qt" not in heads[nhi]:
                    emit_loads(heads[nhi])
                nxt = emit_S(steps[i + 1])
            emit_exp_pv(step, cur[0], cur[1])
            replay(S, items, per_step)
            last_k = (ki == len(heads[hi]["kts"]) - 1)
            if pending_finish is not None and (ki == 5 or last_k):
                emit_finish(pending_finish, cur)
                nl = pending_finish["idx"] + 2
                if nl < len(heads) and "qt" not in heads[nl]:
                    emit_loads(heads[nl])
                pending_finish = None
            if last_k:
                emit_combine(heads[hi])
                heads[hi]["idx"] = hi
                pending_finish = heads[hi]
            cur = nxt
        if pending_finish is not None:
            emit_finish(pending_finish, (ps[:, 0:1024], [psb[0], psb[1]]))
            pending_finish = None
        replay(S, items, len(items))
        if "YG" in G.dbg:
            S.barrier()
            dsem = S.new_dma_sem("dbg")
            S.dma("sp", dsem, [I("dma_start", out=G.dbg["YG"][:, :, :], in_=G.YG[:, :, :])])
        S.barrier()
        S.release_to(mk)


def phase4_lru(G, l):
    nc, S = G.nc, G.S
    ps, psb = G.psall, G.psb
    mk = S.mark()
    SEGS = [(0, SEQ), (SEQ, CTX)]
    CH = [(i * 1024, 1024) for i in range(4)] + [(4096, 256)]
    with ExitStack() as p4:
        sbt = lambda name, shape, dt: p4.enter_context(nc.sbuf_tensor(U(name), list(shape), dt))
        prm = sbt("lprm", [128, 44], F32)
        sp8 = sbt("sp8", [128, 8], F32)
        one_col = sbt("one_col", [128, 1], F32)
        prm_b = Buf("lprm")
        wf = sbt("lwf", [128, 16, 128], F32)
        wb = sbt("lwb", [128, 16, 128], BF16)
        w_b = Buf("lw")
        psem = S.new_dma_sem("lprm")
        S.dma("sp", psem, [I("dma_start", out=prm[:], in_=G.lrup[l, :, :]),
                           I("dma_start", out=wf[:], in_=G.lruw[l].rearrange("a d c p n -> p (a d c) n"))],
              writes=[prm_b, w_b])
        S.op("pool", I("tensor_copy", out=wb[:], in_=wf[:]), reads=[w_b], writes=[w_b])
        S.op("pool", I("memset", ap=one_col[:], constant=1.0), writes=[prm_b])
        S.op("act", I("activation", out=sp8[:], in_=prm[:, 36:44], func=AF.Exp, scale=-1.0), reads=[prm_b], writes=[prm_b])
        S.op("act", I("activation", out=sp8[:], in_=sp8[:], func=AF.Ln, bias=one_col[:, 0:1]), reads=[prm_b], writes=[prm_b])
        S.op("dve", I("tensor_scalar", out=sp8[:], in0=sp8[:], scalar1=-8.0, scalar2=None, op0=ALU.mult),
             reads=[prm_b], writes=[prm_b])

        xr = sbt("xr", [128, T], F32); xr_b = Buf("xr")
        u = sbt("u", [128, T], F32); u_b = Buf("u")
        ub = sbt("ub", [128, T], BF16); ub_b = Buf("ub")
        ra = sbt("ra", [128, T], F32); ra_b = Buf("ra")
        ib = sbt("ib", [128, T], F32); ib_b = Buf("ib")
        tmp = sbt("ltmp", [128, T], F32); tmp_b = Buf("ltmp")
        hf = sbt("hf", [128, T], F32); hf_b = Buf("hf")
        hbr = sbt("hbr", [128, T], F32); hbr_b = Buf("hbr")
        zr = sbt("zr", [128, T], BF16); zr_b = Buf("zr")
        yo = sbt("lyo", [128, T], BF16); yo_b = Buf("lyo")
        xsem = S.new_dma_sem("xr"); zsem = S.new_dma_sem("zr"); ysem = S.new_dma_sem("lyo", fresh=True)
        for ct in range(4):
            S.dma("sp", xsem, [I("dma_start", out=xr[:, :], in_=G.XR[ct * 128:(ct + 1) * 128, :])],
                  reads=[scr_buf(G, "XR", ct)], writes=[xr_b])
            S.dma("sp", zsem, [I("dma_start", out=zr[:, :], in_=G.ZR[ct * 128:(ct + 1) * 128, :])],
                  reads=[scr_buf(G, "ZR", ct)], writes=[zr_b])
            for (s0, n) in SEGS:
                S.op("pool", I("tensor_scalar", out=u[:, s0:s0 + n], in0=xr[:, s0:s0 + n],
                               scalar1=prm[:, ct * 4 + 2:ct * 4 + 3], scalar2=prm[:, 16 + ct:17 + ct],
                               op0=ALU.mult, op1=ALU.add), reads=[xr_b, prm_b], writes=[u_b])
                for (k, oa, ob_, ia, ib_) in ((0, 2, n, 0, n - 2), (1, 1, n, 0, n - 1), (3, 0, n - 1, 1, n)):
                    S.op("dve", I("scalar_tensor_tensor", out=u[:, s0 + oa:s0 + ob_], in0=xr[:, s0 + ia:s0 + ib_],
                                   scalar=prm[:, ct * 4 + k:ct * 4 + k + 1], in1=u[:, s0 + oa:s0 + ob_],
                                   op0=ALU.mult, op1=ALU.add), reads=[xr_b, prm_b, u_b], writes=[u_b])
            S.op("act", I("activation", out=ub[:, :], in_=u[:, :], func=AF.Copy), reads=[u_b], writes=[ub_b])
            for d in range(2):
                for ci, (c0, cn) in enumerate(CH):
                    bk = (ci % 2) * 4
                    for gi in range(2):
                        widx = gi * 8 + d * 4 + ct
                        mm = []
                        for s in range(0, cn, 512):
                            sn = min(512, cn - s)
                            mm.append(I("matmul", out=ps[:, (bk + gi * 2) * 512 + s:(bk + gi * 2) * 512 + s + sn],
                                        lhsT=wb[:, widx, :], rhs=ub[:, c0 + s:c0 + s + sn], start=True, stop=True))
                        S.op("pe", mm, reads=[w_b, ub_b], writes=[psb[bk + gi * 2], psb[bk + gi * 2 + 1]])
                        dst, dst_b = (ra, ra_b) if gi == 0 else (ib, ib_b)
                        bcol = (20 if gi == 0 else 28) + d * 4 + ct
                        S.op("act", I("activation", out=dst[:, c0:c0 + cn], in_=ps[:, (bk + gi * 2) * 512:(bk + gi * 2) * 512 + cn],
                                      func=AF.Sigmoid, bias=prm[:, bcol:bcol + 1]),
                             reads=[psb[bk + gi * 2], psb[bk + gi * 2 + 1], prm_b], writes=[dst_b])
                S.op("act", I("activation", out=ra[:, :], in_=ra[:, :], func=AF.Exp, scale=sp8[:, d * 4 + ct:d * 4 + ct + 1]),
                     reads=[ra_b, prm_b], writes=[ra_b])
                S.op("dve", I("tensor_tensor", out=tmp[:, :], in0=ra[:, :], in1=ra[:, :], op=ALU.mult), reads=[ra_b], writes=[tmp_b])
                S.op("act", I("activation", out=tmp[:, :], in_=tmp[:, :], func=AF.Sqrt, scale=-1.0, bias=one_col[:, 0:1]),
                     reads=[tmp_b, prm_b], writes=[tmp_b])
                S.op("pool", I("tensor_tensor", out=ib[:, :], in0=ib[:, :], in1=u[:, :], op=ALU.mult), reads=[ib_b, u_b], writes=[ib_b])
                S.op("dve", I("tensor_tensor", out=ib[:, :], in0=ib[:, :], in1=tmp[:, :], op=ALU.mult), reads=[ib_b, tmp_b], writes=[ib_b])
                if d == 0:
                    S.op("dve", I("tensor_tensor_scan", out=hf[:, SEQ:T], data0=ra[:, SEQ:T], data1=ib[:, SEQ:T], initial=0.0,
                                  op0=ALU.mult, op1=ALU.add), reads=[ra_b, ib_b], writes=[hf_b])
                    S.op("dve", I("tensor_tensor_scan", out=hf[:, 0:SEQ], data0=ra[:, 0:SEQ], data1=ib[:, 0:SEQ],
                                  initial=hf[:, T - 1:T], op0=ALU.mult, op1=ALU.add), reads=[ra_b, ib_b, hf_b], writes=[hf_b])
                else:
                    S.op("dve", I("tensor_tensor_scan", out=hbr[:, 0:CTX], data0=ra[:, SEQ:T][:, ::-1], data1=ib[:, SEQ:T][:, ::-1],
                                  initial=0.0, op0=ALU.mult, op1=ALU.add), reads=[ra_b, ib_b], writes=[hbr_b])
                    S.op("dve", I("tensor_tensor_scan", out=hbr[:, CTX:T], data0=ra[:, 0:SEQ][:, ::-1], data1=ib[:, 0:SEQ][:, ::-1],
                                  initial=hbr[:, CTX - 1:CTX], op0=ALU.mult, op1=ALU.add), reads=[ra_b, ib_b, hbr_b], writes=[hbr_b])
            S.op("dve", I("tensor_tensor", out=hf[:, 0:SEQ], in0=hf[:, 0:SEQ], in1=hbr[:, CTX:T][:, ::-1], op=ALU.add),
                 reads=[hf_b, hbr_b], writes=[hf_b])
            S.op("dve", I("tensor_tensor", out=hf[:, SEQ:T], in0=hf[:, SEQ:T], in1=hbr[:, 0:CTX][:, ::-1], op=ALU.add),
                 reads=[hf_b, hbr_b], writes=[hf_b])
            S.op("pool", I("tensor_tensor", out=yo[:, :], in0=hf[:, :], in1=zr[:, :], op=ALU.mult),
                 reads=[hf_b, zr_b], writes=[yo_b])
            S.dma("pool", ysem, [I("dma_start", out=G.YG[1, ct * 128:(ct + 1) * 128, :], in_=yo[:, :])],
                  reads=[yo_b], writes=[scr_buf(G, "YG1", ct)])
        if "YG" in G.dbg:
            S.barrier()
            dsem = S.new_dma_sem("dbg")
            S.dma("sp", dsem, [I("dma_start", out=G.dbg["YG"][:, :, :], in_=G.YG[:, :, :])])
        S.barrier()
        S.release_to(mk)


TWO_PI = 2.0 * math.pi
I32 = mybir.dt.int32


class Rec:
    def __init__(self, S):
        self.S = S
        self.items = []

    def new_dma_sem(self, name):
        return self.S.new_dma_sem(name)

    def op(self, eng, fn, reads=(), writes=()):
        self.items.append(("op", eng, fn, list(reads), list(writes)))

    def dma(self, eng, semkey, fns, reads=(), writes=()):
        self.items.append(("dma", eng, semkey, fns, list(reads), list(writes)))


def replay(S, items, n):
    for _ in range(n):
        if not items:
            return
        it = items.pop(0)
        if it[0] == "op":
            S.op(it[1], it[2], it[3], it[4])
        else:
            S.dma(it[1], it[2], it[3], it[4], it[5])


NCH = T // 8


def s5_prep(G, l, stack):
    nc = G.nc
    R = Rec(G.S)
    S = R
    sbt = lambda name, shape, dt: stack.enter_context(nc.sbuf_tensor(U(name), list(shape), dt))
    tab = sbt("s5tab", [128, 36 + 512 + 544 + 128], F32)
    ptab = tab[:, 0:36].rearrange("p (j q) -> p j q", q=9)
    etab = tab[:, 36:548].rearrange("p (e n) -> p e n", n=64)
    ctab = tab[:, 548:1092]
    mask = tab[:, 1092:1220].rearrange("p (g n) -> p g n", n=16)
    dtall = sbt("dtall", [128, 8], F32)
    cst_b = Buf("s5cst")
    csem = S.new_dma_sem("s5c")
    S.dma("sp", csem, [I("dma_start", out=tab[:], in_=G.s5tab[:, :]),
                       I("dma_start", out=dtall[:], in_=G.s5dt[l, :, :])], writes=[cst_b])
    S.op("act", I("activation", out=dtall[:], in_=dtall[:], func=AF.Exp), reads=[cst_b], writes=[cst_b])
    lamN = sbt("lamN", [128, 8], F32)
    BN = sbt("BN", [128, 2, 4, 16], F32)
    CN = sbt("CN", [128, 2, 4, 16], F32)
    nsem = S.new_dma_sem("s5n")
    np_b = Buf("nprm")
    n4 = sbt("n4", [128, 12, 4], F32)
    nP = sbt("nP", [128, 8, 4, 9], F32)
    nPi = sbt("nPi", [128, 4, 9], I32)
    bbN = sbt("bbN", [128, 2, 4, 16], F32)
    tN = sbt("tN", [128, 4, 4, 8, 16], F32)
    Hp = sbt("Hp", [128, 4, 64], F32)
    hsem = S.new_dma_sem("s5h")
    hp_b = Buf("hprm")
    h64 = sbt("h64", [128, 10, 64], F32)
    hE = sbt("hE", [128, 4, 8, 64], F32)
    W1d = sbt("W1d", [128, 8, 128], F32)
    tNf = tN[:].rearrange("p a b c d -> p a (b c d)")
    hEi = tNf[:, 2, :].bitcast(I32).rearrange("p (e n) -> p e n", n=64)
    OUT = [sbt("s5out%d" % i, [128, 8192], BF16) for i in range(2)]
    out_b = [Buf("s5out%d" % i) for i in range(2)]
    out_s = [S.new_dma_sem("s5o%d" % i) for i in range(2)]
    oi = [0]
    tc = sbt("tc", [128, 7, NCH], F32)
    tc_b = Buf("tc")
    tcs = S.new_dma_sem("tc")
    rsem = S.new_dma_sem("rh")

    def trig(turns, ti, tf2, r, out2, bufs, eng="dve"):
        for which in (0, 1):
            tf = tf2[which]
            S.op(eng, I("tensor_scalar", out=tf, in0=turns, scalar1=16.25 - 0.25 * which, scalar2=None, op0=ALU.add),
                 reads=bufs, writes=bufs)
            S.op(eng, I("tensor_copy", out=ti, in_=tf), reads=bufs, writes=bufs)
            S.op(eng, I("tensor_copy", out=r, in_=ti), reads=bufs, writes=bufs)
            S.op(eng, I("tensor_tensor", out=tf, in0=tf, in1=r, op=ALU.subtract), reads=bufs, writes=bufs)
            S.op(eng, I("tensor_single_scalar", out=r, in_=tf, scalar=0.5, op=ALU.is_ge), reads=bufs, writes=bufs)
            S.op(eng, I("tensor_tensor", out=tf, in0=tf, in1=r, op=ALU.subtract), reads=bufs, writes=bufs)
        S.op("act", I("activation", out=out2, in_=tf2[2], func=AF.Sin, scale=TWO_PI), reads=bufs, writes=bufs)

    def cplx_coef(pr1, pi1, lr, li, t, bre, bim, obr, obi, shp_b, bufs, eng="dve", tfull=None):
        nr, den, cr, ci, t4, t5 = t
        S.op(eng, I("tensor_scalar", out=nr, in0=pr1, scalar1=-1.0, scalar2=None, op0=ALU.add), reads=bufs, writes=bufs)
        S.op(eng, I("tensor_tensor", out=den, in0=lr, in1=lr, op=ALU.mult), reads=bufs, writes=bufs)
        S.op(eng, I("tensor_tensor", out=t4, in0=li, in1=li, op=ALU.mult), reads=bufs, writes=bufs)
        S.op(eng, I("tensor_tensor", out=den, in0=den, in1=t4, op=ALU.add), reads=bufs, writes=bufs)
        S.op("dve", I("reciprocal", out=den, in_=den), reads=bufs, writes=bufs)
        S.op(eng, I("tensor_tensor", out=cr, in0=nr, in1=lr, op=ALU.mult), reads=bufs, writes=bufs)
        S.op(eng, I("tensor_tensor", out=t4, in0=pi1, in1=li, op=ALU.mult), reads=bufs, writes=bufs)
        S.op(eng, I("tensor_tensor", out=cr, in0=cr, in1=t4, op=ALU.add), reads=bufs, writes=bufs)
        S.op(eng, I("tensor_tensor", out=cr, in0=cr, in1=den, op=ALU.mult), reads=bufs, writes=bufs)
        S.op(eng, I("tensor_tensor", out=ci, in0=pi1, in1=lr, op=ALU.mult), reads=bufs, writes=bufs)
        S.op(eng, I("tensor_tensor", out=t4, in0=nr, in1=li, op=ALU.mult), reads=bufs, writes=bufs)
        S.op(eng, I("tensor_tensor", out=ci, in0=ci, in1=t4, op=ALU.subtract), reads=bufs, writes=bufs)
        S.op(eng, I("tensor_tensor", out=ci, in0=ci, in1=den, op=ALU.mult), reads=bufs, writes=bufs)
        crb = cr if shp_b is None else cr.unsqueeze(2).broadcast_to(shp_b)
        cib = ci if shp_b is None else ci.unsqueeze(2).broadcast_to(shp_b)
        S.op(eng, I("tensor_tensor", out=obr, in0=bre, in1=crb, op=ALU.mult), reads=bufs, writes=bufs)
        S.op(eng, I("tensor_tensor", out=obi, in0=bim, in1=cib, op=ALU.mult), reads=bufs, writes=bufs)
        S.op(eng, I("tensor_tensor", out=obr, in0=obr, in1=obi, op=ALU.subtract), reads=bufs, writes=bufs)
        S.op(eng, I("tensor_tensor", out=obi, in0=bim, in1=crb, op=ALU.mult), reads=bufs, writes=bufs)
        t6 = t5 if tfull is None else tfull
        S.op(eng, I("tensor_tensor", out=t6, in0=bre, in1=cib, op=ALU.mult), reads=bufs, writes=bufs)
        S.op(eng, I("tensor_tensor", out=obi, in0=obi, in1=t6, op=ALU.add), reads=bufs, writes=bufs)

    def out_slot():
        i = oi[0] % 2
        oi[0] += 1
        return OUT[i], out_b[i], out_s[i]

    cnsem = S.new_dma_sem("s5cn")
    for gt in range(4):
        S.dma("sp", cnsem, [I("dma_start", out=CN[:], in_=G.s5CN[l, gt])], writes=[np_b])
        for d in range(2):
            idx = gt * 2 + d
            dcol = dtall[:, d * 4 + gt:d * 4 + gt + 1]
            nb = [np_b, cst_b]
            S.dma("sp", nsem, [I("dma_start", out=lamN[:], in_=G.s5N[l, d, gt]),
                               I("dma_start", out=BN[:], in_=G.s5BN[l, d, gt])], writes=[np_b])
            ld4, tu4 = n4[:, 0, :], n4[:, 1, :]
            S.op("dve", I("tensor_scalar", out=ld4, in0=lamN[:, 0:4], scalar1=dcol, scalar2=None, op0=ALU.mult), reads=nb, writes=nb)
            S.op("dve", I("tensor_scalar", out=tu4, in0=lamN[:, 4:8], scalar1=dcol, scalar2=1.0 / TWO_PI,
                           op0=ALU.mult, op1=ALU.mult), reads=nb, writes=nb)
            angP, mgP, cosP, sinP, prP, piP, tfP, rP = [nP[:, i] for i in range(8)]
            S.op("dve", I("tensor_tensor", out=angP, in0=ptab, in1=tu4.unsqueeze(2).broadcast_to([128, 4, 9]), op=ALU.mult), reads=nb, writes=nb)
            S.op("dve", I("tensor_tensor", out=mgP, in0=ptab, in1=ld4.unsqueeze(2).broadcast_to([128, 4, 9]), op=ALU.mult), reads=nb, writes=nb)
            S.op("act", I("activation", out=mgP, in_=mgP, func=AF.Exp), reads=nb, writes=nb)
            trig(angP, nPi[:], (nP[:, 4], nP[:, 5], nP[:, 4:6]), rP, nP[:, 2:4], nb)
            S.op("dve", I("tensor_tensor", out=prP, in0=mgP, in1=cosP, op=ALU.mult), reads=nb, writes=nb)
            S.op("dve", I("tensor_tensor", out=piP, in0=mgP, in1=sinP, op=ALU.mult), reads=nb, writes=nb)
            cplx_coef(prP[:, :, 1], piP[:, :, 1], lamN[:, 0:4], lamN[:, 4:8], [n4[:, i, :] for i in range(2, 8)],
                      BN[:, 0], BN[:, 1], bbN[:, 0], bbN[:, 1], [128, 4, 16], nb, tfull=tN[:, 3, :, 0, :])
            S.op("dve", I("tensor_copy", out=n4[:, 10, :], in_=mgP[:, :, 8]), reads=nb, writes=nb)
            S.dma("sp", rsem, [I("dma_start", out=G.S5RH[idx], in_=n4[:, 10, :])], reads=nb)
            f8 = n4[:, 8, :]
            S.op("dve", I("tensor_copy", out=nPi[:, :, 0], in_=angP[:, :, 8]), reads=nb, writes=nb)
            S.op("dve", I("tensor_copy", out=n4[:, 9, :], in_=nPi[:, :, 0]), reads=nb, writes=nb)
            S.op("dve", I("tensor_tensor", out=f8, in0=angP[:, :, 8], in1=n4[:, 9, :], op=ALU.subtract), reads=nb, writes=nb)
            for jc in range(4):
                tb = [tc_b, np_b, cst_b]
                S.op("dve", I("tensor_scalar", out=tc[:, 0, :], in0=ctab, scalar1=f8[:, jc:jc + 1], scalar2=None, op0=ALU.mult),
                     reads=tb, writes=tb)
                trig(tc[:, 0, :], tc[:, 5, :].bitcast(I32), (tc[:, 1, :], tc[:, 2, :], tc[:, 1:3, :]), tc[:, 6, :], tc[:, 3:5, :], tb, eng="dve")
                S.dma("sp", tcs, [I("dma_start", out=G.S5TB[idx, :, 0, jc, :], in_=tc[:, 3, :]),
                                  I("dma_start", out=G.S5TB[idx, :, 1, jc, :], in_=tc[:, 4, :])], reads=tb)
            hb = [hp_b, cst_b]
            S.dma("sp", hsem, [I("dma_start", out=Hp[:], in_=G.s5H[l, d, gt])], writes=[hp_b])
            ldH, tuH = h64[:, 0, :], h64[:, 1, :]
            S.op("dve", I("tensor_scalar", out=ldH, in0=Hp[:, 0, :], scalar1=dcol, scalar2=None, op0=ALU.mult), reads=hb, writes=hb)
            S.op("dve", I("tensor_scalar", out=tuH, in0=Hp[:, 1, :], scalar1=dcol, scalar2=1.0 / TWO_PI,
                           op0=ALU.mult, op1=ALU.mult), reads=hb, writes=hb)
            angE, mgE, cosE, sinE = [hE[:, i] for i in range(4)]
            tfE = tNf[:, 0, :].rearrange("p (e n) -> p e n", n=64)
            rE = tNf[:, 1, :].rearrange("p (e n) -> p e n", n=64)
            S.op("dve", I("tensor_tensor", out=angE, in0=etab, in1=tuH.unsqueeze(1).broadcast_to([128, 8, 64]), op=ALU.mult), reads=hb, writes=hb)
            S.op("dve", I("tensor_tensor", out=mgE, in0=etab, in1=ldH.unsqueeze(1).broadcast_to([128, 8, 64]), op=ALU.mult), reads=hb, writes=hb)
            S.op("act", I("activation", out=mgE, in_=mgE, func=AF.Exp), reads=hb, writes=hb)
            trig(angE, hEi, (tfE, rE, tNf[:, 0:2, :].rearrange("p a (e n) -> p a e n", n=64)),
                 tNf[:, 3, :].rearrange("p (e n) -> p e n", n=64), hE[:, 2:4], hb + [np_b], eng="dve")
            S.op("dve", I("tensor_tensor", out=cosE, in0=mgE, in1=cosE, op=ALU.mult), reads=hb, writes=hb)
            S.op("dve", I("tensor_tensor", out=sinE, in0=mgE, in1=sinE, op=ALU.mult), reads=hb, writes=hb)
            bbrH, bbiH = h64[:, 8, :], h64[:, 9, :]
            cplx_coef(cosE[:, 1, :], sinE[:, 1, :], Hp[:, 0, :], Hp[:, 1, :], [h64[:, i, :] for i in range(2, 8)],
                      Hp[:, 2, :], Hp[:, 3, :], bbrH, bbiH, None, hb, eng="dve")
            bbrB = bbrH.unsqueeze(1).broadcast_to([128, 8, 64]); bbiB = bbiH.unsqueeze(1).broadcast_to([128, 8, 64])
            hb2 = hb + [np_b]
            S.op("dve", I("tensor_tensor", out=W1d[:, :, 0:64], in0=cosE, in1=bbrB, op=ALU.mult), reads=hb, writes=hb)
            S.op("dve", I("tensor_tensor", out=tfE, in0=sinE, in1=bbiB, op=ALU.mult), reads=hb2, writes=hb2)
            S.op("dve", I("tensor_tensor", out=W1d[:, :, 0:64], in0=W1d[:, :, 0:64], in1=tfE, op=ALU.subtract), reads=hb2, writes=hb)
            S.op("dve", I("tensor_tensor", out=W1d[:, :, 64:128], in0=cosE, in1=bbiB, op=ALU.mult), reads=hb, writes=hb)
            S.op("dve", I("tensor_tensor", out=tfE, in0=sinE, in1=bbrB, op=ALU.mult), reads=hb2, writes=hb2)
            S.op("dve", I("tensor_tensor", out=W1d[:, :, 64:128], in0=W1d[:, :, 64:128], in1=tfE, op=ALU.add), reads=hb2, writes=hb)
            ot, otb, ots = out_slot()
            W1bd = ot[:].rearrange("p (e j n) -> p e j n", e=8, j=8)
            for e in range(8):
                S.op("pool" if e % 2 else "dve",
                     I("tensor_tensor", out=W1bd[:, e].rearrange("p j (g n) -> p j g n", n=16),
                       in0=W1d[:, e, :].rearrange("p (j n) -> p j n", n=16).unsqueeze(2).broadcast_to([128, 8, 8, 16]),
                       in1=mask.unsqueeze(1).broadcast_to([128, 8, 8, 16]), op=ALU.mult),
                     reads=hb, writes=[otb])
            S.dma("sp", ots, [I("dma_start", out=G.S5W1[idx], in_=ot[:])], reads=[otb])
            if d == 0:
                prS, piS = prP[:, :, 1:9], piP[:, :, 1:9]
            else:
                prS, piS = prP[:, :, 1:9][:, :, ::-1], piP[:, :, 1:9][:, :, ::-1]
            prB = prS.unsqueeze(3).broadcast_to([128, 4, 8, 16]); piB = piS.unsqueeze(3).broadcast_to([128, 4, 8, 16])
            creB = CN[:, 0].unsqueeze(2).broadcast_to([128, 4, 8, 16]); cimB = CN[:, 1].unsqueeze(2).broadcast_to([128, 4, 8, 16])
            wR, wI, t2, t3 = [tN[:, i] for i in range(4)]
            S.op("dve", I("tensor_tensor", out=wR, in0=creB, in1=prB, op=ALU.mult), reads=nb, writes=nb)
            S.op("dve", I("tensor_tensor", out=t2, in0=cimB, in1=piB, op=ALU.mult), reads=nb, writes=nb)
            S.op("dve", I("tensor_tensor", out=wR, in0=wR, in1=t2, op=ALU.subtract), reads=nb, writes=nb)
            S.op("dve", I("tensor_tensor", out=wI, in0=creB, in1=piB, op=ALU.mult), reads=nb, writes=nb)
            S.op("dve", I("tensor_tensor", out=t2, in0=cimB, in1=prB, op=ALU.mult), reads=nb, writes=nb)
            S.op("dve", I("tensor_tensor", out=wI, in0=wI, in1=t2, op=ALU.add), reads=nb, writes=nb)
            S.op("dve", I("tensor_scalar", out=wI, in0=wI, scalar1=-1.0, scalar2=None, op0=ALU.mult), reads=nb, writes=nb)
            ot, otb, ots = out_slot()
            W3bd = ot[:].rearrange("p (j s n) -> p j s n", j=8, s=8)
            for j in range(8):
                srcw = (wR if j < 4 else wI)[:, j % 4]
                S.op("pool" if j % 2 else "dve",
                     I("tensor_tensor", out=W3bd[:, j].rearrange("p s (g n) -> p s g n", n=16),
                       in0=srcw.unsqueeze(2).broadcast_to([128, 8, 8, 16]),
                       in1=mask.unsqueeze(1).broadcast_to([128, 8, 8, 16]), op=ALU.mult),
                     reads=nb, writes=[otb])
            S.dma("sp", ots, [I("dma_start", out=G.S5W3[idx], in_=ot[:])], reads=[otb])
            prK = prP[:, :, 0:8].unsqueeze(3).broadcast_to([128, 4, 8, 16]); piK = piP[:, :, 0:8].unsqueeze(3).broadcast_to([128, 4, 8, 16])
            bbrB = bbN[:, 0].unsqueeze(2).broadcast_to([128, 4, 8, 16]); bbiB = bbN[:, 1].unsqueeze(2).broadcast_to([128, 4, 8, 16])
            S.op("dve", I("tensor_tensor", out=wR, in0=bbrB, in1=prK, op=ALU.mult), reads=nb, writes=nb)
            S.op("dve", I("tensor_tensor", out=t2, in0=bbiB, in1=piK, op=ALU.mult), reads=nb, writes=nb)
            S.op("dve", I("tensor_tensor", out=wR, in0=wR, in1=t2, op=ALU.subtract), reads=nb, writes=nb)
            S.op("dve", I("tensor_tensor", out=wI, in0=bbiB, in1=prK, op=ALU.mult), reads=nb, writes=nb)
            S.op("dve", I("tensor_tensor", out=t2, in0=bbrB, in1=piK, op=ALU.mult), reads=nb, writes=nb)
            S.op("dve", I("tensor_tensor", out=wI, in0=wI, in1=t2, op=ALU.add), reads=nb, writes=nb)
            ot, otb, ots = out_slot()
            XBD = ot[:].rearrange("p (r q n) -> p r q n", r=2, q=32)
            for ri in range(2):
                srcx = (wR if ri == 0 else wI).rearrange("p j k h -> p (j k) h")
                for half in range(2):
                    S.op("pool" if half else "dve",
                         I("tensor_tensor", out=XBD[:, ri, half * 16:(half + 1) * 16].rearrange("p q (g n) -> p q g n", n=16),
                           in0=srcx[:, half * 16:(half + 1) * 16].unsqueeze(2).broadcast_to([128, 16, 8, 16]),
                           in1=mask.unsqueeze(1).broadcast_to([128, 16, 8, 16]), op=ALU.mult),
                         reads=nb, writes=[otb])
            S.dma("sp", ots, [I("dma_start", out=G.S5XB[idx], in_=ot[:])], reads=[otb])
    return R.items


def phase5_s5(G, l):
    nc, S = G.nc, G.S
    ps, psb = G.psall, G.psb
    if G.s5_items is None:
        mk0 = S.mark()
        with ExitStack() as pp:
            items = s5_prep(G, l, pp)
            replay(S, items, len(items))
            S.barrier()
            S.release_to(mk0)
    else:
        assert not G.s5_items
    G.s5_items = None
    mk = S.mark()
    RANGES = [(0, 256), (256, 256), (512, 32)]
    with ExitStack() as p5:
        sbt = lambda name, shape, dt: p5.enter_context(nc.sbuf_tensor(U(name), list(shape), dt))
        tab = sbt("s5tabm", [128, 128], F32)
        mask = tab[:, 0:128].rearrange("p (g n) -> p g n", n=16)
        misc = sbt("s5misc", [128, 8], F32)
        cst_b = Buf("s5cst")
        csem = S.new_dma_sem("s5c")
        S.dma("sp", csem, [I("dma_start", out=tab[:], in_=G.s5tab[:, 1092:1220]),
                           I("dma_start", out=misc[:], in_=G.s5misc[l, :, :])], writes=[cst_b])
        Ut = sbt("Ut", [128, T], F32); U_b = Buf("Ut")
        Ub = sbt("Ub", [128, T], BF16); Ub_b = Buf("Ub")
        gst, gst_b = Ub, Ub_b
        gsem = S.new_dma_sem("gst", fresh=True)
        yt, y_b = Ut, U_b
        usem = S.new_dma_sem("s5u")
        CN = sbt("CN", [128, 2, 4, 16], F32)
        nsem = S.new_dma_sem("s5n")
        CBD = sbt("CBD", [128, 2, 4, 128], BF16); cbd_b = Buf("CBD")
        Kf = sbt("Kf", [128, 16, 128], BF16); k_b = Buf("Kf")
        K0 = sbt("K0", [128, 128], F32)
        Srot = sbt("Srot", [128, 8, NCH], F32); sr_b = Buf("Srot")
        Gs = sbt("Gs", [128, 8, NCH], F32); gs_b = Buf("Gs")
        Ebf = [sbt("Ebf%d" % d, [128, 8, NCH + 1], BF16) for d in range(2)]
        e_b = [Buf("Ebf%d" % d) for d in range(2)]
        rt = Gs[:, 0:4, :].rearrange("p a c -> p (a c)")[:, 0:2048].rearrange("p (r j c) -> p r j c", r=2, j=4); rt_b = gs_b
        WA = Ring(S, p5, "wa", 2, [128, 8192], BF16, dma=True)
        TBr = Ring(S, p5, "tbr", 2, [128, 2, 4, NCH], F32, dma=True)
        RHr = Ring(S, p5, "rhr", 2, [128, 4], F32, dma=True)
        XBr = Ring(S, p5, "xbr", 1, [128, 8192], BF16, dma=True)
        W3t = [sbt("W3bd%d" % d, [128, 8192], BF16) for d in range(2)]
        w3_b = [Buf("W3bd%d" % d) for d in range(2)]
        w3s = [S.new_dma_sem("w3%d" % d) for d in range(2)]
        loaded = {}

        def load_w(idx):
            if idx in loaded or idx >= 8:
                return
            wa, wab, was = WA.next()
            S.dma("sp", was, [I("dma_start", out=wa[:], in_=G.S5W1[idx])], writes=[wab])
            tb, tbb, tbs = TBr.next()
            S.dma("sp", tbs, [I("dma_start", out=tb[:], in_=G.S5TB[idx])], writes=[tbb])
            rh, rhb, rhs = RHr.next()
            S.dma("sp", rhs, [I("dma_start", out=rh[:], in_=G.S5RH[idx])], writes=[rhb])
            loaded[idx] = (wa, wab, tb, tbb, rh, rhb)

        load_w(0)
        for gt in range(4):
            S.dma("sp", usem, [I("dma_start", out=Ut[:, :], in_=G.US[gt * 128:(gt + 1) * 128, :])], writes=[U_b])
            S.op("act", I("activation", out=Ub[:, :], in_=Ut[:, :], func=AF.Copy), reads=[U_b], writes=[Ub_b])
            S.dma("sp", nsem, [I("dma_start", out=CN[:], in_=G.s5CN[l, gt])], writes=[cbd_b])
            for ri in range(2):
                S.op("pool", I("tensor_tensor", out=CBD[:, ri].rearrange("p j (g n) -> p j g n", n=16),
                               in0=CN[:, ri].unsqueeze(2).broadcast_to([128, 4, 8, 16]),
                               in1=mask.unsqueeze(1).broadcast_to([128, 4, 8, 16]), op=ALU.mult),
                     reads=[cst_b, cbd_b], writes=[cbd_b])
            S.op("pool", I("tensor_scalar", out=CBD[:, 1], in0=CBD[:, 1], scalar1=-1.0, scalar2=None, op0=ALU.mult),
                 reads=[cbd_b], writes=[cbd_b])
            for d in range(2):
                S.dma("sp", w3s[d], [I("dma_start", out=W3t[d][:], in_=G.S5W3[gt * 2 + d])], writes=[w3_b[d]])
            for d in range(2):
                idx = gt * 2 + d
                load_w(idx)
                wa, w1_b, tbl, tbl_b, rho, rho_b = loaded[idx]
                W1bd = wa[:].rearrange("p (e j n) -> p e j n", e=8, j=8)
                cosT, sinT = tbl[:, 0], tbl[:, 1]
                for ri_, (c0, n) in enumerate(RANGES):
                    b0 = (ri_ % 2) * 4
                    for j in range(8):
                        off = b0 * 512 + j * 256
                        S.op("pe", [I("matmul", out=ps[:, off:off + n], lhsT=W1bd[:, (7 - s) if d == 0 else s, j, :],
                                      rhs=Ub[:, c0 * 8 + s:(c0 + n) * 8:8], start=(s == 0), stop=(s == 7)) for s in range(8)],
                             reads=[w1_b, Ub_b], writes=psb[b0:b0 + 4])
                    pv = ps[:, b0 * 512:(b0 + 4) * 512].rearrange("p (j c) -> p j c", c=256)
                    Sre, Sim = pv[:, 0:4, 0:n], pv[:, 4:8, 0:n]
                    if d == 0:
                        cp = c0 + 32 if c0 < 512 else 0
                        cT, sT = cosT[:, :, cp:cp + n], sinT[:, :, cp:cp + n]
                        ore, oim = Srot[:, 0:4, cp:cp + n], Srot[:, 4:8, cp:cp + n]
                    else:
                        lo = NCH - 1 - (c0 + n - 1)
                        cT, sT = cosT[:, :, lo:lo + n][:, :, ::-1], sinT[:, :, lo:lo + n][:, :, ::-1]
                        ore, oim = Srot[:, 0:4, c0:c0 + n], Srot[:, 4:8, c0:c0 + n]
                    rb = psb[b0:b0 + 4] + [tbl_b]
                    S.op("dve", I("tensor_tensor", out=rt[:, 0, :, 0:n], in0=Sre, in1=cT, op=ALU.mult), reads=rb, writes=[rt_b])
                    S.op("dve", I("tensor_tensor", out=rt[:, 1, :, 0:n], in0=Sim, in1=sT, op=ALU.mult), reads=rb, writes=[rt_b])
                    S.op("pool", I("tensor_tensor", out=ore, in0=rt[:, 0, :, 0:n], in1=rt[:, 1, :, 0:n], op=ALU.add), reads=[rt_b], writes=[sr_b])
                    S.op("dve", I("tensor_tensor", out=rt[:, 0, :, 0:n], in0=Sim, in1=cT, op=ALU.mult), reads=rb, writes=[rt_b])
                    S.op("dve", I("tensor_tensor", out=rt[:, 1, :, 0:n], in0=Sre, in1=sT, op=ALU.mult), reads=rb, writes=[rt_b])
                    S.op("pool", I("tensor_tensor", out=oim, in0=rt[:, 0, :, 0:n], in1=rt[:, 1, :, 0:n], op=ALU.subtract), reads=[rt_b], writes=[sr_b])
                xb, xbb, xbs = XBr.next()
                S.dma("sp", xbs, [I("dma_start", out=xb[:], in_=G.S5XB[idx])], writes=[xbb])
                XBD = xb[:].rearrange("p (r q n) -> p r q n", r=2, q=32)
                load_w(idx + 1)
                for j in range(8):
                    src_ = Srot[:, j, :] if d == 0 else Srot[:, j, ::-1]
                    S.op("dve", I("tensor_tensor_scan", out=Gs[:, j, :], data0=rho[:, j % 4:j % 4 + 1].broadcast_to([128, NCH]),
                                  data1=src_, initial=0.0, op0=ALU.mult, op1=ALU.add), reads=[sr_b, rho_b], writes=[gs_b])
                S.op("pool", I("memset", ap=Ebf[d][:, :, 0:1], constant=0.0), writes=[e_b[d]])
                Gre, Gim = Gs[:, 0:4, :], Gs[:, 4:8, :]
                p0, p1 = Srot[:, 0:4, :], Srot[:, 4:8, :]
                S.op("dve", I("tensor_tensor", out=p0, in0=Gre, in1=cosT, op=ALU.mult), reads=[gs_b, tbl_b, sr_b], writes=[sr_b])
                S.op("pool", I("tensor_tensor", out=p1, in0=Gim, in1=sinT, op=ALU.mult), reads=[gs_b, tbl_b, sr_b], writes=[sr_b])
                S.op("dve", I("tensor_tensor", out=Ebf[d][:, 0:4, 1:NCH + 1], in0=p0, in1=p1, op=ALU.subtract),
                     reads=[sr_b], writes=[e_b[d]])
                S.op("dve", I("tensor_tensor", out=p0, in0=Gim, in1=cosT, op=ALU.mult), reads=[gs_b, tbl_b, sr_b], writes=[sr_b])
                S.op("pool", I("tensor_tensor", out=p1, in0=Gre, in1=sinT, op=ALU.mult), reads=[gs_b, tbl_b, sr_b], writes=[sr_b])
                S.op("dve", I("tensor_tensor", out=Ebf[d][:, 4:8, 1:NCH + 1], in0=p0, in1=p1, op=ALU.add),
                     reads=[sr_b], writes=[e_b[d]])
                for k in range(8):
                    pt, pb, _ = G.PS.next()
                    mm = []
                    for ri in range(2):
                        for jc in range(4):
                            mm.append(I("matmul", out=pt[:, 0:128], lhsT=XBD[:, ri, jc * 8 + k, :], rhs=CBD[:, ri, jc, :],
                                        start=(ri == 0 and jc == 0), stop=(ri == 1 and jc == 3)))
                    S.op("pe", mm, reads=[xbb, cbd_b], writes=[pb])
                    if k == 0 and d == 0:
                        S.op("dve", I("tensor_copy", out=K0[:], in_=pt[:, 0:128]), reads=[pb], writes=[k_b])
                    elif k == 0:
                        S.op("dve", I("tensor_tensor", out=Kf[:, 0, :], in0=pt[:, 0:128], in1=K0[:], op=ALU.add), reads=[pb, k_b], writes=[k_b])
                    else:
                        S.op("act", I("activation", out=Kf[:, d * 8 + k, :], in_=pt[:, 0:128], func=AF.Copy), reads=[pb], writes=[k_b])

            W3bd = [W3t[d][:].rearrange("p (j s n) -> p j s n", j=8, s=8) for d in range(2)]
            for ri_, (c0, n) in enumerate(RANGES):
                b0 = (ri_ % 2) * 4
                for s in range(8):
                    off = b0 * 512 + s * 256
                    mm = []
                    for s2 in range(8):
                        kk = (s - s2) if s2 <= s else 8 + (s2 - s)
                        mm.append(I("matmul", out=ps[:, off:off + n], lhsT=Kf[:, kk, :], rhs=Ub[:, c0 * 8 + s2:(c0 + n) * 8:8],
                                    start=(s2 == 0), stop=False))
                    cp = c0 + 32 if c0 < 512 else 0
                    for j in range(8):
                        mm.append(I("matmul", out=ps[:, off:off + n], lhsT=W3bd[0][:, j, s, :], rhs=Ebf[0][:, j, cp:cp + n],
                                    start=False, stop=False))
                    lo = NCH - 1 - (c0 + n - 1)
                    for j in range(8):
                        mm.append(I("matmul", out=ps[:, off:off + n], lhsT=W3bd[1][:, j, s, :], rhs=Ebf[1][:, j, lo:lo + n][:, ::-1],
                                    start=False, stop=(j == 7)))
                    S.op("pe", mm, reads=[k_b, Ub_b, w3_b[0], w3_b[1], e_b[0], e_b[1]], writes=psb[b0:b0 + 4])
                pv = ps[:, b0 * 512:(b0 + 4) * 512].rearrange("p (s c) -> p s c", c=256)[:, :, 0:n]
                S.op("dve", I("scalar_tensor_tensor", out=yt[:, c0 * 8:(c0 + n) * 8].rearrange("p (c s) -> p s c", s=8),
                              in0=Ut[:, c0 * 8:(c0 + n) * 8].rearrange("p (c s) -> p s c", s=8), scalar=misc[:, gt:gt + 1],
                              in1=pv, op0=ALU.mult, op1=ALU.add), reads=psb[b0:b0 + 4] + [U_b, cst_b], writes=[y_b])
            S.op("act", I("activation", out=gst[:, :], in_=yt[:, :], func=AF.Gelu), reads=[y_b], writes=[gst_b])
            S.dma("pool", gsem, [I("dma_start", out=G.GS5[gt * 128:(gt + 1) * 128, :], in_=gst[:, :])],
                  reads=[gst_b], writes=[scr_buf(G, "GS5", gt)])
            if "S5Y" in G.dbg:
                dsem = S.new_dma_sem("dbg")
                S.dma("sp", dsem, [I("dma_start", out=G.dbg["S5Y"][gt * 128:(gt + 1) * 128, :], in_=yt[:, :])], reads=[y_b])

        S.barrier()
        S.release_to(mk)
    mk = S.mark()
    with ExitStack() as p5:
        sbt = lambda name, shape, dt: p5.enter_context(nc.sbuf_tensor(U(name), list(shape), dt))
        misc = sbt("s5misc2", [128, 8], F32)
        cst_b = Buf("s5cst2")
        csem = S.new_dma_sem("s5c2")
        S.dma("sp", csem, [I("dma_start", out=misc[:], in_=G.s5misc[l, :, :])], writes=[cst_b])
        wgf = sbt("wgf", [128, 4, 512], F32)
        wgb = sbt("wgb", [128, 4, 512], BF16); wg_b = Buf("wg")
        wsem = S.new_dma_sem("wg")
        S.dma("sp", wsem, [I("dma_start", out=wgf[:], in_=G.w_glu[l].rearrange("(c p) n -> p c n", p=128))], writes=[wg_b])
        S.op("pool", I("tensor_copy", out=wgb[:], in_=wgf[:]), reads=[wg_b], writes=[wg_b])
        ZSr = Ring(S, p5, "zs", 2, [128, 512], BF16, dma=True)
        SGr = Ring(S, p5, "sg5", 2, [128, 512], BF16)
        YOr = Ring(S, p5, "yo5", 2, [128, 512], BF16, dma=True, fresh=True)
        GTr = Ring(S, p5, "gtr", 2, [128, 4, 512], BF16, dma=True)
        for (t0, tn) in TG:
            gT, gTb, gTs = GTr.next()
            S.dma("sp", gTs, [I("dma_start", out=gT[:, :, 0:tn], in_=G.GS5[:, t0:t0 + tn].rearrange("(c p) n -> p c n", p=128))],
                  reads=[scr_buf(G, "GS5", i) for i in range(4)], writes=[gTb])
            for mo in range(4):
                zs, zsb, zss = ZSr.next()
                S.dma("sp", zss, [I("dma_start", out=zs[:, 0:tn], in_=G.ZS[mo * 128:(mo + 1) * 128, t0:t0 + tn])],
                      reads=[scr_buf(G, "ZS", mo)], writes=[zsb])
                pt, pb, _ = G.PS.next()
                S.op("pe", [I("matmul", out=pt[:, 0:tn], lhsT=wgb[:, kc, mo * 128:(mo + 1) * 128], rhs=gT[:, kc, 0:tn],
                              start=(kc == 0), stop=(kc == 3)) for kc in range(4)], reads=[wg_b, gTb], writes=[pb])
                sg, sgb, _ = SGr.next()
                S.op("act", I("activation", out=sg[:, 0:tn], in_=pt[:, 0:tn], func=AF.Sigmoid, bias=misc[:, 4 + mo:5 + mo]),
                     reads=[pb, cst_b], writes=[sgb])
                S.op("pool", I("tensor_tensor", out=sg[:, 0:tn], in0=sg[:, 0:tn], in1=zs[:, 0:tn], op=ALU.mult),
                     reads=[sgb, zsb], writes=[sgb])
                yo, yob, yos = YOr.next()
                S.op("dve", I("tensor_tensor", out=yo[:, 0:tn], in0=sg[:, 0:tn], in1=gT[:, mo, 0:tn], op=ALU.mult),
                     reads=[sgb, gTb], writes=[yob])
                S.dma("pool", yos, [I("dma_start", out=G.YG[2, mo * 128:(mo + 1) * 128, t0:t0 + tn], in_=yo[:, 0:tn])],
                      reads=[yob], writes=[scr_buf(G, "YG2", mo)])
        if "YG" in G.dbg:
            S.barrier()
            dsem = S.new_dma_sem("dbg")
            S.dma("sp", dsem, [I("dma_start", out=G.dbg["YG"][:, :, :], in_=G.YG[:, :, :])])
        S.barrier()
        S.release_to(mk)


def phase6_merge(G, l):
    nc, S = G.nc, G.S
    ps, psb = G.psall, G.psb
    last = (l == DEPTH - 1)
    mk = S.mark()
    with ExitStack() as p6:
        sbt = lambda name, shape, dt: p6.enter_context(nc.sbuf_tensor(U(name), list(shape), dt))
        wbr = sbt("wbr", [128, 3, 4, D], BF16)
        wou = sbt("wou", [128, 8, D], BF16)
        w_b = Buf("w6")
        STG = Ring(S, p6, "stg6", 2, [128, 4, D], F32, dma=True)
        for n in range(3):
            st, stb, sts = STG.next()
            S.dma("sp", sts, [I("dma_start", out=st[:], in_=G.w_branch[l, n].rearrange("(c p) n -> p c n", p=128))], writes=[stb])
            S.op("pool", I("tensor_copy", out=wbr[:, n], in_=st[:]), reads=[stb], writes=[w_b])
        for hf in range(2):
            st, stb, sts = STG.next()
            S.dma("sp", sts, [I("dma_start", out=st[:], in_=G.w_out[l, hf * 512:(hf + 1) * 512, :].rearrange("(c p) n -> p c n", p=128))],
                  writes=[stb])
            S.op("pool", I("tensor_copy", out=wou[:, hf * 4:(hf + 1) * 4], in_=st[:]), reads=[stb], writes=[w_b])
        if last:
            fgr = sbt("fgr", [1, D], F32)
            fgbc = sbt("fgbc", [128, D], F32)
            fg_b = Buf("fg")
            fsem = S.new_dma_sem("fg")
            S.dma("sp", fsem, [I("dma_start", out=fgr[:], in_=G.final_g[:, :])], writes=[fg_b])
            for n in range(2):
                pt, pb, _ = G.PS.next()
                S.op("pe", I("matmul", out=pt[:, :], lhsT=G.ones_f[0:1, :], rhs=fgr[0:1, n * 512:(n + 1) * 512], start=True, stop=True),
                     reads=[fg_b, G.b_const], writes=[pb])
                S.op("dve", I("tensor_copy", out=fgbc[:, n * 512:(n + 1) * 512], in_=pt[:, :]), reads=[pb], writes=[fg_b])
            junk = sbt("junk6", [128, D], BF16); junk_b = Buf("junk6")
            st4 = Ring(S, p6, "st6", 4, [128, 4], F32)
        YGr = Ring(S, p6, "yg6", 2, [128, 3, 4, 512], BF16, dma=True)
        SGr = Ring(S, p6, "sg6", 3, [128, 3, 512], BF16, dma=True)
        MG = Ring(S, p6, "mg6", 2, [128, 8, 512], BF16)
        TM = Ring(S, p6, "tm6", 2, [128, 3, 512], F32)
        XR_ = Ring(S, p6, "x6", 3, [128, D], F32, dma=True)
        XO = Ring(S, p6, "xo6", 2, [128, D], F32, dma=True, fresh=True)
        groups = TG[:8] if last else TG
        for (t0, tn) in groups:
            v = 0 if t0 < SEQ else 1
            yg, ygb, ygs = YGr.next()
            S.dma("sp", ygs, [I("dma_start", out=yg[:, n, :, 0:tn], in_=G.YG[n, :, t0:t0 + tn].rearrange("(c p) t -> p c t", p=128))
                              for n in range(3)],
                  reads=[scr_buf(G, "YG%d" % n, i) for n in range(3) for i in range(4)], writes=[ygb])
            mg, mgb, _ = MG.next()
            for dc in range(8):
                sg, sgb, sgs = SGr.next()
                S.dma("sp", sgs, [I("dma_start", out=sg[:, :, 0:tn],
                                    in_=G.SG[:, t0:t0 + tn].rearrange("(n c p) t -> c p n t", n=3, c=8)[dc])],
                      reads=[scr_buf(G, "SG", n * 8 + dc) for n in range(3)], writes=[sgb])
                pts = []
                for n in range(3):
                    pt, pb, _ = G.PS.next()
                    S.op("pe", [I("matmul", out=pt[:, 0:tn], lhsT=wbr[:, n, kc, dc * 128:(dc + 1) * 128], rhs=yg[:, n, kc, 0:tn],
                                  start=(kc == 0), stop=(kc == 3)) for kc in range(4)], reads=[w_b, ygb], writes=[pb])
                    pts.append((pt, pb))
                tm, tmb, _ = TM.next()
                for n in range(3):
                    S.op("dve", I("tensor_tensor", out=tm[:, n, 0:tn], in0=pts[n][0][:, 0:tn], in1=sg[:, n, 0:tn], op=ALU.mult),
                         reads=[pts[n][1], sgb], writes=[tmb])
                S.op("pool", I("tensor_tensor", out=tm[:, 0, 0:tn], in0=tm[:, 0, 0:tn], in1=tm[:, 1, 0:tn], op=ALU.add),
                     reads=[tmb], writes=[tmb])
                S.op("pool", I("tensor_tensor", out=mg[:, dc, 0:tn], in0=tm[:, 0, 0:tn], in1=tm[:, 2, 0:tn], op=ALU.add),
                     reads=[tmb], writes=[mgb])
            for tt in range(tn // 128):
                ti = t0 // 128 + tt
                xt, xb, xs = XR_.next()
                S.dma("sp", xs, [I("dma_start", out=xt[:], in_=x_src(G, l, ti))], reads=[G.xres_b[ti]], writes=[xb])
                xo, xob, xos = XO.next()
                for n in range(2):
                    pt, pb, _ = G.PS.next()
                    S.op("pe", [I("matmul", out=pt[:, :], lhsT=mg[:, dc, tt * 128:(tt + 1) * 128], rhs=wou[:, dc, n * 512:(n + 1) * 512],
                                  start=(dc == 0), stop=(dc == 7)) for dc in range(8)], reads=[w_b, mgb], writes=[pb])
                    S.op("dve", I("tensor_tensor", out=xo[:, n * 512:(n + 1) * 512], in0=pt[:, :],
                                  in1=G.gtbc[:, v, n * 512:(n + 1) * 512], op=ALU.mult), reads=[pb, G.gtbc_b], writes=[xob])
                S.op("pool", I("tensor_tensor", out=xo[:, :], in0=xo[:, :], in1=xt[:, :], op=ALU.add), reads=[xob, xb], writes=[xob])
                if not last:
                    S.dma("pool", xos, [I("dma_start", out=G.xres[ti * 128:(ti + 1) * 128, :], in_=xo[:, :])],
                          reads=[xob], writes=[G.xres_b[ti]])
                else:
                    st, stb, _ = st4.next()
                    S.op("act", I("activation", out=junk[:], in_=xo[:], func=AF.Square, accum_out=st[:, 0:1]),
                         reads=[xob], writes=[junk_b, stb])
                    S.op("act", I("activation", out=st[:, 1:2], in_=st[:, 0:1], func=AF.Sqrt, scale=1.0 / D, bias=G.eps_col[:, 0:1]),
                         reads=[stb, G.b_const], writes=[stb])
                    S.op("dve", I("reciprocal", out=st[:, 2:3], in_=st[:, 1:2]), reads=[stb], writes=[stb])
                    S.op("dve", I("scalar_tensor_tensor", out=xo[:, :], in0=xo[:, :], scalar=st[:, 2:3], in1=fgbc[:, :],
                                  op0=ALU.mult, op1=ALU.mult), reads=[xob, stb, fg_b], writes=[xob])
                    S.dma("pool", xos, [I("dma_start", out=G.out[ti * 128:(ti + 1) * 128, :], in_=xo[:, :])],
                          reads=[xob], writes=[G.xres_b[ti]])
        if "XRES" in G.dbg:
            S.barrier()
            dsem = S.new_dma_sem("dbg")
            S.dma("sp", dsem, [I("dma_start", out=G.dbg["XRES"][:, :], in_=G.xres[:, :])])
        S.barrier()
        S.release_to(mk)


def scr_buf(G, name, i):
    return Buf("scr")


def x_src(G, l, ti):
    if l == 0:
        if ti < 32:
            return G.x_in[ti * 128:(ti + 1) * 128, :]
        return G.ctx_in[(ti - 32) * 128:(ti - 31) * 128, :]
    return G.xres[ti * 128:(ti + 1) * 128, :]


def phase1_and_2(G, l, stop_after=None):
    nc, S, PS = G.nc, G.S, G.PS
    with ExitStack() as ph:
        psb = lambda name, shape, dt: ph.enter_context(nc.sbuf_tensor(U(name), list(shape), dt))
        hT = psb("hT", [128, 8, T], BF16)
        hT_b = [Buf("hT%d" % i) for i in range(NT)]
        mcols = psb("mcols", [128, 32], F32)
        gs = psb("gs", [128, 16], F32)
        mc_b = Buf("mcols")
        mk = S.mark()
        with ExitStack() as p1:
            p1sb = lambda name, shape, dt: p1.enter_context(nc.sbuf_tensor(U(name), list(shape), dt))
            mrow = [p1sb("mrow%d" % v, [1, 3 * D], F32) for v in range(2)]
            mrow_b = [Buf("mrow%d" % v) for v in range(2)]
            brow = p1sb("brow", [1, 3 * D], F32)
            brow_b = Buf("brow")
            gcol_sb = p1sb("gcol_sb", [128, 8], F32)
            WM = Ring(S, p1, "wm", 2, [128, 8, 512], F32, dma=True)
            bsem = S.new_dma_sem("brow")
            S.dma("sp", bsem, [I("dma_start", out=brow[:], in_=G.b_mod[l, :, :]),
                               I("dma_start", out=gcol_sb[:], in_=G.gcol[l, :, :])], writes=[brow_b])
            for n in range(6):
                wt, wb, ws = WM.next()
                S.dma("sp", ws, [I("dma_start", out=wt[:],
                                   in_=G.w_mod[l, :, n * 512:(n + 1) * 512].rearrange("(c p) n -> p c n", p=128))],
                      writes=[wb])
                for v in range(2):
                    pt, pb, _ = PS.next()
                    S.op("pe", [I("matmul", out=pt[0:1, :], lhsT=G.silu_c[:, v * 8 + c:v * 8 + c + 1], rhs=wt[:, c, :],
                                  start=(c == 0), stop=(c == 7)) for c in range(8)],
                         reads=[wb, G.b_const], writes=[pb])
                    S.op("dve", I("tensor_tensor", out=mrow[v][0:1, n * 512:(n + 1) * 512], in0=pt[0:1, :],
                                  in1=brow[0:1, n * 512:(n + 1) * 512], op=ALU.add),
                         reads=[pb, brow_b], writes=[mrow_b[v]])
            pt, pb, _ = PS.next()
            mm = []
            for v in range(2):
                for w in range(2):
                    for c in range(8):
                        j = v * 16 + w * 8 + c
                        mm.append(I("matmul", out=pt[:, j:j + 1],
                                    lhsT=mrow[v][0:1, w * 1024 + c * 128: w * 1024 + (c + 1) * 128],
                                    rhs=G.ones_f[0:1, 0:1], start=True, stop=True))
            S.op("pe", mm, reads=[mrow_b[0], mrow_b[1], G.b_const], writes=[pb])
            S.op("dve", I("tensor_copy", out=mcols[:], in_=pt[:, 0:32]), reads=[pb], writes=[mc_b])
            for v in range(2):
                S.op("dve", I("scalar_tensor_tensor", out=gs[:, v * 8:(v + 1) * 8], in0=mcols[:, v * 16 + 8:v * 16 + 16],
                              scalar=1.0, in1=gcol_sb[:], op0=ALU.add, op1=ALU.mult),
                     reads=[mc_b, brow_b], writes=[mc_b])
            for v in range(2):
                for n in range(2):
                    pt, pb, _ = PS.next()
                    S.op("pe", I("matmul", out=pt[:, :], lhsT=G.ones_f[0:1, :],
                                 rhs=mrow[v][0:1, 2048 + n * 512:2048 + (n + 1) * 512], start=True, stop=True),
                         reads=[mrow_b[v], G.b_const], writes=[pb])
                    S.op("dve", I("tensor_copy", out=G.gtbc[:, v, n * 512:(n + 1) * 512], in_=pt[:, :]),
                         reads=[pb], writes=[G.gtbc_b])
            if "mrow" in G.dbg:
                dsem = S.new_dma_sem("dbg")
                S.dma("sp", dsem, [I("dma_start", out=G.dbg["mrow"][0:1, :], in_=mrow[0][:]),
                                   I("dma_start", out=G.dbg["mrow"][1:2, :], in_=mrow[1][:])], reads=mrow_b)
            S.barrier()
            S.release_to(mk)

        with ExitStack() as p1:
            XT = Ring(S, p1, "xt", 3, [128, D], F32, dma=True)
            XN = Ring(S, p1, "xn", 2, [128, D], BF16)
            junk = p1.enter_context(nc.sbuf_tensor(U("junk"), [128, D], BF16))
            junk_b = Buf("junk")
            st4 = Ring(S, p1, "st", 4, [128, 4], F32)
            for ti in range(NT):
                v = 0 if ti < 32 else 1
                xt, xb, xs = XT.next()
                S.dma("sp", xs, [I("dma_start", out=xt[:], in_=x_src(G, l, ti))], reads=[G.xres_b[ti]], writes=[xb])
                st, stb, _ = st4.next()
                S.op("act", I("activation", out=junk[:], in_=xt[:], func=AF.Square, accum_out=st[:, 0:1]),
                     reads=[xb], writes=[junk_b, stb])
                S.op("act", I("activation", out=st[:, 1:2], in_=st[:, 0:1], func=AF.Sqrt, scale=1.0 / D, bias=G.eps_col[:, 0:1]),
                     reads=[stb, G.b_const], writes=[stb])
                S.op("dve", I("reciprocal", out=st[:, 2:3], in_=st[:, 1:2]), reads=[stb], writes=[stb])
                xn, xnb, _ = XN.next()
                S.op("dve", I("tensor_scalar", out=xn[:], in0=xt[:], scalar1=st[:, 2:3], scalar2=None, op0=ALU.mult),
                     reads=[xb, stb], writes=[xnb])
                for half in range(2):
                    pt, pb, _ = PS.next()
                    ptb = pt.bitcast(BF16)
                    S.op("pe", [I("transpose", out=ptb[:, cc * 128:(cc + 1) * 128],
                                  in_=xn[:, (half * 4 + cc) * 128:(half * 4 + cc + 1) * 128], identity=G.ident[:])
                                for cc in range(4)], reads=[xnb, G.b_const], writes=[pb])
                    for cc in range(4):
                        c = half * 4 + cc
                        if cc % 2 == 1:
                            S.op("dve", I("tensor_scalar", out=hT[:, c, ti * 128:(ti + 1) * 128],
                                          in0=ptb[:, cc * 128:(cc + 1) * 128],
                                          scalar1=gs[:, v * 8 + c:v * 8 + c + 1],
                                          scalar2=mcols[:, v * 16 + c:v * 16 + c + 1], op0=ALU.mult, op1=ALU.add),
                                 reads=[pb, mc_b], writes=[hT_b[ti]])
                        else:
                            S.op("act", I("activation", out=hT[:, c, ti * 128:(ti + 1) * 128],
                                          in_=ptb[:, cc * 128:(cc + 1) * 128], func=AF.Identity,
                                          scale=gs[:, v * 8 + c:v * 8 + c + 1],
                                          bias=mcols[:, v * 16 + c:v * 16 + c + 1]),
                                 reads=[pb, mc_b], writes=[hT_b[ti]])
            if "hT" in G.dbg:
                dsem = S.new_dma_sem("dbg")
                S.dma("sp", dsem, [I("dma_start", out=G.dbg["hT"][:, :, :], in_=hT[:])], reads=hT_b)
            S.barrier()
            S.release_to(mk)
        if stop_after == "p1":
            return

        mk = S.mark()
        with ExitStack() as p2:
            WF = Ring(S, p2, "wf", 2, [128, 8, 512], F32, dma=True)
            WB = Ring(S, p2, "wb", 4, [128, 8, 512], BF16)
            RC = Ring(S, p2, "rc", 2, [128, 2, 512], F32, dma=True)
            OB = Ring(S, p2, "ob", 8, [128, 512], BF16, dma=True)
            OF = Ring(S, p2, "of", 6, [128, 512], F32, dma=True)
            TMP = Ring(S, p2, "tmp", 2, [128, 2, 512], F32)

            def load_group(g):
                wf, wfb, wfs = WF.next()
                S.dma("sp", wfs, [I("dma_start", out=wf[:], in_=G.w_in[l, :, g * 512:(g + 1) * 512]
                                    .rearrange("(c p) n -> p c n", p=128))], writes=[wfb])
                wb, wbb, _ = WB.next()
                S.op("pool", I("tensor_copy", out=wb[:, 0:4, :], in_=wf[:, 0:4, :]), reads=[wfb], writes=[wbb])
                S.op("pool", I("tensor_copy", out=wb[:, 4:8, :], in_=wf[:, 4:8, :]), reads=[wfb], writes=[wbb])
                return wb, wbb

            def proj(pt, pb, wb, wbb, j, t0, tn):
                tis = list(range(t0 // 128, (t0 + tn) // 128))
                S.op("pe", [I("matmul", out=pt[:, 0:tn], lhsT=wb[:, c, j * 128:(j + 1) * 128], rhs=hT[:, c, t0:t0 + tn],
                              start=(c == 0), stop=(c == 7)) for c in range(8)],
                     reads=[wbb] + [hT_b[i] for i in tis], writes=[pb])

            wq = [load_group(g) for g in range(4)]
            for (t0, tn) in TG:
                rc, rcb, rcs = RC.next()
                S.dma("sp", rcs, [I("dma_start", out=rc[:, 0, 0:tn], in_=G.ropeC[:, t0:t0 + tn]),
                                  I("dma_start", out=rc[:, 1, 0:tn], in_=G.ropeS[:, t0:t0 + tn])], writes=[rcb])
                for qk in range(2):
                    dst = G.QT if qk == 0 else G.KT
                    for j in range(4):
                        pa, pab, _ = PS.next()
                        proj(pa, pab, wq[2 * qk][0], wq[2 * qk][1], j, t0, tn)
                        pbt, pbb, _ = PS.next()
                        proj(pbt, pbb, wq[2 * qk + 1][0], wq[2 * qk + 1][1], j, t0, tn)
                        tmp, tmpb, _ = TMP.next()
                        S.op("dve", I("tensor_tensor", out=tmp[:, 0, 0:tn], in0=pa[:, 0:tn], in1=rc[:, 0, 0:tn], op=ALU.mult),
                             reads=[pab, rcb], writes=[tmpb])
                        S.op("dve", I("tensor_tensor", out=tmp[:, 1, 0:tn], in0=pbt[:, 0:tn], in1=rc[:, 1, 0:tn], op=ALU.mult),
                             reads=[pbb, rcb], writes=[tmpb])
                        ob, obb, obs = OB.next()
                        S.op("pool", I("tensor_tensor", out=ob[:, 0:tn], in0=tmp[:, 0, 0:tn], in1=tmp[:, 1, 0:tn], op=ALU.add),
                             reads=[tmpb], writes=[obb])
                        S.dma("sp", obs, [I("dma_start", out=dst[j * 128:(j + 1) * 128, t0:t0 + tn], in_=ob[:, 0:tn])],
                              reads=[obb], writes=[scr_buf(G, "QT" if qk == 0 else "KT", j)])
            wv, wvb = load_group(4)
            for ti in range(NT):
                pt, pb, _ = PS.next()
                S.op("pe", [I("matmul", out=pt[:, :], lhsT=hT[:, c, ti * 128:(ti + 1) * 128], rhs=wv[:, c, :],
                              start=(c == 0), stop=(c == 7)) for c in range(8)],
                     reads=[wvb, hT_b[ti]], writes=[pb])
                ob, obb, obs = OB.next()
                S.op("act", I("activation", out=ob[:, :], in_=pt[:, :], func=AF.Copy), reads=[pb], writes=[obb])
                S.dma("sp", obs, [I("dma_start", out=G.Vtm[ti * 128:(ti + 1) * 128, :], in_=ob[:, :])],
                      reads=[obb], writes=[scr_buf(G, "V", ti)])
            plan = [(5, G.ZA, 0, "ZA", 0), (7, G.ZR, 0, "ZR", 0), (9, G.ZS, 0, "ZS", 0),
                    (6, G.XR, 1, "XR", 0), (8, G.US, 1, "US", 0)]
            plan += [(10 + i, G.SG, 2, "SG", i * 4) for i in range(6)]
            for (g, dst, kind, nm, boff) in plan:
                wg, wgb = load_group(g)
                for j in range(4):
                    for (t0, tn) in TG:
                        pt, pb, _ = PS.next()
                        proj(pt, pb, wg, wgb, j, t0, tn)
                        if kind == 1:
                            ob, obb, obs = OF.next()
                            S.op("dve", I("tensor_copy", out=ob[:, 0:tn], in_=pt[:, 0:tn]), reads=[pb], writes=[obb])
                        else:
                            ob, obb, obs = OB.next()
                            S.op("act", I("activation", out=ob[:, 0:tn], in_=pt[:, 0:tn],
                                          func=(AF.Silu if kind == 0 else AF.Sigmoid)), reads=[pb], writes=[obb])
                        r0 = (boff + j) * 128
                        S.dma("sp", obs, [I("dma_start", out=dst[r0:r0 + 128, t0:t0 + tn], in_=ob[:, 0:tn])],
                              reads=[obb], writes=[scr_buf(G, nm, boff + j)])
            if "QT" in G.dbg:
                S.barrier()
                dsem = S.new_dma_sem("dbg")
                for nm in ("QT", "KT", "Vtm", "ZA", "XR", "SG"):
                    if nm in G.dbg:
                        S.dma("sp", dsem, [I("dma_start", out=G.dbg[nm][:, :], in_=getattr(G, nm)[:, :])])
            S.barrier()
            S.release_to(mk)
        if stop_after == "p2":
            return


def _rope_tables():
    rows = SEQ // 64
    r = np.repeat(np.arange(rows, dtype=np.float32), 64)
    col = np.tile(np.arange(64, dtype=np.float32), rows)
    inv = (10000.0 ** (-np.arange(16, dtype=np.float32) / 16)).astype(np.float32)
    ang = np.concatenate([r[:, None] * inv, col[:, None] * inv], axis=-1).astype(np.float32)
    cos = np.cos(ang).T.astype(np.float32)
    sin = np.sin(ang).T.astype(np.float32)
    C = np.ones((128, T), np.float32)
    Sg = np.zeros((128, T), np.float32)
    for p in range(128):
        j = p % 64
        C[p, :SEQ] = cos[j % 32]
        Sg[p, :SEQ] = -sin[j % 32] if j < 32 else sin[j % 32]
    return C, Sg


def _w_in_ext(w_in):
    L = w_in.shape[0]
    perm = np.concatenate([np.arange(0, 64, 2), np.arange(1, 64, 2)])
    swp = np.concatenate([np.arange(1, 64, 2), np.arange(0, 64, 2)])
    idx = []
    for base in (0, 512):
        p_cols = np.concatenate([base + b * 64 + perm for b in range(8)])
        s_cols = np.concatenate([base + b * 64 + swp for b in range(8)])
        idx += [p_cols, s_cols]
    idx.append(np.arange(1024, 7168))
    idx = np.concatenate(idx)
    return np.ascontiguousarray(w_in[:, :, idx])


def make_in_maps(inputs):
    f = lambda a: np.ascontiguousarray(np.asarray(a, dtype=np.float32))
    x = f(inputs["x"]); ctx = f(inputs["ctx"]); c = f(inputs["c"]); c_ctx = f(inputs["c_ctx"])
    w_in_e = _w_in_ext(f(inputs["w_in"]))
    C, Sg = _rope_tables()
    ident = np.eye(128, dtype=np.float32).astype(ml_dtypes.bfloat16)
    gcol = np.ascontiguousarray(f(inputs["norm_g"]).reshape(DEPTH, 8, 128).transpose(0, 2, 1))
    shared = dict(
        w_mod=f(inputs["w_mod"]), b_mod=f(inputs["b_mod"]).reshape(DEPTH, 1, 3 * D), gcol=gcol, w_in=w_in_e,
        ropeC=C, ropeS=Sg, ident=ident, final_g=f(inputs["final_g"]).reshape(1, D),
        lam_qk=f(inputs["lam_qk"]).reshape(DEPTH, 1, 256), subln=f(inputs["subln_g"]).reshape(DEPTH, 128, 1),
    )
    lrup = np.zeros((DEPTH, 128, 44), np.float32)
    cw = f(inputs["conv_w"]); cb = f(inputs["conv_b"])
    for ct in range(4):
        for k in range(4):
            lrup[:, :, ct * 4 + k] = cw[:, k, ct * 128:(ct + 1) * 128]
        lrup[:, :, 16 + ct] = cb[:, ct * 128:(ct + 1) * 128]
        for d in range(2):
            lrup[:, :, 20 + d * 4 + ct] = f(inputs["lru_ba"])[:, d, ct * 128:(ct + 1) * 128]
            lrup[:, :, 28 + d * 4 + ct] = f(inputs["lru_bx"])[:, d, ct * 128:(ct + 1) * 128]
            lrup[:, :, 36 + d * 4 + ct] = f(inputs["lru_lam"])[:, d, ct * 128:(ct + 1) * 128]
    lruw = np.zeros((DEPTH, 2, 2, 4, 128, 128), np.float32)
    for gi, nm in enumerate(("lru_wa", "lru_wx")):
        w = f(inputs[nm])
        for ct in range(4):
            for j in range(2):
                lruw[:, gi, :, ct, j * 64:(j + 1) * 64, j * 64:(j + 1) * 64] = w[:, :, 2 * ct + j]
    shared.update(lrup=lrup, lruw=lruw)
    lre = f(inputs["s5_lam_re"]); lim = f(inputs["s5_lam_im"]); ldt = f(inputs["s5_log_dt"])
    bre = f(inputs["s5_b_re"]); bim = f(inputs["s5_b_im"]); cre = f(inputs["s5_c_re"]); cim = f(inputs["s5_c_im"])
    L = DEPTH
    s5H = np.zeros((L, 2, 4, 128, 4, 64), np.float32)
    lre_g = lre.reshape(L, 2, 4, 8, 64); lim_g = lim.reshape(L, 2, 4, 8, 64)
    s5H[:, :, :, :, 0, :] = np.repeat(lre_g, 16, axis=3)
    s5H[:, :, :, :, 1, :] = np.repeat(lim_g, 16, axis=3)
    s5H[:, :, :, :, 2, :] = bre.reshape(L, 2, 4, 8, 64, 16).transpose(0, 1, 2, 3, 5, 4).reshape(L, 2, 4, 128, 64)
    s5H[:, :, :, :, 3, :] = bim.reshape(L, 2, 4, 8, 64, 16).transpose(0, 1, 2, 3, 5, 4).reshape(L, 2, 4, 128, 64)
    def nlay(a):
        return a.reshape(L, 2, 4, 8, 4, 16).transpose(0, 1, 2, 3, 5, 4).reshape(L, 2, 4, 128, 4)
    s5N = np.concatenate([nlay(lre_g), nlay(lim_g)], axis=-1)
    def bnlay(a):
        return a.reshape(L, 2, 4, 8, 4, 16, 16).transpose(0, 1, 2, 3, 5, 4, 6).reshape(L, 2, 4, 128, 4, 16)
    s5BN = np.stack([bnlay(bre), bnlay(bim)], axis=4)
    def cnlay(a):
        return a.reshape(L, 4, 8, 16, 4, 16).transpose(0, 1, 2, 5, 4, 3).reshape(L, 4, 128, 4, 16)
    s5CN = np.stack([cnlay(cre), cnlay(cim)], axis=3)
    s5dt = np.zeros((L, 128, 8), np.float32)
    for d in range(2):
        for gt in range(4):
            s5dt[:, :, d * 4 + gt] = np.repeat(ldt[:, d, gt * 8:(gt + 1) * 8], 16, axis=1)
    s5misc = np.zeros((L, 128, 8), np.float32)
    s5misc[:, :, 0:4] = f(inputs["s5_d"]).reshape(L, 4, 128).transpose(0, 2, 1)
    s5misc[:, :, 4:8] = f(inputs["s5_b_glu"]).reshape(L, 4, 128).transpose(0, 2, 1)
    tabs = np.zeros((128, 36 + 512 + 544 + 128), np.float32)
    tabs[:, 0:36] = np.tile(np.arange(9, dtype=np.float32), 4)[None, :]
    tabs[:, 36:548] = np.repeat(np.arange(8, dtype=np.float32), 64)[None, :]
    tabs[:, 548:1092] = np.arange(544, dtype=np.float32)[None, :]
    mk = np.zeros((128, 8, 16), np.float32)
    for p in range(128):
        mk[p, p // 16, :] = 1.0
    tabs[:, 1092:1220] = mk.reshape(128, 128)
    shared.update(s5H=s5H, s5N=np.ascontiguousarray(s5N), s5BN=np.ascontiguousarray(s5BN), s5CN=np.ascontiguousarray(s5CN),
                  s5dt=s5dt, s5misc=s5misc, s5tab=tabs, w_glu=f(inputs["s5_w_glu"]),
                  w_branch=f(inputs["w_branch"]), w_out=f(inputs["w_out"]))
    maps = []
    for b in range(8):
        cc = np.concatenate([c[b].reshape(8, 128).T, c_ctx.reshape(8, 128).T], axis=1)
        m = dict(shared)
        m.update(x=x[b], ctx=ctx[b], ccol=np.ascontiguousarray(cc))
        maps.append(m)
    return maps


def kernel(**inputs):
    nc = build_program()
    maps = make_in_maps(inputs)
    res = run_bass_kernel_spmd(nc, maps, core_ids=list(range(8)))
    return np.stack([np.asarray(r["out"], dtype=np.float32) for r in res.results], axis=0)
```

```python
import math
from contextlib import ExitStack

import numpy as np
import ml_dtypes

import concourse.bass as bass
import concourse.mybir as mybir
from concourse.bass_utils import run_bass_kernel_spmd

F32 = mybir.dt.float32
BF16 = mybir.dt.bfloat16
ALU = mybir.AluOpType
AF = mybir.ActivationFunctionType
AX = mybir.AxisListType

D = 1024
SEQ = 4096
CTX = 256
T = SEQ + CTX
NT = T // 128
DEPTH = 4
EPS = 1e-6
NCOL = 8192
TG = [(i * 512, 512) for i in range(8)] + [(4096, 256)]


class Buf:
    __slots__ = ("name", "lw", "rd")

    def __init__(self, name=""):
        self.name = name
        self.lw = None
        self.rd = {}


class Sched:
    ENG = ("pe", "act", "dve", "pool", "sp")

    def __init__(self, nc, stack):
        self.nc = nc
        self.stack = stack
        self.streams = {e: [] for e in self.ENG}
        self.sem = {}
        self.cnt = {}
        self.seen = {e: {} for e in self.ENG}
        for e in self.ENG:
            self.sem[e] = stack.enter_context(nc.semaphore("s_" + e))
            self.cnt[e] = 0
        self.ndma = 0
        self.free = []
        self.live = []

    def new_dma_sem(self, name, fresh=False):
        if fresh:
            key = U("f%d" % self.ndma)
            self.ndma += 1
            self.sem[key] = self.stack.enter_context(self.nc.semaphore(key))
            self.cnt[key] = 0
            return key
        if self.free:
            key = self.free.pop()
        else:
            key = U("d%d" % self.ndma)
            self.ndma += 1
            self.sem[key] = self.stack.enter_context(self.nc.semaphore(key))
            self.cnt[key] = 0
        self.live.append(key)
        return key

    def mark(self):
        return len(self.live)

    def release_to(self, mark):
        while len(self.live) > mark:
            self.free.append(self.live.pop())

    def _waits(self, eng, reads, writes):
        w = {}

        def add(tok):
            if tok is None:
                return
            k, v = tok
            if w.get(k, 0) < v:
                w[k] = v

        for b in reads:
            add(b.lw)
        for b in writes:
            add(b.lw)
            for k, v in b.rd.items():
                add((k, v))
        need = []
        seen = self.seen[eng]
        for k, v in w.items():
            if k == "pe" and eng == "pe":
                continue
            if seen.get(k, 0) < v:
                seen[k] = v
                need.append((k, v))
        return need

    def _commit(self, tok, reads, writes):
        for b in writes:
            b.lw = tok
            b.rd = {}
        k, v = tok
        for b in reads:
            if b.rd.get(k, 0) < v:
                b.rd[k] = v

    def op(self, eng, fn, reads=(), writes=()):
        if isinstance(fn, tuple):
            fn = [fn]
        need = self._waits(eng, reads, writes)
        self.cnt[eng] += 1
        tok = (eng, self.cnt[eng])
        self.streams[eng].append((need, fn, eng, 1))
        self._commit(tok, reads, writes)
        return tok

    def dma(self, eng, semkey, fns, reads=(), writes=()):
        need = self._waits(eng, reads, writes)
        for i, fn in enumerate(fns):
            self.cnt[semkey] += 16
            self.streams[eng].append((need if i == 0 else [], fn, semkey, 16))
        tok = (semkey, self.cnt[semkey])
        self._commit(tok, reads, writes)
        return tok

    def barrier(self):
        for e in self.ENG:
            need = []
            seen = self.seen[e]
            for k, v in self.cnt.items():
                if v > 0 and seen.get(k, 0) < v:
                    seen[k] = v
                    need.append((k, v))
            if need:
                self.streams[e].append((need, None, None, 0))

    def emit(self, block):
        nc = self.nc
        sems = self.sem

        def run(e, stream):
            for need, fn, semkey, inc in stream:
                for k, v in need:
                    e.wait_ge(sems[k], v)
                if fn is not None:
                    if isinstance(fn, tuple):
                        fn = [fn]
                    for m, kw in fn:
                        ins = getattr(e, m)(**kw)
                    ins.then_inc(sems[semkey], inc)

        @block.sync
        def _(e):
            run(e, self.streams["sp"])

        @block.tensor
        def _(e):
            run(e, self.streams["pe"])

        @block.scalar
        def _(e):
            run(e, self.streams["act"])

        @block.vector
        def _(e):
            run(e, self.streams["dve"])

        @block.gpsimd
        def _(e):
            run(e, self.streams["pool"])


class PsRing:
    def __init__(self, G):
        self.G = G
        self.i = 0

    def next(self):
        i = self.i
        self.i = (i + 1) % 8
        return self.G.psall[:, i * 512:(i + 1) * 512], self.G.psb[i], None


class Ring:
    def __init__(self, S, stack, name, n, shape, dtype, psum=False, dma=False, fresh=False):
        self.tiles = []
        self.bufs = []
        self.sems = []
        for i in range(n):
            nm = U("%s%d" % (name, i))
            if psum:
                t = stack.enter_context(S.nc.psum_tensor(nm, shape, dtype))
            else:
                t = stack.enter_context(S.nc.sbuf_tensor(nm, shape, dtype))
            self.tiles.append(t)
            self.bufs.append(Buf(nm))
            self.sems.append(S.new_dma_sem(nm, fresh=fresh) if dma else None)
        self.i = 0
        self.n = n

    def next(self):
        i = self.i
        self.i = (i + 1) % self.n
        return self.tiles[i], self.bufs[i], self.sems[i]


def I(m, **kw):
    return (m, kw)


_UID = [0]


def U(name):
    _UID[0] += 1
    return "%s_%d" % (name, _UID[0])


class Ctx:
    pass


def build_program(n_layers=DEPTH, debug=None, stop_after=None):
    nc = bass.Bass("TRN2", target_bir_lowering=False)
    G = Ctx()
    G.nc = nc

    def din(name, shape, dt=F32):
        return nc.dram_tensor(name, list(shape), dt, kind="ExternalInput").ap()

    def dscr(name, shape, dt):
        return nc.dram_tensor(name, list(shape), dt, kind="Internal").ap()

    G.x_in = din("x", [SEQ, D])
    G.ctx_in = din("ctx", [CTX, D])
    G.ccol = din("ccol", [128, 16])
    G.w_mod = din("w_mod", [DEPTH, D, 3 * D])
    G.b_mod = din("b_mod", [DEPTH, 1, 3 * D])
    G.gcol = din("gcol", [DEPTH, 128, 8])
    G.w_in = din("w_in", [DEPTH, D, NCOL])
    G.ropeC = din("ropeC", [128, T])
    G.ropeS = din("ropeS", [128, T])
    G.ident_in = din("ident", [128, 128], BF16)
    G.final_g = din("final_g", [1, D])
    G.lam_qk = din("lam_qk", [DEPTH, 1, 256])
    G.subln = din("subln", [DEPTH, 128, 1])
    G.lrup = din("lrup", [DEPTH, 128, 44])
    G.lruw = din("lruw", [DEPTH, 2, 2, 4, 128, 128])
    G.s5H = din("s5H", [DEPTH, 2, 4, 128, 4, 64])
    G.s5N = din("s5N", [DEPTH, 2, 4, 128, 8])
    G.s5BN = din("s5BN", [DEPTH, 2, 4, 128, 2, 4, 16])
    G.s5CN = din("s5CN", [DEPTH, 4, 128, 2, 4, 16])
    G.s5dt = din("s5dt", [DEPTH, 128, 8])
    G.s5misc = din("s5misc", [DEPTH, 128, 8])
    G.s5tab = din("s5tab", [128, 36 + 512 + 544 + 128])
    G.w_glu = din("w_glu", [DEPTH, 512, 512])
    G.w_branch = din("w_branch", [DEPTH, 3, 512, D])
    G.w_out = din("w_out", [DEPTH, D, D])
    G.out = nc.dram_tensor("out", [SEQ, D], F32, kind="ExternalOutput").ap()

    G.xres = dscr("xres", [T, D], F32)
    G.QT = dscr("QT", [512, T], BF16)
    G.KT = dscr("KT", [512, T], BF16)
    G.Vtm = dscr("Vtm", [T, 512], BF16)
    G.ZA = dscr("ZA", [512, T], BF16)
    G.ZR = dscr("ZR", [512, T], BF16)
    G.ZS = dscr("ZS", [512, T], BF16)
    G.XR = dscr("XR", [512, T], F32)
    G.US = dscr("US", [512, T], F32)
    G.SG = dscr("SG", [3072, T], BF16)
    G.GS5 = dscr("GS5", [512, T], BF16)
    G.S5W1 = dscr("S5W1", [8, 128, 8192], BF16)
    G.S5W3 = dscr("S5W3", [8, 128, 8192], BF16)
    G.S5XB = dscr("S5XB", [8, 128, 8192], BF16)
    G.S5TB = dscr("S5TB", [8, 128, 2, 4, T // 8], F32)
    G.S5RH = dscr("S5RH", [8, 128, 4], F32)
    G.s5_items = None
    G.YG = dscr("YG", [3, 512, T], BF16)

    G.dbg = {}
    if debug:
        for name, shape, dt in debug:
            G.dbg[name] = nc.dram_tensor("dbg_" + name, list(shape), dt, kind="ExternalOutput").ap()

    with ExitStack() as top:
        S = Sched(nc, top)
        G.S = S
        sb = lambda name, shape, dt: top.enter_context(nc.sbuf_tensor(U(name), list(shape), dt))

        G.ident = sb("ident", [128, 128], BF16)
        G.ones_f = sb("ones_f", [128, 128], F32)
        G.ones_b = sb("ones_b", [128, 128], BF16)
        G.ccol_sb = sb("ccol_sb", [128, 16], F32)
        G.silu_c = sb("silu_c", [128, 16], F32)
        G.eps_col = sb("eps_col", [128, 1], F32)
        G.b_const = Buf("const")
        csem = S.new_dma_sem("const")
        S.dma("sp", csem, [I("dma_start", out=G.ident[:], in_=G.ident_in[:, :]),
                           I("dma_start", out=G.ccol_sb[:], in_=G.ccol[:, :])], writes=[G.b_const])
        S.op("pool", I("memset", ap=G.ones_f[:], constant=1.0), writes=[G.b_const])
        S.op("pool", I("memset", ap=G.ones_b[:], constant=1.0), writes=[G.b_const])
        S.op("pool", I("memset", ap=G.eps_col[:], constant=EPS), writes=[G.b_const])
        S.op("act", I("activation", out=G.silu_c[:], in_=G.ccol_sb[:], func=AF.Silu),
             reads=[G.b_const], writes=[G.b_const])

        G.psall = top.enter_context(nc.psum_tensor(U("psall"), [128, 4096], F32))
        G.psb = [Buf("psb%d" % i) for i in range(8)]
        G.PS = PsRing(G)

        G.xres_b = [Buf("xres%d" % i) for i in range(NT)]
        G.scr_b = {}
        G.gtbc = sb("gtbc", [128, 2, D], F32)
        G.gtbc_b = Buf("gtbc")

        for l in range(n_layers):
            phase1_and_2(G, l, stop_after)
            if stop_after in ("p1", "p2"):
                break
            if stop_after not in ("p4only", "p5only"):
                phase3_attn(G, l)
            if stop_after == "p3":
                break
            if stop_after != "p5only":
                phase4_lru(G, l)
            if stop_after in ("p4", "p4only"):
                break
            phase5_s5(G, l)
            if stop_after in ("p5", "p5only"):
                break
            phase6_merge(G, l)
            if stop_after is not None:
                break

        S.barrier()
        with nc.Block() as block:
            S.emit(block)
    return nc


def phase3_attn(G, l):
    nc, S = G.nc, G.S
    lam_init = 0.8 - 0.6 * math.exp(-0.3 * l)
    need_ctx = l < DEPTH - 1
    ps = G.psall
    psb = G.psb
    mk = S.mark()
    with ExitStack() as p3:
        sbt = lambda name, shape, dt: p3.enter_context(nc.sbuf_tensor(U(name), list(shape), dt))
        KTs = sbt("KTs", [128, 4, T], BF16)
        KT_b = [Buf("KTs%d" % h) for h in range(4)]
        Vs = sbt("Vs", [128, NT, 512], BF16)
        V_b = Buf("Vs")
        lq = sbt("lq", [1, 4, 64], F32)
        lw = sbt("lw", [1, 8], F32)
        prm = sbt("prm", [128, 4], F32)
        prm_b = Buf("prm")
        for h in range(4):
            ksem = S.new_dma_sem("kt%d" % h)
            S.dma("sp", ksem, [I("dma_start", out=KTs[:, h, :], in_=G.KT[h * 128:(h + 1) * 128, :])],
                  reads=[scr_buf(G, "KT", h)], writes=[KT_b[h]])
        vsem = S.new_dma_sem("v")
        S.dma("sp", vsem, [I("dma_start", out=Vs[:, i * 17:(i + 1) * 17, :],
                             in_=G.Vtm[i * 17 * 128:(i + 1) * 17 * 128, :].rearrange("(t p) n -> p t n", p=128))
                           for i in range(2)],
              reads=[scr_buf(G, "V", ti) for ti in range(NT)], writes=[V_b])
        psem = S.new_dma_sem("prm")
        S.dma("sp", psem, [I("dma_start", out=lq[:].rearrange("a b c -> a (b c)"), in_=G.lam_qk[l, :, :]),
                           I("dma_start", out=prm[:, 1:2], in_=G.subln[l, :, :])], writes=[prm_b])
        S.op("dve", I("tensor_tensor", out=lq[0:1, 0::2, :], in0=lq[0:1, 0::2, :], in1=lq[0:1, 1::2, :], op=ALU.mult),
             reads=[prm_b], writes=[prm_b])
        S.op("dve", I("reduce_sum", out=lw[0:1, 0:2], in_=lq[0:1, 0::2, :], axis=AX.X), reads=[prm_b], writes=[prm_b])
        S.op("act", I("activation", out=lw[0:1, 2:4], in_=lw[0:1, 0:2], func=AF.Exp), reads=[prm_b], writes=[prm_b])
        S.op("dve", I("tensor_tensor", out=lw[0:1, 4:5], in0=lw[0:1, 3:4], in1=lw[0:1, 2:3], op=ALU.subtract),
             reads=[prm_b], writes=[prm_b])
        S.op("dve", I("tensor_scalar", out=lw[0:1, 5:6], in0=lw[0:1, 4:5], scalar1=-lam_init, scalar2=None, op0=ALU.add),
             reads=[prm_b], writes=[prm_b])
        S.op("pe", I("matmul", out=ps[:, 0:1], lhsT=G.ones_f[0:1, :], rhs=lw[0:1, 5:6], start=True, stop=True),
             reads=[prm_b, G.b_const], writes=[psb[0]])
        S.op("dve", I("tensor_copy", out=prm[:, 0:1], in_=ps[:, 0:1]), reads=[psb[0]], writes=[prm_b])
        S.op("dve", I("tensor_scalar", out=prm[:, 1:2], in0=prm[:, 1:2], scalar1=1.0 - lam_init, scalar2=None, op0=ALU.mult),
             reads=[prm_b], writes=[prm_b])

        QR = Ring(S, p3, "qr", 3, [128, 512], BF16, dma=True)
        ZR_ = Ring(S, p3, "za", 3, [128, 512], BF16, dma=True)
        PT = Ring(S, p3, "pT", 3, [128, 1024], BF16)
        WK = Ring(S, p3, "wk", 2, [128, 4, 512], F32)
        SQ = Ring(S, p3, "sq", 2, [128, 512], BF16)
        YO = Ring(S, p3, "yo", 2, [128, 512], BF16, dma=True, fresh=True)
        ACC = Ring(S, p3, "acc", 2, [128, 512], F32)
        ACC1 = Ring(S, p3, "acc1", 2, [128, 512], F32)
        sc_i = [0]

        groups = [(t0, tn, list(range(NT))) for (t0, tn) in TG[:8]]
        if need_ctx:
            groups.append((4096, 256, [32, 33]))
        heads = []
        for (t0, tn, ktiles) in groups:
            for h in range(4):
                heads.append(dict(t0=t0, tn=tn, kts=ktiles, h=h))
        steps = []
        for hi, hd in enumerate(heads):
            for ki, kt in enumerate(hd["kts"]):
                steps.append((hi, ki, kt))

        def emit_loads(hd):
            t0, tn, h = hd["t0"], hd["tn"], hd["h"]
            qt, qtb, qts = QR.next()
            S.dma("sp", qts, [I("dma_start", out=qt[:, 0:tn], in_=G.QT[h * 128:(h + 1) * 128, t0:t0 + tn])],
                  reads=[scr_buf(G, "QT", h)], writes=[qtb])
            za, zab, zas = ZR_.next()
            S.dma("sp", zas, [I("dma_start", out=za[:, 0:tn], in_=G.ZA[h * 128:(h + 1) * 128, t0:t0 + tn])],
                  reads=[scr_buf(G, "ZA", h)], writes=[zab])
            hd.update(qt=qt, qtb=qtb, za=za, zab=zab)

        def emit_S(step):
            hi, ki, kt = step
            hd = heads[hi]
            tn, h = hd["tn"], hd["h"]
            sb0 = (sc_i[0] % 2) * 2
            sc_i[0] += 1
            sc = ps[:, sb0 * 512:(sb0 + 2) * 512]
            scb = [psb[sb0], psb[sb0 + 1]]
            S.op("pe", [I("matmul", out=sc[:, c * 512:c * 512 + tn], lhsT=KTs[c * 64:(c + 1) * 64, h, kt * 128:(kt + 1) * 128],
                          rhs=hd["qt"][c * 64:(c + 1) * 64, 0:tn], start=True, stop=True) for c in range(2)],
                 reads=[KT_b[h], hd["qtb"]], writes=scb)
            return sc, scb

        def emit_exp_pv(step, sc, scb):
            hi, ki, kt = step
            hd = heads[hi]
            tn, h = hd["tn"], hd["h"]
            nk = len(hd["kts"])
            pT, pTb, _ = PT.next()
            if tn == 512:
                S.op("act", I("activation", out=pT[:, :], in_=sc[:, :], func=AF.Exp, scale=0.125), reads=scb, writes=[pTb])
            else:
                S.op("act", I("activation", out=pT[:].rearrange("p (c n) -> p c n", c=2)[:, :, 0:tn],
                              in_=sc.rearrange("p (c n) -> p c n", c=2)[:, :, 0:tn], func=AF.Exp, scale=0.125),
                     reads=scb, writes=[pTb])
            mm = []
            for c in range(2):
                mm.append(I("matmul", out=ps[:, (4 + 2 * c) * 512:(4 + 2 * c) * 512 + tn],
                            lhsT=Vs[:, kt, h * 128:(h + 1) * 128], rhs=pT[:, c * 512:c * 512 + tn],
                            start=(ki == 0), stop=(ki == nk - 1)))
            mm.append(I("matmul", out=ps[:, 7 * 512:7 * 512 + tn], lhsT=G.ones_b[:, :], rhs=pT[:, 512:512 + tn],
                        start=(ki == 0), stop=(ki == nk - 1)))
            mm.append(I("matmul", out=ps[:, 5 * 512:5 * 512 + tn], lhsT=G.ones_b[:, :], rhs=pT[:, 0:tn],
                        start=(ki == 0), stop=(ki == nk - 1)))
            S.op("pe", mm, reads=[V_b, pTb, G.b_const], writes=[psb[4], psb[5], psb[6], psb[7]])

        def emit_combine(hd):
            t0, tn, h = hd["t0"], hd["tn"], hd["h"]
            wk, wkb, _ = WK.next()
            S.op("dve", I("tensor_copy", out=wk[:, 0, 0:tn], in_=ps[:, 4 * 512:4 * 512 + tn]), reads=[psb[4]], writes=[wkb])
            S.op("dve", I("tensor_copy", out=wk[:, 1, 0:tn], in_=ps[:, 6 * 512:6 * 512 + tn]), reads=[psb[6]], writes=[wkb])
            S.op("dve", I("tensor_copy", out=wk[:, 3, 0:tn], in_=ps[:, 7 * 512:7 * 512 + tn]), reads=[psb[7]], writes=[wkb])
            S.op("dve", I("tensor_copy", out=wk[:, 2, 0:tn], in_=ps[:, 5 * 512:5 * 512 + tn]), reads=[psb[5]], writes=[wkb])
            S.op("dve", I("reciprocal", out=wk[:, 2, 0:tn], in_=wk[:, 2, 0:tn]), reads=[wkb], writes=[wkb])
            S.op("dve", I("tensor_tensor", out=wk[:, 0, 0:tn], in0=wk[:, 0, 0:tn], in1=wk[:, 2, 0:tn], op=ALU.mult),
                 reads=[wkb], writes=[wkb])
            S.op("dve", I("reciprocal", out=wk[:, 3, 0:tn], in_=wk[:, 3, 0:tn]), reads=[wkb], writes=[wkb])
            S.op("dve", I("tensor_tensor", out=wk[:, 1, 0:tn], in0=wk[:, 1, 0:tn], in1=wk[:, 3, 0:tn], op=ALU.mult),
                 reads=[wkb], writes=[wkb])
            S.op("dve", I("scalar_tensor_tensor", out=wk[:, 2, 0:tn], in0=wk[:, 1, 0:tn], scalar=prm[:, 0:1],
                          in1=wk[:, 0, 0:tn], op0=ALU.mult, op1=ALU.add), reads=[wkb, prm_b], writes=[wkb])
            sq, sqb, _ = SQ.next()
            S.op("pool", I("tensor_tensor", out=sq[:, 0:tn], in0=wk[:, 2, 0:tn], in1=wk[:, 2, 0:tn], op=ALU.mult),
                 reads=[wkb], writes=[sqb])
            hd.update(wk=wk, wkb=wkb, sq=sq, sqb=sqb)

        def emit_finish(hd, slot):
            t0, tn, h = hd["t0"], hd["tn"], hd["h"]
            za, zab, wk, wkb, sq, sqb = hd["za"], hd["zab"], hd["wk"], hd["wkb"], hd["sq"], hd["sqb"]
            sc_, scb_ = slot
            S.op("pe", I("matmul", out=sc_[:, 0:tn], lhsT=G.ones_b[:, :], rhs=sq[:, 0:tn],
                         start=True, stop=True), reads=[sqb, G.b_const], writes=[scb_[0]])
            S.op("act", I("activation", out=wk[:, 3, 0:tn], in_=sc_[:, 0:tn], func=AF.Ln,
                          scale=1.0 / 128, bias=G.eps_col[:, 0:1]), reads=[scb_[0], G.b_const], writes=[wkb])
            S.op("act", I("activation", out=wk[:, 3, 0:tn], in_=wk[:, 3, 0:tn], func=AF.Exp, scale=-0.5),
                 reads=[wkb], writes=[wkb])
            S.op("dve", I("tensor_tensor", out=wk[:, 2, 0:tn], in0=wk[:, 2, 0:tn], in1=wk[:, 3, 0:tn], op=ALU.mult),
                 reads=[wkb], writes=[wkb])
            yo, yob, yos = YO.next()
            S.op("dve", I("scalar_tensor_tensor", out=yo[:, 0:tn], in0=wk[:, 2, 0:tn], scalar=prm[:, 1:2],
                          in1=za[:, 0:tn], op0=ALU.mult, op1=ALU.mult), reads=[wkb, prm_b, zab], writes=[yob])
            S.dma("pool", yos, [I("dma_start", out=G.YG[0, h * 128:(h + 1) * 128, t0:t0 + tn], in_=yo[:, 0:tn])],
                  reads=[yob], writes=[scr_buf(G, "YG0", h)])

        items = s5_prep(G, l, p3)
        G.s5_items = items
        per_step = -(-len(items) // max(1, len(steps) - 8))
        emit_loads(heads[0])
        if len(heads) > 1:
            emit_loads(heads[1])
        cur = emit_S(steps[0])
        pending_finish = None
        for i, step in enumerate(steps):
            hi, ki, kt = step
            nxt = None
            if i + 1 < len(steps):
                nhi = steps[i + 1][0]
                if "qt" not in heads[nhi]:
                    emit_loads(heads[nhi])
                nxt = emit_S(steps[i + 1])
            emit_exp_pv(step, cur[0], cur[1])
            replay(S, items, per_step)
            last_k = (ki == len(heads[hi]["kts"]) - 1)
            if pending_finish is not None and (ki == 5 or last_k):
                emit_finish(pending_finish, cur)
                nl = pending_finish["idx"] + 2
                if nl < len(heads) and "qt" not in heads[nl]:
                    emit_loads(heads[nl])
                pending_finish = None
            if last_k:
                emit_combine(heads[hi])
                heads[hi]["idx"] = hi
                pending_finish = heads[hi]
            cur = nxt
        if pending_finish is not None:
            emit_finish(pending_finish, (ps[:, 0:1024], [psb[0], psb[1]]))
            pending_finish = None
        replay(S, items, len(items))
        if "YG" in G.dbg:
            S.barrier()
            dsem = S.new_dma_sem("dbg")
            S.dma("sp", dsem, [I("dma_start", out=G.dbg["YG"][:, :, :], in_=G.YG[:, :, :])])
        S.barrier()
        S.release_to(mk)


def phase4_lru(G, l):
    nc, S = G.nc, G.S
    ps, psb = G.psall, G.psb
    mk = S.mark()
    SEGS = [(0, SEQ), (SEQ, CTX)]
    CH = [(i * 1024, 1024) for i in range(4)] + [(4096, 256)]
    with ExitStack() as p4:
        sbt = lambda name, shape, dt: p4.enter_context(nc.sbuf_tensor(U(name), list(shape), dt))
        prm = sbt("lprm", [128, 44], F32)
        sp8 = sbt("sp8", [128, 8], F32)
        one_col = sbt("one_col", [128, 1], F32)
        prm_b = Buf("lprm")
        wf = sbt("lwf", [128, 16, 128], F32)
        wb = sbt("lwb", [128, 16, 128], BF16)
        w_b = Buf("lw")
        psem = S.new_dma_sem("lprm")
        S.dma("sp", psem, [I("dma_start", out=prm[:], in_=G.lrup[l, :, :]),
                           I("dma_start", out=wf[:], in_=G.lruw[l].rearrange("a d c p n -> p (a d c) n"))],
              writes=[prm_b, w_b])
        S.op("pool", I("tensor_copy", out=wb[:], in_=wf[:]), reads=[w_b], writes=[w_b])
        S.op("pool", I("memset", ap=one_col[:], constant=1.0), writes=[prm_b])
        S.op("act", I("activation", out=sp8[:], in_=prm[:, 36:44], func=AF.Exp, scale=-1.0), reads=[prm_b], writes=[prm_b])
        S.op("act", I("activation", out=sp8[:], in_=sp8[:], func=AF.Ln, bias=one_col[:, 0:1]), reads=[prm_b], writes=[prm_b])
        S.op("dve", I("tensor_scalar", out=sp8[:], in0=sp8[:], scalar1=-8.0, scalar2=None, op0=ALU.mult),
             reads=[prm_b], writes=[prm_b])

        xr = sbt("xr", [128, T], F32); xr_b = Buf("xr")
        u = sbt("u", [128, T], F32); u_b = Buf("u")
        ub = sbt("ub", [128, T], BF16); ub_b = Buf("ub")
        ra = sbt("ra", [128, T], F32); ra_b = Buf("ra")
        ib = sbt("ib", [128, T], F32); ib_b = Buf("ib")
        tmp = sbt("ltmp", [128, T], F32); tmp_b = Buf("ltmp")
        hf = sbt("hf", [128, T], F32); hf_b = Buf("hf")
        hbr = sbt("hbr", [128, T], F32); hbr_b = Buf("hbr")
        zr = sbt("zr", [128, T], BF16); zr_b = Buf("zr")
        yo = sbt("lyo", [128, T], BF16); yo_b = Buf("lyo")
        xsem = S.new_dma_sem("xr"); zsem = S.new_dma_sem("zr"); ysem = S.new_dma_sem("lyo", fresh=True)
        for ct in range(4):
            S.dma("sp", xsem, [I("dma_start", out=xr[:, :], in_=G.XR[ct * 128:(ct + 1) * 128, :])],
                  reads=[scr_buf(G, "XR", ct)], writes=[xr_b])
            S.dma("sp", zsem, [I("dma_start", out=zr[:, :], in_=G.ZR[ct * 128:(ct + 1) * 128, :])],
                  reads=[scr_buf(G, "ZR", ct)], writes=[zr_b])
            for (s0, n) in SEGS:
                S.op("pool", I("tensor_scalar", out=u[:, s0:s0 + n], in0=xr[:, s0:s0 + n],
                               scalar1=prm[:, ct * 4 + 2:ct * 4 + 3], scalar2=prm[:, 16 + ct:17 + ct],
                               op0=ALU.mult, op1=ALU.add), reads=[xr_b, prm_b], writes=[u_b])
                for (k, oa, ob_, ia, ib_) in ((0, 2, n, 0, n - 2), (1, 1, n, 0, n - 1), (3, 0, n - 1, 1, n)):
                    S.op("dve", I("scalar_tensor_tensor", out=u[:, s0 + oa:s0 + ob_], in0=xr[:, s0 + ia:s0 + ib_],
                                   scalar=prm[:, ct * 4 + k:ct * 4 + k + 1], in1=u[:, s0 + oa:s0 + ob_],
                                   op0=ALU.mult, op1=ALU.add), reads=[xr_b, prm_b, u_b], writes=[u_b])
            S.op("act", I("activation", out=ub[:, :], in_=u[:, :], func=AF.Copy), reads=[u_b], writes=[ub_b])
            for d in range(2):
                for ci, (c0, cn) in enumerate(CH):
                    bk = (ci % 2) * 4
                    for gi in range(2):
                        widx = gi * 8 + d * 4 + ct
                        mm = []
                        for s in range(0, cn, 512):
                            sn = min(512, cn - s)
                            mm.append(I("matmul", out=ps[:, (bk + gi * 2) * 512 + s:(bk + gi * 2) * 512 + s + sn],
                                        lhsT=wb[:, widx, :], rhs=ub[:, c0 + s:c0 + s + sn], start=True, stop=True))
                        S.op("pe", mm, reads=[w_b, ub_b], writes=[psb[bk + gi * 2], psb[bk + gi * 2 + 1]])
                        dst, dst_b = (ra, ra_b) if gi == 0 else (ib, ib_b)
                        bcol = (20 if gi == 0 else 28) + d * 4 + ct
                        S.op("act", I("activation", out=dst[:, c0:c0 + cn], in_=ps[:, (bk + gi * 2) * 512:(bk + gi * 2) * 512 + cn],
                                      func=AF.Sigmoid, bias=prm[:, bcol:bcol + 1]),
                             reads=[psb[bk + gi * 2], psb[bk + gi * 2 + 1], prm_b], writes=[dst_b])
                S.op("act", I("activation", out=ra[:, :], in_=ra[:, :], func=AF.Exp, scale=sp8[:, d * 4 + ct:d * 4 + ct + 1]),
                     reads=[ra_b, prm_b], writes=[ra_b])
                S.op("dve", I("tensor_tensor", out=tmp[:, :], in0=ra[:, :], in1=ra[:, :], op=ALU.mult), reads=[ra_b], writes=[tmp_b])
                S.op("act", I("activation", out=tmp[:, :], in_=tmp[:, :], func=AF.Sqrt, scale=-1.0, bias=one_col[:, 0:1]),
                     reads=[tmp_b, prm_b], writes=[tmp_b])
                S.op("pool", I("tensor_tensor", out=ib[:, :], in0=ib[:, :], in1=u[:, :], op=ALU.mult), reads=[ib_b, u_b], writes=[ib_b])
                S.op("dve", I("tensor_tensor", out=ib[:, :], in0=ib[:, :], in1=tmp[:, :], op=ALU.mult), reads=[ib_b, tmp_b], writes=[ib_b])
                if d == 0:
                    S.op("dve", I("tensor_tensor_scan", out=hf[:, SEQ:T], data0=ra[:, SEQ:T], data1=ib[:, SEQ:T], initial=0.0,
                                  op0=ALU.mult, op1=ALU.add), reads=[ra_b, ib_b], writes=[hf_b])
                    S.op("dve", I("tensor_tensor_scan", out=hf[:, 0:SEQ], data0=ra[:, 0:SEQ], data1=ib[:, 0:SEQ],
                                  initial=hf[:, T - 1:T], op0=ALU.mult, op1=ALU.add), reads=[ra_b, ib_b, hf_b], writes=[hf_b])
                else:
                    S.op("dve", I("tensor_tensor_scan", out=hbr[:, 0:CTX], data0=ra[:, SEQ:T][:, ::-1], data1=ib[:, SEQ:T][:, ::-1],
                                  initial=0.0, op0=ALU.mult, op1=ALU.add), reads=[ra_b, ib_b], writes=[hbr_b])
                    S.op("dve", I("tensor_tensor_scan", out=hbr[:, CTX:T], data0=ra[:, 0:SEQ][:, ::-1], data1=ib[:, 0:SEQ][:, ::-1],
                                  initial=hbr[:, CTX - 1:CTX], op0=ALU.mult, op1=ALU.add), reads=[ra_b, ib_b, hbr_b], writes=[hbr_b])
            S.op("dve", I("tensor_tensor", out=hf[:, 0:SEQ], in0=hf[:, 0:SEQ], in1=hbr[:, CTX:T][:, ::-1], op=ALU.add),
                 reads=[hf_b, hbr_b], writes=[hf_b])
            S.op("dve", I("tensor_tensor", out=hf[:, SEQ:T], in0=hf[:, SEQ:T], in1=hbr[:, 0:CTX][:, ::-1], op=ALU.add),
                 reads=[hf_b, hbr_b], writes=[hf_b])
            S.op("pool", I("tensor_tensor", out=yo[:, :], in0=hf[:, :], in1=zr[:, :], op=ALU.mult),
                 reads=[hf_b, zr_b], writes=[yo_b])
            S.dma("pool", ysem, [I("dma_start", out=G.YG[1, ct * 128:(ct + 1) * 128, :], in_=yo[:, :])],
                  reads=[yo_b], writes=[scr_buf(G, "YG1", ct)])
        if "YG" in G.dbg:
            S.barrier()
            dsem = S.new_dma_sem("dbg")
            S.dma("sp", dsem, [I("dma_start", out=G.dbg["YG"][:, :, :], in_=G.YG[:, :, :])])
        S.barrier()
        S.release_to(mk)


TWO_PI = 2.0 * math.pi
I32 = mybir.dt.int32


class Rec:
    def __init__(self, S):
        self.S = S
        self.items = []

    def new_dma_sem(self, name):
        return self.S.new_dma_sem(name)

    def op(self, eng, fn, reads=(), writes=()):
        self.items.append(("op", eng, fn, list(reads), list(writes)))

    def dma(self, eng, semkey, fns, reads=(), writes=()):
        self.items.append(("dma", eng, semkey, fns, list(reads), list(writes)))


def replay(S, items, n):
    for _ in range(n):
        if not items:
            return
        it = items.pop(0)
        if it[0] == "op":
            S.op(it[1], it[2], it[3], it[4])
        else:
            S.dma(it[1], it[2], it[3], it[4], it[5])


NCH = T // 8


def s5_prep(G, l, stack):
    nc = G.nc
    R = Rec(G.S)
    S = R
    sbt = lambda name, shape, dt: stack.enter_context(nc.sbuf_tensor(U(name), list(shape), dt))
    tab = sbt("s5tab", [128, 36 + 512 + 544 + 128], F32)
    ptab = tab[:, 0:36].rearrange("p (j q) -> p j q", q=9)
    etab = tab[:, 36:548].rearrange("p (e n) -> p e n", n=64)
    ctab = tab[:, 548:1092]
    mask = tab[:, 1092:1220].rearrange("p (g n) -> p g n", n=16)
    dtall = sbt("dtall", [128, 8], F32)
    cst_b = Buf("s5cst")
    csem = S.new_dma_sem("s5c")
    S.dma("sp", csem, [I("dma_start", out=tab[:], in_=G.s5tab[:, :]),
                       I("dma_start", out=dtall[:], in_=G.s5dt[l, :, :])], writes=[cst_b])
    S.op("act", I("activation", out=dtall[:], in_=dtall[:], func=AF.Exp), reads=[cst_b], writes=[cst_b])
    lamN = sbt("lamN", [128, 8], F32)
    BN = sbt("BN", [128, 2, 4, 16], F32)
    CN = sbt("CN", [128, 2, 4, 16], F32)
    nsem = S.new_dma_sem("s5n")
    np_b = Buf("nprm")
    n4 = sbt("n4", [128, 12, 4], F32)
    nP = sbt("nP", [128, 8, 4, 9], F32)
    nPi = sbt("nPi", [128, 4, 9], I32)
    bbN = sbt("bbN", [128, 2, 4, 16], F32)
    tN = sbt("tN", [128, 4, 4, 8, 16], F32)
    Hp = sbt("Hp", [128, 4, 64], F32)
    hsem = S.new_dma_sem("s5h")
    hp_b = Buf("hprm")
    h64 = sbt("h64", [128, 10, 64], F32)
    hE = sbt("hE", [128, 4, 8, 64], F32)
    W1d = sbt("W1d", [128, 8, 128], F32)
    tNf = tN[:].rearrange("p a b c d -> p a (b c d)")
    hEi = tNf[:, 2, :].bitcast(I32).rearrange("p (e n) -> p e n", n=64)
    OUT = [sbt("s5out%d" % i, [128, 8192], BF16) for i in range(2)]
    out_b = [Buf("s5out%d" % i) for i in range(2)]
    out_s = [S.new_dma_sem("s5o%d" % i) for i in range(2)]
    oi = [0]
    tc = sbt("tc", [128, 7, NCH], F32)
    tc_b = Buf("tc")
    tcs = S.new_dma_sem("tc")
    rsem = S.new_dma_sem("rh")

    def trig(turns, ti, tf2, r, out2, bufs, eng="dve"):
        for which in (0, 1):
            tf = tf2[which]
            S.op(eng, I("tensor_scalar", out=tf, in0=turns, scalar1=16.25 - 0.25 * which, scalar2=None, op0=ALU.add),
                 reads=bufs, writes=bufs)
            S.op(eng, I("tensor_copy", out=ti, in_=tf), reads=bufs, writes=bufs)
            S.op(eng, I("tensor_copy", out=r, in_=ti), reads=bufs, writes=bufs)
            S.op(eng, I("tensor_tensor", out=tf, in0=tf, in1=r, op=ALU.subtract), reads=bufs, writes=bufs)
            S.op(eng, I("tensor_single_scalar", out=r, in_=tf, scalar=0.5, op=ALU.is_ge), reads=bufs, writes=bufs)
            S.op(eng, I("tensor_tensor", out=tf, in0=tf, in1=r, op=ALU.subtract), reads=bufs, writes=bufs)
        S.op("act", I("activation", out=out2, in_=tf2[2], func=AF.Sin, scale=TWO_PI), reads=bufs, writes=bufs)

    def cplx_coef(pr1, pi1, lr, li, t, bre, bim, obr, obi, shp_b, bufs, eng="dve", tfull=None):
        nr, den, cr, ci, t4, t5 = t
        S.op(eng, I("tensor_scalar", out=nr, in0=pr1, scalar1=-1.0, scalar2=None, op0=ALU.add), reads=bufs, writes=bufs)
        S.op(eng, I("tensor_tensor", out=den, in0=lr, in1=lr, op=ALU.mult), reads=bufs, writes=bufs)
        S.op(eng, I("tensor_tensor", out=t4, in0=li, in1=li, op=ALU.mult), reads=bufs, writes=bufs)
        S.op(eng, I("tensor_tensor", out=den, in0=den, in1=t4, op=ALU.add), reads=bufs, writes=bufs)
        S.op("dve", I("reciprocal", out=den, in_=den), reads=bufs, writes=bufs)
        S.op(eng, I("tensor_tensor", out=cr, in0=nr, in1=lr, op=ALU.mult), reads=bufs, writes=bufs)
        S.op(eng, I("tensor_tensor", out=t4, in0=pi1, in1=li, op=ALU.mult), reads=bufs, writes=bufs)
        S.op(eng, I("tensor_tensor", out=cr, in0=cr, in1=t4, op=ALU.add), reads=bufs, writes=bufs)
        S.op(eng, I("tensor_tensor", out=cr, in0=cr, in1=den, op=ALU.mult), reads=bufs, writes=bufs)
        S.op(eng, I("tensor_tensor", out=ci, in0=pi1, in1=lr, op=ALU.mult), reads=bufs, writes=bufs)
        S.op(eng, I("tensor_tensor", out=t4, in0=nr, in1=li, op=ALU.mult), reads=bufs, writes=bufs)
        S.op(eng, I("tensor_tensor", out=ci, in0=ci, in1=t4, op=ALU.subtract), reads=bufs, writes=bufs)
        S.op(eng, I("tensor_tensor", out=ci, in0=ci, in1=den, op=ALU.mult), reads=bufs, writes=bufs)
        crb = cr if shp_b is None else cr.unsqueeze(2).broadcast_to(shp_b)
        cib = ci if shp_b is None else ci.unsqueeze(2).broadcast_to(shp_b)
        S.op(eng, I("tensor_tensor", out=obr, in0=bre, in1=crb, op=ALU.mult), reads=bufs, writes=bufs)
        S.op(eng, I("tensor_tensor", out=obi, in0=bim, in1=cib, op=ALU.mult), reads=bufs, writes=bufs)
        S.op(eng, I("tensor_tensor", out=obr, in0=obr, in1=obi, op=ALU.subtract), reads=bufs, writes=bufs)
        S.op(eng, I("tensor_tensor", out=obi, in0=bim, in1=crb, op=ALU.mult), reads=bufs, writes=bufs)
        t6 = t5 if tfull is None else tfull
        S.op(eng, I("tensor_tensor", out=t6, in0=bre, in1=cib, op=ALU.mult), reads=bufs, writes=bufs)
        S.op(eng, I("tensor_tensor", out=obi, in0=obi, in1=t6, op=ALU.add), reads=bufs, writes=bufs)

    def out_slot():
        i = oi[0] % 2
        oi[0] += 1
        return OUT[i], out_b[i], out_s[i]

    cnsem = S.new_dma_sem("s5cn")
    for gt in range(4):
        S.dma("sp", cnsem, [I("dma_start", out=CN[:], in_=G.s5CN[l, gt])], writes=[np_b])
        for d in range(2):
            idx = gt * 2 + d
            dcol = dtall[:, d * 4 + gt:d * 4 + gt + 1]
            nb = [np_b, cst_b]
            S.dma("sp", nsem, [I("dma_start", out=lamN[:], in_=G.s5N[l, d, gt]),
                               I("dma_start", out=BN[:], in_=G.s5BN[l, d, gt])], writes=[np_b])
            ld4, tu4 = n4[:, 0, :], n4[:, 1, :]
            S.op("dve", I("tensor_scalar", out=ld4, in0=lamN[:, 0:4], scalar1=dcol, scalar2=None, op0=ALU.mult), reads=nb, writes=nb)
            S.op("dve", I("tensor_scalar", out=tu4, in0=lamN[:, 4:8], scalar1=dcol, scalar2=1.0 / TWO_PI,
                           op0=ALU.mult, op1=ALU.mult), reads=nb, writes=nb)
            angP, mgP, cosP, sinP, prP, piP, tfP, rP = [nP[:, i] for i in range(8)]
            S.op("dve", I("tensor_tensor", out=angP, in0=ptab, in1=tu4.unsqueeze(2).broadcast_to([128, 4, 9]), op=ALU.mult), reads=nb, writes=nb)
            S.op("dve", I("tensor_tensor", out=mgP, in0=ptab, in1=ld4.unsqueeze(2).broadcast_to([128, 4, 9]), op=ALU.mult), reads=nb, writes=nb)
            S.op("act", I("activation", out=mgP, in_=mgP, func=AF.Exp), reads=nb, writes=nb)
            trig(angP, nPi[:], (nP[:, 4], nP[:, 5], nP[:, 4:6]), rP, nP[:, 2:4], nb)
            S.op("dve", I("tensor_tensor", out=prP, in0=mgP, in1=cosP, op=ALU.mult), reads=nb, writes=nb)
            S.op("dve", I("tensor_tensor", out=piP, in0=mgP, in1=sinP, op=ALU.mult), reads=nb, writes=nb)
            cplx_coef(prP[:, :, 1], piP[:, :, 1], lamN[:, 0:4], lamN[:, 4:8], [n4[:, i, :] for i in range(2, 8)],
                      BN[:, 0], BN[:, 1], bbN[:, 0], bbN[:, 1], [128, 4, 16], nb, tfull=tN[:, 3, :, 0, :])
            S.op("dve", I("tensor_copy", out=n4[:, 10, :], in_=mgP[:, :, 8]), reads=nb, writes=nb)
            S.dma("sp", rsem, [I("dma_start", out=G.S5RH[idx], in_=n4[:, 10, :])], reads=nb)
            f8 = n4[:, 8, :]
            S.op("dve", I("tensor_copy", out=nPi[:, :, 0], in_=angP[:, :, 8]), reads=nb, writes=nb)
            S.op("dve", I("tensor_copy", out=n4[:, 9, :], in_=nPi[:, :, 0]), reads=nb, writes=nb)
            S.op("dve", I("tensor_tensor", out=f8, in0=angP[:, :, 8], in1=n4[:, 9, :], op=ALU.subtract), reads=nb, writes=nb)
            for jc in range(4):
                tb = [tc_b, np_b, cst_b]
                S.op("dve", I("tensor_scalar", out=tc[:, 0, :], in0=ctab, scalar1=f8[:, jc:jc + 1], scalar2=None, op0=ALU.mult),
                     reads=tb, writes=tb)
                trig(tc[:, 0, :], tc[:, 5, :].bitcast(I32), (tc[:, 1, :], tc[:, 2, :], tc[:, 1:3, :]), tc[:, 6, :], tc[:, 3:5, :], tb, eng="dve")
                S.dma("sp", tcs, [I("dma_start", out=G.S5TB[idx, :, 0, jc, :], in_=tc[:, 3, :]),
                                  I("dma_start", out=G.S5TB[idx, :, 1, jc, :], in_=tc[:, 4, :])], reads=tb)
            hb = [hp_b, cst_b]
            S.dma("sp", hsem, [I("dma_start", out=Hp[:], in_=G.s5H[l, d, gt])], writes=[hp_b])
            ldH, tuH = h64[:, 0, :], h64[:, 1, :]
            S.op("dve", I("tensor_scalar", out=ldH, in0=Hp[:, 0, :], scalar1=dcol, scalar2=None, op0=ALU.mult), reads=hb, writes=hb)
            S.op("dve", I("tensor_scalar", out=tuH, in0=Hp[:, 1, :], scalar1=dcol, scalar2=1.0 / TWO_PI,
                           op0=ALU.mult, op1=ALU.mult), reads=hb, writes=hb)
            angE, mgE, cosE, sinE = [hE[:, i] for i in range(4)]
            tfE = tNf[:, 0, :].rearrange("p (e n) -> p e n", n=64)
            rE = tNf[:, 1, :].rearrange("p (e n) -> p e n", n=64)
            S.op("dve", I("tensor_tensor", out=angE, in0=etab, in1=tuH.unsqueeze(1).broadcast_to([128, 8, 64]), op=ALU.mult), reads=hb, writes=hb)
            S.op("dve", I("tensor_tensor", out=mgE, in0=etab, in1=ldH.unsqueeze(1).broadcast_to([128, 8, 64]), op=ALU.mult), reads=hb, writes=hb)
            S.op("act", I("activation", out=mgE, in_=mgE, func=AF.Exp), reads=hb, writes=hb)
            trig(angE, hEi, (tfE, rE, tNf[:, 0:2, :].rearrange("p a (e n) -> p a e n", n=64)),
                 tNf[:, 3, :].rearrange("p (e n) -> p e n", n=64), hE[:, 2:4], hb + [np_b], eng="dve")
            S.op("dve", I("tensor_tensor", out=cosE, in0=mgE, in1=cosE, op=ALU.mult), reads=hb, writes=hb)
            S.op("dve", I("tensor_tensor", out=sinE, in0=mgE, in1=sinE, op=ALU.mult), reads=hb, writes=hb)
            bbrH, bbiH = h64[:, 8, :], h64[:, 9, :]
            cplx_coef(cosE[:, 1, :], sinE[:, 1, :], Hp[:, 0, :], Hp[:, 1, :], [h64[:, i, :] for i in range(2, 8)],
                      Hp[:, 2, :], Hp[:, 3, :], bbrH, bbiH, None, hb, eng="dve")
            bbrB = bbrH.unsqueeze(1).broadcast_to([128, 8, 64]); bbiB = bbiH.unsqueeze(1).broadcast_to([128, 8, 64])
            hb2 = hb + [np_b]
            S.op("dve", I("tensor_tensor", out=W1d[:, :, 0:64], in0=cosE, in1=bbrB, op=ALU.mult), reads=hb, writes=hb)
            S.op("dve", I("tensor_tensor", out=tfE, in0=sinE, in1=bbiB, op=ALU.mult), reads=hb2, writes=hb2)
            S.op("dve", I("tensor_tensor", out=W1d[:, :, 0:64], in0=W1d[:, :, 0:64], in1=tfE, op=ALU.subtract), reads=hb2, writes=hb)
            S.op("dve", I("tensor_tensor", out=W1d[:, :, 64:128], in0=cosE, in1=bbiB, op=ALU.mult), reads=hb, writes=hb)
            S.op("dve", I("tensor_tensor", out=tfE, in0=sinE, in1=bbrB, op=ALU.mult), reads=hb2, writes=hb2)
            S.op("dve", I("tensor_tensor", out=W1d[:, :, 64:128], in0=W1d[:, :, 64:128], in1=tfE, op=ALU.add), reads=hb2, writes=hb)
            ot, otb, ots = out_slot()
            W1bd = ot[:].rearrange("p (e j n) -> p e j n", e=8, j=8)
            for e in range(8):
                S.op("pool" if e % 2 else "dve",
                     I("tensor_tensor", out=W1bd[:, e].rearrange("p j (g n) -> p j g n", n=16),
                       in0=W1d[:, e, :].rearrange("p (j n) -> p j n", n=16).unsqueeze(2).broadcast_to([128, 8, 8, 16]),
                       in1=mask.unsqueeze(1).broadcast_to([128, 8, 8, 16]), op=ALU.mult),
                     reads=hb, writes=[otb])
            S.dma("sp", ots, [I("dma_start", out=G.S5W1[idx], in_=ot[:])], reads=[otb])
            if d == 0:
                prS, piS = prP[:, :, 1:9], piP[:, :, 1:9]
            else:
                prS, piS = prP[:, :, 1:9][:, :, ::-1], piP[:, :, 1:9][:, :, ::-1]
            prB = prS.unsqueeze(3).broadcast_to([128, 4, 8, 16]); piB = piS.unsqueeze(3).broadcast_to([128, 4, 8, 16])
            creB = CN[:, 0].unsqueeze(2).broadcast_to([128, 4, 8, 16]); cimB = CN[:, 1].unsqueeze(2).broadcast_to([128, 4, 8, 16])
            wR, wI, t2, t3 = [tN[:, i] for i in range(4)]
            S.op("dve", I("tensor_tensor", out=wR, in0=creB, in1=prB, op=ALU.mult), reads=nb, writes=nb)
            S.op("dve", I("tensor_tensor", out=t2, in0=cimB, in1=piB, op=ALU.mult), reads=nb, writes=nb)
            S.op("dve", I("tensor_tensor", out=wR, in0=wR, in1=t2, op=ALU.subtract), reads=nb, writes=nb)
            S.op("dve", I("tensor_tensor", out=wI, in0=creB, in1=piB, op=ALU.mult), reads=nb, writes=nb)
            S.op("dve", I("tensor_tensor", out=t2, in0=cimB, in1=prB, op=ALU.mult), reads=nb, writes=nb)
            S.op("dve", I("tensor_tensor", out=wI, in0=wI, in1=t2, op=ALU.add), reads=nb, writes=nb)
            S.op("dve", I("tensor_scalar", out=wI, in0=wI, scalar1=-1.0, scalar2=None, op0=ALU.mult), reads=nb, writes=nb)
            ot, otb, ots = out_slot()
            W3bd = ot[:].rearrange("p (j s n) -> p j s n", j=8, s=8)
            for j in range(8):
                srcw = (wR if j < 4 else wI)[:, j % 4]
                S.op("pool" if j % 2 else "dve",
                     I("tensor_tensor", out=W3bd[:, j].rearrange("p s (g n) -> p s g n", n=16),
                       in0=srcw.unsqueeze(2).broadcast_to([128, 8, 8, 16]),
                       in1=mask.unsqueeze(1).broadcast_to([128, 8, 8, 16]), op=ALU.mult),
                     reads=nb, writes=[otb])
            S.dma("sp", ots, [I("dma_start", out=G.S5W3[idx], in_=ot[:])], reads=[otb])
            prK = prP[:, :, 0:8].unsqueeze(3).broadcast_to([128, 4, 8, 16]); piK = piP[:, :, 0:8].unsqueeze(3).broadcast_to([128, 4, 8, 16])
            bbrB = bbN[:, 0].unsqueeze(2).broadcast_to([128, 4, 8, 16]); bbiB = bbN[:, 1].unsqueeze(2).broadcast_to([128, 4, 8, 16])
            S.op("dve", I("tensor_tensor", out=wR, in0=bbrB, in1=prK, op=ALU.mult), reads=nb, writes=nb)
            S.op("dve", I("tensor_tensor", out=t2, in0=bbiB, in1=piK, op=ALU.mult), reads=nb, writes=nb)
            S.op("dve", I("tensor_tensor", out=wR, in0=wR, in1=t2, op=ALU.subtract), reads=nb, writes=nb)
            S.op("dve", I("tensor_tensor", out=wI, in0=bbiB, in1=prK, op=ALU.mult), reads=nb, writes=nb)
            S.op("dve", I("tensor_tensor", out=t2, in0=bbrB, in1=piK, op=ALU.mult), reads=nb, writes=nb)
            S.op("dve", I("tensor_tensor", out=wI, in0=wI, in1=t2, op=ALU.add), reads=nb, writes=nb)
            ot, otb, ots = out_slot()
            XBD = ot[:].rearrange("p (r q n) -> p r q n", r=2, q=32)
            for ri in range(2):
                srcx = (wR if ri == 0 else wI).rearrange("p j k h -> p (j k) h")
                for half in range(2):
                    S.op("pool" if half else "dve",
                         I("tensor_tensor", out=XBD[:, ri, half * 16:(half + 1) * 16].rearrange("p q (g n) -> p q g n", n=16),
                           in0=srcx[:, half * 16:(half + 1) * 16].unsqueeze(2).broadcast_to([128, 16, 8, 16]),
                           in1=mask.unsqueeze(1).broadcast_to([128, 16, 8, 16]), op=ALU.mult),
                         reads=nb, writes=[otb])
            S.dma("sp", ots, [I("dma_start", out=G.S5XB[idx], in_=ot[:])], reads=[otb])
    return R.items


def phase5_s5(G, l):
    nc, S = G.nc, G.S
    ps, psb = G.psall, G.psb
    if G.s5_items is None:
        mk0 = S.mark()
        with ExitStack() as pp:
            items = s5_prep(G, l, pp)
            replay(S, items, len(items))
            S.barrier()
            S.release_to(mk0)
    else:
        assert not G.s5_items
    G.s5_items = None
    mk = S.mark()
    RANGES = [(0, 256), (256, 256), (512, 32)]
    with ExitStack() as p5:
        sbt = lambda name, shape, dt: p5.enter_context(nc.sbuf_tensor(U(name), list(shape), dt))
        tab = sbt("s5tabm", [128, 128], F32)
        mask = tab[:, 0:128].rearrange("p (g n) -> p g n", n=16)
        misc = sbt("s5misc", [128, 8], F32)
        cst_b = Buf("s5cst")
        csem = S.new_dma_sem("s5c")
        S.dma("sp", csem, [I("dma_start", out=tab[:], in_=G.s5tab[:, 1092:1220]),
                           I("dma_start", out=misc[:], in_=G.s5misc[l, :, :])], writes=[cst_b])
        Ut = sbt("Ut", [128, T], F32); U_b = Buf("Ut")
        Ub = sbt("Ub", [128, T], BF16); Ub_b = Buf("Ub")
        gst, gst_b = Ub, Ub_b
        gsem = S.new_dma_sem("gst", fresh=True)
        yt, y_b = Ut, U_b
        usem = S.new_dma_sem("s5u")
        CN = sbt("CN", [128, 2, 4, 16], F32)
        nsem = S.new_dma_sem("s5n")
        CBD = sbt("CBD", [128, 2, 4, 128], BF16); cbd_b = Buf("CBD")
        Kf = sbt("Kf", [128, 16, 128], BF16); k_b = Buf("Kf")
        K0 = sbt("K0", [128, 128], F32)
        Srot = sbt("Srot", [128, 8, NCH], F32); sr_b = Buf("Srot")
        Gs = sbt("Gs", [128, 8, NCH], F32); gs_b = Buf("Gs")
        Ebf = [sbt("Ebf%d" % d, [128, 8, NCH + 1], BF16) for d in range(2)]
        e_b = [Buf("Ebf%d" % d) for d in range(2)]
        rt = Gs[:, 0:4, :].rearrange("p a c -> p (a c)")[:, 0:2048].rearrange("p (r j c) -> p r j c", r=2, j=4); rt_b = gs_b
        WA = Ring(S, p5, "wa", 2, [128, 8192], BF16, dma=True)
        TBr = Ring(S, p5, "tbr", 2, [128, 2, 4, NCH], F32, dma=True)
        RHr = Ring(S, p5, "rhr", 2, [128, 4], F32, dma=True)
        XBr = Ring(S, p5, "xbr", 1, [128, 8192], BF16, dma=True)
        W3t = [sbt("W3bd%d" % d, [128, 8192], BF16) for d in range(2)]
        w3_b = [Buf("W3bd%d" % d) for d in range(2)]
        w3s = [S.new_dma_sem("w3%d" % d) for d in range(2)]
        loaded = {}

        def load_w(idx):
            if idx in loaded or idx >= 8:
                return
            wa, wab, was = WA.next()
            S.dma("sp", was, [I("dma_start", out=wa[:], in_=G.S5W1[idx])], writes=[wab])
            tb, tbb, tbs = TBr.next()
            S.dma("sp", tbs, [I("dma_start", out=tb[:], in_=G.S5TB[idx])], writes=[tbb])
            rh, rhb, rhs = RHr.next()
            S.dma("sp", rhs, [I("dma_start", out=rh[:], in_=G.S5RH[idx])], writes=[rhb])
            loaded[idx] = (wa, wab, tb, tbb, rh, rhb)

        load_w(0)
        for gt in range(4):
            S.dma("sp", usem, [I("dma_start", out=Ut[:, :], in_=G.US[gt * 128:(gt + 1) * 128, :])], writes=[U_b])
            S.op("act", I("activation", out=Ub[:, :], in_=Ut[:, :], func=AF.Copy), reads=[U_b], writes=[Ub_b])
            S.dma("sp", nsem, [I("dma_start", out=CN[:], in_=G.s5CN[l, gt])], writes=[cbd_b])
            for ri in range(2):
                S.op("pool", I("tensor_tensor", out=CBD[:, ri].rearrange("p j (g n) -> p j g n", n=16),
                               in0=CN[:, ri].unsqueeze(2).broadcast_to([128, 4, 8, 16]),
                               in1=mask.unsqueeze(1).broadcast_to([128, 4, 8, 16]), op=ALU.mult),
                     reads=[cst_b, cbd_b], writes=[cbd_b])
            S.op("pool", I("tensor_scalar", out=CBD[:, 1], in0=CBD[:, 1], scalar1=-1.0, scalar2=None, op0=ALU.mult),
                 reads=[cbd_b], writes=[cbd_b])
            for d in range(2):
                S.dma("sp", w3s[d], [I("dma_start", out=W3t[d][:], in_=G.S5W3[gt * 2 + d])], writes=[w3_b[d]])
            for d in range(2):
                idx = gt * 2 + d
                load_w(idx)
                wa, w1_b, tbl, tbl_b, rho, rho_b = loaded[idx]
                W1bd = wa[:].rearrange("p (e j n) -> p e j n", e=8, j=8)
                cosT, sinT = tbl[:, 0], tbl[:, 1]
                for ri_, (c0, n) in enumerate(RANGES):
                    b0 = (ri_ % 2) * 4
                    for j in range(8):
                        off = b0 * 512 + j * 256
                        S.op("pe", [I("matmul", out=ps[:, off:off + n], lhsT=W1bd[:, (7 - s) if d == 0 else s, j, :],
                                      rhs=Ub[:, c0 * 8 + s:(c0 + n) * 8:8], start=(s == 0), stop=(s == 7)) for s in range(8)],
                             reads=[w1_b, Ub_b], writes=psb[b0:b0 + 4])
                    pv = ps[:, b0 * 512:(b0 + 4) * 512].rearrange("p (j c) -> p j c", c=256)
                    Sre, Sim = pv[:, 0:4, 0:n], pv[:, 4:8, 0:n]
                    if d == 0:
                        cp = c0 + 32 if c0 < 512 else 0
                        cT, sT = cosT[:, :, cp:cp + n], sinT[:, :, cp:cp + n]
                        ore, oim = Srot[:, 0:4, cp:cp + n], Srot[:, 4:8, cp:cp + n]
                    else:
                        lo = NCH - 1 - (c0 + n - 1)
                        cT, sT = cosT[:, :, lo:lo + n][:, :, ::-1], sinT[:, :, lo:lo + n][:, :, ::-1]
                        ore, oim = Srot[:, 0:4, c0:c0 + n], Srot[:, 4:8, c0:c0 + n]
                    rb = psb[b0:b0 + 4] + [tbl_b]
                    S.op("dve", I("tensor_tensor", out=rt[:, 0, :, 0:n], in0=Sre, in1=cT, op=ALU.mult), reads=rb, writes=[rt_b])
                    S.op("dve", I("tensor_tensor", out=rt[:, 1, :, 0:n], in0=Sim, in1=sT, op=ALU.mult), reads=rb, writes=[rt_b])
                    S.op("pool", I("tensor_tensor", out=ore, in0=rt[:, 0, :, 0:n], in1=rt[:, 1, :, 0:n], op=ALU.add), reads=[rt_b], writes=[sr_b])
                    S.op("dve", I("tensor_tensor", out=rt[:, 0, :, 0:n], in0=Sim, in1=cT, op=ALU.mult), reads=rb, writes=[rt_b])
                    S.op("dve", I("tensor_tensor", out=rt[:, 1, :, 0:n], in0=Sre, in1=sT, op=ALU.mult), reads=rb, writes=[rt_b])
                    S.op("pool", I("tensor_tensor", out=oim, in0=rt[:, 0, :, 0:n], in1=rt[:, 1, :, 0:n], op=ALU.subtract), reads=[rt_b], writes=[sr_b])
                xb, xbb, xbs = XBr.next()
                S.dma("sp", xbs, [I("dma_start", out=xb[:], in_=G.S5XB[idx])], writes=[xbb])
                XBD = xb[:].rearrange("p (r q n) -> p r q n", r=2, q=32)
                load_w(idx + 1)
                for j in range(8):
                    src_ = Srot[:, j, :] if d == 0 else Srot[:, j, ::-1]
                    S.op("dve", I("tensor_tensor_scan", out=Gs[:, j, :], data0=rho[:, j % 4:j % 4 + 1].broadcast_to([128, NCH]),
                                  data1=src_, initial=0.0, op0=ALU.mult, op1=ALU.add), reads=[sr_b, rho_b], writes=[gs_b])
                S.op("pool", I("memset", ap=Ebf[d][:, :, 0:1], constant=0.0), writes=[e_b[d]])
                Gre, Gim = Gs[:, 0:4, :], Gs[:, 4:8, :]
                p0, p1 = Srot[:, 0:4, :], Srot[:, 4:8, :]
                S.op("dve", I("tensor_tensor", out=p0, in0=Gre, in1=cosT, op=ALU.mult), reads=[gs_b, tbl_b, sr_b], writes=[sr_b])
                S.op("pool", I("tensor_tensor", out=p1, in0=Gim, in1=sinT, op=ALU.mult), reads=[gs_b, tbl_b, sr_b], writes=[sr_b])
                S.op("dve", I("tensor_tensor", out=Ebf[d][:, 0:4, 1:NCH + 1], in0=p0, in1=p1, op=ALU.subtract),
                     reads=[sr_b], writes=[e_b[d]])
                S.op("dve", I("tensor_tensor", out=p0, in0=Gim, in1=cosT, op=ALU.mult), reads=[gs_b, tbl_b, sr_b], writes=[sr_b])
                S.op("pool", I("tensor_tensor", out=p1, in0=Gre, in1=sinT, op=ALU.mult), reads=[gs_b, tbl_b, sr_b], writes=[sr_b])
                S.op("dve", I("tensor_tensor", out=Ebf[d][:, 4:8, 1:NCH + 1], in0=p0, in1=p1, op=ALU.add),
                     reads=[sr_b], writes=[e_b[d]])
                for k in range(8):
                    pt, pb, _ = G.PS.next()
                    mm = []
                    for ri in range(2):
                        for jc in range(4):
                            mm.append(I("matmul", out=pt[:, 0:128], lhsT=XBD[:, ri, jc * 8 + k, :], rhs=CBD[:, ri, jc, :],
                                        start=(ri == 0 and jc == 0), stop=(ri == 1 and jc == 3)))
                    S.op("pe", mm, reads=[xbb, cbd_b], writes=[pb])
                    if k == 0 and d == 0:
                        S.op("dve", I("tensor_copy", out=K0[:], in_=pt[:, 0:128]), reads=[pb], writes=[k_b])
                    elif k == 0:
                        S.op("dve", I("tensor_tensor", out=Kf[:, 0, :], in0=pt[:, 0:128], in1=K0[:], op=ALU.add), reads=[pb, k_b], writes=[k_b])
                    else:
                        S.op("act", I("activation", out=Kf[:, d * 8 + k, :], in_=pt[:, 0:128], func=AF.Copy), reads=[pb], writes=[k_b])

            W3bd = [W3t[d][:].rearrange("p (j s n) -> p j s n", j=8, s=8) for d in range(2)]
            for ri_, (c0, n) in enumerate(RANGES):
                b0 = (ri_ % 2) * 4
                for s in range(8):
                    off = b0 * 512 + s * 256
                    mm = []
                    for s2 in range(8):
                        kk = (s - s2) if s2 <= s else 8 + (s2 - s)
                        mm.append(I("matmul", out=ps[:, off:off + n], lhsT=Kf[:, kk, :], rhs=Ub[:, c0 * 8 + s2:(c0 + n) * 8:8],
                                    start=(s2 == 0), stop=False))
                    cp = c0 + 32 if c0 < 512 else 0
                    for j in range(8):
                        mm.append(I("matmul", out=ps[:, off:off + n], lhsT=W3bd[0][:, j, s, :], rhs=Ebf[0][:, j, cp:cp + n],
                                    start=False, stop=False))
                    lo = NCH - 1 - (c0 + n - 1)
                    for j in range(8):
                        mm.append(I("matmul", out=ps[:, off:off + n], lhsT=W3bd[1][:, j, s, :], rhs=Ebf[1][:, j, lo:lo + n][:, ::-1],
                                    start=False, stop=(j == 7)))
                    S.op("pe", mm, reads=[k_b, Ub_b, w3_b[0], w3_b[1], e_b[0], e_b[1]], writes=psb[b0:b0 + 4])
                pv = ps[:, b0 * 512:(b0 + 4) * 512].rearrange("p (s c) -> p s c", c=256)[:, :, 0:n]
                S.op("dve", I("scalar_tensor_tensor", out=yt[:, c0 * 8:(c0 + n) * 8].rearrange("p (c s) -> p s c", s=8),
                              in0=Ut[:, c0 * 8:(c0 + n) * 8].rearrange("p (c s) -> p s c", s=8), scalar=misc[:, gt:gt + 1],
                              in1=pv, op0=ALU.mult, op1=ALU.add), reads=psb[b0:b0 + 4] + [U_b, cst_b], writes=[y_b])
            S.op("act", I("activation", out=gst[:, :], in_=yt[:, :], func=AF.Gelu), reads=[y_b], writes=[gst_b])
            S.dma("pool", gsem, [I("dma_start", out=G.GS5[gt * 128:(gt + 1) * 128, :], in_=gst[:, :])],
                  reads=[gst_b], writes=[scr_buf(G, "GS5", gt)])
            if "S5Y" in G.dbg:
                dsem = S.new_dma_sem("dbg")
                S.dma("sp", dsem, [I("dma_start", out=G.dbg["S5Y"][gt * 128:(gt + 1) * 128, :], in_=yt[:, :])], reads=[y_b])

        S.barrier()
        S.release_to(mk)
    mk = S.mark()
    with ExitStack() as p5:
        sbt = lambda name, shape, dt: p5.enter_context(nc.sbuf_tensor(U(name), list(shape), dt))
        misc = sbt("s5misc2", [128, 8], F32)
        cst_b = Buf("s5cst2")
        csem = S.new_dma_sem("s5c2")
        S.dma("sp", csem, [I("dma_start", out=misc[:], in_=G.s5misc[l, :, :])], writes=[cst_b])
        wgf = sbt("wgf", [128, 4, 512], F32)
        wgb = sbt("wgb", [128, 4, 512], BF16); wg_b = Buf("wg")
        wsem = S.new_dma_sem("wg")
        S.dma("sp", wsem, [I("dma_start", out=wgf[:], in_=G.w_glu[l].rearrange("(c p) n -> p c n", p=128))], writes=[wg_b])
        S.op("pool", I("tensor_copy", out=wgb[:], in_=wgf[:]), reads=[wg_b], writes=[wg_b])
        ZSr = Ring(S, p5, "zs", 2, [128, 512], BF16, dma=True)
        SGr = Ring(S, p5, "sg5", 2, [128, 512], BF16)
        YOr = Ring(S, p5, "yo5", 2, [128, 512], BF16, dma=True, fresh=True)
        GTr = Ring(S, p5, "gtr", 2, [128, 4, 512], BF16, dma=True)
        for (t0, tn) in TG:
            gT, gTb, gTs = GTr.next()
            S.dma("sp", gTs, [I("dma_start", out=gT[:, :, 0:tn], in_=G.GS5[:, t0:t0 + tn].rearrange("(c p) n -> p c n", p=128))],
                  reads=[scr_buf(G, "GS5", i) for i in range(4)], writes=[gTb])
            for mo in range(4):
                zs, zsb, zss = ZSr.next()
                S.dma("sp", zss, [I("dma_start", out=zs[:, 0:tn], in_=G.ZS[mo * 128:(mo + 1) * 128, t0:t0 + tn])],
                      reads=[scr_buf(G, "ZS", mo)], writes=[zsb])
                pt, pb, _ = G.PS.next()
                S.op("pe", [I("matmul", out=pt[:, 0:tn], lhsT=wgb[:, kc, mo * 128:(mo + 1) * 128], rhs=gT[:, kc, 0:tn],
                              start=(kc == 0), stop=(kc == 3)) for kc in range(4)], reads=[wg_b, gTb], writes=[pb])
                sg, sgb, _ = SGr.next()
                S.op("act", I("activation", out=sg[:, 0:tn], in_=pt[:, 0:tn], func=AF.Sigmoid, bias=misc[:, 4 + mo:5 + mo]),
                     reads=[pb, cst_b], writes=[sgb])
                S.op("pool", I("tensor_tensor", out=sg[:, 0:tn], in0=sg[:, 0:tn], in1=zs[:, 0:tn], op=ALU.mult),
                     reads=[sgb, zsb], writes=[sgb])
                yo, yob, yos = YOr.next()
                S.op("dve", I("tensor_tensor", out=yo[:, 0:tn], in0=sg[:, 0:tn], in1=gT[:, mo, 0:tn], op=ALU.mult),
                     reads=[sgb, gTb], writes=[yob])
                S.dma("pool", yos, [I("dma_start", out=G.YG[2, mo * 128:(mo + 1) * 128, t0:t0 + tn], in_=yo[:, 0:tn])],
                      reads=[yob], writes=[scr_buf(G, "YG2", mo)])
        if "YG" in G.dbg:
            S.barrier()
            dsem = S.new_dma_sem("dbg")
            S.dma("sp", dsem, [I("dma_start", out=G.dbg["YG"][:, :, :], in_=G.YG[:, :, :])])
        S.barrier()
        S.release_to(mk)


def phase6_merge(G, l):
    nc, S = G.nc, G.S
    ps, psb = G.psall, G.psb
    last = (l == DEPTH - 1)
    mk = S.mark()
    with ExitStack() as p6:
        sbt = lambda name, shape, dt: p6.enter_context(nc.sbuf_tensor(U(name), list(shape), dt))
        wbr = sbt("wbr", [128, 3, 4, D], BF16)
        wou = sbt("wou", [128, 8, D], BF16)
        w_b = Buf("w6")
        STG = Ring(S, p6, "stg6", 2, [128, 4, D], F32, dma=True)
        for n in range(3):
            st, stb, sts = STG.next()
            S.dma("sp", sts, [I("dma_start", out=st[:], in_=G.w_branch[l, n].rearrange("(c p) n -> p c n", p=128))], writes=[stb])
            S.op("pool", I("tensor_copy", out=wbr[:, n], in_=st[:]), reads=[stb], writes=[w_b])
        for hf in range(2):
            st, stb, sts = STG.next()
            S.dma("sp", sts, [I("dma_start", out=st[:], in_=G.w_out[l, hf * 512:(hf + 1) * 512, :].rearrange("(c p) n -> p c n", p=128))],
                  writes=[stb])
            S.op("pool", I("tensor_copy", out=wou[:, hf * 4:(hf + 1) * 4], in_=st[:]), reads=[stb], writes=[w_b])
        if last:
            fgr = sbt("fgr", [1, D], F32)
            fgbc = sbt("fgbc", [128, D], F32)
            fg_b = Buf("fg")
            fsem = S.new_dma_sem("fg")
            S.dma("sp", fsem, [I("dma_start", out=fgr[:], in_=G.final_g[:, :])], writes=[fg_b])
            for n in range(2):
                pt, pb, _ = G.PS.next()
                S.op("pe", I("matmul", out=pt[:, :], lhsT=G.ones_f[0:1, :], rhs=fgr[0:1, n * 512:(n + 1) * 512], start=True, stop=True),
                     reads=[fg_b, G.b_const], writes=[pb])
                S.op("dve", I("tensor_copy", out=fgbc[:, n * 512:(n + 1) * 512], in_=pt[:, :]), reads=[pb], writes=[fg_b])
            junk = sbt("junk6", [128, D], BF16); junk_b = Buf("junk6")
            st4 = Ring(S, p6, "st6", 4, [128, 4], F32)
        YGr = Ring(S, p6, "yg6", 2, [128, 3, 4, 512], BF16, dma=True)
        SGr = Ring(S, p6, "sg6", 3, [128, 3, 512], BF16, dma=True)
        MG = Ring(S, p6, "mg6", 2, [128, 8, 512], BF16)
        TM = Ring(S, p6, "tm6", 2, [128, 3, 512], F32)
        XR_ = Ring(S, p6, "x6", 3, [128, D], F32, dma=True)
        XO = Ring(S, p6, "xo6", 2, [128, D], F32, dma=True, fresh=True)
        groups = TG[:8] if last else TG
        for (t0, tn) in groups:
            v = 0 if t0 < SEQ else 1
            yg, ygb, ygs = YGr.next()
            S.dma("sp", ygs, [I("dma_start", out=yg[:, n, :, 0:tn], in_=G.YG[n, :, t0:t0 + tn].rearrange("(c p) t -> p c t", p=128))
                              for n in range(3)],
                  reads=[scr_buf(G, "YG%d" % n, i) for n in range(3) for i in range(4)], writes=[ygb])
            mg, mgb, _ = MG.next()
            for dc in range(8):
                sg, sgb, sgs = SGr.next()
                S.dma("sp", sgs, [I("dma_start", out=sg[:, :, 0:tn],
                                    in_=G.SG[:, t0:t0 + tn].rearrange("(n c p) t -> c p n t", n=3, c=8)[dc])],
                      reads=[scr_buf(G, "SG", n * 8 + dc) for n in range(3)], writes=[sgb])
                pts = []
                for n in range(3):
                    pt, pb, _ = G.PS.next()
                    S.op("pe", [I("matmul", out=pt[:, 0:tn], lhsT=wbr[:, n, kc, dc * 128:(dc + 1) * 128], rhs=yg[:, n, kc, 0:tn],
                                  start=(kc == 0), stop=(kc == 3)) for kc in range(4)], reads=[w_b, ygb], writes=[pb])
                    pts.append((pt, pb))
                tm, tmb, _ = TM.next()
                for n in range(3):
                    S.op("dve", I("tensor_tensor", out=tm[:, n, 0:tn], in0=pts[n][0][:, 0:tn], in1=sg[:, n, 0:tn], op=ALU.mult),
                         reads=[pts[n][1], sgb], writes=[tmb])
                S.op("pool", I("tensor_tensor", out=tm[:, 0, 0:tn], in0=tm[:, 0, 0:tn], in1=tm[:, 1, 0:tn], op=ALU.add),
                     reads=[tmb], writes=[tmb])
                S.op("pool", I("tensor_tensor", out=mg[:, dc, 0:tn], in0=tm[:, 0, 0:tn], in1=tm[:, 2, 0:tn], op=ALU.add),
                     reads=[tmb], writes=[mgb])
            for tt in range(tn // 128):
                ti = t0 // 128 + tt
                xt, xb, xs = XR_.next()
                S.dma("sp", xs, [I("dma_start", out=xt[:], in_=x_src(G, l, ti))], reads=[G.xres_b[ti]], writes=[xb])
                xo, xob, xos = XO.next()
                for n in range(2):
                    pt, pb, _ = G.PS.next()
                    S.op("pe", [I("matmul", out=pt[:, :], lhsT=mg[:, dc, tt * 128:(tt + 1) * 128], rhs=wou[:, dc, n * 512:(n + 1) * 512],
                                  start=(dc == 0), stop=(dc == 7)) for dc in range(8)], reads=[w_b, mgb], writes=[pb])
                    S.op("dve", I("tensor_tensor", out=xo[:, n * 512:(n + 1) * 512], in0=pt[:, :],
                                  in1=G.gtbc[:, v, n * 512:(n + 1) * 512], op=ALU.mult), reads=[pb, G.gtbc_b], writes=[xob])
                S.op("pool", I("tensor_tensor", out=xo[:, :], in0=xo[:, :], in1=xt[:, :], op=ALU.add), reads=[xob, xb], writes=[xob])
                if not last:
                    S.dma("pool", xos, [I("dma_start", out=G.xres[ti * 128:(ti + 1) * 128, :], in_=xo[:, :])],
                          reads=[xob], writes=[G.xres_b[ti]])
                else:
                    st, stb, _ = st4.next()
                    S.op("act", I("activation", out=junk[:], in_=xo[:], func=AF.Square, accum_out=st[:, 0:1]),
                         reads=[xob], writes=[junk_b, stb])
                    S.op("act", I("activation", out=st[:, 1:2], in_=st[:, 0:1], func=AF.Sqrt, scale=1.0 / D, bias=G.eps_col[:, 0:1]),
                         reads=[stb, G.b_const], writes=[stb])
                    S.op("dve", I("reciprocal", out=st[:, 2:3], in_=st[:, 1:2]), reads=[stb], writes=[stb])
                    S.op("dve", I("scalar_tensor_tensor", out=xo[:, :], in0=xo[:, :], scalar=st[:, 2:3], in1=fgbc[:, :],
                                  op0=ALU.mult, op1=ALU.mult), reads=[xob, stb, fg_b], writes=[xob])
                    S.dma("pool", xos, [I("dma_start", out=G.out[ti * 128:(ti + 1) * 128, :], in_=xo[:, :])],
                          reads=[xob], writes=[G.xres_b[ti]])
        if "XRES" in G.dbg:
            S.barrier()
            dsem = S.new_dma_sem("dbg")
            S.dma("sp", dsem, [I("dma_start", out=G.dbg["XRES"][:, :], in_=G.xres[:, :])])
        S.barrier()
        S.release_to(mk)


def scr_buf(G, name, i):
    return Buf("scr")


def x_src(G, l, ti):
    if l == 0:
        if ti < 32:
            return G.x_in[ti * 128:(ti + 1) * 128, :]
        return G.ctx_in[(ti - 32) * 128:(ti - 31) * 128, :]
    return G.xres[ti * 128:(ti + 1) * 128, :]


def phase1_and_2(G, l, stop_after=None):
    nc, S, PS = G.nc, G.S, G.PS
    with ExitStack() as ph:
        psb = lambda name, shape, dt: ph.enter_context(nc.sbuf_tensor(U(name), list(shape), dt))
        hT = psb("hT", [128, 8, T], BF16)
        hT_b = [Buf("hT%d" % i) for i in range(NT)]
        mcols = psb("mcols", [128, 32], F32)
        gs = psb("gs", [128, 16], F32)
        mc_b = Buf("mcols")
        mk = S.mark()
        with ExitStack() as p1:
            p1sb = lambda name, shape, dt: p1.enter_context(nc.sbuf_tensor(U(name), list(shape), dt))
            mrow = [p1sb("mrow%d" % v, [1, 3 * D], F32) for v in range(2)]
            mrow_b = [Buf("mrow%d" % v) for v in range(2)]
            brow = p1sb("brow", [1, 3 * D], F32)
            brow_b = Buf("brow")
            gcol_sb = p1sb("gcol_sb", [128, 8], F32)
            WM = Ring(S, p1, "wm", 2, [128, 8, 512], F32, dma=True)
            bsem = S.new_dma_sem("brow")
            S.dma("sp", bsem, [I("dma_start", out=brow[:], in_=G.b_mod[l, :, :]),
                               I("dma_start", out=gcol_sb[:], in_=G.gcol[l, :, :])], writes=[brow_b])
            for n in range(6):
                wt, wb, ws = WM.next()
                S.dma("sp", ws, [I("dma_start", out=wt[:],
                                   in_=G.w_mod[l, :, n * 512:(n + 1) * 512].rearrange("(c p) n -> p c n", p=128))],
                      writes=[wb])
                for v in range(2):
                    pt, pb, _ = PS.next()
                    S.op("pe", [I("matmul", out=pt[0:1, :], lhsT=G.silu_c[:, v * 8 + c:v * 8 + c + 1], rhs=wt[:, c, :],
                                  start=(c == 0), stop=(c == 7)) for c in range(8)],
                         reads=[wb, G.b_const], writes=[pb])
                    S.op("dve", I("tensor_tensor", out=mrow[v][0:1, n * 512:(n + 1) * 512], in0=pt[0:1, :],
                                  in1=brow[0:1, n * 512:(n + 1) * 512], op=ALU.add),
                         reads=[pb, brow_b], writes=[mrow_b[v]])
            pt, pb, _ = PS.next()
            mm = []
            for v in range(2):
                for w in range(2):
                    for c in range(8):
                        j = v * 16 + w * 8 + c
                        mm.append(I("matmul", out=pt[:, j:j + 1],
                                    lhsT=mrow[v][0:1, w * 1024 + c * 128: w * 1024 + (c + 1) * 128],
                                    rhs=G.ones_f[0:1, 0:1], start=True, stop=True))
            S.op("pe", mm, reads=[mrow_b[0], mrow_b[1], G.b_const], writes=[pb])
            S.op("dve", I("tensor_copy", out=mcols[:], in_=pt[:, 0:32]), reads=[pb], writes=[mc_b])
            for v in range(2):
                S.op("dve", I("scalar_tensor_tensor", out=gs[:, v * 8:(v + 1) * 8], in0=mcols[:, v * 16 + 8:v * 16 + 16],
                              scalar=1.0, in1=gcol_sb[:], op0=ALU.add, op1=ALU.mult),
                     reads=[mc_b, brow_b], writes=[mc_b])
            for v in range(2):
                for n in range(2):
                    pt, pb, _ = PS.next()
                    S.op("pe", I("matmul", out=pt[:, :], lhsT=G.ones_f[0:1, :],
                                 rhs=mrow[v][0:1, 2048 + n * 512:2048 + (n + 1) * 512], start=True, stop=True),
                         reads=[mrow_b[v], G.b_const], writes=[pb])
                    S.op("dve", I("tensor_copy", out=G.gtbc[:, v, n * 512:(n + 1) * 512], in_=pt[:, :]),
                         reads=[pb], writes=[G.gtbc_b])
            if "mrow" in G.dbg:
                dsem = S.new_dma_sem("dbg")
                S.dma("sp", dsem, [I("dma_start", out=G.dbg["mrow"][0:1, :], in_=mrow[0][:]),
                                   I("dma_start", out=G.dbg["mrow"][1:2, :], in_=mrow[1][:])], reads=mrow_b)
            S.barrier()
            S.release_to(mk)

        with ExitStack() as p1:
            XT = Ring(S, p1, "xt", 4, [128, D], F32, dma=True)
            XN = Ring(S, p1, "xn", 4, [128, D], BF16)
            junk = p1.enter_context(nc.sbuf_tensor(U("junk"), [128, D], BF16))
            junk_b = Buf("junk")
            st4 = Ring(S, p1, "st", 4, [128, 4], F32)
            def stage_a(ti):
                xt, xb, xs = XT.next()
                S.dma("sp", xs, [I("dma_start", out=xt[:], in_=x_src(G, l, ti))], reads=[G.xres_b[ti]], writes=[xb])
                st, stb, _ = st4.next()
                S.op("act", I("activation", out=junk[:], in_=xt[:], func=AF.Square, accum_out=st[:, 0:1]),
                     reads=[xb], writes=[junk_b, stb])
                S.op("act", I("activation", out=st[:, 1:2], in_=st[:, 0:1], func=AF.Sqrt, scale=1.0 / D, bias=G.eps_col[:, 0:1]),
                     reads=[stb, G.b_const], writes=[stb])
                S.op("dve", I("reciprocal", out=st[:, 2:3], in_=st[:, 1:2]), reads=[stb], writes=[stb])
                xn, xnb, _ = XN.next()
                S.op("dve", I("tensor_scalar", out=xn[:], in0=xt[:], scalar1=st[:, 2:3], scalar2=None, op0=ALU.mult),
                     reads=[xb, stb], writes=[xnb])
                return xn, xnb

            def stage_b(ti, xn, xnb):
                v = 0 if ti < 32 else 1
                for half in range(2):
                    pt, pb, _ = PS.next()
                    ptb = pt.bitcast(BF16)
                    S.op("pe", [I("transpose", out=ptb[:, cc * 128:(cc + 1) * 128],
                                  in_=xn[:, (half * 4 + cc) * 128:(half * 4 + cc + 1) * 128], identity=G.ident[:])
                                for cc in range(4)], reads=[xnb, G.b_const], writes=[pb])
                    for cc in range(4):
                        c = half * 4 + cc
                        if cc % 2 == 1:
                            S.op("dve", I("tensor_scalar", out=hT[:, c, ti * 128:(ti + 1) * 128],
                                          in0=ptb[:, cc * 128:(cc + 1) * 128],
                                          scalar1=gs[:, v * 8 + c:v * 8 + c + 1],
                                          scalar2=mcols[:, v * 16 + c:v * 16 + c + 1], op0=ALU.mult, op1=ALU.add),
                                 reads=[pb, mc_b], writes=[hT_b[ti]])
                        else:
                            S.op("act", I("activation", out=hT[:, c, ti * 128:(ti + 1) * 128],
                                          in_=ptb[:, cc * 128:(cc + 1) * 128], func=AF.Identity,
                                          scale=gs[:, v * 8 + c:v * 8 + c + 1],
                                          bias=mcols[:, v * 16 + c:v * 16 + c + 1]),
                                 reads=[pb, mc_b], writes=[hT_b[ti]])

            pend = [stage_a(0), stage_a(1)]
            for ti in range(NT):
                if ti + 2 < NT:
                    pend.append(stage_a(ti + 2))
                xn, xnb = pend.pop(0)
                stage_b(ti, xn, xnb)
            if "hT" in G.dbg:
                dsem = S.new_dma_sem("dbg")
                S.dma("sp", dsem, [I("dma_start", out=G.dbg["hT"][:, :, :], in_=hT[:])], reads=hT_b)
            S.barrier()
            S.release_to(mk)
        if stop_after == "p1":
            return

        mk = S.mark()
        with ExitStack() as p2:
            WF = Ring(S, p2, "wf", 2, [128, 8, 512], F32, dma=True)
            WB = Ring(S, p2, "wb", 4, [128, 8, 512], BF16)
            RC = Ring(S, p2, "rc", 2, [128, 2, 512], F32, dma=True)
            OB = Ring(S, p2, "ob", 8, [128, 512], BF16, dma=True)
            OF = Ring(S, p2, "of", 6, [128, 512], F32, dma=True)
            TMP = Ring(S, p2, "tmp", 2, [128, 2, 512], F32)

            def load_group(g):
                wf, wfb, wfs = WF.next()
                S.dma("sp", wfs, [I("dma_start", out=wf[:], in_=G.w_in[l, :, g * 512:(g + 1) * 512]
                                    .rearrange("(c p) n -> p c n", p=128))], writes=[wfb])
                wb, wbb, _ = WB.next()
                S.op("pool", I("tensor_copy", out=wb[:, 0:4, :], in_=wf[:, 0:4, :]), reads=[wfb], writes=[wbb])
                S.op("pool", I("tensor_copy", out=wb[:, 4:8, :], in_=wf[:, 4:8, :]), reads=[wfb], writes=[wbb])
                return wb, wbb

            def proj(pt, pb, wb, wbb, j, t0, tn):
                tis = list(range(t0 // 128, (t0 + tn) // 128))
                S.op("pe", [I("matmul", out=pt[:, 0:tn], lhsT=wb[:, c, j * 128:(j + 1) * 128], rhs=hT[:, c, t0:t0 + tn],
                              start=(c == 0), stop=(c == 7)) for c in range(8)],
                     reads=[wbb] + [hT_b[i] for i in tis], writes=[pb])

            wq = [load_group(g) for g in range(4)]
            for (t0, tn) in TG:
                rc, rcb, rcs = RC.next()
                S.dma("sp", rcs, [I("dma_start", out=rc[:, 0, 0:tn], in_=G.ropeC[:, t0:t0 + tn]),
                                  I("dma_start", out=rc[:, 1, 0:tn], in_=G.ropeS[:, t0:t0 + tn])], writes=[rcb])
                for qk in range(2):
                    dst = G.QT if qk == 0 else G.KT
                    for j in range(4):
                        pa, pab, _ = PS.next()
                        proj(pa, pab, wq[2 * qk][0], wq[2 * qk][1], j, t0, tn)
                        pbt, pbb, _ = PS.next()
                        proj(pbt, pbb, wq[2 * qk + 1][0], wq[2 * qk + 1][1], j, t0, tn)
                        tmp, tmpb, _ = TMP.next()
                        S.op("dve", I("tensor_tensor", out=tmp[:, 0, 0:tn], in0=pa[:, 0:tn], in1=rc[:, 0, 0:tn], op=ALU.mult),
                             reads=[pab, rcb], writes=[tmpb])
                        S.op("dve", I("tensor_tensor", out=tmp[:, 1, 0:tn], in0=pbt[:, 0:tn], in1=rc[:, 1, 0:tn], op=ALU.mult),
                             reads=[pbb, rcb], writes=[tmpb])
                        ob, obb, obs = OB.next()
                        S.op("pool", I("tensor_tensor", out=ob[:, 0:tn], in0=tmp[:, 0, 0:tn], in1=tmp[:, 1, 0:tn], op=ALU.add),
                             reads=[tmpb], writes=[obb])
                        S.dma("sp", obs, [I("dma_start", out=dst[j * 128:(j + 1) * 128, t0:t0 + tn], in_=ob[:, 0:tn])],
                              reads=[obb], writes=[scr_buf(G, "QT" if qk == 0 else "KT", j)])
            wv, wvb = load_group(4)
            for ti in range(NT):
                pt, pb, _ = PS.next()
                S.op("pe", [I("matmul", out=pt[:, :], lhsT=hT[:, c, ti * 128:(ti + 1) * 128], rhs=wv[:, c, :],
                              start=(c == 0), stop=(c == 7)) for c in range(8)],
                     reads=[wvb, hT_b[ti]], writes=[pb])
                ob, obb, obs = OB.next()
                S.op("act", I("activation", out=ob[:, :], in_=pt[:, :], func=AF.Copy), reads=[pb], writes=[obb])
                S.dma("sp", obs, [I("dma_start", out=G.Vtm[ti * 128:(ti + 1) * 128, :], in_=ob[:, :])],
                      reads=[obb], writes=[scr_buf(G, "V", ti)])
            plan = [(5, G.ZA, 0, "ZA", 0), (7, G.ZR, 0, "ZR", 0), (9, G.ZS, 0, "ZS", 0),
                    (6, G.XR, 1, "XR", 0), (8, G.US, 1, "US", 0)]
            plan += [(10 + i, G.SG, 2, "SG", i * 4) for i in range(6)]
            for (g, dst, kind, nm, boff) in plan:
                wg, wgb = load_group(g)
                for j in range(4):
                    for (t0, tn) in TG:
                        pt, pb, _ = PS.next()
                        proj(pt, pb, wg, wgb, j, t0, tn)
                        if kind == 1:
                            ob, obb, obs = OF.next()
                            S.op("dve", I("tensor_copy", out=ob[:, 0:tn], in_=pt[:, 0:tn]), reads=[pb], writes=[obb])
                        else:
                            ob, obb, obs = OB.next()
                            S.op("act", I("activation", out=ob[:, 0:tn], in_=pt[:, 0:tn],
                                          func=(AF.Silu if kind == 0 else AF.Sigmoid)), reads=[pb], writes=[obb])
                        r0 = (boff + j) * 128
                        S.dma("sp", obs, [I("dma_start", out=dst[r0:r0 + 128, t0:t0 + tn], in_=ob[:, 0:tn])],
                              reads=[obb], writes=[scr_buf(G, nm, boff + j)])
            if "QT" in G.dbg:
                S.barrier()
                dsem = S.new_dma_sem("dbg")
                for nm in ("QT", "KT", "Vtm", "ZA", "XR", "SG"):
                    if nm in G.dbg:
                        S.dma("sp", dsem, [I("dma_start", out=G.dbg[nm][:, :], in_=getattr(G, nm)[:, :])])
            S.barrier()
            S.release_to(mk)
        if stop_after == "p2":
            return


def _rope_tables():
    rows = SEQ // 64
    r = np.repeat(np.arange(rows, dtype=np.float32), 64)
    col = np.tile(np.arange(64, dtype=np.float32), rows)
    inv = (10000.0 ** (-np.arange(16, dtype=np.float32) / 16)).astype(np.float32)
    ang = np.concatenate([r[:, None] * inv, col[:, None] * inv], axis=-1).astype(np.float32)
    cos = np.cos(ang).T.astype(np.float32)
    sin = np.sin(ang).T.astype(np.float32)
    C = np.ones((128, T), np.float32)
    Sg = np.zeros((128, T), np.float32)
    for p in range(128):
        j = p % 64
        C[p, :SEQ] = cos[j % 32]
        Sg[p, :SEQ] = -sin[j % 32] if j < 32 else sin[j % 32]
    return C, Sg


def _w_in_ext(w_in):
    L = w_in.shape[0]
    perm = np.concatenate([np.arange(0, 64, 2), np.arange(1, 64, 2)])
    swp = np.concatenate([np.arange(1, 64, 2), np.arange(0, 64, 2)])
    idx = []
    for base in (0, 512):
        p_cols = np.concatenate([base + b * 64 + perm for b in range(8)])
        s_cols = np.concatenate([base + b * 64 + swp for b in range(8)])
        idx += [p_cols, s_cols]
    idx.append(np.arange(1024, 7168))
    idx = np.concatenate(idx)
    return np.ascontiguousarray(w_in[:, :, idx])


def make_in_maps(inputs):
    f = lambda a: np.ascontiguousarray(np.asarray(a, dtype=np.float32))
    x = f(inputs["x"]); ctx = f(inputs["ctx"]); c = f(inputs["c"]); c_ctx = f(inputs["c_ctx"])
    w_in_e = _w_in_ext(f(inputs["w_in"]))
    C, Sg = _rope_tables()
    ident = np.eye(128, dtype=np.float32).astype(ml_dtypes.bfloat16)
    gcol = np.ascontiguousarray(f(inputs["norm_g"]).reshape(DEPTH, 8, 128).transpose(0, 2, 1))
    shared = dict(
        w_mod=f(inputs["w_mod"]), b_mod=f(inputs["b_mod"]).reshape(DEPTH, 1, 3 * D), gcol=gcol, w_in=w_in_e,
        ropeC=C, ropeS=Sg, ident=ident, final_g=f(inputs["final_g"]).reshape(1, D),
        lam_qk=f(inputs["lam_qk"]).reshape(DEPTH, 1, 256), subln=f(inputs["subln_g"]).reshape(DEPTH, 128, 1),
    )
    lrup = np.zeros((DEPTH, 128, 44), np.float32)
    cw = f(inputs["conv_w"]); cb = f(inputs["conv_b"])
    for ct in range(4):
        for k in range(4):
            lrup[:, :, ct * 4 + k] = cw[:, k, ct * 128:(ct + 1) * 128]
        lrup[:, :, 16 + ct] = cb[:, ct * 128:(ct + 1) * 128]
        for d in range(2):
            lrup[:, :, 20 + d * 4 + ct] = f(inputs["lru_ba"])[:, d, ct * 128:(ct + 1) * 128]
            lrup[:, :, 28 + d * 4 + ct] = f(inputs["lru_bx"])[:, d, ct * 128:(ct + 1) * 128]
            lrup[:, :, 36 + d * 4 + ct] = f(inputs["lru_lam"])[:, d, ct * 128:(ct + 1) * 128]
    lruw = np.zeros((DEPTH, 2, 2, 4, 128, 128), np.float32)
    for gi, nm in enumerate(("lru_wa", "lru_wx")):
        w = f(inputs[nm])
        for ct in range(4):
            for j in range(2):
                lruw[:, gi, :, ct, j * 64:(j + 1) * 64, j * 64:(j + 1) * 64] = w[:, :, 2 * ct + j]
    shared.update(lrup=lrup, lruw=lruw)
    lre = f(inputs["s5_lam_re"]); lim = f(inputs["s5_lam_im"]); ldt = f(inputs["s5_log_dt"])
    bre = f(inputs["s5_b_re"]); bim = f(inputs["s5_b_im"]); cre = f(inputs["s5_c_re"]); cim = f(inputs["s5_c_im"])
    L = DEPTH
    s5H = np.zeros((L, 2, 4, 128, 4, 64), np.float32)
    lre_g = lre.reshape(L, 2, 4, 8, 64); lim_g = lim.reshape(L, 2, 4, 8, 64)
    s5H[:, :, :, :, 0, :] = np.repeat(lre_g, 16, axis=3)
    s5H[:, :, :, :, 1, :] = np.repeat(lim_g, 16, axis=3)
    s5H[:, :, :, :, 2, :] = bre.reshape(L, 2, 4, 8, 64, 16).transpose(0, 1, 2, 3, 5, 4).reshape(L, 2, 4, 128, 64)
    s5H[:, :, :, :, 3, :] = bim.reshape(L, 2, 4, 8, 64, 16).transpose(0, 1, 2, 3, 5, 4).reshape(L, 2, 4, 128, 64)
    def nlay(a):
        return a.reshape(L, 2, 4, 8, 4, 16).transpose(0, 1, 2, 3, 5, 4).reshape(L, 2, 4, 128, 4)
    s5N = np.concatenate([nlay(lre_g), nlay(lim_g)], axis=-1)
    def bnlay(a):
        return a.reshape(L, 2, 4, 8, 4, 16, 16).transpose(0, 1, 2, 3, 5, 4, 6).reshape(L, 2, 4, 128, 4, 16)
    s5BN = np.stack([bnlay(bre), bnlay(bim)], axis=4)
    def cnlay(a):
        return a.reshape(L, 4, 8, 16, 4, 16).transpose(0, 1, 2, 5, 4, 3).reshape(L, 4, 128, 4, 16)
    s5CN = np.stack([cnlay(cre), cnlay(cim)], axis=3)
    s5dt = np.zeros((L, 128, 8), np.float32)
    for d in range(2):
        for gt in range(4):
            s5dt[:, :, d * 4 + gt] = np.repeat(ldt[:, d, gt * 8:(gt + 1) * 8], 16, axis=1)
    s5misc = np.zeros((L, 128, 8), np.float32)
    s5misc[:, :, 0:4] = f(inputs["s5_d"]).reshape(L, 4, 128).transpose(0, 2, 1)
    s5misc[:, :, 4:8] = f(inputs["s5_b_glu"]).reshape(L, 4, 128).transpose(0, 2, 1)
    tabs = np.zeros((128, 36 + 512 + 544 + 128), np.float32)
    tabs[:, 0:36] = np.tile(np.arange(9, dtype=np.float32), 4)[None, :]
    tabs[:, 36:548] = np.repeat(np.arange(8, dtype=np.float32), 64)[None, :]
    tabs[:, 548:1092] = np.arange(544, dtype=np.float32)[None, :]
    mk = np.zeros((128, 8, 16), np.float32)
    for p in range(128):
        mk[p, p // 16, :] = 1.0
    tabs[:, 1092:1220] = mk.reshape(128, 128)
    shared.update(s5H=s5H, s5N=np.ascontiguousarray(s5N), s5BN=np.ascontiguousarray(s5BN), s5CN=np.ascontiguousarray(s5CN),
                  s5dt=s5dt, s5misc=s5misc, s5tab=tabs, w_glu=f(inputs["s5_w_glu"]),
                  w_branch=f(inputs["w_branch"]), w_out=f(inputs["w_out"]))
    maps = []
    for b in range(8):
        cc = np.concatenate([c[b].reshape(8, 128).T, c_ctx.reshape(8, 128).T], axis=1)
        m = dict(shared)
        m.update(x=x[b], ctx=ctx[b], ccol=np.ascontiguousarray(cc))
        maps.append(m)
    return maps


def kernel(**inputs):
    nc = build_program()
    maps = make_in_maps(inputs)
    res = run_bass_kernel_spmd(nc, maps, core_ids=list(range(8)))
    return np.stack([np.asarray(r["out"], dtype=np.float32) for r in res.results], axis=0)
```
